# Optimizing a Trainium2 kernel written in Bass

```python
import math
import jax, jax.numpy as jnp
from jax import lax
import numpy as np

D_MODEL = 1024
BATCH = 2
SEQ = 8192
DEPTH = 1

CHUNK = 64
SSD_HEADS = 16
SSD_HEAD_DIM = 64
SSD_WIDTH = SSD_HEADS * SSD_HEAD_DIM
SSD_GROUPS = 2
SSD_HPG = SSD_HEADS // SSD_GROUPS
SSD_STATE = 128
CONV_WIDTH = 4
CONV_CH = SSD_WIDTH + 2 * SSD_GROUPS * SSD_STATE
DA_HEADS = 8
DA_HEAD_DIM = 64
DA_QK = DA_HEADS * 2 * DA_HEAD_DIM
DA_WIDTH = DA_HEADS * 2 * DA_HEAD_DIM
ROPE_THETA = 500000.0
ROT_DIM = DA_HEAD_DIM // 4
Q_BLOCK = 128
N_BRANCH = 2
EPS = 1e-5
ALPHA = (2.0 * DEPTH) ** 0.25
BETA = (8.0 * DEPTH) ** -0.25
IN_SIZES = (SSD_WIDTH, CONV_CH, SSD_HEADS, DA_QK, DA_QK, DA_WIDTH, DA_WIDTH, N_BRANCH * D_MODEL)
IN_TOTAL = sum(IN_SIZES)

kernel_name = "hybrid_ssd_diffattn_gated_deepnorm"


def _split_points(sizes):
    pts, acc = [], 0
    for s in sizes[:-1]:
        acc += s
        pts.append(acc)
    return pts


def _rms(t, eps=EPS):
    tf = t.astype(jnp.float32)
    return tf * lax.rsqrt(jnp.mean(tf * tf, axis=-1, keepdims=True) + eps)


def _layernorm(t, g, b):
    tf = t.astype(jnp.float32)
    mu = jnp.mean(tf, axis=-1, keepdims=True)
    var = jnp.mean(jnp.square(tf - mu), axis=-1, keepdims=True)
    return ((tf - mu) * lax.rsqrt(var + EPS) * g + b).astype(t.dtype)


def _causal_conv(u, w, b):
    out = lax.conv_general_dilated(
        u, w[:, None, :].astype(u.dtype), window_strides=(1,),
        padding=[(CONV_WIDTH - 1, 0)], dimension_numbers=("NWC", "WIO", "NWC"),
        feature_group_count=u.shape[-1])
    return out + b


def _rope(t, cos, sin):
    half = ROT_DIM // 2
    c = cos[:, None, None, :]
    s = sin[:, None, None, :]
    t1, t2, tp = t[..., :half], t[..., half:ROT_DIM], t[..., ROT_DIM:]
    out = jnp.concatenate([t1 * c - t2 * s, t2 * c + t1 * s, tp.astype(jnp.float32)], axis=-1)
    return out.astype(t.dtype)


def _ssd(xs, dt, a, bm, cm):
    b, s = xs.shape[0], xs.shape[1]
    nc = s // CHUNK
    X = xs.reshape(b, nc, CHUNK, SSD_GROUPS, SSD_HPG, SSD_HEAD_DIM)
    dtc = dt.reshape(b, nc, CHUNK, SSD_GROUPS, SSD_HPG)
    a_dt = dtc * a.astype(jnp.float32).reshape(SSD_GROUPS, SSD_HPG)
    xdt = X * dtc[..., None]
    Bc = bm.reshape(b, nc, CHUNK, SSD_GROUPS, SSD_STATE)
    Cc = cm.reshape(b, nc, CHUNK, SSD_GROUPS, SSD_STATE)
    a_cs = jnp.cumsum(a_dt, axis=2)
    seg = a_cs[:, :, :, None] - a_cs[:, :, None, :]
    causal = jnp.tril(jnp.ones((CHUNK, CHUNK), dtype=bool))[:, :, None, None]
    L = jnp.exp(jnp.where(causal, seg, -jnp.inf))
    cb = jnp.einsum("bclgn,bcsgn->bclsg", Cc, Bc)
    y_diag = jnp.einsum("bclsgj,bcsgjp->bclgjp", cb[..., None] * L, xdt)
    decay = jnp.exp(a_cs[:, :, -1:] - a_cs)
    states = jnp.einsum("bclgn,bclgj,bclgjp->bcgjpn", Bc, decay, xdt)
    chunk_decay = jnp.exp(a_cs[:, :, -1])

    def step(h, inp):
        s_c, d_c = inp
        return d_c[..., None, None] * h + s_c, h

    h0 = jnp.zeros_like(states[:, 0])
    _, prev = lax.scan(step, h0, (jnp.moveaxis(states, 1, 0), jnp.moveaxis(chunk_decay, 1, 0)))
    prev = jnp.moveaxis(prev, 0, 1)
    y_off = jnp.einsum("bclgn,bcgjpn,bclgj->bclgjp", Cc, prev, jnp.exp(a_cs))
    return (y_diag + y_off).reshape(b, s, SSD_HEADS, SSD_HEAD_DIM)


def _diff_attention(q, k, v, lam):
    b, s = q.shape[0], q.shape[1]
    nb = s // Q_BLOCK
    scale = DA_HEAD_DIM ** -0.5
    qb = jnp.moveaxis(q.reshape(b, nb, Q_BLOCK, DA_HEADS, 2, DA_HEAD_DIM), 1, 0)
    k_chunk = jnp.arange(s) // CHUNK

    def block(args):
        qi, i = args
        q_chunk = (i * Q_BLOCK + jnp.arange(Q_BLOCK)) // CHUNK
        mask = k_chunk[None, :] <= q_chunk[:, None]
        sc = jnp.einsum("bqhmd,bkhmd->bhmqk", qi, k).astype(jnp.float32) * scale
        p = jax.nn.softmax(jnp.where(mask, sc, -jnp.inf), axis=-1)
        att = p[:, :, 0] - lam * p[:, :, 1]
        return jnp.einsum("bhqk,bkhe->bqhe", att.astype(v.dtype), v)

    out = lax.map(block, (qb, jnp.arange(nb)))
    return jnp.moveaxis(out, 0, 1).reshape(b, s, DA_HEADS, 2 * DA_HEAD_DIM)


def setup_inputs(seed: int = 0) -> dict:
    key = jax.random.key(seed)
    ks = jax.random.split(key, 20)
    f32 = jnp.float32
    x = jax.random.normal(ks[0], (BATCH, SEQ, D_MODEL), f32)
    w_in = jax.random.normal(ks[1], (DEPTH, D_MODEL, IN_TOTAL), f32) * D_MODEL ** -0.5
    b_gate = 0.02 * jax.random.normal(ks[2], (DEPTH, N_BRANCH * D_MODEL), f32)
    conv_w = 0.5 * jax.random.normal(ks[3], (DEPTH, CONV_WIDTH, CONV_CH), f32)
    conv_b = 0.02 * jax.random.normal(ks[4], (DEPTH, CONV_CH), f32)
    dt0 = jnp.exp(jax.random.uniform(ks[5], (DEPTH, SSD_HEADS), f32, math.log(1e-3), math.log(1e-1)))
    dt_bias = dt0 + jnp.log(-jnp.expm1(-dt0))
    a_log = jnp.log(jax.random.uniform(ks[6], (DEPTH, SSD_HEADS), f32, 1.0, 16.0))
    d_skip = 1.0 + 0.02 * jax.random.normal(ks[7], (DEPTH, SSD_HEADS), f32)
    ssd_norm_w = 1.0 + 0.02 * jax.random.normal(ks[8], (DEPTH, SSD_WIDTH), f32)
    lambda_q1 = 0.1 * jax.random.normal(ks[9], (DEPTH, DA_HEAD_DIM), f32)
    lambda_k1 = 0.1 * jax.random.normal(ks[10], (DEPTH, DA_HEAD_DIM), f32)
    lambda_q2 = 0.1 * jax.random.normal(ks[11], (DEPTH, DA_HEAD_DIM), f32)
    lambda_k2 = 0.1 * jax.random.normal(ks[12], (DEPTH, DA_HEAD_DIM), f32)
    subln_w = 1.0 + 0.02 * jax.random.normal(ks[13], (DEPTH, 2 * DA_HEAD_DIM), f32)
    w_a = jax.random.normal(ks[14], (DEPTH, SSD_WIDTH, D_MODEL), f32) * SSD_WIDTH ** -0.5 * BETA
    w_b = jax.random.normal(ks[15], (DEPTH, DA_WIDTH, D_MODEL), f32) * DA_WIDTH ** -0.5 * BETA
    w_o = jax.random.normal(ks[16], (DEPTH, D_MODEL, D_MODEL), f32) * D_MODEL ** -0.5 * BETA
    ln_g = 1.0 + 0.02 * jax.random.normal(ks[17], (DEPTH, D_MODEL), f32)
    ln_b = 0.02 * jax.random.normal(ks[18], (DEPTH, D_MODEL), f32)
    return {"x": x, "w_in": w_in, "b_gate": b_gate, "conv_w": conv_w, "conv_b": conv_b,
            "dt_bias": dt_bias, "a_log": a_log, "d_skip": d_skip, "ssd_norm_w": ssd_norm_w,
            "lambda_q1": lambda_q1, "lambda_k1": lambda_k1, "lambda_q2": lambda_q2,
            "lambda_k2": lambda_k2, "subln_w": subln_w, "w_a": w_a, "w_b": w_b, "w_o": w_o,
            "ln_g": ln_g, "ln_b": ln_b}


def reference(x, w_in, b_gate, conv_w, conv_b, dt_bias, a_log, d_skip, ssd_norm_w,
              lambda_q1, lambda_k1, lambda_q2, lambda_k2, subln_w, w_a, w_b, w_o,
              ln_g, ln_b):
    b, s, _ = x.shape
    pos = jnp.arange(s, dtype=jnp.float32)
    inv_freq = ROPE_THETA ** (-jnp.arange(0, ROT_DIM, 2, dtype=jnp.float32) / ROT_DIM)
    ang = pos[:, None] * inv_freq[None, :]
    cos, sin = jnp.cos(ang), jnp.sin(ang)

    for l in range(DEPTH):
        h = jnp.einsum("bsd,de->bse", x, w_in[l])
        z, xbc, dt_raw, q, k, v, g_b, g_merge = jnp.split(h, _split_points(IN_SIZES), axis=-1)

        xbc = jax.nn.silu(_causal_conv(xbc, conv_w[l], conv_b[l]))
        xs, bm, cm = jnp.split(xbc, [SSD_WIDTH, SSD_WIDTH + SSD_GROUPS * SSD_STATE], axis=-1)
        xs = xs.reshape(b, s, SSD_HEADS, SSD_HEAD_DIM)
        bm = bm.reshape(b, s, SSD_GROUPS, SSD_STATE)
        cm = cm.reshape(b, s, SSD_GROUPS, SSD_STATE)
        dt = jax.nn.softplus(dt_raw.astype(jnp.float32) + dt_bias[l].astype(jnp.float32))
        a = -jnp.exp(a_log[l].astype(jnp.float32))
        y_ssd = _ssd(xs, dt, a, bm, cm) + d_skip[l][:, None] * xs
        gated = (y_ssd.reshape(b, s, SSD_WIDTH) * jax.nn.silu(z)).reshape(
            b, s, SSD_GROUPS, SSD_WIDTH // SSD_GROUPS)
        y_ssd = (_rms(gated).reshape(b, s, SSD_WIDTH) * ssd_norm_w[l]).astype(x.dtype)
        branch_a = jnp.einsum("bse,ed->bsd", y_ssd, w_a[l])

        q = _rope(q.reshape(b, s, DA_HEADS, 2, DA_HEAD_DIM), cos, sin)
        k = _rope(k.reshape(b, s, DA_HEADS, 2, DA_HEAD_DIM), cos, sin)
        v = v.reshape(b, s, DA_HEADS, 2 * DA_HEAD_DIM)
        lambda_init = 0.8 - 0.6 * math.exp(-0.3 * l)
        lam = (jnp.exp(jnp.sum(lambda_q1[l].astype(jnp.float32) * lambda_k1[l]))
               - jnp.exp(jnp.sum(lambda_q2[l].astype(jnp.float32) * lambda_k2[l]))
               + lambda_init)
        o = _diff_attention(q, k, v, lam)
        o = _rms(o) * subln_w[l] * (1.0 - lambda_init)
        o = (o.reshape(b, s, DA_WIDTH) * jax.nn.silu(g_b)).astype(x.dtype)
        branch_b = jnp.einsum("bse,ed->bsd", o, w_b[l])

        gates = jax.nn.sigmoid(g_merge + b_gate[l]).reshape(b, s, N_BRANCH, D_MODEL)
        merged = gates[:, :, 0] * branch_a + gates[:, :, 1] * branch_b
        y = jnp.einsum("bsd,de->bse", merged, w_o[l])
        x = _layernorm(ALPHA * x + y, ln_g[l], ln_b[l])
    return x
```

```python
import math
import sys
import os
from contextlib import ExitStack
import numpy as np
import concourse.bass as bass
import concourse.mybir as mybir
from concourse.bass_utils import run_bass_kernel_spmd

F32 = mybir.dt.float32
BF16 = mybir.dt.bfloat16
AF = mybir.ActivationFunctionType
ALU = mybir.AluOpType

D = 1024
S = 8192
NT = 16
KLEN = (4, 8, 12, 16)
EPS = 1e-5
ALPHA = 2.0 ** 0.25
LAMBDA_INIT = 0.2
NEG = -30000.0

C_ID, C_LT, C_SU, C_ONE, C_PERM = 0, 128, 256, 384, 512
C_ALOG, C_DTB, C_DSK = 640, 656, 672
C_LQ1, C_LK1, C_LQ2, C_LK2 = 688, 752, 816, 880
C_CW, C_CB, C_BG, C_SLN, C_SEL, C_NW = 944, 992, 1004, 1020, 1021, 1037
NCST = 1045


class _Op:
    __slots__ = ("eng", "fn", "deps", "dma", "idx", "need", "sem", "val", "src")


class Prog:
    def __init__(self, tag):
        self.tag = tag
        self.ops = []
        self.wr = {}
        self.rd = {}
        self.base = {}
        self.limit = None

    def add(self, eng, fn, r=(), w=(), dma=None):
        if self.limit is not None and len(self.ops) >= self.limit:
            return None
        o = _Op()
        o.eng, o.fn, o.dma, o.idx, o.need = eng, fn, dma, len(self.ops), False
        f = sys._getframe(1)
        o.src = (f.f_lineno, f.f_back.f_lineno if f.f_back else 0)
        deps = {}
        for k in r:
            for x in self.wr.get(k, ()):
                deps[x] = "raw"
        for k in w:
            if self.rd.get(k):
                self.base[k] = list(self.wr.get(k, ())) + list(self.rd[k])
                self.wr[k] = []
                self.rd[k] = []
            for x in self.base.get(k, ()):
                deps.setdefault(x, "war")
        for k in w:
            self.wr.setdefault(k, []).append(o.idx)
            self.rd.setdefault(k, [])
        for k in r:
            self.rd.setdefault(k, []).append(o.idx)
            self.wr.setdefault(k, [])
        deps.pop(o.idx, None)
        o.deps = deps
        self.ops.append(o)
        return o

    def finalize(self):
        cnt = {}
        for o in self.ops:
            if o.dma is not None:
                o.sem = self.tag + "d_" + o.dma
                cnt[o.sem] = cnt.get(o.sem, 0) + 16
                o.val = cnt[o.sem]
        for o in self.ops:
            best = {}
            for d, kind in o.deps.items():
                p = self.ops[d]
                if p.dma is None:
                    if p.eng == o.eng and o.eng == "pe" and o.dma is None:
                        continue
                    ch = "e_" + p.eng
                else:
                    ch = p.sem
                if ch not in best or best[ch] < d:
                    best[ch] = d
            o.deps = list(best.values())
            for d in o.deps:
                self.ops[d].need = True
        for o in self.ops:
            if o.dma is None and o.need:
                o.sem = self.tag + "e_" + o.eng
                cnt[o.sem] = cnt.get(o.sem, 0) + 1
                o.val = cnt[o.sem]
        return sorted(cnt.keys())

    def emit(self, eng, e, sems):
        waited = {}
        for o in self.ops:
            if o.eng != eng:
                continue
            for d in o.deps:
                p = self.ops[d]
                if waited.get(p.sem, 0) < p.val:
                    e.wait_ge(sems[p.sem], p.val)
                    waited[p.sem] = p.val
            ins = o.fn(e)
            if o.dma is not None:
                ins.then_inc(sems[o.sem], 16)
            elif o.need:
                ins.then_inc(sems[o.sem], 1)


def build_nc(stop=None, nta=NT, limit=None):
    nc = bass.Bass("TRN2", target_bir_lowering=False)
    dt_in = lambda n, shp, t=F32: nc.dram_tensor(n, shp, t, kind="ExternalInput").ap()
    xT_all = dt_in("xT_all", [128, 8, S])
    xT_own = dt_in("xT_own", [128, 8, 4, 515])
    x_own = dt_in("x_own", [4, 512, D])
    w_k = dt_in("w_k", [128, 8, 1024])
    w_v = dt_in("w_v", [128, 8, 1024])
    w_x = dt_in("w_x", [128, 8, 1536])
    w_dt = dt_in("w_dt", [128, 8, 16])
    w_q = dt_in("w_q", [128, 8, 1024])
    w_z = dt_in("w_z", [128, 8, 1024])
    w_gb = dt_in("w_gb", [128, 8, 1024])
    w_gm = dt_in("w_gm", [128, 8, 2048])
    w_a = dt_in("w_a", [128, 8, 1024])
    w_b = dt_in("w_b", [128, 8, 1024])
    w_o = dt_in("w_o", [128, 8, 1024])
    cst_d = dt_in("cst", [128, NCST])
    cbrow_d = dt_in("cbrow", [1, 1536])
    lnp_d = dt_in("lnp", [128, 2048])
    msk_d = dt_in("msk", [8, 512 + 16 * 512])
    ropeA = dt_in("ropeA", [128, 2, S])
    ropeO = dt_in("ropeO", [128, 2, 4, 512])
    y_out = nc.dram_tensor("y_out", [4, 512, D], F32, kind="ExternalOutput").ap()
    ksc = nc.dram_tensor("ksc", [8, 128, S], BF16).ap()
    vsc = nc.dram_tensor("vsc", [S, 1024], BF16).ap()

    hs_d = nc.dram_tensor("hs_d", [4, 128, 1024], F32).ap()
    ysd_d = nc.dram_tensor("ysd_d", [4, 128, 4096], BF16).ap()
    od_d = nc.dram_tensor("od_d", [4, 128, 4096], BF16).ap()

    es = ExitStack()
    st = ExitStack()
    P = Prog("s")
    pending = []
    sb = lambda n, shp, t=F32: es.enter_context(nc.sbuf_tensor(n, shp, t))
    sl = lambda n, shp, t=F32: st.enter_context(nc.sbuf_tensor(n, shp, t))
    cst = sb("cstt", [128, NCST])
    cbf = sb("cbf", [128, 640], BF16)
    diag = sb("diag", [128, 48 * 128], BF16)
    cbrow_b = sb("cbrowb", [1, 1536], BF16)
    onesrow_b = sb("onesrow", [1, 128], BF16)
    sm = sb("sm", [128, 64])
    PS = [es.enter_context(nc.psum_tensor("ps%d" % i, [128, 1024], F32)) for i in range(4)]

    def run_stage(P, st):
        if os.environ.get("KDBG"):
            print("STAGE", P.tag, "nops", len(P.ops), "last", P.ops[-1].eng, P.ops[-1].src)
        P.limit = None
        P.add("sp", lambda e: e.nop(), r=list(pending), w=["done"])
        del pending[:]
        names = P.finalize()
        sems = {n: st.enter_context(nc.semaphore(n)) for n in names}
        with nc.Block() as block:
            @block.sync
            def _(e):
                P.emit("sp", e, sems)

            @block.tensor
            def _(e):
                P.emit("pe", e, sems)

            @block.scalar
            def _(e):
                P.emit("act", e, sems)

            @block.vector
            def _(e):
                P.emit("dve", e, sems)

            @block.gpsimd
            def _(e):
                P.emit("pool", e, sems)
        st.close()
        nc.all_engine_barrier()

    WB = sl("WB", [128, 28800], BF16)
    wst = [sl("wst%d" % i, [128, 2048], F32) for i in range(2)]
    xst = [sl("xst0", [128, 4 * 515], F32)] * 2
    xb = [sl("xb%d" % i, [128, 8 * 515], BF16) for i in range(2)]
    uT = sl("uT", [128, 12 * 515], BF16)
    rt = [sl("rt0", [128, 1024])] * 2
    kc = [sl("kc%d" % i, [128, 512], BF16) for i in range(2)]
    tmpf = [sl("tmpf%d" % i, [128, 1536 if i == 3 else 512]) for i in range(4)]
    kTall = [sl("kTall0", [128, 8 * 512], BF16)] * 2
    vall = [sl("vall0", [128, 4 * 1024], BF16)] * 2
    xs_t = sl("xs_t", [128, 4 * 1024], BF16)
    B_t = sl("B_t", [128, 4 * 256], BF16)
    dtt = sl("dtt", [128, 64])
    adt = sl("adt", [128, 64])
    dk = sl("dk", [128, 64])
    wl = sl("wl", [128, 64])
    cdT = sl("cdT", [128, 16])
    xdtd = sl("xdtd", [128, 4 * 1024], BF16)
    hcur = sl("hcur", [128, 1024])
    hs = [sl("hs%d" % i, [128, 1024]) for i in range(4)]

    def bank(i):
        return PS[i // 2][:, (i % 2) * 512:(i % 2) * 512 + 512]

    cI = cst[:, C_ID:C_ID + 128]
    cLT = cst[:, C_LT:C_LT + 128]
    cSU = cst[:, C_SU:C_SU + 128]
    cONE = cst[:, C_ONE:C_ONE + 128]
    bI = cbf[:, 0:128]
    bONE = cbf[:, 384:512]
    bPERM = cbf[:, 512:640]
    SM_A, SM_NLAM, SM_T0, SM_T1, SM_SLN, SM_NM, SM_SS, SM_RS = 0, 16, 17, 18, 19, 20, 24, 28

    dma_rr = [0]

    def DMA(out, in_, r, w, key, eng="sp"):
        P.add(eng, lambda e, o=out, i=in_: e.dma_start(out=o, in_=i), r=r, w=w, dma=key)

    def MM(out, lhsT, rhs, start, stop, r, w):
        P.add("pe", lambda e, o=out, l=lhsT, rr=rhs, s=start, t=stop: e.matmul(o, lhsT=l, rhs=rr, start=s, stop=t),
              r=r, w=w)

    def ACT(out, in_, func, r, w, bias=None, scale=None, accum=None):
        def fn(e, o=out, i=in_, f=func, b=bias, sc=scale, a=accum):
            kw = {}
            if b is not None:
                kw["bias"] = b
            if sc is not None:
                kw["scale"] = sc
            if a is not None:
                kw["accum_out"] = a
            return e.activation(o, i, f, **kw)
        P.add("act", fn, r=r, w=w)

    def TT(eng, out, in0, in1, op, r, w):
        P.add(eng, lambda e, o=out, a=in0, b=in1, p=op: e.tensor_tensor(out=o, in0=a, in1=b, op=p), r=r, w=w)

    def TS(eng, out, in0, s1, s2, op0, op1, r, w):
        if op1 is None:
            P.add(eng, lambda e, o=out, a=in0, x=s1, p=op0: e.tensor_scalar(o, a, x, None, p), r=r, w=w)
        else:
            P.add(eng, lambda e, o=out, a=in0, x=s1, y=s2, p=op0, q=op1: e.tensor_scalar(o, a, x, y, p, q), r=r, w=w)

    def STT(eng, out, in0, sc, in1, op0, op1, r, w):
        P.add(eng, lambda e, o=out, a=in0, s=sc, b=in1, p=op0, q=op1: e.scalar_tensor_tensor(o, a, s, b, p, q), r=r, w=w)

    def CP(eng, out, in_, r, w):
        P.add(eng, lambda e, o=out, i=in_: e.tensor_copy(out=o, in_=i), r=r, w=w)

    def MSET(eng, ap, v, w):
        P.add(eng, lambda e, a=ap, x=v: e.memset(a, x), r=(), w=w)

    DMA(cst[:], cst_d, (), ["cst"], "c0")
    DMA(tmpf[3][0:1, 0:1536], cbrow_d, (), ["tmpf3"], "c1")
    CP("dve", cbf[:], cst[:, 0:640], ["cst"], ["cbf"])
    CP("dve", cbrow_b[:], tmpf[3][0:1, 0:1536], ["tmpf3"], ["cbrowb"])
    MSET("dve", onesrow_b[:], 1.0, ["onesrow"])
    MSET("dve", uT[:], 0.0, ["uT%d" % b for b in range(12)])
    MSET("dve", hcur[:], 0.0, ["hcur"])
    for i in range(4):
        MSET("dve", hs[i][:], 0.0, ["hs%d" % i])
    for blk in range(12):
        for w in range(4):
            TS("dve", diag[:, (blk * 4 + w) * 128:(blk * 4 + w + 1) * 128], cI,
               cst[:, C_CW + blk * 4 + w:C_CW + blk * 4 + w + 1], None, ALU.mult, None, ["cst"], ["diag"])
    ACT(sm[:, SM_A:SM_A + 16], cst[:, C_ALOG:C_ALOG + 16], AF.Exp, ["cst"], ["sm_a0"])
    TS("dve", sm[:, SM_A:SM_A + 16], sm[:, SM_A:SM_A + 16], -1.0, None, ALU.mult, None, ["sm_a0"], ["sm_a"])
    TT("dve", tmpf[0][:, 0:64], cst[:, C_LQ1:C_LQ1 + 64], cst[:, C_LK1:C_LK1 + 64], ALU.mult, ["cst"], ["lamt0"])
    TT("dve", tmpf[0][:, 64:128], cst[:, C_LQ2:C_LQ2 + 64], cst[:, C_LK2:C_LK2 + 64], ALU.mult, ["cst"], ["lamt1"])
    P.add("dve", lambda e: e.reduce_sum(sm[:, SM_T0:SM_T0 + 1], tmpf[0][:, 0:64], mybir.AxisListType.X), r=["lamt0"], w=["lam_s0"])
    P.add("dve", lambda e: e.reduce_sum(sm[:, SM_T1:SM_T1 + 1], tmpf[0][:, 64:128], mybir.AxisListType.X), r=["lamt1"], w=["lam_s1"])
    ACT(sm[:, SM_T0:SM_T0 + 2], sm[:, SM_T0:SM_T0 + 2], AF.Exp, ["lam_s0", "lam_s1"], ["lam_e"])
    TT("dve", sm[:, SM_NLAM:SM_NLAM + 1], sm[:, SM_T1:SM_T1 + 1], sm[:, SM_T0:SM_T0 + 1], ALU.subtract, ["lam_e"], ["nlam0"])
    TS("dve", sm[:, SM_NLAM:SM_NLAM + 1], sm[:, SM_NLAM:SM_NLAM + 1], -LAMBDA_INIT, None, ALU.add, None, ["nlam0"], ["nlam"])
    TS("dve", sm[:, SM_SLN:SM_SLN + 1], cst[:, C_SLN:C_SLN + 1], 1.0 - LAMBDA_INIT, None, ALU.mult, None, ["cst"], ["sln"])

    wstate = {"i": 0}

    def load_weights(dram, ncols, wb_off, key, scale_rows=None):
        c0 = 0
        while c0 < ncols:
            cw = min(256, ncols - c0)
            i = wstate["i"] % 2
            wstate["i"] += 1
            stv = wst[i][:, 0:8 * cw].rearrange("p (a c) -> p a c", a=8)
            DMA(stv, dram[:, :, c0:c0 + cw], (), ["wst%d" % i], "wst%d" % i)
            dst = WB[:, wb_off:wb_off + 8 * ncols].rearrange("p (a c) -> p a c", a=8)[:, :, c0:c0 + cw]
            CP("pool", dst, stv, ["wst%d" % i], [key])
            c0 += cw

    def wv(wb_off, ncols):
        return WB[:, wb_off:wb_off + 8 * ncols].rearrange("p (a c) -> p a c", a=8)

    def load_x(src, ncol, i, eng="pool"):
        xv = xb[i][:, 0:8 * ncol].rearrange("p (a c) -> p a c", a=8)
        for hf in range(2):
            stv = xst[0][:, 0:4 * ncol].rearrange("p (a c) -> p a c", a=4)
            DMA(stv, src[:, hf * 4:(hf + 1) * 4, :], (), ["xst0"], "xst0")
            CP(eng, xv[:, hf * 4:(hf + 1) * 4, :], stv, ["xst0"], ["xb%d" % i])
        return xv

    ev = {"i": 0}

    def evac(out, in_, r, w):
        ev["i"] += 1
        if ev["i"] % 2:
            P.add("act", lambda e, o=out, i=in_: e.copy(o, i), r=r, w=w)
        else:
            CP("dve", out, in_, r, w)

    def softplus_dt(psap, k, rk):
        d = dtt[:, k * 16:(k + 1) * 16]
        TT("dve", d, psap, cst[:, C_DTB:C_DTB + 16], ALU.add, [rk, "cst"], ["dtt%d" % k])
        ACT(d, d, AF.Exp, ["dtt%d" % k], ["dtt%d" % k])
        ACT(d, d, AF.Ln, ["dtt%d" % k], ["dtt%d" % k], bias=1.0)
        TT("dve", adt[:, k * 16:(k + 1) * 16], d, sm[:, SM_A:SM_A + 16], ALU.mult, ["dtt%d" % k, "sm_a"], ["adt%d" % k])

    def conv_tok(k, blks, pb, xoff):
        for gi in range(0, len(blks), 4):
            grp = blks[gi:gi + 4]
            pk = "ps%d" % pb
            for j, blk in enumerate(grp):
                o = bank(pb)[:, j * 128:(j + 1) * 128]
                for w in range(4):
                    MM(o, uT[:, blk * 515 + k * 128 + w: blk * 515 + k * 128 + w + 128],
                       diag[:, (blk * 4 + w) * 128:(blk * 4 + w + 1) * 128], w == 0, False,
                       ["uT%d" % blk, "diag"], [pk])
                MM(o, onesrow_b[0:1, :], cbrow_b[0:1, blk * 128:(blk + 1) * 128], False, True,
                   ["onesrow", "cbrowb"], [pk])
            n = len(grp) * 128
            b0 = grp[0]
            if b0 < 8:
                ACT(xs_t[:, k * 1024 + b0 * 128:k * 1024 + b0 * 128 + n], bank(pb)[:, 0:n], AF.Silu, [pk], ["xs_t%d" % k])
            else:
                ACT(B_t[:, k * 256:(k + 1) * 256], bank(pb)[:, 0:n], AF.Silu, [pk], ["B_t%d" % k])
            pb = pb + 1 if pb % 2 == 0 else pb - 1
        return pb

    def proj_u(xv, blks, wx_off, col0, pbs, halo):
        wx = wv(wx_off, 1536)
        for bi, blk in enumerate(blks):
            pb = pbs[bi % len(pbs)]
            pk = "ps%d" % pb
            uk = "uT%d" % blk
            if halo == "carry":
                CP("dve", uT[:, blk * 515:blk * 515 + 3], uT[:, blk * 515 + 512:blk * 515 + 515], [uk], [uk])
            else:
                for dc in range(8):
                    MM(bank(pb)[:, 0:3], wx[:, dc, blk * 128:(blk + 1) * 128], xv[:, dc, 0:3], dc == 0, dc == 7,
                       ["wx", halo], [pk])
                evac(uT[:, blk * 515:blk * 515 + 3], bank(pb)[:, 0:3], [pk], [uk])
            for dc in range(8):
                MM(bank(pb), wx[:, dc, blk * 128:(blk + 1) * 128], xv[:, dc, col0:col0 + 512], dc == 0, dc == 7,
                   ["wx", halo if halo != "carry" else "xbcur"], [pk])
            evac(uT[:, blk * 515 + 3:blk * 515 + 515], bank(pb), [pk], [uk])

    OFF_K, OFF_V, OFF_X, OFF_DT = 0, 8192, 16384, 16384 + 12288
    load_weights(w_k, 1024, OFF_K, "wk")
    load_weights(w_v, 1024, OFF_V, "wvv")
    load_weights(w_x, 1536, OFF_X, "wx")
    load_weights(w_dt, 16, OFF_DT, "wdt")
    wk_v, wv_v, wdt_v = wv(OFF_K, 1024), wv(OFF_V, 1024), wv(OFF_DT, 16)

    def rope_block(h, pb, srcw, srck, xv, col0, rtab, dst, dstk, xk):
        pk = "ps%d" % pb
        for dc in range(8):
            MM(bank(pb), srcw[:, dc, h * 128:(h + 1) * 128], xv[:, dc, col0:col0 + 512], dc == 0, dc == 7, [srck, xk], [pk])
        kcb = kc[h % 2]
        kk = "kc%d" % (h % 2)
        P.add("act", lambda e, o=kcb[:], i=bank(pb): e.copy(o, i), r=[pk], w=[kk])
        pb2 = pb + 2
        pk2 = "ps%d" % pb2
        MM(bank(pb2), bPERM, kcb[:], True, True, ["cbf", kk], [pk2])
        t = tmpf[h % 2][:, 0:512]
        tk = "tmpf%d" % (h % 2)
        TT("dve", t, bank(pb2), rtab[:, 512:1024], ALU.mult, [pk2, "rt"], [tk])
        TT("dve", tmpf[2 + h % 2][:, 0:512], kcb[:], rtab[:, 0:512], ALU.mult, [kk, "rt"], ["tmpf%d" % (2 + h % 2)])
        TT("dve", dst, tmpf[2 + h % 2][:, 0:512], t, ALU.add, [tk, "tmpf%d" % (2 + h % 2)], [dstk])

    xv_next = load_x(xT_all[:, :, 0:512], 512, 0)
    for T in range(nta):
        i = T % 2
        xv = xv_next
        xk = "xb%d" % i
        if T + 1 < nta:
            xv_next = load_x(xT_all[:, :, (T + 1) * 512:(T + 2) * 512], 512, 1 - i)
        rtab = rt[i]
        DMA(rtab[:].rearrange("p (a c) -> p a c", a=2), ropeA[:, :, T * 512:(T + 1) * 512], (), ["rt0"], "rt0")
        def kproj(h):
            pk = h % 2
            for dc in range(8):
                MM(bank(pk), wk_v[:, dc, h * 128:(h + 1) * 128], xv[:, dc, :], dc == 0, dc == 7, ["wk", xk], ["ps%d" % pk])
        kproj(0)
        for h in range(8):
            pk = h % 2
            kcb = kc[h % 2]
            kk = "kc%d" % (h % 2)
            P.add("act", lambda e, o=kcb[:], ii=bank(pk): e.copy(o, ii), r=["ps%d" % pk], w=[kk])
            if h + 1 < 8:
                kproj(h + 1)
            MM(bank(2 + pk), bPERM, kcb[:], True, True, ["cbf", kk], ["ps%d" % (2 + pk)])
            t = tmpf[h % 2][:, 0:512]
            TT("dve", t, bank(2 + pk), rtab[:, 512:1024], ALU.mult, ["ps%d" % (2 + pk), "rt0"], ["tmpf%d" % (h % 2)])
            TT("dve", tmpf[2 + h % 2][:, 0:512], kcb[:], rtab[:, 0:512], ALU.mult, [kk, "rt0"], ["tmpf%d" % (2 + h % 2)])
            TT("dve", kTall[i][:, h * 512:(h + 1) * 512], tmpf[2 + h % 2][:, 0:512], t, ALU.add,
               ["tmpf%d" % (h % 2), "tmpf%d" % (2 + h % 2)], ["kTall0"])
        DMA(ksc[:, :, T * 512:(T + 1) * 512].rearrange("h r t -> r h t"),
            kTall[i][:].rearrange("p (h t) -> p h t", h=8), ["kTall0"], ["ksc%d" % T], "kst0", eng="pool")
        pending.append("ksc%d" % T)
        for k in range(4):
            for half in range(2):
                pb = 4 + (k * 2 + half) % 2
                for dc in range(8):
                    MM(bank(pb), xv[:, dc, k * 128:(k + 1) * 128], wv_v[:, dc, half * 512:(half + 1) * 512], dc == 0, dc == 7,
                       ["wvv", xk], ["ps%d" % pb])
                evac(vall[i][:, k * 1024 + half * 512:k * 1024 + half * 512 + 512], bank(pb), ["ps%d" % pb], ["vall0"])
        DMA(vsc[T * 512:(T + 1) * 512, :].rearrange("(k p) c -> p k c", p=128),
            vall[i][:].rearrange("p (k c) -> p k c", k=4), ["vall0"], ["vsc%d" % T], "vst0", eng="pool")
        pending.append("vsc%d" % T)
        wx = wv(OFF_X, 1536)
        for bi, blk in enumerate(range(10)):
            pb = 6 + bi % 2
            uk = "uT%d" % blk
            CP("dve", uT[:, blk * 515:blk * 515 + 3], uT[:, blk * 515 + 512:blk * 515 + 515], [uk], [uk])
            for dc in range(8):
                MM(bank(pb), wx[:, dc, blk * 128:(blk + 1) * 128], xv[:, dc, :], dc == 0, dc == 7, ["wx", xk], ["ps%d" % pb])
            evac(uT[:, blk * 515 + 3:blk * 515 + 515], bank(pb), ["ps%d" % pb], [uk])
        for k in range(4):
            o = bank(4)[:, k * 16:(k + 1) * 16]
            for dc in range(8):
                MM(o, xv[:, dc, k * 128:(k + 1) * 128], wdt_v[:, dc, :], dc == 0, dc == 7, ["wdt", xk], ["ps4"])
        for k in range(4):
            softplus_dt(bank(4)[:, k * 16:(k + 1) * 16], k, "ps4")
        pb = 0
        for k in range(4):
            pb = conv_tok(k, list(range(8)), pb, 0)
            pb = conv_tok(k, [8, 9], pb, 0)
        for k in range(4):
            o = bank(5)[:, k * 16:(k + 1) * 16]
            MM(o, cSU, adt[:, k * 16:(k + 1) * 16], True, k == 3, ["cst", "adt%d" % k], ["ps5"])
            for k2 in range(k + 1, 4):
                MM(o, cONE, adt[:, k2 * 16:(k2 + 1) * 16], False, k2 == 3, ["cst", "adt%d" % k2], ["ps5"])
        o = bank(5)[:, 64:80]
        for k in range(4):
            MM(o, cONE, adt[:, k * 16:(k + 1) * 16], k == 0, k == 3, ["cst", "adt%d" % k], ["ps5"])
        ACT(dk[:, 0:64], bank(5)[:, 0:64], AF.Exp, ["ps5"], ["dk"])
        ACT(cdT[:], bank(5)[:, 64:80], AF.Exp, ["ps5"], ["cdT"])
        TT("dve", wl[:], dtt[:], dk[:], ALU.mult, ["dk"] + ["dtt%d" % k for k in range(4)], ["wl"])
        for k in range(4):
            TT("dve", xdtd[:, k * 1024:(k + 1) * 1024].rearrange("p (h q) -> p h q", h=16),
               xs_t[:, k * 1024:(k + 1) * 1024].rearrange("p (h q) -> p h q", h=16),
               wl[:, k * 16:(k + 1) * 16].unsqueeze(2).to_broadcast([128, 16, 64]), ALU.mult,
               ["xs_t%d" % k, "wl"], ["xdtd%d" % k])
        for g in range(2):
            for k in range(4):
                MM(bank(6 + g), B_t[:, k * 256 + g * 128:k * 256 + (g + 1) * 128], xdtd[:, k * 1024 + g * 512:k * 1024 + (g + 1) * 512],
                   k == 0, k == 3, ["B_t%d" % k, "xdtd%d" % k], ["ps%d" % (6 + g)])
        STT("dve", hs[T // 4][:], hcur[:], cst[:, C_SEL + T:C_SEL + T + 1], hs[T // 4][:], ALU.mult, ALU.add,
            ["hcur", "cst", "hs%d" % (T // 4)], ["hs%d" % (T // 4)])
        TT("dve", hcur[:].rearrange("p (h q) -> p h q", h=16), hcur[:].rearrange("p (h q) -> p h q", h=16),
           cdT[:].unsqueeze(2).to_broadcast([128, 16, 64]), ALU.mult, ["hcur", "cdT"], ["hcur"])
        TT("dve", hcur[:], hcur[:], PS[3][:], ALU.add, ["hcur", "ps6", "ps7"], ["hcur"])

    for i in range(4):
        DMA(hs_d[i], hs[i][:], ["hs%d" % i], ["hsd%d" % i], "hst", eng="sp")
        pending.append("hsd%d" % i)
    run_stage(P, st)
    if stop == "A":
        es.close()
        return nc

    st = ExitStack()
    P = Prog("o")
    if stop == "O1":
        P.limit = limit
    WB = sl("WB1", [128, 28800], BF16)
    wst = [sl("wst%d_1" % i, [128, 2048], F32) for i in range(2)]
    xst = [sl("xst0_1", [128, 4 * 515], F32)] * 2
    xb = [sl("xb0_1", [128, 8 * 515], BF16)] * 2
    uT = sl("uT_1", [128, 12 * 515], BF16)
    tmpf = [sl("tmpf%d_1" % i, [128, 1024]) for i in range(3)]
    xs_t = sl("xs_t_1", [128, 4 * 1024], BF16)
    B_t = sl("B_t_1", [128, 4 * 256], BF16)
    zs = sl("zs", [128, 4 * 1024], BF16)
    BCT = sl("BCT", [128, 4 * 512], BF16)
    dtt = sl("dtt_1", [128, 64])
    adt = sl("adt_1", [128, 64])
    ex3 = sl("ex3", [128, 48])
    xdtd = sl("xdtd_1", [128, 1024], BF16)
    xdt = sl("xdt", [128, 1024], BF16)
    hcur = sl("hcur_1", [128, 1024])
    prevb = sl("prevb", [128, 1024], BF16)
    Rm = sl("Rm", [128, 2048])
    Lex = sl("Lex", [128, 2048], BF16)
    cbm = sl("cbm", [128, 256])
    MT = sl("MT", [128, 2048], BF16)
    ystage = sl("ystage", [128, 4096], BF16)

    OFF_Z = 0
    load_weights(w_z, 1024, OFF_Z, "wk")
    load_weights(w_x, 1536, OFF_X, "wx")
    load_weights(w_dt, 16, OFF_DT, "wdt")
    wdt_v = wv(OFF_DT, 16)
    wz_v = wv(OFF_Z, 1024)
    wx = wv(OFF_X, 1536)
    normw = cst[:, C_NW:C_NW + 8]
    for s in range(4):
        i = 0
        xv = load_x(xT_own[:, :, s, :], 515, i)
        xk = "xb%d" % i
        for bi, blk in enumerate(range(12)):
            pb = bi % 2
            uk = "uT%d" % blk
            for dc in range(8):
                MM(bank(pb)[:, 0:3], wx[:, dc, blk * 128:(blk + 1) * 128], xv[:, dc, 0:3], dc == 0, dc == 7, ["wx", xk], ["ps%d" % pb])
            evac(uT[:, blk * 515:blk * 515 + 3], bank(pb)[:, 0:3], ["ps%d" % pb], [uk])
            for dc in range(8):
                MM(bank(pb), wx[:, dc, blk * 128:(blk + 1) * 128], xv[:, dc, 3:515], dc == 0, dc == 7, ["wx", xk], ["ps%d" % pb])
            evac(uT[:, blk * 515 + 3:blk * 515 + 515], bank(pb), ["ps%d" % pb], [uk])
        for bi, blk in enumerate((8, 9, 10, 11)):
            pb = 2 + bi % 2
            for w in range(4):
                MM(bank(pb), diag[:, (blk * 4 + w) * 128:(blk * 4 + w + 1) * 128], uT[:, blk * 515 + w:blk * 515 + w + 512],
                   w == 0, w == 3, ["uT%d" % blk, "diag"], ["ps%d" % pb])
            ACT(BCT[:, bi * 512:(bi + 1) * 512], bank(pb), AF.Silu, ["ps%d" % pb, "cst"], ["BCT%d" % bi],
                bias=cst[:, C_CB + blk:C_CB + blk + 1])
        for k in range(4):
            o = bank(4)[:, k * 16:(k + 1) * 16]
            for dc in range(8):
                MM(o, xv[:, dc, 3 + k * 128:3 + (k + 1) * 128], wdt_v[:, dc, :], dc == 0, dc == 7, ["wdt", xk], ["ps4"])
        for k in range(4):
            softplus_dt(bank(4)[:, k * 16:(k + 1) * 16], k, "ps4")
        pb = 6
        for k in range(4):
            pb = conv_tok(k, list(range(8)), pb, 0)
            pb = conv_tok(k, [8, 9], pb, 0)
        for k in range(4):
            for half in range(2):
                pb = (k * 2 + half) % 2
                for dc in range(8):
                    MM(bank(pb), xv[:, dc, 3 + k * 128:3 + (k + 1) * 128], wz_v[:, dc, half * 512:(half + 1) * 512], dc == 0, dc == 7,
                       ["wk", xk], ["ps%d" % pb])
                ACT(zs[:, k * 1024 + half * 512:k * 1024 + half * 512 + 512], bank(pb), AF.Silu, ["ps%d" % pb], ["zs%d" % k])
        DMA(hcur[:], hs_d[s], (), ["hcur"], "hld")
        for k in range(4):
            CP("pool", prevb[:], hcur[:], ["hcur"], ["prevb"])
            a_k = adt[:, k * 16:(k + 1) * 16]
            ak = "adt%d" % k
            TT("dve", Rm[:].rearrange("p (h l) -> p h l", h=16), a_k.unsqueeze(2).to_broadcast([128, 16, 128]),
               cLT.unsqueeze(1).to_broadcast([128, 16, 128]), ALU.mult, [ak, "cst"], ["Rm"])
            for q in range(4):
                MM(bank(q), cSU, Rm[:, q * 512:(q + 1) * 512], True, True, ["cst", "Rm"], ["ps%d" % q])
            ACT(Lex[:, 0:1024], PS[0][:], AF.Exp, ["ps0", "ps1"], ["Lex0"])
            ACT(Lex[:, 1024:2048], PS[1][:], AF.Exp, ["ps2", "ps3"], ["Lex1"])
            for g in range(2):
                MM(bank(4)[:, g * 128:(g + 1) * 128], BCT[:, g * 512 + k * 128:g * 512 + (k + 1) * 128],
                   BCT[:, (2 + g) * 512 + k * 128:(2 + g) * 512 + (k + 1) * 128], True, True, ["BCT%d" % g, "BCT%d" % (2 + g)], ["ps4"])
            TT("dve", cbm[:].rearrange("p (g l) -> p g l", g=2), bank(4)[:, 0:256].rearrange("p (g l) -> p g l", g=2),
               cLT.unsqueeze(1).to_broadcast([128, 2, 128]), ALU.mult, ["ps4", "cst"], ["cbm"])
            TT("dve", MT[:].rearrange("p (g h l) -> p g h l", g=2, h=8), Lex[:].rearrange("p (g h l) -> p g h l", g=2, h=8),
               cbm[:].rearrange("p (g l) -> p g l", g=2).unsqueeze(2).to_broadcast([128, 2, 8, 128]), ALU.mult,
               ["Lex0", "Lex1", "cbm"], ["MT"])
            MM(bank(5)[:, 0:16], cLT, a_k, True, True, ["cst", ak], ["ps5"])
            MM(bank(5)[:, 16:32], cSU, a_k, True, True, ["cst", ak], ["ps5"])
            MM(bank(5)[:, 32:48], cONE, a_k, True, True, ["cst", ak], ["ps5"])
            ACT(ex3[:], bank(5)[:, 0:48], AF.Exp, ["ps5"], ["ex3"])
            xsk = xs_t[:, k * 1024:(k + 1) * 1024].rearrange("p (h q) -> p h q", h=16)
            TT("dve", xdt[:].rearrange("p (h q) -> p h q", h=16), xsk,
               dtt[:, k * 16:(k + 1) * 16].unsqueeze(2).to_broadcast([128, 16, 64]), ALU.mult, ["xs_t%d" % k, "dtt%d" % k], ["xdt"])
            TT("dve", xdtd[:, 0:1024].rearrange("p (h q) -> p h q", h=16), xdt[:].rearrange("p (h q) -> p h q", h=16),
               ex3[:, 16:32].unsqueeze(2).to_broadcast([128, 16, 64]), ALU.mult, ["xdt", "ex3"], ["xdtd0"])
            for h in range(16):
                MM(PS[0][:, h * 64:(h + 1) * 64], MT[:, h * 128:(h + 1) * 128], xdt[:, h * 64:(h + 1) * 64], True, True,
                   ["MT", "xdt"], ["ps%d" % (h // 8)])
            for g in range(2):
                MM(bank(2 + g), BCT[:, (2 + g) * 512 + k * 128:(2 + g) * 512 + (k + 1) * 128], prevb[:, g * 512:(g + 1) * 512], True, True,
                   ["BCT%d" % (2 + g), "prevb"], ["ps%d" % (2 + g)])
            t1, t2 = tmpf[0], tmpf[1]
            TT("dve", t1[:].rearrange("p (h q) -> p h q", h=16), PS[1][:].rearrange("p (h q) -> p h q", h=16),
               ex3[:, 0:16].unsqueeze(2).to_broadcast([128, 16, 64]), ALU.mult, ["ps2", "ps3", "ex3"], ["tmpf0"])
            TT("dve", t1[:], t1[:], PS[0][:], ALU.add, ["tmpf0", "ps0", "ps1"], ["tmpf0"])
            TT("pool", t2[:].rearrange("p (h q) -> p h q", h=16), xsk,
               cst[:, C_DSK:C_DSK + 16].unsqueeze(2).to_broadcast([128, 16, 64]), ALU.mult, ["xs_t%d" % k, "cst"], ["tmpf1"])
            TT("dve", t1[:], t1[:], t2[:], ALU.add, ["tmpf0", "tmpf1"], ["tmpf0"])
            TT("dve", t1[:], t1[:], zs[:, k * 1024:(k + 1) * 1024], ALU.mult, ["tmpf0", "zs%d" % k], ["tmpf0"])
            MSET("dve", sm[:, SM_SS:SM_SS + 2], 0.0, ["ssq0", "ssq1"])
            for g in range(2):
                ACT(t2[:, g * 512:(g + 1) * 512], t1[:, g * 512:(g + 1) * 512], AF.Square, ["tmpf0", "ssq%d" % g], ["tmpf1", "ssq%d" % g],
                    accum=sm[:, SM_SS + g:SM_SS + g + 1])
            TS("dve", sm[:, SM_RS:SM_RS + 2], sm[:, SM_SS:SM_SS + 2], 1.0 / 512.0, EPS, ALU.mult, ALU.add, ["ssq0", "ssq1"], ["rs0"])
            ACT(sm[:, SM_RS:SM_RS + 2], sm[:, SM_RS:SM_RS + 2], AF.Sqrt, ["rs0"], ["rs1"])
            P.add("dve", lambda e: e.reciprocal(sm[:, SM_RS:SM_RS + 2], sm[:, SM_RS:SM_RS + 2]), r=["rs1"], w=["rs"])
            t3 = tmpf[2]
            for g in range(2):
                TS("dve", t3[:, g * 512:(g + 1) * 512], t1[:, g * 512:(g + 1) * 512], sm[:, SM_RS + g:SM_RS + g + 1], None,
                   ALU.mult, None, ["tmpf0", "rs"], ["tmpf2"])
            for cc in range(8):
                P.add("pe", lambda e, o=PS[3][:, cc * 128:(cc + 1) * 128], ii=t3[:, cc * 128:(cc + 1) * 128]: e.transpose(o, ii, cI),
                      r=["tmpf2", "cst"], w=["ps%d" % (6 + cc // 4)])
            for cc in range(8):
                ACT(ystage[:, cc * 512 + k * 128:cc * 512 + (k + 1) * 128],
                    PS[3][:, cc * 128:(cc + 1) * 128], AF.Copy, ["ps%d" % (6 + cc // 4), "cst"], ["ystage"], scale=normw[:, cc:cc + 1])
            if k < 3:
                for g in range(2):
                    MM(bank(6 + g), B_t[:, k * 256 + g * 128:k * 256 + (g + 1) * 128], xdtd[:, g * 512:(g + 1) * 512], True, True,
                       ["B_t%d" % k, "xdtd0"], ["ps%d" % (6 + g)])
                TT("dve", hcur[:].rearrange("p (h q) -> p h q", h=16), hcur[:].rearrange("p (h q) -> p h q", h=16),
                   ex3[:, 32:48].unsqueeze(2).to_broadcast([128, 16, 64]), ALU.mult, ["hcur", "ex3"], ["hcur"])
                TT("dve", hcur[:], hcur[:], PS[3][:], ALU.add, ["hcur", "ps6", "ps7"], ["hcur"])
        DMA(ysd_d[s], ystage[:], ["ystage"], ["ysd%d" % s], "yst", eng="sp")
        pending.append("ysd%d" % s)

    run_stage(P, st)
    if stop == "O1":
        es.close()
        return nc

    st = ExitStack()
    P = Prog("a")
    if stop == "O2":
        P.limit = limit
    WB = sl("WB2", [128, 16384], BF16)
    wst = [sl("wst%d_2" % i, [128, 2048], F32) for i in range(2)]
    xst = [sl("xst0_2", [128, 4 * 515], F32)] * 2
    xb = [sl("xb0_2", [128, 8 * 515], BF16)] * 2
    rt = [sl("rt0_2", [128, 1024])] * 2
    kc = [sl("kc%d_2" % i, [128, 512], BF16) for i in range(2)]
    tmpf = [sl("tmpf%d_2" % i, [128, 2560 if i == 3 else 512]) for i in range(4)]
    kTall = [sl("qT_2", [128, 8 * 512], BF16), sl("gbT_2", [128, 8 * 512], BF16)]
    vall = [sl("vall%d_2" % i, [128, 4 * 1024], BF16) for i in range(2)]
    pT = [sl("pT%d" % i, [128, 1024], BF16) for i in range(8)]
    sAB = [sl("sAB%d" % i, [128, 1024], BF16) for i in range(3)]
    accS = [sl("accS%d" % i, [128, 1024]) for i in range(2)]
    m01 = sl("m01", [128, 16 * 512], BF16)
    b01 = sl("b01", [8, 2048], BF16)
    osqb = sl("osqb", [128, 512], BF16)
    mskb = sl("mskb", [8, 2560], BF16)
    ostage = sl("ostage", [128, 4096], BF16)

    OFF_Q, OFF_GB = 0, 8192
    load_weights(w_q, 1024, OFF_Q, "wk")
    load_weights(w_gb, 1024, OFF_GB, "wvv")
    wq_v, wgb_v = wv(OFF_Q, 1024), wv(OFF_GB, 1024)
    mA = mskb[0:8, 0:512]
    ldc = [0]
    for s in range(4):
        i = 0
        xv = load_x(xT_own[:, :, s, :], 515, i)
        xk = "xb%d" % i
        rtab = rt[i]
        DMA(rtab[:].rearrange("p (a c) -> p a c", a=2), ropeO[:, :, s, :], (), ["rt0"], "rt0")
        DMA(tmpf[3][0:8, 0:512], msk_d[:, 0:512], (), ["tmpf3"], "mk0")
        DMA(tmpf[3][0:8, 512:2560], msk_d[:, 512 + s * 2048:512 + (s + 1) * 2048], (), ["tmpf3"], "mk1")
        CP("dve", mskb[:], tmpf[3][0:8, 0:2560], ["tmpf3"], ["mskb"])
        qT = kTall[0]
        gbT = kTall[1]
        for h in range(8):
            pk = h % 2
            for dc in range(8):
                MM(bank(pk), wq_v[:, dc, h * 128:(h + 1) * 128], xv[:, dc, 3:515], dc == 0, dc == 7, ["wk", xk], ["ps%d" % pk])
            kcb = kc[h % 2]
            kk = "kc%d" % (h % 2)
            P.add("act", lambda e, o=kcb[:], ii=bank(pk): e.copy(o, ii), r=["ps%d" % pk], w=[kk])
            MM(bank(2 + pk), bPERM, kcb[:], True, True, ["cbf", kk], ["ps%d" % (2 + pk)])
            t = tmpf[h % 2][:, 0:512]
            TT("dve", t, bank(2 + pk), rtab[:, 512:1024], ALU.mult, ["ps%d" % (2 + pk), "rt0"], ["tmpf%d" % (h % 2)])
            TT("dve", tmpf[2 + h % 2][:, 0:512], kcb[:], rtab[:, 0:512], ALU.mult, [kk, "rt0"], ["tmpf%d" % (2 + h % 2)])
            TT("dve", qT[:, h * 512:(h + 1) * 512], tmpf[2 + h % 2][:, 0:512], t, ALU.add,
               ["tmpf%d" % (h % 2), "tmpf%d" % (2 + h % 2)], ["kTall0"])
            pb = 4 + h % 2
            for dc in range(8):
                MM(bank(pb), wgb_v[:, dc, h * 128:(h + 1) * 128], xv[:, dc, 3:515], dc == 0, dc == 7, ["wvv", xk], ["ps%d" % pb])
            ACT(gbT[:, h * 512:(h + 1) * 512], bank(pb), AF.Silu, ["ps%d" % pb], ["kTall1"])
        TS("dve", b01[:], mskb[0:8, 512:2560], 0.0, None, ALU.is_equal, None, ["mskb"], ["b01"])
        for pair in range(4):
            for kb in range(4):
                idx = pair * 4 + kb
                pbm = idx % 4
                MM(bank(pbm), mA[:, kb * 128:(kb + 1) * 128], b01[0:8, pair * 512:(pair + 1) * 512], True, True, ["mskb", "b01"], ["ps%d" % pbm])
                evac(m01[:, idx * 512:(idx + 1) * 512], bank(pbm), ["ps%d" % pbm], ["m01"])
        fin_q = []
        L = KLEN[s]
        for h in range(8):
            units = []
            ng = L // 4
            jbase = ldc[0]
            ldc[0] += ng
            gbuf = {}
            for kg in range(ng):
                j = (jbase + kg) % 2
                kTg = vall[j][:, 0:2048]
                vg = vall[j][:, 2048:4096].rearrange("p (kb e) -> p kb e", kb=16)
                gbuf[kg] = (j, kTg, vg)
                for ktl in range(4):
                    for kb in range(4):
                        units.append((j, kTg, vg, ktl, kg * 4 + ktl, kb))

            def issue(kg):
                j, kTg, vg = gbuf[kg]
                DMA(kTg, ksc[h, :, kg * 2048:(kg + 1) * 2048], ["ksc%d" % t_ for t_ in range(kg * 4, kg * 4 + 4)], ["kTg%d" % j], "kld%d" % j)
                DMA(vg, vsc[kg * 2048:(kg + 1) * 2048, h * 128:(h + 1) * 128].rearrange("(kb p) e -> p kb e", p=128),
                    ["vsc%d" % t_ for t_ in range(kg * 4, kg * 4 + 4)], ["vg%d" % j], "vld%d" % j)

            issue(0)
            if ng > 1:
                issue(1)
            nu = len(units)

            def qk(u):
                j, kTg, vg, ktl, kt, kb = units[u]
                for m in range(2):
                    pb = (u % 2) * 2 + m
                    MM(bank(pb), kTg[m * 64:(m + 1) * 64, ktl * 512 + kb * 128:ktl * 512 + (kb + 1) * 128],
                       qT[m * 64:(m + 1) * 64, h * 512:(h + 1) * 512], True, True, ["kTg%d" % j, "kTall0"], ["ps%d" % pb])

            def ex(u):
                j, kTg, vg, ktl, kt, kb = units[u]
                pk_ = "pT%d" % (u % 8)
                ACT(pT[u % 8][:], PS[u % 2][:], AF.Exp, ["ps%d" % ((u % 2) * 2), "ps%d" % ((u % 2) * 2 + 1)], [pk_], scale=0.125)
                if kt >= L - 4:
                    idx = (kt - (L - 4)) * 4 + kb
                    pv3 = pT[u % 8][:].rearrange("p (m q) -> p m q", m=2)
                    TT("dve", pv3, pv3, m01[:, idx * 512:(idx + 1) * 512].unsqueeze(1).to_broadcast([128, 2, 512]), ALU.mult,
                       [pk_, "m01"], [pk_])

            def pv(u):
                j, kTg, vg, ktl, kt, kb = units[u]
                for m in range(2):
                    MM(bank(4 + m), vg[:, ktl * 4 + kb, :], pT[u % 8][:, m * 512:(m + 1) * 512], u == 0, u == nu - 1,
                       ["vg%d" % j, "pT%d" % (u % 8)], ["ps%d" % (4 + m)])

            acc = accS[h % 2]
            acck = "accS%d" % (h % 2)

            def adds(g):
                u0 = 4 * g
                TT("dve", sAB[0][:], pT[u0 % 8][:], pT[(u0 + 1) % 8][:], ALU.add, ["pT%d" % (u0 % 8), "pT%d" % ((u0 + 1) % 8)], ["sAB0"])
                TT("pool", sAB[1][:], pT[(u0 + 2) % 8][:], pT[(u0 + 3) % 8][:], ALU.add, ["pT%d" % ((u0 + 2) % 8), "pT%d" % ((u0 + 3) % 8)], ["sAB1"])
                if g == 0:
                    TT("dve", acc[:], sAB[0][:], sAB[1][:], ALU.add, ["sAB0", "sAB1"], [acck])
                else:
                    TT("dve", sAB[2][:], sAB[0][:], sAB[1][:], ALU.add, ["sAB0", "sAB1"], ["sAB2"])
                    TT("dve", acc[:], acc[:], sAB[2][:], ALU.add, [acck, "sAB2"], [acck])

            qk(0)
            qk(1)
            for u in range(nu):
                if u % 16 == 0 and u // 16 >= 1 and u // 16 + 1 < ng:
                    issue(u // 16 + 1)
                ex(u)
                if u + 2 < nu:
                    qk(u + 2)
                pv(u)
                if u % 4 == 3:
                    adds(u // 4)
                if fin_q and u >= 1:
                    fin_q.pop(0)()
            ob0, ob1, r0 = tmpf[1][:, 0:512], tmpf[2][:, 0:512], tmpf[0][:, 0:512]
            CP("dve", ob0, bank(4), ["ps4"], ["tmpf1"])
            CP("dve", ob1, bank(5), ["ps5"], ["tmpf2"])

            def mk_fin(h=h, acc=acc, acck=acck):
                q = []
                def f_a():
                    for m in range(2):
                        MM(bank(6 + m), cONE, acc[:, m * 512:(m + 1) * 512], True, True, ["cst", acck], ["ps%d" % (6 + m)])
                q.append(f_a)
                q.append(lambda: ACT(r0, bank(6), AF.Ln, ["ps6"], ["tmpf0"]))
                q.append(lambda: ACT(r0, r0, AF.Exp, ["tmpf0"], ["tmpf0"], scale=-1.0))
                q.append(lambda: TT("dve", ob0, ob0, r0, ALU.mult, ["tmpf1", "tmpf0"], ["tmpf1"]))
                q.append(lambda: ACT(r0, bank(7), AF.Ln, ["ps7"], ["tmpf0"]))
                q.append(lambda: ACT(r0, r0, AF.Exp, ["tmpf0"], ["tmpf0"], scale=-1.0))
                q.append(lambda: TT("dve", ob1, ob1, r0, ALU.mult, ["tmpf2", "tmpf0"], ["tmpf2"]))
                q.append(lambda: STT("dve", ob0, ob1, sm[:, SM_NLAM:SM_NLAM + 1], ob0, ALU.mult, ALU.add, ["tmpf1", "tmpf2", "nlam"], ["tmpf1"]))
                q.append(lambda: TT("dve", osqb[:], ob0, ob0, ALU.mult, ["tmpf1"], ["osqb"]))
                q.append(lambda: MM(bank(6), bONE, osqb[:], True, True, ["cbf", "osqb"], ["ps6"]))
                q.append(lambda: ACT(r0, bank(6), AF.Ln, ["ps6"], ["tmpf0"], scale=1.0 / 128.0, bias=EPS))
                q.append(lambda: ACT(r0, r0, AF.Exp, ["tmpf0"], ["tmpf0"], scale=-0.5))
                q.append(lambda: TT("dve", ob0, ob0, r0, ALU.mult, ["tmpf1", "tmpf0"], ["tmpf1"]))
                q.append(lambda: STT("dve", ostage[:, h * 512:(h + 1) * 512], ob0, sm[:, SM_SLN:SM_SLN + 1], gbT[:, h * 512:(h + 1) * 512],
                                     ALU.mult, ALU.mult, ["tmpf1", "sln", "kTall1"], ["ostage"]))
                return q
            while fin_q:
                fin_q.pop(0)()
            fin_q.extend(mk_fin())
        while fin_q:
            fin_q.pop(0)()
        DMA(od_d[s], ostage[:], ["ostage"], ["od%d" % s], "ost", eng="sp")
        pending.append("od%d" % s)

    run_stage(P, st)
    if stop == "O2":
        es.close()
        return nc

    st = ExitStack()
    P = Prog("m")
    if stop == "O3":
        P.limit = limit
    WB = sl("WB3", [128, 28800], BF16)
    wst = [sl("wst%d_3" % i, [128, 2048], F32) for i in range(2)]
    xst = [sl("xst0_3", [128, 4 * 515], F32)] * 2
    xb = [sl("xb0_3", [128, 8 * 515], BF16)] * 2
    lnp = sl("lnpt", [128, 2048])
    yT = sl("yT", [128, 4096], BF16)
    oTs = sl("oTs", [128, 4096], BF16)
    merged = sl("merged", [128, 8 * 512], BF16)
    gt = [sl("gt%d" % i, [128, 512]) for i in range(2)]
    xtok = sl("xtok", [128, 4 * 1024])
    tmpf = [sl("tmpf%d_3" % i, [128, 1024]) for i in range(4)]
    DMA(lnp[:], lnp_d, (), ["lnp"], "c2")

    OFF_A, OFF_B, OFF_O, OFF_GM = 0, 8192, 16384, 24576
    load_weights(w_a, 1024, OFF_A, "wk")
    load_weights(w_b, 1024, OFF_B, "wvv")
    load_weights(w_o, 1024, OFF_O, "wx")
    wa_v, wb_v, wo_v = wv(OFF_A, 1024), wv(OFF_B, 1024), wv(OFF_O, 1024)
    gmi = [0]
    for s in range(4):
        i = 0
        xv = load_x(xT_own[:, :, s, :], 515, i)
        DMA(yT[:], ysd_d[s], (), ["yT"], "yld")
        DMA(oTs[:], od_d[s], (), ["oTs"], "old")
        xk = "xb%d" % i
        DMA(xtok[:].rearrange("p (k c) -> p k c", k=4), x_own[s].rearrange("(k p) c -> p k c", p=128), (), ["xtok"], "xtok")
        for db in range(8):
            j = gmi[0] % 2
            gmi[0] += 1
            goff = OFF_GM + j * 2048
            stv = wst[j][:, 0:2048].rearrange("p (a b c) -> p a b c", a=8, b=2)
            DMA(stv, w_gm[:, :, :].rearrange("p a (b c) -> p a b c", b=2)[:, :, :, db * 128:(db + 1) * 128], (), ["wst%d" % j], "wst%d" % j)
            CP("dve", WB[:, goff:goff + 2048].rearrange("p (a b c) -> p a b c", a=8, b=2), stv, ["wst%d" % j], ["wgm%d" % j])
            wg = WB[:, goff:goff + 2048].rearrange("p (a b c) -> p a b c", a=8, b=2)
            for br in range(2):
                for dc in range(8):
                    MM(bank(br), wg[:, dc, br, :], xv[:, dc, 3:515], dc == 0, dc == 7, ["wgm%d" % j, xk], ["ps%d" % br])
                ACT(gt[br][:], bank(br), AF.Sigmoid, ["ps%d" % br, "cst"], ["gt%d" % br], bias=cst[:, C_BG + br * 8 + db:C_BG + br * 8 + db + 1])
            for cc in range(8):
                MM(bank(2), wa_v[:, cc, db * 128:(db + 1) * 128], yT[:, cc * 512:(cc + 1) * 512], cc == 0, cc == 7,
                   ["wk", "yT"], ["ps2"])
            for cc in range(8):
                MM(bank(3), wb_v[:, cc, db * 128:(db + 1) * 128], oTs[:, cc * 512:(cc + 1) * 512], cc == 0, cc == 7,
                   ["wvv", "oTs"], ["ps3"])
            TT("dve", gt[0][:], gt[0][:], bank(2), ALU.mult, ["gt0", "ps2"], ["gt0"])
            TT("dve", gt[1][:], gt[1][:], bank(3), ALU.mult, ["gt1", "ps3"], ["gt1"])
            TT("dve", merged[:, db * 512:(db + 1) * 512], gt[0][:], gt[1][:], ALU.add, ["gt0", "gt1"], ["merged"])
        for k in range(4):
            vb = tmpf[k % 2]
            vk = "tmpf%d" % (k % 2)
            for half in range(2):
                pb = 4 + half
                for db in range(8):
                    MM(bank(pb), merged[:, db * 512 + k * 128:db * 512 + (k + 1) * 128], wo_v[:, db, half * 512:(half + 1) * 512], db == 0, db == 7,
                       ["merged", "wx"], ["ps%d" % pb])
            STT("dve", vb[:], xtok[:, k * 1024:(k + 1) * 1024], ALPHA, PS[2][:], ALU.mult, ALU.add, ["xtok", "ps4", "ps5"], [vk])
            c0 = SM_NM + (k % 2) * 4
            P.add("dve", lambda e, o=sm[:, c0:c0 + 1], ii=vb[:]: e.reduce_sum(o, ii, mybir.AxisListType.X), r=[vk], w=["ln_a%d" % (k % 2)])
            TS("dve", sm[:, c0:c0 + 1], sm[:, c0:c0 + 1], -1.0 / D, None, ALU.mult, None, ["ln_a%d" % (k % 2)], ["ln_b%d" % (k % 2)])
            sq = tmpf[2 + k % 2]
            MSET("dve", sm[:, c0 + 1:c0 + 2], 0.0, ["ln_c%d" % (k % 2)])
            ACT(sq[:], vb[:], AF.Square, [vk, "ln_b%d" % (k % 2), "ln_c%d" % (k % 2)], ["tmpf%d" % (2 + k % 2), "ln_c%d" % (k % 2)],
                bias=sm[:, c0:c0 + 1], accum=sm[:, c0 + 1:c0 + 2])
            TS("dve", sm[:, c0 + 2:c0 + 3], sm[:, c0 + 1:c0 + 2], 1.0 / D, EPS, ALU.mult, ALU.add, ["ln_c%d" % (k % 2)], ["ln_d%d" % (k % 2)])
            ACT(sm[:, c0 + 2:c0 + 3], sm[:, c0 + 2:c0 + 3], AF.Sqrt, ["ln_d%d" % (k % 2)], ["ln_d2%d" % (k % 2)])
            P.add("dve", lambda e, o=sm[:, c0 + 2:c0 + 3]: e.reciprocal(o, o), r=["ln_d2%d" % (k % 2)], w=["ln_e%d" % (k % 2)])
            TS("dve", vb[:], vb[:], sm[:, c0:c0 + 1], sm[:, c0 + 2:c0 + 3], ALU.add, ALU.mult, [vk, "ln_b%d" % (k % 2), "ln_e%d" % (k % 2)], [vk])
            TT("dve", vb[:], vb[:], lnp[:, 0:1024], ALU.mult, [vk, "lnp"], [vk])
            TT("pool", sq[:], vb[:], lnp[:, 1024:2048], ALU.add, [vk, "lnp"], ["tmpf%d" % (2 + k % 2)])
            DMA(y_out[s, k * 128:(k + 1) * 128, :], sq[:], ["tmpf%d" % (2 + k % 2)], ["yout%d_%d" % (s, k)], "yo%d" % (k % 2))
            pending.append("yout%d_%d" % (s, k))
    run_stage(P, st)
    es.close()
    return nc


def _prep_inputs(inputs):
    f = lambda a: np.ascontiguousarray(np.asarray(a, dtype=np.float32))
    x = f(inputs["x"])
    w_in = f(inputs["w_in"])[0]

    def wl(c0, c1):
        return np.ascontiguousarray(w_in[:, c0:c1].reshape(8, 128, c1 - c0).transpose(1, 0, 2))

    def wsq(w):
        return np.ascontiguousarray(f(w)[0].reshape(8, 128, 1024).transpose(1, 0, 2))

    common = {
        "w_z": wl(0, 1024), "w_x": wl(1024, 2560), "w_dt": wl(2560, 2576), "w_q": wl(2576, 3600),
        "w_k": wl(3600, 4624), "w_v": wl(4624, 5648), "w_gb": wl(5648, 6672), "w_gm": wl(6672, 8720),
        "w_a": wsq(inputs["w_a"]), "w_b": wsq(inputs["w_b"]), "w_o": wsq(inputs["w_o"]),
    }
    r = np.arange(128)
    ident = (r[:, None] == r[None, :]).astype(np.float32)
    LT = (r[:, None] <= r[None, :]).astype(np.float32)
    SU = (r[:, None] > r[None, :]).astype(np.float32)
    ones = np.ones((128, 128), np.float32)
    perm = np.zeros((128, 128), np.float32)
    for m in range(2):
        for d in range(8):
            perm[m * 64 + d + 8, m * 64 + d] = -1.0
            perm[m * 64 + d, m * 64 + d + 8] = 1.0
    bc = lambda v, n: np.broadcast_to(f(v).reshape(1, n), (128, n))
    conv_w = f(inputs["conv_w"])[0]
    cw = np.zeros((128, 48), np.float32)
    for blk in range(12):
        for w in range(4):
            cw[:, blk * 4 + w] = conv_w[w, blk * 128:(blk + 1) * 128]
    conv_b = f(inputs["conv_b"])[0]
    cb = conv_b.reshape(12, 128).T
    bg = f(inputs["b_gate"])[0].reshape(16, 128).T
    sln = f(inputs["subln_w"])[0].reshape(128, 1)
    nw = f(inputs["ssd_norm_w"])[0].reshape(8, 128).T
    lnp = np.concatenate([bc(inputs["ln_g"], 1024), bc(inputs["ln_b"], 1024)], axis=1)
    pos = np.arange(S, dtype=np.float32)
    inv_freq = (np.float32(500000.0) ** (-np.arange(0, 16, 2, dtype=np.float32) / np.float32(16))).astype(np.float32)
    ang = (pos[:, None] * inv_freq[None, :]).astype(np.float32)
    cos, sin = np.cos(ang).astype(np.float32), np.sin(ang).astype(np.float32)
    rope = np.zeros((128, 2, S), np.float32)
    rope[:, 0, :] = 1.0
    for m in range(2):
        for dd in range(16):
            rope[m * 64 + dd, 0, :] = cos[:, dd % 8]
            rope[m * 64 + dd, 1, :] = sin[:, dd % 8]
    maskA = np.zeros((8, 512), np.float32)
    for k in range(512):
        maskA[k // 64, k] = 1.0
    in_maps = []
    for c in range(8):
        b, j = c // 4, c % 4
        tiles = [j, 7 - j, 8 + j, 15 - j]
        xb = x[b]
        xT = np.ascontiguousarray(xb.T.reshape(8, 128, S).transpose(1, 0, 2))
        xo = np.zeros((128, 8, 4, 515), np.float32)
        xtok = np.zeros((4, 512, D), np.float32)
        ropeO = np.zeros((128, 2, 4, 512), np.float32)
        sel = np.zeros((128, 16), np.float32)
        maskB = np.zeros((8, 16, 512), np.float32)
        for s, t in enumerate(tiles):
            lo = t * 512
            if t > 0:
                xo[:, :, s, :] = xT[:, :, lo - 3:lo + 512]
            else:
                xo[:, :, s, 3:] = xT[:, :, 0:512]
            xtok[s] = xb[lo:lo + 512]
            ropeO[:, :, s, :] = rope[:, :, lo:lo + 512]
            sel[:, t] = 1.0
            L = KLEN[s]
            for pi in range(4):
                kt = L - 4 + pi
                if kt < t:
                    pass
                elif kt > t:
                    maskB[:, pi + 4 * s, :] = NEG
                else:
                    for rr in range(8):
                        q = np.arange(512)
                        maskB[rr, pi + 4 * s, :] = np.where(q // 64 >= rr, 0.0, NEG)
        cstv = np.concatenate([ident, LT, SU, ones, perm, bc(inputs["a_log"], 16), bc(inputs["dt_bias"], 16), bc(inputs["d_skip"], 16),
                               bc(inputs["lambda_q1"], 64), bc(inputs["lambda_k1"], 64), bc(inputs["lambda_q2"], 64), bc(inputs["lambda_k2"], 64),
                               cw, cb, bg, sln, sel, nw], axis=1).astype(np.float32)
        assert cstv.shape[1] == NCST
        m = dict(common)
        m.update({"xT_all": xT, "xT_own": xo, "x_own": xtok, "cst": np.ascontiguousarray(cstv),
                  "cbrow": conv_b.reshape(1, 1536).copy(), "lnp": np.ascontiguousarray(lnp),
                  "msk": np.ascontiguousarray(np.concatenate([maskA, maskB.reshape(8, 8192)], axis=1)),
                  "ropeA": rope, "ropeO": ropeO})
        in_maps.append(m)
    return in_maps


def kernel(**inputs):
    in_maps = _prep_inputs(inputs)
    nc = build_nc()
    res = run_bass_kernel_spmd(nc, in_maps, core_ids=list(range(8)))
    out = np.zeros((2, S, D), np.float32)
    for c in range(8):
        b, j = c // 4, c % 4
        tiles = [j, 7 - j, 8 + j, 15 - j]
        y = np.asarray(res.results[c]["y_out"], dtype=np.float32)
        for s, t in enumerate(tiles):
            out[b, t * 512:(t + 1) * 512] = y[s]
    return out
```

```python
import math
import sys
import os
from contextlib import ExitStack
import numpy as np
import concourse.bass as bass
import concourse.mybir as mybir
from concourse.bass_utils import run_bass_kernel_spmd

F32 = mybir.dt.float32
BF16 = mybir.dt.bfloat16
AF = mybir.ActivationFunctionType
ALU = mybir.AluOpType

D = 1024
S = 8192
NT = 16
KLEN = (4, 8, 12, 16)
EPS = 1e-5
ALPHA = 2.0 ** 0.25
LAMBDA_INIT = 0.2
NEG = -30000.0

C_ID, C_LT, C_SU, C_ONE, C_PERM = 0, 128, 256, 384, 512
C_ALOG, C_DTB, C_DSK = 640, 656, 672
C_LQ1, C_LK1, C_LQ2, C_LK2 = 688, 752, 816, 880
C_CW, C_CB, C_BG, C_SLN, C_SEL, C_NW = 944, 992, 1004, 1020, 1021, 1037
NCST = 1045


class _Op:
    __slots__ = ("eng", "fn", "deps", "dma", "idx", "need", "sem", "val", "src")


class Prog:
    def __init__(self, tag):
        self.tag = tag
        self.ops = []
        self.wr = {}
        self.rd = {}
        self.base = {}
        self.limit = None

    def add(self, eng, fn, r=(), w=(), dma=None):
        if self.limit is not None and len(self.ops) >= self.limit:
            return None
        o = _Op()
        o.eng, o.fn, o.dma, o.idx, o.need = eng, fn, dma, len(self.ops), False
        f = sys._getframe(1)
        o.src = (f.f_lineno, f.f_back.f_lineno if f.f_back else 0)
        deps = {}
        for k in r:
            for x in self.wr.get(k, ()):
                deps[x] = "raw"
        for k in w:
            if self.rd.get(k):
                self.base[k] = list(self.wr.get(k, ())) + list(self.rd[k])
                self.wr[k] = []
                self.rd[k] = []
            for x in self.base.get(k, ()):
                deps.setdefault(x, "war")
        for k in w:
            self.wr.setdefault(k, []).append(o.idx)
            self.rd.setdefault(k, [])
        for k in r:
            self.rd.setdefault(k, []).append(o.idx)
            self.wr.setdefault(k, [])
        deps.pop(o.idx, None)
        o.deps = deps
        self.ops.append(o)
        return o

    def finalize(self):
        cnt = {}
        for o in self.ops:
            if o.dma is not None:
                o.sem = self.tag + "d_" + o.dma
                cnt[o.sem] = cnt.get(o.sem, 0) + 16
                o.val = cnt[o.sem]
        for o in self.ops:
            best = {}
            for d, kind in o.deps.items():
                p = self.ops[d]
                if p.dma is None:
                    if p.eng == o.eng and o.eng == "pe" and o.dma is None:
                        continue
                    ch = "e_" + p.eng
                else:
                    ch = p.sem
                if ch not in best or best[ch] < d:
                    best[ch] = d
            o.deps = list(best.values())
            for d in o.deps:
                self.ops[d].need = True
        for o in self.ops:
            if o.dma is None and o.need:
                o.sem = self.tag + "e_" + o.eng
                cnt[o.sem] = cnt.get(o.sem, 0) + 1
                o.val = cnt[o.sem]
        return sorted(cnt.keys())

    def emit(self, eng, e, sems):
        waited = {}
        for o in self.ops:
            if o.eng != eng:
                continue
            for d in o.deps:
                p = self.ops[d]
                if waited.get(p.sem, 0) < p.val:
                    e.wait_ge(sems[p.sem], p.val)
                    waited[p.sem] = p.val
            ins = o.fn(e)
            if o.dma is not None:
                ins.then_inc(sems[o.sem], 16)
            elif o.need:
                ins.then_inc(sems[o.sem], 1)


def build_nc(stop=None, nta=NT, limit=None):
    nc = bass.Bass("TRN2", target_bir_lowering=False)
    dt_in = lambda n, shp, t=F32: nc.dram_tensor(n, shp, t, kind="ExternalInput").ap()
    xT_all = dt_in("xT_all", [128, 8, S])
    xT_own = dt_in("xT_own", [128, 8, 4, 515])
    x_own = dt_in("x_own", [4, 512, D])
    w_k = dt_in("w_k", [128, 8, 1024])
    w_v = dt_in("w_v", [128, 8, 1024])
    w_x = dt_in("w_x", [128, 8, 1536])
    w_dt = dt_in("w_dt", [128, 8, 16])
    w_q = dt_in("w_q", [128, 8, 1024])
    w_z = dt_in("w_z", [128, 8, 1024])
    w_gb = dt_in("w_gb", [128, 8, 1024])
    w_gm = dt_in("w_gm", [128, 8, 2048])
    w_a = dt_in("w_a", [128, 8, 1024])
    w_b = dt_in("w_b", [128, 8, 1024])
    w_o = dt_in("w_o", [128, 8, 1024])
    cst_d = dt_in("cst", [128, NCST])
    cbrow_d = dt_in("cbrow", [1, 1536])
    lnp_d = dt_in("lnp", [128, 2048])
    msk_d = dt_in("msk", [8, 512 + 16 * 512])
    ropeA = dt_in("ropeA", [128, 2, S])
    ropeO = dt_in("ropeO", [128, 2, 4, 512])
    y_out = nc.dram_tensor("y_out", [4, 512, D], F32, kind="ExternalOutput").ap()
    ksc = nc.dram_tensor("ksc", [8, 128, S], BF16).ap()
    vsc = nc.dram_tensor("vsc", [S, 1024], BF16).ap()

    hs_d = nc.dram_tensor("hs_d", [4, 128, 1024], F32).ap()
    ysd_d = nc.dram_tensor("ysd_d", [4, 128, 4096], BF16).ap()
    od_d = nc.dram_tensor("od_d", [4, 128, 4096], BF16).ap()

    es = ExitStack()
    st = ExitStack()
    P = Prog("s")
    pending = []
    sb = lambda n, shp, t=F32: es.enter_context(nc.sbuf_tensor(n, shp, t))
    sl = lambda n, shp, t=F32: st.enter_context(nc.sbuf_tensor(n, shp, t))
    cst = sb("cstt", [128, NCST])
    cbf = sb("cbf", [128, 640], BF16)
    diag = sb("diag", [128, 48 * 128], BF16)
    cbrow_b = sb("cbrowb", [1, 1536], BF16)
    onesrow_b = sb("onesrow", [1, 128], BF16)
    sm = sb("sm", [128, 64])
    PS = [es.enter_context(nc.psum_tensor("ps%d" % i, [128, 1024], F32)) for i in range(4)]

    def run_stage(P, st):
        if os.environ.get("KDBG"):
            print("STAGE", P.tag, "nops", len(P.ops), "last", P.ops[-1].eng, P.ops[-1].src)
        P.limit = None
        P.add("sp", lambda e: e.nop(), r=list(pending), w=["done"])
        del pending[:]
        names = P.finalize()
        sems = {n: st.enter_context(nc.semaphore(n)) for n in names}
        with nc.Block() as block:
            @block.sync
            def _(e):
                P.emit("sp", e, sems)

            @block.tensor
            def _(e):
                P.emit("pe", e, sems)

            @block.scalar
            def _(e):
                P.emit("act", e, sems)

            @block.vector
            def _(e):
                P.emit("dve", e, sems)

            @block.gpsimd
            def _(e):
                P.emit("pool", e, sems)
        st.close()
        nc.all_engine_barrier()

    WB = sl("WB", [128, 28800], BF16)
    wst = [sl("wst%d" % i, [128, 2048], F32) for i in range(2)]
    xst = [sl("xst0", [128, 4 * 515], F32)] * 2
    xb = [sl("xb%d" % i, [128, 8 * 515], BF16) for i in range(2)]
    uT = sl("uT", [128, 12 * 515], BF16)
    rt = [sl("rt0", [128, 1024])] * 2
    kc = [sl("kc%d" % i, [128, 512], BF16) for i in range(2)]
    tmpf = [sl("tmpf%d" % i, [128, 1536 if i == 3 else 512]) for i in range(4)]
    kTall = [sl("kTall0", [128, 8 * 512], BF16)] * 2
    vall = [sl("vall0", [128, 4 * 1024], BF16)] * 2
    xs_t = sl("xs_t", [128, 4 * 1024], BF16)
    B_t = sl("B_t", [128, 4 * 256], BF16)
    dtt = sl("dtt", [128, 64])
    adt = sl("adt", [128, 64])
    dk = sl("dk", [128, 64])
    wl = sl("wl", [128, 64])
    cdT = sl("cdT", [128, 16])
    xdtd = sl("xdtd", [128, 4 * 1024], BF16)
    hcur = sl("hcur", [128, 1024])
    hs = [sl("hs%d" % i, [128, 1024]) for i in range(4)]

    def bank(i):
        return PS[i // 2][:, (i % 2) * 512:(i % 2) * 512 + 512]

    cI = cst[:, C_ID:C_ID + 128]
    cLT = cst[:, C_LT:C_LT + 128]
    cSU = cst[:, C_SU:C_SU + 128]
    cONE = cst[:, C_ONE:C_ONE + 128]
    bI = cbf[:, 0:128]
    bONE = cbf[:, 384:512]
    bPERM = cbf[:, 512:640]
    SM_A, SM_NLAM, SM_T0, SM_T1, SM_SLN, SM_NM, SM_SS, SM_RS = 0, 16, 17, 18, 19, 20, 24, 28

    dma_rr = [0]

    def DMA(out, in_, r, w, key, eng="sp"):
        P.add(eng, lambda e, o=out, i=in_: e.dma_start(out=o, in_=i), r=r, w=w, dma=key)

    def MM(out, lhsT, rhs, start, stop, r, w):
        P.add("pe", lambda e, o=out, l=lhsT, rr=rhs, s=start, t=stop: e.matmul(o, lhsT=l, rhs=rr, start=s, stop=t),
              r=r, w=w)

    def ACT(out, in_, func, r, w, bias=None, scale=None, accum=None):
        def fn(e, o=out, i=in_, f=func, b=bias, sc=scale, a=accum):
            kw = {}
            if b is not None:
                kw["bias"] = b
            if sc is not None:
                kw["scale"] = sc
            if a is not None:
                kw["accum_out"] = a
            return e.activation(o, i, f, **kw)
        P.add("act", fn, r=r, w=w)

    def TT(eng, out, in0, in1, op, r, w):
        P.add(eng, lambda e, o=out, a=in0, b=in1, p=op: e.tensor_tensor(out=o, in0=a, in1=b, op=p), r=r, w=w)

    def TS(eng, out, in0, s1, s2, op0, op1, r, w):
        if op1 is None:
            P.add(eng, lambda e, o=out, a=in0, x=s1, p=op0: e.tensor_scalar(o, a, x, None, p), r=r, w=w)
        else:
            P.add(eng, lambda e, o=out, a=in0, x=s1, y=s2, p=op0, q=op1: e.tensor_scalar(o, a, x, y, p, q), r=r, w=w)

    def STT(eng, out, in0, sc, in1, op0, op1, r, w):
        P.add(eng, lambda e, o=out, a=in0, s=sc, b=in1, p=op0, q=op1: e.scalar_tensor_tensor(o, a, s, b, p, q), r=r, w=w)

    def CP(eng, out, in_, r, w):
        P.add(eng, lambda e, o=out, i=in_: e.tensor_copy(out=o, in_=i), r=r, w=w)

    def MSET(eng, ap, v, w):
        P.add(eng, lambda e, a=ap, x=v: e.memset(a, x), r=(), w=w)

    DMA(cst[:], cst_d, (), ["cst"], "c0")
    DMA(tmpf[3][0:1, 0:1536], cbrow_d, (), ["tmpf3"], "c1")
    CP("dve", cbf[:], cst[:, 0:640], ["cst"], ["cbf"])
    CP("dve", cbrow_b[:], tmpf[3][0:1, 0:1536], ["tmpf3"], ["cbrowb"])
    MSET("dve", onesrow_b[:], 1.0, ["onesrow"])
    MSET("dve", uT[:], 0.0, ["uT%d" % b for b in range(12)])
    MSET("dve", hcur[:], 0.0, ["hcur"])
    for i in range(4):
        MSET("dve", hs[i][:], 0.0, ["hs%d" % i])
    for blk in range(12):
        for w in range(4):
            TS("dve", diag[:, (blk * 4 + w) * 128:(blk * 4 + w + 1) * 128], cI,
               cst[:, C_CW + blk * 4 + w:C_CW + blk * 4 + w + 1], None, ALU.mult, None, ["cst"], ["diag"])
    ACT(sm[:, SM_A:SM_A + 16], cst[:, C_ALOG:C_ALOG + 16], AF.Exp, ["cst"], ["sm_a0"])
    TS("dve", sm[:, SM_A:SM_A + 16], sm[:, SM_A:SM_A + 16], -1.0, None, ALU.mult, None, ["sm_a0"], ["sm_a"])
    TT("dve", tmpf[0][:, 0:64], cst[:, C_LQ1:C_LQ1 + 64], cst[:, C_LK1:C_LK1 + 64], ALU.mult, ["cst"], ["lamt0"])
    TT("dve", tmpf[0][:, 64:128], cst[:, C_LQ2:C_LQ2 + 64], cst[:, C_LK2:C_LK2 + 64], ALU.mult, ["cst"], ["lamt1"])
    P.add("dve", lambda e: e.reduce_sum(sm[:, SM_T0:SM_T0 + 1], tmpf[0][:, 0:64], mybir.AxisListType.X), r=["lamt0"], w=["lam_s0"])
    P.add("dve", lambda e: e.reduce_sum(sm[:, SM_T1:SM_T1 + 1], tmpf[0][:, 64:128], mybir.AxisListType.X), r=["lamt1"], w=["lam_s1"])
    ACT(sm[:, SM_T0:SM_T0 + 2], sm[:, SM_T0:SM_T0 + 2], AF.Exp, ["lam_s0", "lam_s1"], ["lam_e"])
    TT("dve", sm[:, SM_NLAM:SM_NLAM + 1], sm[:, SM_T1:SM_T1 + 1], sm[:, SM_T0:SM_T0 + 1], ALU.subtract, ["lam_e"], ["nlam0"])
    TS("dve", sm[:, SM_NLAM:SM_NLAM + 1], sm[:, SM_NLAM:SM_NLAM + 1], -LAMBDA_INIT, None, ALU.add, None, ["nlam0"], ["nlam"])
    TS("dve", sm[:, SM_SLN:SM_SLN + 1], cst[:, C_SLN:C_SLN + 1], 1.0 - LAMBDA_INIT, None, ALU.mult, None, ["cst"], ["sln"])

    wstate = {"i": 0}

    def load_weights(dram, ncols, wb_off, key, scale_rows=None):
        c0 = 0
        while c0 < ncols:
            cw = min(256, ncols - c0)
            i = wstate["i"] % 2
            wstate["i"] += 1
            stv = wst[i][:, 0:8 * cw].rearrange("p (a c) -> p a c", a=8)
            DMA(stv, dram[:, :, c0:c0 + cw], (), ["wst%d" % i], "wst%d" % i)
            dst = WB[:, wb_off:wb_off + 8 * ncols].rearrange("p (a c) -> p a c", a=8)[:, :, c0:c0 + cw]
            CP("pool", dst, stv, ["wst%d" % i], [key])
            c0 += cw

    def wv(wb_off, ncols):
        return WB[:, wb_off:wb_off + 8 * ncols].rearrange("p (a c) -> p a c", a=8)

    def load_x(src, ncol, i, eng="pool"):
        xv = xb[i][:, 0:8 * ncol].rearrange("p (a c) -> p a c", a=8)
        for hf in range(2):
            stv = xst[0][:, 0:4 * ncol].rearrange("p (a c) -> p a c", a=4)
            DMA(stv, src[:, hf * 4:(hf + 1) * 4, :], (), ["xst0"], "xst0")
            CP(eng, xv[:, hf * 4:(hf + 1) * 4, :], stv, ["xst0"], ["xb%d" % i])
        return xv

    ev = {"i": 0}

    def evac(out, in_, r, w):
        ev["i"] += 1
        if ev["i"] % 2:
            P.add("act", lambda e, o=out, i=in_: e.copy(o, i), r=r, w=w)
        else:
            CP("dve", out, in_, r, w)

    def softplus_dt(psap, k, rk):
        d = dtt[:, k * 16:(k + 1) * 16]
        TT("dve", d, psap, cst[:, C_DTB:C_DTB + 16], ALU.add, [rk, "cst"], ["dtt%d" % k])
        ACT(d, d, AF.Exp, ["dtt%d" % k], ["dtt%d" % k])
        ACT(d, d, AF.Ln, ["dtt%d" % k], ["dtt%d" % k], bias=1.0)
        TT("dve", adt[:, k * 16:(k + 1) * 16], d, sm[:, SM_A:SM_A + 16], ALU.mult, ["dtt%d" % k, "sm_a"], ["adt%d" % k])

    def conv_tok(k, blks, pb, xoff):
        for gi in range(0, len(blks), 4):
            grp = blks[gi:gi + 4]
            pk = "ps%d" % pb
            for j, blk in enumerate(grp):
                o = bank(pb)[:, j * 128:(j + 1) * 128]
                for w in range(4):
                    MM(o, uT[:, blk * 515 + k * 128 + w: blk * 515 + k * 128 + w + 128],
                       diag[:, (blk * 4 + w) * 128:(blk * 4 + w + 1) * 128], w == 0, False,
                       ["uT%d" % blk, "diag"], [pk])
                MM(o, onesrow_b[0:1, :], cbrow_b[0:1, blk * 128:(blk + 1) * 128], False, True,
                   ["onesrow", "cbrowb"], [pk])
            n = len(grp) * 128
            b0 = grp[0]
            if b0 < 8:
                ACT(xs_t[:, k * 1024 + b0 * 128:k * 1024 + b0 * 128 + n], bank(pb)[:, 0:n], AF.Silu, [pk], ["xs_t%d" % k])
            else:
                ACT(B_t[:, k * 256:(k + 1) * 256], bank(pb)[:, 0:n], AF.Silu, [pk], ["B_t%d" % k])
            pb = pb + 1 if pb % 2 == 0 else pb - 1
        return pb

    def proj_u(xv, blks, wx_off, col0, pbs, halo):
        wx = wv(wx_off, 1536)
        for bi, blk in enumerate(blks):
            pb = pbs[bi % len(pbs)]
            pk = "ps%d" % pb
            uk = "uT%d" % blk
            if halo == "carry":
                CP("dve", uT[:, blk * 515:blk * 515 + 3], uT[:, blk * 515 + 512:blk * 515 + 515], [uk], [uk])
            else:
                for dc in range(8):
                    MM(bank(pb)[:, 0:3], wx[:, dc, blk * 128:(blk + 1) * 128], xv[:, dc, 0:3], dc == 0, dc == 7,
                       ["wx", halo], [pk])
                evac(uT[:, blk * 515:blk * 515 + 3], bank(pb)[:, 0:3], [pk], [uk])
            for dc in range(8):
                MM(bank(pb), wx[:, dc, blk * 128:(blk + 1) * 128], xv[:, dc, col0:col0 + 512], dc == 0, dc == 7,
                   ["wx", halo if halo != "carry" else "xbcur"], [pk])
            evac(uT[:, blk * 515 + 3:blk * 515 + 515], bank(pb), [pk], [uk])

    OFF_K, OFF_V, OFF_X, OFF_DT = 0, 8192, 16384, 16384 + 12288
    load_weights(w_k, 1024, OFF_K, "wk")
    load_weights(w_v, 1024, OFF_V, "wvv")
    load_weights(w_x, 1536, OFF_X, "wx")
    load_weights(w_dt, 16, OFF_DT, "wdt")
    wk_v, wv_v, wdt_v = wv(OFF_K, 1024), wv(OFF_V, 1024), wv(OFF_DT, 16)

    def rope_block(h, pb, srcw, srck, xv, col0, rtab, dst, dstk, xk):
        pk = "ps%d" % pb
        for dc in range(8):
            MM(bank(pb), srcw[:, dc, h * 128:(h + 1) * 128], xv[:, dc, col0:col0 + 512], dc == 0, dc == 7, [srck, xk], [pk])
        kcb = kc[h % 2]
        kk = "kc%d" % (h % 2)
        P.add("act", lambda e, o=kcb[:], i=bank(pb): e.copy(o, i), r=[pk], w=[kk])
        pb2 = pb + 2
        pk2 = "ps%d" % pb2
        MM(bank(pb2), bPERM, kcb[:], True, True, ["cbf", kk], [pk2])
        t = tmpf[h % 2][:, 0:512]
        tk = "tmpf%d" % (h % 2)
        TT("dve", t, bank(pb2), rtab[:, 512:1024], ALU.mult, [pk2, "rt"], [tk])
        TT("dve", tmpf[2 + h % 2][:, 0:512], kcb[:], rtab[:, 0:512], ALU.mult, [kk, "rt"], ["tmpf%d" % (2 + h % 2)])
        TT("dve", dst, tmpf[2 + h % 2][:, 0:512], t, ALU.add, [tk, "tmpf%d" % (2 + h % 2)], [dstk])

    xv_next = load_x(xT_all[:, :, 0:512], 512, 0)
    for T in range(nta):
        i = T % 2
        xv = xv_next
        xk = "xb%d" % i
        if T + 1 < nta:
            xv_next = load_x(xT_all[:, :, (T + 1) * 512:(T + 2) * 512], 512, 1 - i)
        rtab = rt[i]
        DMA(rtab[:].rearrange("p (a c) -> p a c", a=2), ropeA[:, :, T * 512:(T + 1) * 512], (), ["rt0"], "rt0")
        def kproj(h):
            pk = h % 2
            for dc in range(8):
                MM(bank(pk), wk_v[:, dc, h * 128:(h + 1) * 128], xv[:, dc, :], dc == 0, dc == 7, ["wk", xk], ["ps%d" % pk])
        kproj(0)
        for h in range(8):
            pk = h % 2
            kcb = kc[h % 2]
            kk = "kc%d" % (h % 2)
            P.add("act", lambda e, o=kcb[:], ii=bank(pk): e.copy(o, ii), r=["ps%d" % pk], w=[kk])
            if h + 1 < 8:
                kproj(h + 1)
            MM(bank(2 + pk), bPERM, kcb[:], True, True, ["cbf", kk], ["ps%d" % (2 + pk)])
            t = tmpf[h % 2][:, 0:512]
            TT("dve", t, bank(2 + pk), rtab[:, 512:1024], ALU.mult, ["ps%d" % (2 + pk), "rt0"], ["tmpf%d" % (h % 2)])
            TT("dve", tmpf[2 + h % 2][:, 0:512], kcb[:], rtab[:, 0:512], ALU.mult, [kk, "rt0"], ["tmpf%d" % (2 + h % 2)])
            TT("dve", kTall[i][:, h * 512:(h + 1) * 512], tmpf[2 + h % 2][:, 0:512], t, ALU.add,
               ["tmpf%d" % (h % 2), "tmpf%d" % (2 + h % 2)], ["kTall0"])
        DMA(ksc[:, :, T * 512:(T + 1) * 512].rearrange("h r t -> r h t"),
            kTall[i][:].rearrange("p (h t) -> p h t", h=8), ["kTall0"], ["ksc%d" % T], "kst0", eng="pool")
        pending.append("ksc%d" % T)
        for k in range(4):
            for half in range(2):
                pb = 4 + (k * 2 + half) % 2
                for dc in range(8):
                    MM(bank(pb), xv[:, dc, k * 128:(k + 1) * 128], wv_v[:, dc, half * 512:(half + 1) * 512], dc == 0, dc == 7,
                       ["wvv", xk], ["ps%d" % pb])
                evac(vall[i][:, k * 1024 + half * 512:k * 1024 + half * 512 + 512], bank(pb), ["ps%d" % pb], ["vall0"])
        DMA(vsc[T * 512:(T + 1) * 512, :].rearrange("(k p) c -> p k c", p=128),
            vall[i][:].rearrange("p (k c) -> p k c", k=4), ["vall0"], ["vsc%d" % T], "vst0", eng="pool")
        pending.append("vsc%d" % T)
        wx = wv(OFF_X, 1536)
        for bi, blk in enumerate(range(10)):
            pb = 6 + bi % 2
            uk = "uT%d" % blk
            CP("dve", uT[:, blk * 515:blk * 515 + 3], uT[:, blk * 515 + 512:blk * 515 + 515], [uk], [uk])
            for dc in range(8):
                MM(bank(pb), wx[:, dc, blk * 128:(blk + 1) * 128], xv[:, dc, :], dc == 0, dc == 7, ["wx", xk], ["ps%d" % pb])
            evac(uT[:, blk * 515 + 3:blk * 515 + 515], bank(pb), ["ps%d" % pb], [uk])
        for k in range(4):
            o = bank(4)[:, k * 16:(k + 1) * 16]
            for dc in range(8):
                MM(o, xv[:, dc, k * 128:(k + 1) * 128], wdt_v[:, dc, :], dc == 0, dc == 7, ["wdt", xk], ["ps4"])
        for k in range(4):
            softplus_dt(bank(4)[:, k * 16:(k + 1) * 16], k, "ps4")
        pb = 0
        for k in range(4):
            pb = conv_tok(k, list(range(8)), pb, 0)
            pb = conv_tok(k, [8, 9], pb, 0)
        for k in range(4):
            o = bank(5)[:, k * 16:(k + 1) * 16]
            MM(o, cSU, adt[:, k * 16:(k + 1) * 16], True, k == 3, ["cst", "adt%d" % k], ["ps5"])
            for k2 in range(k + 1, 4):
                MM(o, cONE, adt[:, k2 * 16:(k2 + 1) * 16], False, k2 == 3, ["cst", "adt%d" % k2], ["ps5"])
        o = bank(5)[:, 64:80]
        for k in range(4):
            MM(o, cONE, adt[:, k * 16:(k + 1) * 16], k == 0, k == 3, ["cst", "adt%d" % k], ["ps5"])
        ACT(dk[:, 0:64], bank(5)[:, 0:64], AF.Exp, ["ps5"], ["dk"])
        ACT(cdT[:], bank(5)[:, 64:80], AF.Exp, ["ps5"], ["cdT"])
        TT("dve", wl[:], dtt[:], dk[:], ALU.mult, ["dk"] + ["dtt%d" % k for k in range(4)], ["wl"])
        for k in range(4):
            TT("dve", xdtd[:, k * 1024:(k + 1) * 1024].rearrange("p (h q) -> p h q", h=16),
               xs_t[:, k * 1024:(k + 1) * 1024].rearrange("p (h q) -> p h q", h=16),
               wl[:, k * 16:(k + 1) * 16].unsqueeze(2).to_broadcast([128, 16, 64]), ALU.mult,
               ["xs_t%d" % k, "wl"], ["xdtd%d" % k])
        for g in range(2):
            for k in range(4):
                MM(bank(6 + g), B_t[:, k * 256 + g * 128:k * 256 + (g + 1) * 128], xdtd[:, k * 1024 + g * 512:k * 1024 + (g + 1) * 512],
                   k == 0, k == 3, ["B_t%d" % k, "xdtd%d" % k], ["ps%d" % (6 + g)])
        STT("dve", hs[T // 4][:], hcur[:], cst[:, C_SEL + T:C_SEL + T + 1], hs[T // 4][:], ALU.mult, ALU.add,
            ["hcur", "cst", "hs%d" % (T // 4)], ["hs%d" % (T // 4)])
        TT("dve", hcur[:].rearrange("p (h q) -> p h q", h=16), hcur[:].rearrange("p (h q) -> p h q", h=16),
           cdT[:].unsqueeze(2).to_broadcast([128, 16, 64]), ALU.mult, ["hcur", "cdT"], ["hcur"])
        TT("dve", hcur[:], hcur[:], PS[3][:], ALU.add, ["hcur", "ps6", "ps7"], ["hcur"])

    for i in range(4):
        DMA(hs_d[i], hs[i][:], ["hs%d" % i], ["hsd%d" % i], "hst", eng="sp")
        pending.append("hsd%d" % i)
    run_stage(P, st)
    if stop == "A":
        es.close()
        return nc

    st = ExitStack()
    P = Prog("o")
    if stop == "O1":
        P.limit = limit
    WB = sl("WB1", [128, 28800], BF16)
    wst = [sl("wst%d_1" % i, [128, 2048], F32) for i in range(2)]
    xst = [sl("xst0_1", [128, 4 * 515], F32)] * 2
    xb = [sl("xb0_1", [128, 8 * 515], BF16)] * 2
    uT = sl("uT_1", [128, 12 * 515], BF16)
    tmpf = [sl("tmpf%d_1" % i, [128, 1024]) for i in range(3)]
    xs_t = sl("xs_t_1", [128, 4 * 1024], BF16)
    B_t = sl("B_t_1", [128, 4 * 256], BF16)
    zs = sl("zs", [128, 4 * 1024], BF16)
    BCT = sl("BCT", [128, 4 * 512], BF16)
    dtt = sl("dtt_1", [128, 64])
    adt = sl("adt_1", [128, 64])
    ex3 = sl("ex3", [128, 48])
    xdtd = sl("xdtd_1", [128, 1024], BF16)
    xdt = sl("xdt", [128, 1024], BF16)
    hcur = sl("hcur_1", [128, 1024])
    prevb = sl("prevb", [128, 1024], BF16)
    Rm = sl("Rm", [128, 2048])
    Lex = sl("Lex", [128, 2048], BF16)
    cbm = sl("cbm", [128, 256])
    MT = sl("MT", [128, 2048], BF16)
    ystage = sl("ystage", [128, 4096], BF16)

    OFF_Z = 0
    load_weights(w_z, 1024, OFF_Z, "wk")
    load_weights(w_x, 1536, OFF_X, "wx")
    load_weights(w_dt, 16, OFF_DT, "wdt")
    wdt_v = wv(OFF_DT, 16)
    wz_v = wv(OFF_Z, 1024)
    wx = wv(OFF_X, 1536)
    normw = cst[:, C_NW:C_NW + 8]
    for s in range(4):
        i = 0
        xv = load_x(xT_own[:, :, s, :], 515, i)
        xk = "xb%d" % i
        for bi, blk in enumerate(range(12)):
            pb = bi % 2
            uk = "uT%d" % blk
            for dc in range(8):
                MM(bank(pb)[:, 0:3], wx[:, dc, blk * 128:(blk + 1) * 128], xv[:, dc, 0:3], dc == 0, dc == 7, ["wx", xk], ["ps%d" % pb])
            evac(uT[:, blk * 515:blk * 515 + 3], bank(pb)[:, 0:3], ["ps%d" % pb], [uk])
            for dc in range(8):
                MM(bank(pb), wx[:, dc, blk * 128:(blk + 1) * 128], xv[:, dc, 3:515], dc == 0, dc == 7, ["wx", xk], ["ps%d" % pb])
            evac(uT[:, blk * 515 + 3:blk * 515 + 515], bank(pb), ["ps%d" % pb], [uk])
        for bi, blk in enumerate((8, 9, 10, 11)):
            pb = 2 + bi % 2
            for w in range(4):
                MM(bank(pb), diag[:, (blk * 4 + w) * 128:(blk * 4 + w + 1) * 128], uT[:, blk * 515 + w:blk * 515 + w + 512],
                   w == 0, w == 3, ["uT%d" % blk, "diag"], ["ps%d" % pb])
            ACT(BCT[:, bi * 512:(bi + 1) * 512], bank(pb), AF.Silu, ["ps%d" % pb, "cst"], ["BCT%d" % bi],
                bias=cst[:, C_CB + blk:C_CB + blk + 1])
        for k in range(4):
            o = bank(4)[:, k * 16:(k + 1) * 16]
            for dc in range(8):
                MM(o, xv[:, dc, 3 + k * 128:3 + (k + 1) * 128], wdt_v[:, dc, :], dc == 0, dc == 7, ["wdt", xk], ["ps4"])
        for k in range(4):
            softplus_dt(bank(4)[:, k * 16:(k + 1) * 16], k, "ps4")
        pb = 6
        for k in range(4):
            pb = conv_tok(k, list(range(8)), pb, 0)
            pb = conv_tok(k, [8, 9], pb, 0)
        for k in range(4):
            for half in range(2):
                pb = (k * 2 + half) % 2
                for dc in range(8):
                    MM(bank(pb), xv[:, dc, 3 + k * 128:3 + (k + 1) * 128], wz_v[:, dc, half * 512:(half + 1) * 512], dc == 0, dc == 7,
                       ["wk", xk], ["ps%d" % pb])
                ACT(zs[:, k * 1024 + half * 512:k * 1024 + half * 512 + 512], bank(pb), AF.Silu, ["ps%d" % pb], ["zs%d" % k])
        DMA(hcur[:], hs_d[s], (), ["hcur"], "hld")
        for k in range(4):
            CP("pool", prevb[:], hcur[:], ["hcur"], ["prevb"])
            a_k = adt[:, k * 16:(k + 1) * 16]
            ak = "adt%d" % k
            TT("dve", Rm[:].rearrange("p (h l) -> p h l", h=16), a_k.unsqueeze(2).to_broadcast([128, 16, 128]),
               cLT.unsqueeze(1).to_broadcast([128, 16, 128]), ALU.mult, [ak, "cst"], ["Rm"])
            for q in range(4):
                MM(bank(q), cSU, Rm[:, q * 512:(q + 1) * 512], True, True, ["cst", "Rm"], ["ps%d" % q])
            ACT(Lex[:, 0:1024], PS[0][:], AF.Exp, ["ps0", "ps1"], ["Lex0"])
            ACT(Lex[:, 1024:2048], PS[1][:], AF.Exp, ["ps2", "ps3"], ["Lex1"])
            for g in range(2):
                MM(bank(4)[:, g * 128:(g + 1) * 128], BCT[:, g * 512 + k * 128:g * 512 + (k + 1) * 128],
                   BCT[:, (2 + g) * 512 + k * 128:(2 + g) * 512 + (k + 1) * 128], True, True, ["BCT%d" % g, "BCT%d" % (2 + g)], ["ps4"])
            TT("dve", cbm[:].rearrange("p (g l) -> p g l", g=2), bank(4)[:, 0:256].rearrange("p (g l) -> p g l", g=2),
               cLT.unsqueeze(1).to_broadcast([128, 2, 128]), ALU.mult, ["ps4", "cst"], ["cbm"])
            TT("dve", MT[:].rearrange("p (g h l) -> p g h l", g=2, h=8), Lex[:].rearrange("p (g h l) -> p g h l", g=2, h=8),
               cbm[:].rearrange("p (g l) -> p g l", g=2).unsqueeze(2).to_broadcast([128, 2, 8, 128]), ALU.mult,
               ["Lex0", "Lex1", "cbm"], ["MT"])
            MM(bank(5)[:, 0:16], cLT, a_k, True, True, ["cst", ak], ["ps5"])
            MM(bank(5)[:, 16:32], cSU, a_k, True, True, ["cst", ak], ["ps5"])
            MM(bank(5)[:, 32:48], cONE, a_k, True, True, ["cst", ak], ["ps5"])
            ACT(ex3[:], bank(5)[:, 0:48], AF.Exp, ["ps5"], ["ex3"])
            xsk = xs_t[:, k * 1024:(k + 1) * 1024].rearrange("p (h q) -> p h q", h=16)
            TT("dve", xdt[:].rearrange("p (h q) -> p h q", h=16), xsk,
               dtt[:, k * 16:(k + 1) * 16].unsqueeze(2).to_broadcast([128, 16, 64]), ALU.mult, ["xs_t%d" % k, "dtt%d" % k], ["xdt"])
            TT("dve", xdtd[:, 0:1024].rearrange("p (h q) -> p h q", h=16), xdt[:].rearrange("p (h q) -> p h q", h=16),
               ex3[:, 16:32].unsqueeze(2).to_broadcast([128, 16, 64]), ALU.mult, ["xdt", "ex3"], ["xdtd0"])
            for h in range(16):
                MM(PS[0][:, h * 64:(h + 1) * 64], MT[:, h * 128:(h + 1) * 128], xdt[:, h * 64:(h + 1) * 64], True, True,
                   ["MT", "xdt"], ["ps%d" % (h // 8)])
            for g in range(2):
                MM(bank(2 + g), BCT[:, (2 + g) * 512 + k * 128:(2 + g) * 512 + (k + 1) * 128], prevb[:, g * 512:(g + 1) * 512], True, True,
                   ["BCT%d" % (2 + g), "prevb"], ["ps%d" % (2 + g)])
            t1, t2 = tmpf[0], tmpf[1]
            TT("dve", t1[:].rearrange("p (h q) -> p h q", h=16), PS[1][:].rearrange("p (h q) -> p h q", h=16),
               ex3[:, 0:16].unsqueeze(2).to_broadcast([128, 16, 64]), ALU.mult, ["ps2", "ps3", "ex3"], ["tmpf0"])
            TT("dve", t1[:], t1[:], PS[0][:], ALU.add, ["tmpf0", "ps0", "ps1"], ["tmpf0"])
            TT("pool", t2[:].rearrange("p (h q) -> p h q", h=16), xsk,
               cst[:, C_DSK:C_DSK + 16].unsqueeze(2).to_broadcast([128, 16, 64]), ALU.mult, ["xs_t%d" % k, "cst"], ["tmpf1"])
            TT("dve", t1[:], t1[:], t2[:], ALU.add, ["tmpf0", "tmpf1"], ["tmpf0"])
            TT("dve", t1[:], t1[:], zs[:, k * 1024:(k + 1) * 1024], ALU.mult, ["tmpf0", "zs%d" % k], ["tmpf0"])
            MSET("dve", sm[:, SM_SS:SM_SS + 2], 0.0, ["ssq0", "ssq1"])
            for g in range(2):
                ACT(t2[:, g * 512:(g + 1) * 512], t1[:, g * 512:(g + 1) * 512], AF.Square, ["tmpf0", "ssq%d" % g], ["tmpf1", "ssq%d" % g],
                    accum=sm[:, SM_SS + g:SM_SS + g + 1])
            TS("dve", sm[:, SM_RS:SM_RS + 2], sm[:, SM_SS:SM_SS + 2], 1.0 / 512.0, EPS, ALU.mult, ALU.add, ["ssq0", "ssq1"], ["rs0"])
            ACT(sm[:, SM_RS:SM_RS + 2], sm[:, SM_RS:SM_RS + 2], AF.Sqrt, ["rs0"], ["rs1"])
            P.add("dve", lambda e: e.reciprocal(sm[:, SM_RS:SM_RS + 2], sm[:, SM_RS:SM_RS + 2]), r=["rs1"], w=["rs"])
            t3 = tmpf[2]
            for g in range(2):
                TS("dve", t3[:, g * 512:(g + 1) * 512], t1[:, g * 512:(g + 1) * 512], sm[:, SM_RS + g:SM_RS + g + 1], None,
                   ALU.mult, None, ["tmpf0", "rs"], ["tmpf2"])
            for cc in range(8):
                P.add("pe", lambda e, o=PS[3][:, cc * 128:(cc + 1) * 128], ii=t3[:, cc * 128:(cc + 1) * 128]: e.transpose(o, ii, cI),
                      r=["tmpf2", "cst"], w=["ps%d" % (6 + cc // 4)])
            for cc in range(8):
                ACT(ystage[:, cc * 512 + k * 128:cc * 512 + (k + 1) * 128],
                    PS[3][:, cc * 128:(cc + 1) * 128], AF.Copy, ["ps%d" % (6 + cc // 4), "cst"], ["ystage"], scale=normw[:, cc:cc + 1])
            if k < 3:
                for g in range(2):
                    MM(bank(6 + g), B_t[:, k * 256 + g * 128:k * 256 + (g + 1) * 128], xdtd[:, g * 512:(g + 1) * 512], True, True,
                       ["B_t%d" % k, "xdtd0"], ["ps%d" % (6 + g)])
                TT("dve", hcur[:].rearrange("p (h q) -> p h q", h=16), hcur[:].rearrange("p (h q) -> p h q", h=16),
                   ex3[:, 32:48].unsqueeze(2).to_broadcast([128, 16, 64]), ALU.mult, ["hcur", "ex3"], ["hcur"])
                TT("dve", hcur[:], hcur[:], PS[3][:], ALU.add, ["hcur", "ps6", "ps7"], ["hcur"])
        DMA(ysd_d[s], ystage[:], ["ystage"], ["ysd%d" % s], "yst", eng="sp")
        pending.append("ysd%d" % s)

    run_stage(P, st)
    if stop == "O1":
        es.close()
        return nc

    st = ExitStack()
    P = Prog("a")
    if stop == "O2":
        P.limit = limit
    WB = sl("WB2", [128, 16384], BF16)
    wst = [sl("wst%d_2" % i, [128, 2048], F32) for i in range(2)]
    xst = [sl("xst0_2", [128, 4 * 515], F32)] * 2
    xb = [sl("xb0_2", [128, 8 * 515], BF16)] * 2
    rt = [sl("rt0_2", [128, 1024])] * 2
    kc = [sl("kc%d_2" % i, [128, 512], BF16) for i in range(2)]
    tmpf = [sl("tmpf%d_2" % i, [128, 2560 if i == 3 else 512]) for i in range(4)]
    kTall = [sl("qT_2", [128, 8 * 512], BF16), sl("gbT_2", [128, 8 * 512], BF16)]
    vall = [sl("vall%d_2" % i, [128, 4 * 1024], BF16) for i in range(2)]
    pT = [sl("pT%d" % i, [128, 1024], BF16) for i in range(8)]
    sAB = [sl("sAB%d" % i, [128, 1024], BF16) for i in range(3)]
    sC2 = [sl("sC2_%d" % i, [128, 1024], BF16) for i in range(2)]
    m01 = sl("m01", [128, 16 * 512], BF16)
    b01 = sl("b01", [8, 2048], BF16)
    osqb = sl("osqb", [128, 512], BF16)
    mskb = sl("mskb", [8, 2560], BF16)
    ostage = sl("ostage", [128, 4096], BF16)

    OFF_Q, OFF_GB = 0, 8192
    load_weights(w_q, 1024, OFF_Q, "wk")
    load_weights(w_gb, 1024, OFF_GB, "wvv")
    wq_v, wgb_v = wv(OFF_Q, 1024), wv(OFF_GB, 1024)
    mA = mskb[0:8, 0:512]
    ldc = [0]
    for s in range(4):
        i = 0
        xv = load_x(xT_own[:, :, s, :], 515, i)
        xk = "xb%d" % i
        rtab = rt[i]
        DMA(rtab[:].rearrange("p (a c) -> p a c", a=2), ropeO[:, :, s, :], (), ["rt0"], "rt0")
        DMA(tmpf[3][0:8, 0:512], msk_d[:, 0:512], (), ["tmpf3"], "mk0")
        DMA(tmpf[3][0:8, 512:2560], msk_d[:, 512 + s * 2048:512 + (s + 1) * 2048], (), ["tmpf3"], "mk1")
        CP("dve", mskb[:], tmpf[3][0:8, 0:2560], ["tmpf3"], ["mskb"])
        qT = kTall[0]
        gbT = kTall[1]
        for h in range(8):
            pk = h % 2
            for dc in range(8):
                MM(bank(pk), wq_v[:, dc, h * 128:(h + 1) * 128], xv[:, dc, 3:515], dc == 0, dc == 7, ["wk", xk], ["ps%d" % pk])
            kcb = kc[h % 2]
            kk = "kc%d" % (h % 2)
            P.add("act", lambda e, o=kcb[:], ii=bank(pk): e.copy(o, ii), r=["ps%d" % pk], w=[kk])
            MM(bank(2 + pk), bPERM, kcb[:], True, True, ["cbf", kk], ["ps%d" % (2 + pk)])
            t = tmpf[h % 2][:, 0:512]
            TT("dve", t, bank(2 + pk), rtab[:, 512:1024], ALU.mult, ["ps%d" % (2 + pk), "rt0"], ["tmpf%d" % (h % 2)])
            TT("dve", tmpf[2 + h % 2][:, 0:512], kcb[:], rtab[:, 0:512], ALU.mult, [kk, "rt0"], ["tmpf%d" % (2 + h % 2)])
            TT("dve", qT[:, h * 512:(h + 1) * 512], tmpf[2 + h % 2][:, 0:512], t, ALU.add,
               ["tmpf%d" % (h % 2), "tmpf%d" % (2 + h % 2)], ["kTall0"])
            pb = 4 + h % 2
            for dc in range(8):
                MM(bank(pb), wgb_v[:, dc, h * 128:(h + 1) * 128], xv[:, dc, 3:515], dc == 0, dc == 7, ["wvv", xk], ["ps%d" % pb])
            ACT(gbT[:, h * 512:(h + 1) * 512], bank(pb), AF.Silu, ["ps%d" % pb], ["kTall1"])
        TS("dve", b01[:], mskb[0:8, 512:2560], 0.0, None, ALU.is_equal, None, ["mskb"], ["b01"])
        for pair in range(4):
            for kb in range(4):
                idx = pair * 4 + kb
                pbm = idx % 4
                MM(bank(pbm), mA[:, kb * 128:(kb + 1) * 128], b01[0:8, pair * 512:(pair + 1) * 512], True, True, ["mskb", "b01"], ["ps%d" % pbm])
                evac(m01[:, idx * 512:(idx + 1) * 512], bank(pbm), ["ps%d" % pbm], ["m01"])
        L = KLEN[s]
        for h in range(8):
            units = []
            ng = L // 4
            jbase = ldc[0]
            ldc[0] += ng
            gbuf = {}
            for kg in range(ng):
                j = (jbase + kg) % 2
                kTg = vall[j][:, 0:2048]
                vg = vall[j][:, 2048:4096].rearrange("p (kb e) -> p kb e", kb=16)
                gbuf[kg] = (j, kTg, vg)
                for ktl in range(4):
                    for kb in range(4):
                        units.append((j, kTg, vg, ktl, kg * 4 + ktl, kb))

            def issue(kg):
                j, kTg, vg = gbuf[kg]
                DMA(kTg, ksc[h, :, kg * 2048:(kg + 1) * 2048], ["ksc%d" % t_ for t_ in range(kg * 4, kg * 4 + 4)], ["kTg%d" % j], "kld%d" % j)
                DMA(vg, vsc[kg * 2048:(kg + 1) * 2048, h * 128:(h + 1) * 128].rearrange("(kb p) e -> p kb e", p=128),
                    ["vsc%d" % t_ for t_ in range(kg * 4, kg * 4 + 4)], ["vg%d" % j], "vld%d" % j)

            issue(0)
            if ng > 1:
                issue(1)
            nu = len(units)

            def qk(u):
                j, kTg, vg, ktl, kt, kb = units[u]
                for m in range(2):
                    pb = (u % 2) * 2 + m
                    MM(bank(pb), kTg[m * 64:(m + 1) * 64, ktl * 512 + kb * 128:ktl * 512 + (kb + 1) * 128],
                       qT[m * 64:(m + 1) * 64, h * 512:(h + 1) * 512], True, True, ["kTg%d" % j, "kTall0"], ["ps%d" % pb])

            def ex(u):
                j, kTg, vg, ktl, kt, kb = units[u]
                pk_ = "pT%d" % (u % 8)
                ACT(pT[u % 8][:], PS[u % 2][:], AF.Exp, ["ps%d" % ((u % 2) * 2), "ps%d" % ((u % 2) * 2 + 1)], [pk_], scale=0.125)
                if kt >= L - 4:
                    idx = (kt - (L - 4)) * 4 + kb
                    pv3 = pT[u % 8][:].rearrange("p (m q) -> p m q", m=2)
                    TT("dve", pv3, pv3, m01[:, idx * 512:(idx + 1) * 512].unsqueeze(1).to_broadcast([128, 2, 512]), ALU.mult,
                       [pk_, "m01"], [pk_])

            def pv(u):
                j, kTg, vg, ktl, kt, kb = units[u]
                for m in range(2):
                    MM(bank(4 + m), vg[:, ktl * 4 + kb, :], pT[u % 8][:, m * 512:(m + 1) * 512], u == 0, u == nu - 1,
                       ["vg%d" % j, "pT%d" % (u % 8)], ["ps%d" % (4 + m)])

            def adds(g):
                u0 = 4 * g
                TT("dve", sAB[0][:], pT[u0 % 8][:], pT[(u0 + 1) % 8][:], ALU.add, ["pT%d" % (u0 % 8), "pT%d" % ((u0 + 1) % 8)], ["sAB0"])
                TT("pool", sAB[1][:], pT[(u0 + 2) % 8][:], pT[(u0 + 3) % 8][:], ALU.add, ["pT%d" % ((u0 + 2) % 8), "pT%d" % ((u0 + 3) % 8)], ["sAB1"])
                sc = sC2[g % 2]
                TT("dve", sc[:], sAB[0][:], sAB[1][:], ALU.add, ["sAB0", "sAB1"], ["sC%d" % (g % 2)])

            def sums(g):
                sc = sC2[g % 2]
                for m in range(2):
                    MM(bank(6 + m), bONE, sc[:, m * 512:(m + 1) * 512], g == 0, g == nu // 4 - 1, ["cbf", "sC%d" % (g % 2)], ["ps%d" % (6 + m)])

            qk(0)
            qk(1)
            for u in range(nu):
                if u % 16 == 1 and u // 16 >= 1 and u // 16 + 1 < ng:
                    issue(u // 16 + 1)
                ex(u)
                if u >= 1:
                    pv(u - 1)
                if u + 2 < nu:
                    qk(u + 2)
                if u % 4 == 3:
                    adds(u // 4)
                if u % 4 == 2 and u >= 6:
                    sums(u // 4 - 1)
            pv(nu - 1)
            sums(nu // 4 - 1)
            ob0, ob1, r0 = tmpf[1][:, 0:512], tmpf[2][:, 0:512], tmpf[0][:, 0:512]
            ACT(r0, bank(6), AF.Ln, ["ps6"], ["tmpf0"])
            ACT(r0, r0, AF.Exp, ["tmpf0"], ["tmpf0"], scale=-1.0)
            TT("dve", ob0, bank(4), r0, ALU.mult, ["ps4", "tmpf0"], ["tmpf1"])
            ACT(r0, bank(7), AF.Ln, ["ps7"], ["tmpf0"])
            ACT(r0, r0, AF.Exp, ["tmpf0"], ["tmpf0"], scale=-1.0)
            TT("dve", ob1, bank(5), r0, ALU.mult, ["ps5", "tmpf0"], ["tmpf2"])
            STT("dve", ob0, ob1, sm[:, SM_NLAM:SM_NLAM + 1], ob0, ALU.mult, ALU.add, ["tmpf1", "tmpf2", "nlam"], ["tmpf1"])
            TT("dve", osqb[:], ob0, ob0, ALU.mult, ["tmpf1"], ["osqb"])
            MM(bank(6), bONE, osqb[:], True, True, ["cbf", "osqb"], ["ps6"])
            ACT(r0, bank(6), AF.Ln, ["ps6"], ["tmpf0"], scale=1.0 / 128.0, bias=EPS)
            ACT(r0, r0, AF.Exp, ["tmpf0"], ["tmpf0"], scale=-0.5)
            TT("dve", ob0, ob0, r0, ALU.mult, ["tmpf1", "tmpf0"], ["tmpf1"])
            STT("dve", ostage[:, h * 512:(h + 1) * 512], ob0, sm[:, SM_SLN:SM_SLN + 1], gbT[:, h * 512:(h + 1) * 512],
                ALU.mult, ALU.mult, ["tmpf1", "sln", "kTall1"], ["ostage"])
        DMA(od_d[s], ostage[:], ["ostage"], ["od%d" % s], "ost", eng="sp")
        pending.append("od%d" % s)

    run_stage(P, st)
    if stop == "O2":
        es.close()
        return nc

    st = ExitStack()
    P = Prog("m")
    if stop == "O3":
        P.limit = limit
    WB = sl("WB3", [128, 28800], BF16)
    wst = [sl("wst%d_3" % i, [128, 2048], F32) for i in range(2)]
    xst = [sl("xst0_3", [128, 4 * 515], F32)] * 2
    xb = [sl("xb0_3", [128, 8 * 515], BF16)] * 2
    lnp = sl("lnpt", [128, 2048])
    yT = sl("yT", [128, 4096], BF16)
    oTs = sl("oTs", [128, 4096], BF16)
    merged = sl("merged", [128, 8 * 512], BF16)
    gt = [sl("gt%d" % i, [128, 512]) for i in range(2)]
    xtok = sl("xtok", [128, 4 * 1024])
    tmpf = [sl("tmpf%d_3" % i, [128, 1024]) for i in range(4)]
    DMA(lnp[:], lnp_d, (), ["lnp"], "c2")

    OFF_A, OFF_B, OFF_O, OFF_GM = 0, 8192, 16384, 24576
    load_weights(w_a, 1024, OFF_A, "wk")
    load_weights(w_b, 1024, OFF_B, "wvv")
    load_weights(w_o, 1024, OFF_O, "wx")
    wa_v, wb_v, wo_v = wv(OFF_A, 1024), wv(OFF_B, 1024), wv(OFF_O, 1024)
    gmi = [0]
    for s in range(4):
        i = 0
        xv = load_x(xT_own[:, :, s, :], 515, i)
        DMA(yT[:], ysd_d[s], (), ["yT"], "yld")
        DMA(oTs[:], od_d[s], (), ["oTs"], "old")
        xk = "xb%d" % i
        DMA(xtok[:].rearrange("p (k c) -> p k c", k=4), x_own[s].rearrange("(k p) c -> p k c", p=128), (), ["xtok"], "xtok")
        for db in range(8):
            j = gmi[0] % 2
            gmi[0] += 1
            goff = OFF_GM + j * 2048
            stv = wst[j][:, 0:2048].rearrange("p (a b c) -> p a b c", a=8, b=2)
            DMA(stv, w_gm[:, :, :].rearrange("p a (b c) -> p a b c", b=2)[:, :, :, db * 128:(db + 1) * 128], (), ["wst%d" % j], "wst%d" % j)
            CP("dve", WB[:, goff:goff + 2048].rearrange("p (a b c) -> p a b c", a=8, b=2), stv, ["wst%d" % j], ["wgm%d" % j])
            wg = WB[:, goff:goff + 2048].rearrange("p (a b c) -> p a b c", a=8, b=2)
            for br in range(2):
                for dc in range(8):
                    MM(bank(br), wg[:, dc, br, :], xv[:, dc, 3:515], dc == 0, dc == 7, ["wgm%d" % j, xk], ["ps%d" % br])
                ACT(gt[br][:], bank(br), AF.Sigmoid, ["ps%d" % br, "cst"], ["gt%d" % br], bias=cst[:, C_BG + br * 8 + db:C_BG + br * 8 + db + 1])
            for cc in range(8):
                MM(bank(2), wa_v[:, cc, db * 128:(db + 1) * 128], yT[:, cc * 512:(cc + 1) * 512], cc == 0, cc == 7,
                   ["wk", "yT"], ["ps2"])
            for cc in range(8):
                MM(bank(3), wb_v[:, cc, db * 128:(db + 1) * 128], oTs[:, cc * 512:(cc + 1) * 512], cc == 0, cc == 7,
                   ["wvv", "oTs"], ["ps3"])
            TT("dve", gt[0][:], gt[0][:], bank(2), ALU.mult, ["gt0", "ps2"], ["gt0"])
            TT("dve", gt[1][:], gt[1][:], bank(3), ALU.mult, ["gt1", "ps3"], ["gt1"])
            TT("dve", merged[:, db * 512:(db + 1) * 512], gt[0][:], gt[1][:], ALU.add, ["gt0", "gt1"], ["merged"])
        for k in range(4):
            vb = tmpf[k % 2]
            vk = "tmpf%d" % (k % 2)
            for half in range(2):
                pb = 4 + half
                for db in range(8):
                    MM(bank(pb), merged[:, db * 512 + k * 128:db * 512 + (k + 1) * 128], wo_v[:, db, half * 512:(half + 1) * 512], db == 0, db == 7,
                       ["merged", "wx"], ["ps%d" % pb])
            STT("dve", vb[:], xtok[:, k * 1024:(k + 1) * 1024], ALPHA, PS[2][:], ALU.mult, ALU.add, ["xtok", "ps4", "ps5"], [vk])
            c0 = SM_NM + (k % 2) * 4
            P.add("dve", lambda e, o=sm[:, c0:c0 + 1], ii=vb[:]: e.reduce_sum(o, ii, mybir.AxisListType.X), r=[vk], w=["ln_a%d" % (k % 2)])
            TS("dve", sm[:, c0:c0 + 1], sm[:, c0:c0 + 1], -1.0 / D, None, ALU.mult, None, ["ln_a%d" % (k % 2)], ["ln_b%d" % (k % 2)])
            sq = tmpf[2 + k % 2]
            MSET("dve", sm[:, c0 + 1:c0 + 2], 0.0, ["ln_c%d" % (k % 2)])
            ACT(sq[:], vb[:], AF.Square, [vk, "ln_b%d" % (k % 2), "ln_c%d" % (k % 2)], ["tmpf%d" % (2 + k % 2), "ln_c%d" % (k % 2)],
                bias=sm[:, c0:c0 + 1], accum=sm[:, c0 + 1:c0 + 2])
            TS("dve", sm[:, c0 + 2:c0 + 3], sm[:, c0 + 1:c0 + 2], 1.0 / D, EPS, ALU.mult, ALU.add, ["ln_c%d" % (k % 2)], ["ln_d%d" % (k % 2)])
            ACT(sm[:, c0 + 2:c0 + 3], sm[:, c0 + 2:c0 + 3], AF.Sqrt, ["ln_d%d" % (k % 2)], ["ln_d2%d" % (k % 2)])
            P.add("dve", lambda e, o=sm[:, c0 + 2:c0 + 3]: e.reciprocal(o, o), r=["ln_d2%d" % (k % 2)], w=["ln_e%d" % (k % 2)])
            TS("dve", vb[:], vb[:], sm[:, c0:c0 + 1], sm[:, c0 + 2:c0 + 3], ALU.add, ALU.mult, [vk, "ln_b%d" % (k % 2), "ln_e%d" % (k % 2)], [vk])
            TT("dve", vb[:], vb[:], lnp[:, 0:1024], ALU.mult, [vk, "lnp"], [vk])
            TT("pool", sq[:], vb[:], lnp[:, 1024:2048], ALU.add, [vk, "lnp"], ["tmpf%d" % (2 + k % 2)])
            DMA(y_out[s, k * 128:(k + 1) * 128, :], sq[:], ["tmpf%d" % (2 + k % 2)], ["yout%d_%d" % (s, k)], "yo%d" % (k % 2))
            pending.append("yout%d_%d" % (s, k))
    run_stage(P, st)
    es.close()
    return nc


def _prep_inputs(inputs):
    f = lambda a: np.ascontiguousarray(np.asarray(a, dtype=np.float32))
    x = f(inputs["x"])
    w_in = f(inputs["w_in"])[0]

    def wl(c0, c1):
        return np.ascontiguousarray(w_in[:, c0:c1].reshape(8, 128, c1 - c0).transpose(1, 0, 2))

    def wsq(w):
        return np.ascontiguousarray(f(w)[0].reshape(8, 128, 1024).transpose(1, 0, 2))

    common = {
        "w_z": wl(0, 1024), "w_x": wl(1024, 2560), "w_dt": wl(2560, 2576), "w_q": wl(2576, 3600),
        "w_k": wl(3600, 4624), "w_v": wl(4624, 5648), "w_gb": wl(5648, 6672), "w_gm": wl(6672, 8720),
        "w_a": wsq(inputs["w_a"]), "w_b": wsq(inputs["w_b"]), "w_o": wsq(inputs["w_o"]),
    }
    r = np.arange(128)
    ident = (r[:, None] == r[None, :]).astype(np.float32)
    LT = (r[:, None] <= r[None, :]).astype(np.float32)
    SU = (r[:, None] > r[None, :]).astype(np.float32)
    ones = np.ones((128, 128), np.float32)
    perm = np.zeros((128, 128), np.float32)
    for m in range(2):
        for d in range(8):
            perm[m * 64 + d + 8, m * 64 + d] = -1.0
            perm[m * 64 + d, m * 64 + d + 8] = 1.0
    bc = lambda v, n: np.broadcast_to(f(v).reshape(1, n), (128, n))
    conv_w = f(inputs["conv_w"])[0]
    cw = np.zeros((128, 48), np.float32)
    for blk in range(12):
        for w in range(4):
            cw[:, blk * 4 + w] = conv_w[w, blk * 128:(blk + 1) * 128]
    conv_b = f(inputs["conv_b"])[0]
    cb = conv_b.reshape(12, 128).T
    bg = f(inputs["b_gate"])[0].reshape(16, 128).T
    sln = f(inputs["subln_w"])[0].reshape(128, 1)
    nw = f(inputs["ssd_norm_w"])[0].reshape(8, 128).T
    lnp = np.concatenate([bc(inputs["ln_g"], 1024), bc(inputs["ln_b"], 1024)], axis=1)
    pos = np.arange(S, dtype=np.float32)
    inv_freq = (np.float32(500000.0) ** (-np.arange(0, 16, 2, dtype=np.float32) / np.float32(16))).astype(np.float32)
    ang = (pos[:, None] * inv_freq[None, :]).astype(np.float32)
    cos, sin = np.cos(ang).astype(np.float32), np.sin(ang).astype(np.float32)
    rope = np.zeros((128, 2, S), np.float32)
    rope[:, 0, :] = 1.0
    for m in range(2):
        for dd in range(16):
            rope[m * 64 + dd, 0, :] = cos[:, dd % 8]
            rope[m * 64 + dd, 1, :] = sin[:, dd % 8]
    maskA = np.zeros((8, 512), np.float32)
    for k in range(512):
        maskA[k // 64, k] = 1.0
    in_maps = []
    for c in range(8):
        b, j = c // 4, c % 4
        tiles = [j, 7 - j, 8 + j, 15 - j]
        xb = x[b]
        xT = np.ascontiguousarray(xb.T.reshape(8, 128, S).transpose(1, 0, 2))
        xo = np.zeros((128, 8, 4, 515), np.float32)
        xtok = np.zeros((4, 512, D), np.float32)
        ropeO = np.zeros((128, 2, 4, 512), np.float32)
        sel = np.zeros((128, 16), np.float32)
        maskB = np.zeros((8, 16, 512), np.float32)
        for s, t in enumerate(tiles):
            lo = t * 512
            if t > 0:
                xo[:, :, s, :] = xT[:, :, lo - 3:lo + 512]
            else:
                xo[:, :, s, 3:] = xT[:, :, 0:512]
            xtok[s] = xb[lo:lo + 512]
            ropeO[:, :, s, :] = rope[:, :, lo:lo + 512]
            sel[:, t] = 1.0
            L = KLEN[s]
            for pi in range(4):
                kt = L - 4 + pi
                if kt < t:
                    pass
                elif kt > t:
                    maskB[:, pi + 4 * s, :] = NEG
                else:
                    for rr in range(8):
                        q = np.arange(512)
                        maskB[rr, pi + 4 * s, :] = np.where(q // 64 >= rr, 0.0, NEG)
        cstv = np.concatenate([ident, LT, SU, ones, perm, bc(inputs["a_log"], 16), bc(inputs["dt_bias"], 16), bc(inputs["d_skip"], 16),
                               bc(inputs["lambda_q1"], 64), bc(inputs["lambda_k1"], 64), bc(inputs["lambda_q2"], 64), bc(inputs["lambda_k2"], 64),
                               cw, cb, bg, sln, sel, nw], axis=1).astype(np.float32)
        assert cstv.shape[1] == NCST
        m = dict(common)
        m.update({"xT_all": xT, "xT_own": xo, "x_own": xtok, "cst": np.ascontiguousarray(cstv),
                  "cbrow": conv_b.reshape(1, 1536).copy(), "lnp": np.ascontiguousarray(lnp),
                  "msk": np.ascontiguousarray(np.concatenate([maskA, maskB.reshape(8, 8192)], axis=1)),
                  "ropeA": rope, "ropeO": ropeO})
        in_maps.append(m)
    return in_maps


def kernel(**inputs):
    in_maps = _prep_inputs(inputs)
    nc = build_nc()
    res = run_bass_kernel_spmd(nc, in_maps, core_ids=list(range(8)))
    out = np.zeros((2, S, D), np.float32)
    for c in range(8):
        b, j = c // 4, c % 4
        tiles = [j, 7 - j, 8 + j, 15 - j]
        y = np.asarray(res.results[c]["y_out"], dtype=np.float32)
        for s, t in enumerate(tiles):
            out[b, t * 512:(t + 1) * 512] = y[s]
    return out
```

```python
import math
import sys
import os
from contextlib import ExitStack
import numpy as np
import concourse.bass as bass
import concourse.mybir as mybir
from concourse.bass_utils import run_bass_kernel_spmd

F32 = mybir.dt.float32
BF16 = mybir.dt.bfloat16
AF = mybir.ActivationFunctionType
ALU = mybir.AluOpType

D = 1024
S = 8192
NT = 16
KLEN = (4, 8, 12, 16)
EPS = 1e-5
ALPHA = 2.0 ** 0.25
LAMBDA_INIT = 0.2
NEG = -30000.0

C_ID, C_LT, C_SU, C_ONE, C_PERM = 0, 128, 256, 384, 512
C_ALOG, C_DTB, C_DSK = 640, 656, 672
C_LQ1, C_LK1, C_LQ2, C_LK2 = 688, 752, 816, 880
C_CW, C_CB, C_BG, C_SLN, C_SEL, C_NW = 944, 992, 1004, 1020, 1021, 1037
NCST = 1045


class _Op:
    __slots__ = ("eng", "fn", "deps", "dma", "idx", "need", "sem", "val", "src")


class Prog:
    def __init__(self, tag):
        self.tag = tag
        self.ops = []
        self.wr = {}
        self.rd = {}
        self.base = {}
        self.limit = None

    def add(self, eng, fn, r=(), w=(), dma=None):
        if self.limit is not None and len(self.ops) >= self.limit:
            return None
        o = _Op()
        o.eng, o.fn, o.dma, o.idx, o.need = eng, fn, dma, len(self.ops), False
        f = sys._getframe(1)
        o.src = (f.f_lineno, f.f_back.f_lineno if f.f_back else 0)
        deps = {}
        for k in r:
            for x in self.wr.get(k, ()):
                deps[x] = "raw"
        for k in w:
            if self.rd.get(k):
                self.base[k] = list(self.wr.get(k, ())) + list(self.rd[k])
                self.wr[k] = []
                self.rd[k] = []
            for x in self.base.get(k, ()):
                deps.setdefault(x, "war")
        for k in w:
            self.wr.setdefault(k, []).append(o.idx)
            self.rd.setdefault(k, [])
        for k in r:
            self.rd.setdefault(k, []).append(o.idx)
            self.wr.setdefault(k, [])
        deps.pop(o.idx, None)
        o.deps = deps
        self.ops.append(o)
        return o

    def finalize(self):
        cnt = {}
        for o in self.ops:
            if o.dma is not None:
                o.sem = self.tag + "d_" + o.dma
                cnt[o.sem] = cnt.get(o.sem, 0) + 16
                o.val = cnt[o.sem]
        for o in self.ops:
            best = {}
            for d, kind in o.deps.items():
                p = self.ops[d]
                if p.dma is None:
                    if p.eng == o.eng and o.eng == "pe" and o.dma is None:
                        continue
                    ch = "e_" + p.eng
                else:
                    ch = p.sem
                if ch not in best or best[ch] < d:
                    best[ch] = d
            o.deps = list(best.values())
            for d in o.deps:
                self.ops[d].need = True
        for o in self.ops:
            if o.dma is None and o.need:
                o.sem = self.tag + "e_" + o.eng
                cnt[o.sem] = cnt.get(o.sem, 0) + 1
                o.val = cnt[o.sem]
        return sorted(cnt.keys())

    def emit(self, eng, e, sems):
        waited = {}
        for o in self.ops:
            if o.eng != eng:
                continue
            for d in o.deps:
                p = self.ops[d]
                if waited.get(p.sem, 0) < p.val:
                    e.wait_ge(sems[p.sem], p.val)
                    waited[p.sem] = p.val
            ins = o.fn(e)
            if o.dma is not None:
                ins.then_inc(sems[o.sem], 16)
            elif o.need:
                ins.then_inc(sems[o.sem], 1)


def build_nc(stop=None, nta=NT, limit=None):
    nc = bass.Bass("TRN2", target_bir_lowering=False)
    dt_in = lambda n, shp, t=F32: nc.dram_tensor(n, shp, t, kind="ExternalInput").ap()
    xT_all = dt_in("xT_all", [128, 8, S])
    xT_own = dt_in("xT_own", [128, 8, 4, 515])
    x_own = dt_in("x_own", [4, 512, D])
    w_k = dt_in("w_k", [128, 8, 1024])
    w_v = dt_in("w_v", [128, 8, 1024])
    w_x = dt_in("w_x", [128, 8, 1536])
    w_dt = dt_in("w_dt", [128, 8, 16])
    w_q = dt_in("w_q", [128, 8, 1024])
    w_z = dt_in("w_z", [128, 8, 1024])
    w_gb = dt_in("w_gb", [128, 8, 1024])
    w_gm = dt_in("w_gm", [128, 8, 2048])
    w_a = dt_in("w_a", [128, 8, 1024])
    w_b = dt_in("w_b", [128, 8, 1024])
    w_o = dt_in("w_o", [128, 8, 1024])
    cst_d = dt_in("cst", [128, NCST])
    cbrow_d = dt_in("cbrow", [1, 1536])
    lnp_d = dt_in("lnp", [128, 2048])
    msk_d = dt_in("msk", [8, 512 + 16 * 512])
    ropeA = dt_in("ropeA", [128, 2, S])
    ropeO = dt_in("ropeO", [128, 2, 4, 512])
    y_out = nc.dram_tensor("y_out", [4, 512, D], F32, kind="ExternalOutput").ap()
    ksc = nc.dram_tensor("ksc", [8, 128, S], BF16).ap()
    vsc = nc.dram_tensor("vsc", [S, 1024], BF16).ap()

    hs_d = nc.dram_tensor("hs_d", [4, 128, 1024], F32).ap()
    ysd_d = nc.dram_tensor("ysd_d", [4, 128, 4096], BF16).ap()
    od_d = nc.dram_tensor("od_d", [4, 128, 4096], BF16).ap()

    es = ExitStack()
    st = ExitStack()
    P = Prog("s")
    pending = []
    sb = lambda n, shp, t=F32: es.enter_context(nc.sbuf_tensor(n, shp, t))
    sl = lambda n, shp, t=F32: st.enter_context(nc.sbuf_tensor(n, shp, t))
    cst = sb("cstt", [128, NCST])
    cbf = sb("cbf", [128, 640], BF16)
    diag = sb("diag", [128, 48 * 128], BF16)
    cbrow_b = sb("cbrowb", [1, 1536], BF16)
    onesrow_b = sb("onesrow", [1, 128], BF16)
    sm = sb("sm", [128, 64])
    PS = [es.enter_context(nc.psum_tensor("ps%d" % i, [128, 1024], F32)) for i in range(4)]

    def run_stage(P, st):
        if os.environ.get("KDBG"):
            print("STAGE", P.tag, "nops", len(P.ops), "last", P.ops[-1].eng, P.ops[-1].src)
        P.limit = None
        P.add("sp", lambda e: e.nop(), r=list(pending), w=["done"])
        del pending[:]
        names = P.finalize()
        sems = {n: st.enter_context(nc.semaphore(n)) for n in names}
        with nc.Block() as block:
            @block.sync
            def _(e):
                P.emit("sp", e, sems)

            @block.tensor
            def _(e):
                P.emit("pe", e, sems)

            @block.scalar
            def _(e):
                P.emit("act", e, sems)

            @block.vector
            def _(e):
                P.emit("dve", e, sems)

            @block.gpsimd
            def _(e):
                P.emit("pool", e, sems)
        st.close()
        nc.all_engine_barrier()

    WB = sl("WB", [128, 28800], BF16)
    wst = [sl("wst%d" % i, [128, 2048], F32) for i in range(2)]
    xst = [sl("xst0", [128, 4 * 515], F32)] * 2
    xb = [sl("xb%d" % i, [128, 8 * 515], BF16) for i in range(2)]
    uT = sl("uT", [128, 12 * 515], BF16)
    rt = [sl("rt0", [128, 1024])] * 2
    kc = [sl("kc%d" % i, [128, 512], BF16) for i in range(2)]
    tmpf = [sl("tmpf%d" % i, [128, 1536 if i == 3 else 512]) for i in range(4)]
    kTall = [sl("kTall0", [128, 8 * 512], BF16)] * 2
    vall = [sl("vall0", [128, 4 * 1024], BF16)] * 2
    xs_t = sl("xs_t", [128, 4 * 1024], BF16)
    B_t = sl("B_t", [128, 4 * 256], BF16)
    dtt = sl("dtt", [128, 64])
    adt = sl("adt", [128, 64])
    dk = sl("dk", [128, 64])
    wl = sl("wl", [128, 64])
    cdT = sl("cdT", [128, 16])
    xdtd = sl("xdtd", [128, 4 * 1024], BF16)
    hcur = sl("hcur", [128, 1024])
    hs = [sl("hs%d" % i, [128, 1024]) for i in range(4)]

    def bank(i):
        return PS[i // 2][:, (i % 2) * 512:(i % 2) * 512 + 512]

    cI = cst[:, C_ID:C_ID + 128]
    cLT = cst[:, C_LT:C_LT + 128]
    cSU = cst[:, C_SU:C_SU + 128]
    cONE = cst[:, C_ONE:C_ONE + 128]
    bI = cbf[:, 0:128]
    bONE = cbf[:, 384:512]
    bPERM = cbf[:, 512:640]
    SM_A, SM_NLAM, SM_T0, SM_T1, SM_SLN, SM_NM, SM_SS, SM_RS = 0, 16, 17, 18, 19, 20, 24, 28

    dma_rr = [0]

    def DMA(out, in_, r, w, key, eng="sp"):
        P.add(eng, lambda e, o=out, i=in_: e.dma_start(out=o, in_=i), r=r, w=w, dma=key)

    def MM(out, lhsT, rhs, start, stop, r, w):
        P.add("pe", lambda e, o=out, l=lhsT, rr=rhs, s=start, t=stop: e.matmul(o, lhsT=l, rhs=rr, start=s, stop=t),
              r=r, w=w)

    def ACT(out, in_, func, r, w, bias=None, scale=None, accum=None):
        def fn(e, o=out, i=in_, f=func, b=bias, sc=scale, a=accum):
            kw = {}
            if b is not None:
                kw["bias"] = b
            if sc is not None:
                kw["scale"] = sc
            if a is not None:
                kw["accum_out"] = a
            return e.activation(o, i, f, **kw)
        P.add("act", fn, r=r, w=w)

    def TT(eng, out, in0, in1, op, r, w):
        P.add(eng, lambda e, o=out, a=in0, b=in1, p=op: e.tensor_tensor(out=o, in0=a, in1=b, op=p), r=r, w=w)

    def TS(eng, out, in0, s1, s2, op0, op1, r, w):
        if op1 is None:
            P.add(eng, lambda e, o=out, a=in0, x=s1, p=op0: e.tensor_scalar(o, a, x, None, p), r=r, w=w)
        else:
            P.add(eng, lambda e, o=out, a=in0, x=s1, y=s2, p=op0, q=op1: e.tensor_scalar(o, a, x, y, p, q), r=r, w=w)

    def STT(eng, out, in0, sc, in1, op0, op1, r, w):
        P.add(eng, lambda e, o=out, a=in0, s=sc, b=in1, p=op0, q=op1: e.scalar_tensor_tensor(o, a, s, b, p, q), r=r, w=w)

    def CP(eng, out, in_, r, w):
        P.add(eng, lambda e, o=out, i=in_: e.tensor_copy(out=o, in_=i), r=r, w=w)

    def MSET(eng, ap, v, w):
        P.add(eng, lambda e, a=ap, x=v: e.memset(a, x), r=(), w=w)

    DMA(cst[:], cst_d, (), ["cst"], "c0")
    DMA(tmpf[3][0:1, 0:1536], cbrow_d, (), ["tmpf3"], "c1")
    CP("dve", cbf[:], cst[:, 0:640], ["cst"], ["cbf"])
    CP("dve", cbrow_b[:], tmpf[3][0:1, 0:1536], ["tmpf3"], ["cbrowb"])
    MSET("dve", onesrow_b[:], 1.0, ["onesrow"])
    MSET("dve", uT[:], 0.0, ["uT%d" % b for b in range(12)])
    MSET("dve", hcur[:], 0.0, ["hcur"])
    for i in range(4):
        MSET("dve", hs[i][:], 0.0, ["hs%d" % i])
    for blk in range(12):
        for w in range(4):
            TS("dve", diag[:, (blk * 4 + w) * 128:(blk * 4 + w + 1) * 128], cI,
               cst[:, C_CW + blk * 4 + w:C_CW + blk * 4 + w + 1], None, ALU.mult, None, ["cst"], ["diag"])
    ACT(sm[:, SM_A:SM_A + 16], cst[:, C_ALOG:C_ALOG + 16], AF.Exp, ["cst"], ["sm_a0"])
    TS("dve", sm[:, SM_A:SM_A + 16], sm[:, SM_A:SM_A + 16], -1.0, None, ALU.mult, None, ["sm_a0"], ["sm_a"])
    TT("dve", tmpf[0][:, 0:64], cst[:, C_LQ1:C_LQ1 + 64], cst[:, C_LK1:C_LK1 + 64], ALU.mult, ["cst"], ["lamt0"])
    TT("dve", tmpf[0][:, 64:128], cst[:, C_LQ2:C_LQ2 + 64], cst[:, C_LK2:C_LK2 + 64], ALU.mult, ["cst"], ["lamt1"])
    P.add("dve", lambda e: e.reduce_sum(sm[:, SM_T0:SM_T0 + 1], tmpf[0][:, 0:64], mybir.AxisListType.X), r=["lamt0"], w=["lam_s0"])
    P.add("dve", lambda e: e.reduce_sum(sm[:, SM_T1:SM_T1 + 1], tmpf[0][:, 64:128], mybir.AxisListType.X), r=["lamt1"], w=["lam_s1"])
    ACT(sm[:, SM_T0:SM_T0 + 2], sm[:, SM_T0:SM_T0 + 2], AF.Exp, ["lam_s0", "lam_s1"], ["lam_e"])
    TT("dve", sm[:, SM_NLAM:SM_NLAM + 1], sm[:, SM_T1:SM_T1 + 1], sm[:, SM_T0:SM_T0 + 1], ALU.subtract, ["lam_e"], ["nlam0"])
    TS("dve", sm[:, SM_NLAM:SM_NLAM + 1], sm[:, SM_NLAM:SM_NLAM + 1], -LAMBDA_INIT, None, ALU.add, None, ["nlam0"], ["nlam"])
    TS("dve", sm[:, SM_SLN:SM_SLN + 1], cst[:, C_SLN:C_SLN + 1], 1.0 - LAMBDA_INIT, None, ALU.mult, None, ["cst"], ["sln"])

    wstate = {"i": 0}

    def load_weights(dram, ncols, wb_off, key, scale_rows=None):
        c0 = 0
        while c0 < ncols:
            cw = min(256, ncols - c0)
            i = wstate["i"] % 2
            wstate["i"] += 1
            stv = wst[i][:, 0:8 * cw].rearrange("p (a c) -> p a c", a=8)
            DMA(stv, dram[:, :, c0:c0 + cw], (), ["wst%d" % i], "wst%d" % i)
            dst = WB[:, wb_off:wb_off + 8 * ncols].rearrange("p (a c) -> p a c", a=8)[:, :, c0:c0 + cw]
            CP("pool", dst, stv, ["wst%d" % i], [key])
            c0 += cw

    def wv(wb_off, ncols):
        return WB[:, wb_off:wb_off + 8 * ncols].rearrange("p (a c) -> p a c", a=8)

    def load_x(src, ncol, i, eng="pool"):
        xv = xb[i][:, 0:8 * ncol].rearrange("p (a c) -> p a c", a=8)
        for hf in range(2):
            stv = xst[0][:, 0:4 * ncol].rearrange("p (a c) -> p a c", a=4)
            DMA(stv, src[:, hf * 4:(hf + 1) * 4, :], (), ["xst0"], "xst0")
            CP(eng, xv[:, hf * 4:(hf + 1) * 4, :], stv, ["xst0"], ["xb%d" % i])
        return xv

    ev = {"i": 0}

    def evac(out, in_, r, w):
        ev["i"] += 1
        if ev["i"] % 2:
            P.add("act", lambda e, o=out, i=in_: e.copy(o, i), r=r, w=w)
        else:
            CP("dve", out, in_, r, w)

    def softplus_dt(psap, k, rk):
        d = dtt[:, k * 16:(k + 1) * 16]
        TT("dve", d, psap, cst[:, C_DTB:C_DTB + 16], ALU.add, [rk, "cst"], ["dtt%d" % k])
        ACT(d, d, AF.Exp, ["dtt%d" % k], ["dtt%d" % k])
        ACT(d, d, AF.Ln, ["dtt%d" % k], ["dtt%d" % k], bias=1.0)
        TT("dve", adt[:, k * 16:(k + 1) * 16], d, sm[:, SM_A:SM_A + 16], ALU.mult, ["dtt%d" % k, "sm_a"], ["adt%d" % k])

    def conv_tok(k, blks, pb, xoff):
        for gi in range(0, len(blks), 4):
            grp = blks[gi:gi + 4]
            pk = "ps%d" % pb
            for j, blk in enumerate(grp):
                o = bank(pb)[:, j * 128:(j + 1) * 128]
                for w in range(4):
                    MM(o, uT[:, blk * 515 + k * 128 + w: blk * 515 + k * 128 + w + 128],
                       diag[:, (blk * 4 + w) * 128:(blk * 4 + w + 1) * 128], w == 0, False,
                       ["uT%d" % blk, "diag"], [pk])
                MM(o, onesrow_b[0:1, :], cbrow_b[0:1, blk * 128:(blk + 1) * 128], False, True,
                   ["onesrow", "cbrowb"], [pk])
            n = len(grp) * 128
            b0 = grp[0]
            if b0 < 8:
                ACT(xs_t[:, k * 1024 + b0 * 128:k * 1024 + b0 * 128 + n], bank(pb)[:, 0:n], AF.Silu, [pk], ["xs_t%d" % k])
            else:
                ACT(B_t[:, k * 256:(k + 1) * 256], bank(pb)[:, 0:n], AF.Silu, [pk], ["B_t%d" % k])
            pb = pb + 1 if pb % 2 == 0 else pb - 1
        return pb

    def proj_u(xv, blks, wx_off, col0, pbs, halo):
        wx = wv(wx_off, 1536)
        for bi, blk in enumerate(blks):
            pb = pbs[bi % len(pbs)]
            pk = "ps%d" % pb
            uk = "uT%d" % blk
            if halo == "carry":
                CP("dve", uT[:, blk * 515:blk * 515 + 3], uT[:, blk * 515 + 512:blk * 515 + 515], [uk], [uk])
            else:
                for dc in range(8):
                    MM(bank(pb)[:, 0:3], wx[:, dc, blk * 128:(blk + 1) * 128], xv[:, dc, 0:3], dc == 0, dc == 7,
                       ["wx", halo], [pk])
                evac(uT[:, blk * 515:blk * 515 + 3], bank(pb)[:, 0:3], [pk], [uk])
            for dc in range(8):
                MM(bank(pb), wx[:, dc, blk * 128:(blk + 1) * 128], xv[:, dc, col0:col0 + 512], dc == 0, dc == 7,
                   ["wx", halo if halo != "carry" else "xbcur"], [pk])
            evac(uT[:, blk * 515 + 3:blk * 515 + 515], bank(pb), [pk], [uk])

    OFF_K, OFF_V, OFF_X, OFF_DT = 0, 8192, 16384, 16384 + 12288
    load_weights(w_k, 1024, OFF_K, "wk")
    load_weights(w_v, 1024, OFF_V, "wvv")
    load_weights(w_x, 1536, OFF_X, "wx")
    load_weights(w_dt, 16, OFF_DT, "wdt")
    wk_v, wv_v, wdt_v = wv(OFF_K, 1024), wv(OFF_V, 1024), wv(OFF_DT, 16)

    def rope_block(h, pb, srcw, srck, xv, col0, rtab, dst, dstk, xk):
        pk = "ps%d" % pb
        for dc in range(8):
            MM(bank(pb), srcw[:, dc, h * 128:(h + 1) * 128], xv[:, dc, col0:col0 + 512], dc == 0, dc == 7, [srck, xk], [pk])
        kcb = kc[h % 2]
        kk = "kc%d" % (h % 2)
        P.add("act", lambda e, o=kcb[:], i=bank(pb): e.copy(o, i), r=[pk], w=[kk])
        pb2 = pb + 2
        pk2 = "ps%d" % pb2
        MM(bank(pb2), bPERM, kcb[:], True, True, ["cbf", kk], [pk2])
        t = tmpf[h % 2][:, 0:512]
        tk = "tmpf%d" % (h % 2)
        TT("dve", t, bank(pb2), rtab[:, 512:1024], ALU.mult, [pk2, "rt"], [tk])
        TT("dve", tmpf[2 + h % 2][:, 0:512], kcb[:], rtab[:, 0:512], ALU.mult, [kk, "rt"], ["tmpf%d" % (2 + h % 2)])
        TT("dve", dst, tmpf[2 + h % 2][:, 0:512], t, ALU.add, [tk, "tmpf%d" % (2 + h % 2)], [dstk])

    xv_next = load_x(xT_all[:, :, 0:512], 512, 0)
    for T in range(nta):
        i = T % 2
        xv = xv_next
        xk = "xb%d" % i
        if T + 1 < nta:
            xv_next = load_x(xT_all[:, :, (T + 1) * 512:(T + 2) * 512], 512, 1 - i)
        rtab = rt[i]
        DMA(rtab[:].rearrange("p (a c) -> p a c", a=2), ropeA[:, :, T * 512:(T + 1) * 512], (), ["rt0"], "rt0")
        def kproj(h):
            pk = h % 2
            for dc in range(8):
                MM(bank(pk), wk_v[:, dc, h * 128:(h + 1) * 128], xv[:, dc, :], dc == 0, dc == 7, ["wk", xk], ["ps%d" % pk])
        kproj(0)
        for h in range(8):
            pk = h % 2
            kcb = kc[h % 2]
            kk = "kc%d" % (h % 2)
            P.add("act", lambda e, o=kcb[:], ii=bank(pk): e.copy(o, ii), r=["ps%d" % pk], w=[kk])
            if h + 1 < 8:
                kproj(h + 1)
            MM(bank(2 + pk), bPERM, kcb[:], True, True, ["cbf", kk], ["ps%d" % (2 + pk)])
            t = tmpf[h % 2][:, 0:512]
            TT("dve", t, bank(2 + pk), rtab[:, 512:1024], ALU.mult, ["ps%d" % (2 + pk), "rt0"], ["tmpf%d" % (h % 2)])
            TT("dve", tmpf[2 + h % 2][:, 0:512], kcb[:], rtab[:, 0:512], ALU.mult, [kk, "rt0"], ["tmpf%d" % (2 + h % 2)])
            TT("dve", kTall[i][:, h * 512:(h + 1) * 512], tmpf[2 + h % 2][:, 0:512], t, ALU.add,
               ["tmpf%d" % (h % 2), "tmpf%d" % (2 + h % 2)], ["kTall0"])
        DMA(ksc[:, :, T * 512:(T + 1) * 512].rearrange("h r t -> r h t"),
            kTall[i][:].rearrange("p (h t) -> p h t", h=8), ["kTall0"], ["ksc%d" % T], "kst0", eng="pool")
        pending.append("ksc%d" % T)
        for k in range(4):
            for half in range(2):
                pb = 4 + (k * 2 + half) % 2
                for dc in range(8):
                    MM(bank(pb), xv[:, dc, k * 128:(k + 1) * 128], wv_v[:, dc, half * 512:(half + 1) * 512], dc == 0, dc == 7,
                       ["wvv", xk], ["ps%d" % pb])
                evac(vall[i][:, k * 1024 + half * 512:k * 1024 + half * 512 + 512], bank(pb), ["ps%d" % pb], ["vall0"])
        DMA(vsc[T * 512:(T + 1) * 512, :].rearrange("(k p) c -> p k c", p=128),
            vall[i][:].rearrange("p (k c) -> p k c", k=4), ["vall0"], ["vsc%d" % T], "vst0", eng="pool")
        pending.append("vsc%d" % T)
        wx = wv(OFF_X, 1536)
        for bi, blk in enumerate(range(10)):
            pb = 6 + bi % 2
            uk = "uT%d" % blk
            CP("dve", uT[:, blk * 515:blk * 515 + 3], uT[:, blk * 515 + 512:blk * 515 + 515], [uk], [uk])
            for dc in range(8):
                MM(bank(pb), wx[:, dc, blk * 128:(blk + 1) * 128], xv[:, dc, :], dc == 0, dc == 7, ["wx", xk], ["ps%d" % pb])
            evac(uT[:, blk * 515 + 3:blk * 515 + 515], bank(pb), ["ps%d" % pb], [uk])
        for k in range(4):
            o = bank(4)[:, k * 16:(k + 1) * 16]
            for dc in range(8):
                MM(o, xv[:, dc, k * 128:(k + 1) * 128], wdt_v[:, dc, :], dc == 0, dc == 7, ["wdt", xk], ["ps4"])
        for k in range(4):
            softplus_dt(bank(4)[:, k * 16:(k + 1) * 16], k, "ps4")
        pb = 0
        for k in range(4):
            pb = conv_tok(k, list(range(8)), pb, 0)
            pb = conv_tok(k, [8, 9], pb, 0)
        for k in range(4):
            o = bank(5)[:, k * 16:(k + 1) * 16]
            MM(o, cSU, adt[:, k * 16:(k + 1) * 16], True, k == 3, ["cst", "adt%d" % k], ["ps5"])
            for k2 in range(k + 1, 4):
                MM(o, cONE, adt[:, k2 * 16:(k2 + 1) * 16], False, k2 == 3, ["cst", "adt%d" % k2], ["ps5"])
        o = bank(5)[:, 64:80]
        for k in range(4):
            MM(o, cONE, adt[:, k * 16:(k + 1) * 16], k == 0, k == 3, ["cst", "adt%d" % k], ["ps5"])
        ACT(dk[:, 0:64], bank(5)[:, 0:64], AF.Exp, ["ps5"], ["dk"])
        ACT(cdT[:], bank(5)[:, 64:80], AF.Exp, ["ps5"], ["cdT"])
        TT("dve", wl[:], dtt[:], dk[:], ALU.mult, ["dk"] + ["dtt%d" % k for k in range(4)], ["wl"])
        for k in range(4):
            TT("dve", xdtd[:, k * 1024:(k + 1) * 1024].rearrange("p (h q) -> p h q", h=16),
               xs_t[:, k * 1024:(k + 1) * 1024].rearrange("p (h q) -> p h q", h=16),
               wl[:, k * 16:(k + 1) * 16].unsqueeze(2).to_broadcast([128, 16, 64]), ALU.mult,
               ["xs_t%d" % k, "wl"], ["xdtd%d" % k])
        for g in range(2):
            for k in range(4):
                MM(bank(6 + g), B_t[:, k * 256 + g * 128:k * 256 + (g + 1) * 128], xdtd[:, k * 1024 + g * 512:k * 1024 + (g + 1) * 512],
                   k == 0, k == 3, ["B_t%d" % k, "xdtd%d" % k], ["ps%d" % (6 + g)])
        STT("dve", hs[T // 4][:], hcur[:], cst[:, C_SEL + T:C_SEL + T + 1], hs[T // 4][:], ALU.mult, ALU.add,
            ["hcur", "cst", "hs%d" % (T // 4)], ["hs%d" % (T // 4)])
        TT("dve", hcur[:].rearrange("p (h q) -> p h q", h=16), hcur[:].rearrange("p (h q) -> p h q", h=16),
           cdT[:].unsqueeze(2).to_broadcast([128, 16, 64]), ALU.mult, ["hcur", "cdT"], ["hcur"])
        TT("dve", hcur[:], hcur[:], PS[3][:], ALU.add, ["hcur", "ps6", "ps7"], ["hcur"])

    for i in range(4):
        DMA(hs_d[i], hs[i][:], ["hs%d" % i], ["hsd%d" % i], "hst", eng="sp")
        pending.append("hsd%d" % i)
    run_stage(P, st)
    if stop == "A":
        es.close()
        return nc

    st = ExitStack()
    P = Prog("o")
    if stop == "O1":
        P.limit = limit
    WB = sl("WB1", [128, 28800], BF16)
    wst = [sl("wst%d_1" % i, [128, 2048], F32) for i in range(2)]
    xst = [sl("xst0_1", [128, 4 * 515], F32)] * 2
    xb = [sl("xb0_1", [128, 8 * 515], BF16)] * 2
    uT = sl("uT_1", [128, 12 * 515], BF16)
    tmpf = [sl("tmpf%d_1" % i, [128, 1024]) for i in range(3)]
    xs_t = sl("xs_t_1", [128, 4 * 1024], BF16)
    B_t = sl("B_t_1", [128, 4 * 256], BF16)
    zs = sl("zs", [128, 4 * 1024], BF16)
    BCT = sl("BCT", [128, 4 * 512], BF16)
    dtt = sl("dtt_1", [128, 64])
    adt = sl("adt_1", [128, 64])
    ex3 = sl("ex3", [128, 48])
    xdtd = sl("xdtd_1", [128, 1024], BF16)
    xdt = sl("xdt", [128, 1024], BF16)
    hcur = sl("hcur_1", [128, 1024])
    prevb = sl("prevb", [128, 1024], BF16)
    Rm = sl("Rm", [128, 2048])
    Lex = sl("Lex", [128, 2048], BF16)
    cbm = sl("cbm", [128, 256])
    MT = sl("MT", [128, 2048], BF16)
    ystage = sl("ystage", [128, 4096], BF16)

    OFF_Z = 0
    load_weights(w_z, 1024, OFF_Z, "wk")
    load_weights(w_x, 1536, OFF_X, "wx")
    load_weights(w_dt, 16, OFF_DT, "wdt")
    wdt_v = wv(OFF_DT, 16)
    wz_v = wv(OFF_Z, 1024)
    wx = wv(OFF_X, 1536)
    normw = cst[:, C_NW:C_NW + 8]
    for s in range(4):
        i = 0
        xv = load_x(xT_own[:, :, s, :], 515, i)
        xk = "xb%d" % i
        for bi, blk in enumerate(range(12)):
            pb = bi % 2
            uk = "uT%d" % blk
            for dc in range(8):
                MM(bank(pb)[:, 0:3], wx[:, dc, blk * 128:(blk + 1) * 128], xv[:, dc, 0:3], dc == 0, dc == 7, ["wx", xk], ["ps%d" % pb])
            evac(uT[:, blk * 515:blk * 515 + 3], bank(pb)[:, 0:3], ["ps%d" % pb], [uk])
            for dc in range(8):
                MM(bank(pb), wx[:, dc, blk * 128:(blk + 1) * 128], xv[:, dc, 3:515], dc == 0, dc == 7, ["wx", xk], ["ps%d" % pb])
            evac(uT[:, blk * 515 + 3:blk * 515 + 515], bank(pb), ["ps%d" % pb], [uk])
        for bi, blk in enumerate((8, 9, 10, 11)):
            pb = 2 + bi % 2
            for w in range(4):
                MM(bank(pb), diag[:, (blk * 4 + w) * 128:(blk * 4 + w + 1) * 128], uT[:, blk * 515 + w:blk * 515 + w + 512],
                   w == 0, w == 3, ["uT%d" % blk, "diag"], ["ps%d" % pb])
            ACT(BCT[:, bi * 512:(bi + 1) * 512], bank(pb), AF.Silu, ["ps%d" % pb, "cst"], ["BCT%d" % bi],
                bias=cst[:, C_CB + blk:C_CB + blk + 1])
        for k in range(4):
            o = bank(4)[:, k * 16:(k + 1) * 16]
            for dc in range(8):
                MM(o, xv[:, dc, 3 + k * 128:3 + (k + 1) * 128], wdt_v[:, dc, :], dc == 0, dc == 7, ["wdt", xk], ["ps4"])
        for k in range(4):
            softplus_dt(bank(4)[:, k * 16:(k + 1) * 16], k, "ps4")
        pb = 6
        for k in range(4):
            pb = conv_tok(k, list(range(8)), pb, 0)
            pb = conv_tok(k, [8, 9], pb, 0)
        for k in range(4):
            for half in range(2):
                pb = (k * 2 + half) % 2
                for dc in range(8):
                    MM(bank(pb), xv[:, dc, 3 + k * 128:3 + (k + 1) * 128], wz_v[:, dc, half * 512:(half + 1) * 512], dc == 0, dc == 7,
                       ["wk", xk], ["ps%d" % pb])
                ACT(zs[:, k * 1024 + half * 512:k * 1024 + half * 512 + 512], bank(pb), AF.Silu, ["ps%d" % pb], ["zs%d" % k])
        DMA(hcur[:], hs_d[s], (), ["hcur"], "hld")
        for k in range(4):
            CP("pool", prevb[:], hcur[:], ["hcur"], ["prevb"])
            a_k = adt[:, k * 16:(k + 1) * 16]
            ak = "adt%d" % k
            TT("dve", Rm[:].rearrange("p (h l) -> p h l", h=16), a_k.unsqueeze(2).to_broadcast([128, 16, 128]),
               cLT.unsqueeze(1).to_broadcast([128, 16, 128]), ALU.mult, [ak, "cst"], ["Rm"])
            for q in range(4):
                MM(bank(q), cSU, Rm[:, q * 512:(q + 1) * 512], True, True, ["cst", "Rm"], ["ps%d" % q])
            ACT(Lex[:, 0:1024], PS[0][:], AF.Exp, ["ps0", "ps1"], ["Lex0"])
            ACT(Lex[:, 1024:2048], PS[1][:], AF.Exp, ["ps2", "ps3"], ["Lex1"])
            for g in range(2):
                MM(bank(4)[:, g * 128:(g + 1) * 128], BCT[:, g * 512 + k * 128:g * 512 + (k + 1) * 128],
                   BCT[:, (2 + g) * 512 + k * 128:(2 + g) * 512 + (k + 1) * 128], True, True, ["BCT%d" % g, "BCT%d" % (2 + g)], ["ps4"])
            TT("dve", cbm[:].rearrange("p (g l) -> p g l", g=2), bank(4)[:, 0:256].rearrange("p (g l) -> p g l", g=2),
               cLT.unsqueeze(1).to_broadcast([128, 2, 128]), ALU.mult, ["ps4", "cst"], ["cbm"])
            TT("dve", MT[:].rearrange("p (g h l) -> p g h l", g=2, h=8), Lex[:].rearrange("p (g h l) -> p g h l", g=2, h=8),
               cbm[:].rearrange("p (g l) -> p g l", g=2).unsqueeze(2).to_broadcast([128, 2, 8, 128]), ALU.mult,
               ["Lex0", "Lex1", "cbm"], ["MT"])
            MM(bank(5)[:, 0:16], cLT, a_k, True, True, ["cst", ak], ["ps5"])
            MM(bank(5)[:, 16:32], cSU, a_k, True, True, ["cst", ak], ["ps5"])
            MM(bank(5)[:, 32:48], cONE, a_k, True, True, ["cst", ak], ["ps5"])
            ACT(ex3[:], bank(5)[:, 0:48], AF.Exp, ["ps5"], ["ex3"])
            xsk = xs_t[:, k * 1024:(k + 1) * 1024].rearrange("p (h q) -> p h q", h=16)
            TT("dve", xdt[:].rearrange("p (h q) -> p h q", h=16), xsk,
               dtt[:, k * 16:(k + 1) * 16].unsqueeze(2).to_broadcast([128, 16, 64]), ALU.mult, ["xs_t%d" % k, "dtt%d" % k], ["xdt"])
            TT("dve", xdtd[:, 0:1024].rearrange("p (h q) -> p h q", h=16), xdt[:].rearrange("p (h q) -> p h q", h=16),
               ex3[:, 16:32].unsqueeze(2).to_broadcast([128, 16, 64]), ALU.mult, ["xdt", "ex3"], ["xdtd0"])
            for h in range(16):
                MM(PS[0][:, h * 64:(h + 1) * 64], MT[:, h * 128:(h + 1) * 128], xdt[:, h * 64:(h + 1) * 64], True, True,
                   ["MT", "xdt"], ["ps%d" % (h // 8)])
            for g in range(2):
                MM(bank(2 + g), BCT[:, (2 + g) * 512 + k * 128:(2 + g) * 512 + (k + 1) * 128], prevb[:, g * 512:(g + 1) * 512], True, True,
                   ["BCT%d" % (2 + g), "prevb"], ["ps%d" % (2 + g)])
            t1, t2 = tmpf[0], tmpf[1]
            TT("dve", t1[:].rearrange("p (h q) -> p h q", h=16), PS[1][:].rearrange("p (h q) -> p h q", h=16),
               ex3[:, 0:16].unsqueeze(2).to_broadcast([128, 16, 64]), ALU.mult, ["ps2", "ps3", "ex3"], ["tmpf0"])
            TT("dve", t1[:], t1[:], PS[0][:], ALU.add, ["tmpf0", "ps0", "ps1"], ["tmpf0"])
            TT("pool", t2[:].rearrange("p (h q) -> p h q", h=16), xsk,
               cst[:, C_DSK:C_DSK + 16].unsqueeze(2).to_broadcast([128, 16, 64]), ALU.mult, ["xs_t%d" % k, "cst"], ["tmpf1"])
            TT("dve", t1[:], t1[:], t2[:], ALU.add, ["tmpf0", "tmpf1"], ["tmpf0"])
            TT("dve", t1[:], t1[:], zs[:, k * 1024:(k + 1) * 1024], ALU.mult, ["tmpf0", "zs%d" % k], ["tmpf0"])
            MSET("dve", sm[:, SM_SS:SM_SS + 2], 0.0, ["ssq0", "ssq1"])
            for g in range(2):
                ACT(t2[:, g * 512:(g + 1) * 512], t1[:, g * 512:(g + 1) * 512], AF.Square, ["tmpf0", "ssq%d" % g], ["tmpf1", "ssq%d" % g],
                    accum=sm[:, SM_SS + g:SM_SS + g + 1])
            TS("dve", sm[:, SM_RS:SM_RS + 2], sm[:, SM_SS:SM_SS + 2], 1.0 / 512.0, EPS, ALU.mult, ALU.add, ["ssq0", "ssq1"], ["rs0"])
            ACT(sm[:, SM_RS:SM_RS + 2], sm[:, SM_RS:SM_RS + 2], AF.Sqrt, ["rs0"], ["rs1"])
            P.add("dve", lambda e: e.reciprocal(sm[:, SM_RS:SM_RS + 2], sm[:, SM_RS:SM_RS + 2]), r=["rs1"], w=["rs"])
            t3 = tmpf[2]
            for g in range(2):
                TS("dve", t3[:, g * 512:(g + 1) * 512], t1[:, g * 512:(g + 1) * 512], sm[:, SM_RS + g:SM_RS + g + 1], None,
                   ALU.mult, None, ["tmpf0", "rs"], ["tmpf2"])
            for cc in range(8):
                P.add("pe", lambda e, o=PS[3][:, cc * 128:(cc + 1) * 128], ii=t3[:, cc * 128:(cc + 1) * 128]: e.transpose(o, ii, cI),
                      r=["tmpf2", "cst"], w=["ps%d" % (6 + cc // 4)])
            for cc in range(8):
                ACT(ystage[:, cc * 512 + k * 128:cc * 512 + (k + 1) * 128],
                    PS[3][:, cc * 128:(cc + 1) * 128], AF.Copy, ["ps%d" % (6 + cc // 4), "cst"], ["ystage"], scale=normw[:, cc:cc + 1])
            if k < 3:
                for g in range(2):
                    MM(bank(6 + g), B_t[:, k * 256 + g * 128:k * 256 + (g + 1) * 128], xdtd[:, g * 512:(g + 1) * 512], True, True,
                       ["B_t%d" % k, "xdtd0"], ["ps%d" % (6 + g)])
                TT("dve", hcur[:].rearrange("p (h q) -> p h q", h=16), hcur[:].rearrange("p (h q) -> p h q", h=16),
                   ex3[:, 32:48].unsqueeze(2).to_broadcast([128, 16, 64]), ALU.mult, ["hcur", "ex3"], ["hcur"])
                TT("dve", hcur[:], hcur[:], PS[3][:], ALU.add, ["hcur", "ps6", "ps7"], ["hcur"])
        DMA(ysd_d[s], ystage[:], ["ystage"], ["ysd%d" % s], "yst", eng="sp")
        pending.append("ysd%d" % s)

    run_stage(P, st)
    if stop == "O1":
        es.close()
        return nc

    st = ExitStack()
    P = Prog("a")
    if stop == "O2":
        P.limit = limit
    WB = sl("WB2", [128, 16384], BF16)
    wst = [sl("wst%d_2" % i, [128, 2048], F32) for i in range(2)]
    xst = [sl("xst0_2", [128, 4 * 515], F32)] * 2
    xb = [sl("xb0_2", [128, 8 * 515], BF16)] * 2
    rt = [sl("rt0_2", [128, 1024])] * 2
    kc = [sl("kc%d_2" % i, [128, 512], BF16) for i in range(2)]
    tmpf = [sl("tmpf%d_2" % i, [128, 2560 if i == 3 else 512]) for i in range(4)]
    kTall = [sl("qT_2", [128, 8 * 512], BF16), sl("gbT_2", [128, 8 * 512], BF16)]
    vall = [sl("vall%d_2" % i, [128, 4 * 1024], BF16) for i in range(2)]
    pT = [sl("pT%d" % i, [128, 1024], BF16) for i in range(8)]
    sAB = [sl("sAB%d" % i, [128, 1024], BF16) for i in range(3)]
    sC2 = [sl("sC2_%d" % i, [128, 1024], BF16) for i in range(2)]
    m01 = sl("m01", [128, 16 * 512], BF16)
    b01 = sl("b01", [8, 2048], BF16)
    osqb = sl("osqb", [128, 512], BF16)
    mskb = sl("mskb", [8, 2560], BF16)
    ostage = sl("ostage", [128, 4096], BF16)

    OFF_Q, OFF_GB = 0, 8192
    load_weights(w_q, 1024, OFF_Q, "wk")
    load_weights(w_gb, 1024, OFF_GB, "wvv")
    wq_v, wgb_v = wv(OFF_Q, 1024), wv(OFF_GB, 1024)
    mA = mskb[0:8, 0:512]
    ldc = [0]
    for s in range(4):
        i = 0
        xv = load_x(xT_own[:, :, s, :], 515, i)
        xk = "xb%d" % i
        rtab = rt[i]
        DMA(rtab[:].rearrange("p (a c) -> p a c", a=2), ropeO[:, :, s, :], (), ["rt0"], "rt0")
        DMA(tmpf[3][0:8, 0:512], msk_d[:, 0:512], (), ["tmpf3"], "mk0")
        DMA(tmpf[3][0:8, 512:2560], msk_d[:, 512 + s * 2048:512 + (s + 1) * 2048], (), ["tmpf3"], "mk1")
        CP("dve", mskb[:], tmpf[3][0:8, 0:2560], ["tmpf3"], ["mskb"])
        qT = kTall[0]
        gbT = kTall[1]
        for h in range(8):
            pk = h % 2
            for dc in range(8):
                MM(bank(pk), wq_v[:, dc, h * 128:(h + 1) * 128], xv[:, dc, 3:515], dc == 0, dc == 7, ["wk", xk], ["ps%d" % pk])
            kcb = kc[h % 2]
            kk = "kc%d" % (h % 2)
            P.add("act", lambda e, o=kcb[:], ii=bank(pk): e.copy(o, ii), r=["ps%d" % pk], w=[kk])
            MM(bank(2 + pk), bPERM, kcb[:], True, True, ["cbf", kk], ["ps%d" % (2 + pk)])
            t = tmpf[h % 2][:, 0:512]
            TT("dve", t, bank(2 + pk), rtab[:, 512:1024], ALU.mult, ["ps%d" % (2 + pk), "rt0"], ["tmpf%d" % (h % 2)])
            TT("dve", tmpf[2 + h % 2][:, 0:512], kcb[:], rtab[:, 0:512], ALU.mult, [kk, "rt0"], ["tmpf%d" % (2 + h % 2)])
            TT("dve", qT[:, h * 512:(h + 1) * 512], tmpf[2 + h % 2][:, 0:512], t, ALU.add,
               ["tmpf%d" % (h % 2), "tmpf%d" % (2 + h % 2)], ["kTall0"])
            pb = 4 + h % 2
            for dc in range(8):
                MM(bank(pb), wgb_v[:, dc, h * 128:(h + 1) * 128], xv[:, dc, 3:515], dc == 0, dc == 7, ["wvv", xk], ["ps%d" % pb])
            ACT(gbT[:, h * 512:(h + 1) * 512], bank(pb), AF.Silu, ["ps%d" % pb], ["kTall1"])
        TS("dve", b01[:], mskb[0:8, 512:2560], 0.0, None, ALU.is_equal, None, ["mskb"], ["b01"])
        for pair in range(4):
            for kb in range(4):
                idx = pair * 4 + kb
                pbm = idx % 4
                MM(bank(pbm), mA[:, kb * 128:(kb + 1) * 128], b01[0:8, pair * 512:(pair + 1) * 512], True, True, ["mskb", "b01"], ["ps%d" % pbm])
                evac(m01[:, idx * 512:(idx + 1) * 512], bank(pbm), ["ps%d" % pbm], ["m01"])
        L = KLEN[s]
        for h in range(8):
            units = []
            ng = L // 4
            jbase = ldc[0]
            ldc[0] += ng
            gbuf = {}
            for kg in range(ng):
                j = (jbase + kg) % 2
                kTg = vall[j][:, 0:2048]
                vg = vall[j][:, 2048:4096].rearrange("p (kb e) -> p kb e", kb=16)
                gbuf[kg] = (j, kTg, vg)
                for ktl in range(4):
                    for kb in range(4):
                        units.append((j, kTg, vg, ktl, kg * 4 + ktl, kb))

            def issue(kg):
                j, kTg, vg = gbuf[kg]
                DMA(kTg, ksc[h, :, kg * 2048:(kg + 1) * 2048], ["ksc%d" % t_ for t_ in range(kg * 4, kg * 4 + 4)], ["kTg%d" % j], "kld%d" % j)
                DMA(vg, vsc[kg * 2048:(kg + 1) * 2048, h * 128:(h + 1) * 128].rearrange("(kb p) e -> p kb e", p=128),
                    ["vsc%d" % t_ for t_ in range(kg * 4, kg * 4 + 4)], ["vg%d" % j], "vld%d" % j)

            issue(0)
            if ng > 1:
                issue(1)
            nu = len(units)

            def qk(u):
                j, kTg, vg, ktl, kt, kb = units[u]
                for m in range(2):
                    pb = (u % 2) * 2 + m
                    MM(bank(pb), kTg[m * 64:(m + 1) * 64, ktl * 512 + kb * 128:ktl * 512 + (kb + 1) * 128],
                       qT[m * 64:(m + 1) * 64, h * 512:(h + 1) * 512], True, True, ["kTg%d" % j, "kTall0"], ["ps%d" % pb])

            def ex(u):
                j, kTg, vg, ktl, kt, kb = units[u]
                pk_ = "pT%d" % (u % 8)
                ACT(pT[u % 8][:], PS[u % 2][:], AF.Exp, ["ps%d" % ((u % 2) * 2), "ps%d" % ((u % 2) * 2 + 1)], [pk_], scale=0.125)
                if kt >= L - 4:
                    idx = (kt - (L - 4)) * 4 + kb
                    pv3 = pT[u % 8][:].rearrange("p (m q) -> p m q", m=2)
                    TT("dve", pv3, pv3, m01[:, idx * 512:(idx + 1) * 512].unsqueeze(1).to_broadcast([128, 2, 512]), ALU.mult,
                       [pk_, "m01"], [pk_])

            def pv(u):
                j, kTg, vg, ktl, kt, kb = units[u]
                for m in range(2):
                    MM(bank(4 + m), vg[:, ktl * 4 + kb, :], pT[u % 8][:, m * 512:(m + 1) * 512], u == 0, u == nu - 1,
                       ["vg%d" % j, "pT%d" % (u % 8)], ["ps%d" % (4 + m)])

            def adds(g):
                u0 = 4 * g
                TT("dve", sAB[0][:], pT[u0 % 8][:], pT[(u0 + 1) % 8][:], ALU.add, ["pT%d" % (u0 % 8), "pT%d" % ((u0 + 1) % 8)], ["sAB0"])
                TT("dve", sAB[1][:], pT[(u0 + 2) % 8][:], pT[(u0 + 3) % 8][:], ALU.add, ["pT%d" % ((u0 + 2) % 8), "pT%d" % ((u0 + 3) % 8)], ["sAB1"])
                sc = sC2[g % 2]
                TT("dve", sc[:], sAB[0][:], sAB[1][:], ALU.add, ["sAB0", "sAB1"], ["sC%d" % (g % 2)])

            def sums(g):
                sc = sC2[g % 2]
                for m in range(2):
                    MM(bank(6 + m), bONE, sc[:, m * 512:(m + 1) * 512], g == 0, g == nu // 4 - 1, ["cbf", "sC%d" % (g % 2)], ["ps%d" % (6 + m)])

            qk(0)
            qk(1)
            for u in range(nu):
                if u % 16 == 1 and u // 16 >= 1 and u // 16 + 1 < ng:
                    issue(u // 16 + 1)
                ex(u)
                if u + 2 < nu:
                    qk(u + 2)
                if u >= 1:
                    pv(u - 1)
                if u % 4 == 3:
                    adds(u // 4)
                if u % 4 == 2 and u >= 6:
                    sums(u // 4 - 1)
            pv(nu - 1)
            sums(nu // 4 - 1)
            ob0, ob1, r0 = tmpf[1][:, 0:512], tmpf[2][:, 0:512], tmpf[0][:, 0:512]
            ACT(r0, bank(6), AF.Ln, ["ps6"], ["tmpf0"])
            ACT(r0, r0, AF.Exp, ["tmpf0"], ["tmpf0"], scale=-1.0)
            TT("dve", ob0, bank(4), r0, ALU.mult, ["ps4", "tmpf0"], ["tmpf1"])
            ACT(r0, bank(7), AF.Ln, ["ps7"], ["tmpf0"])
            ACT(r0, r0, AF.Exp, ["tmpf0"], ["tmpf0"], scale=-1.0)
            TT("dve", ob1, bank(5), r0, ALU.mult, ["ps5", "tmpf0"], ["tmpf2"])
            STT("dve", ob0, ob1, sm[:, SM_NLAM:SM_NLAM + 1], ob0, ALU.mult, ALU.add, ["tmpf1", "tmpf2", "nlam"], ["tmpf1"])
            TT("dve", osqb[:], ob0, ob0, ALU.mult, ["tmpf1"], ["osqb"])
            MM(bank(6), bONE, osqb[:], True, True, ["cbf", "osqb"], ["ps6"])
            ACT(r0, bank(6), AF.Ln, ["ps6"], ["tmpf0"], scale=1.0 / 128.0, bias=EPS)
            ACT(r0, r0, AF.Exp, ["tmpf0"], ["tmpf0"], scale=-0.5)
            TT("dve", ob0, ob0, r0, ALU.mult, ["tmpf1", "tmpf0"], ["tmpf1"])
            STT("dve", ostage[:, h * 512:(h + 1) * 512], ob0, sm[:, SM_SLN:SM_SLN + 1], gbT[:, h * 512:(h + 1) * 512],
                ALU.mult, ALU.mult, ["tmpf1", "sln", "kTall1"], ["ostage"])
        DMA(od_d[s], ostage[:], ["ostage"], ["od%d" % s], "ost", eng="sp")
        pending.append("od%d" % s)

    run_stage(P, st)
    if stop == "O2":
        es.close()
        return nc

    st = ExitStack()
    P = Prog("m")
    if stop == "O3":
        P.limit = limit
    WB = sl("WB3", [128, 28800], BF16)
    wst = [sl("wst%d_3" % i, [128, 2048], F32) for i in range(2)]
    xst = [sl("xst0_3", [128, 4 * 515], F32)] * 2
    xb = [sl("xb0_3", [128, 8 * 515], BF16)] * 2
    lnp = sl("lnpt", [128, 2048])
    yT = sl("yT", [128, 4096], BF16)
    oTs = sl("oTs", [128, 4096], BF16)
    merged = sl("merged", [128, 8 * 512], BF16)
    gt = [sl("gt%d" % i, [128, 512]) for i in range(2)]
    xtok = sl("xtok", [128, 4 * 1024])
    tmpf = [sl("tmpf%d_3" % i, [128, 1024]) for i in range(4)]
    DMA(lnp[:], lnp_d, (), ["lnp"], "c2")

    OFF_A, OFF_B, OFF_O, OFF_GM = 0, 8192, 16384, 24576
    load_weights(w_a, 1024, OFF_A, "wk")
    load_weights(w_b, 1024, OFF_B, "wvv")
    load_weights(w_o, 1024, OFF_O, "wx")
    wa_v, wb_v, wo_v = wv(OFF_A, 1024), wv(OFF_B, 1024), wv(OFF_O, 1024)
    gmi = [0]
    for s in range(4):
        i = 0
        xv = load_x(xT_own[:, :, s, :], 515, i)
        DMA(yT[:], ysd_d[s], (), ["yT"], "yld")
        DMA(oTs[:], od_d[s], (), ["oTs"], "old")
        xk = "xb%d" % i
        DMA(xtok[:].rearrange("p (k c) -> p k c", k=4), x_own[s].rearrange("(k p) c -> p k c", p=128), (), ["xtok"], "xtok")
        for db in range(8):
            j = gmi[0] % 2
            gmi[0] += 1
            goff = OFF_GM + j * 2048
            stv = wst[j][:, 0:2048].rearrange("p (a b c) -> p a b c", a=8, b=2)
            DMA(stv, w_gm[:, :, :].rearrange("p a (b c) -> p a b c", b=2)[:, :, :, db * 128:(db + 1) * 128], (), ["wst%d" % j], "wst%d" % j)
            CP("dve", WB[:, goff:goff + 2048].rearrange("p (a b c) -> p a b c", a=8, b=2), stv, ["wst%d" % j], ["wgm%d" % j])
            wg = WB[:, goff:goff + 2048].rearrange("p (a b c) -> p a b c", a=8, b=2)
            for br in range(2):
                for dc in range(8):
                    MM(bank(br), wg[:, dc, br, :], xv[:, dc, 3:515], dc == 0, dc == 7, ["wgm%d" % j, xk], ["ps%d" % br])
                ACT(gt[br][:], bank(br), AF.Sigmoid, ["ps%d" % br, "cst"], ["gt%d" % br], bias=cst[:, C_BG + br * 8 + db:C_BG + br * 8 + db + 1])
            for cc in range(8):
                MM(bank(2), wa_v[:, cc, db * 128:(db + 1) * 128], yT[:, cc * 512:(cc + 1) * 512], cc == 0, cc == 7,
                   ["wk", "yT"], ["ps2"])
            for cc in range(8):
                MM(bank(3), wb_v[:, cc, db * 128:(db + 1) * 128], oTs[:, cc * 512:(cc + 1) * 512], cc == 0, cc == 7,
                   ["wvv", "oTs"], ["ps3"])
            TT("dve", gt[0][:], gt[0][:], bank(2), ALU.mult, ["gt0", "ps2"], ["gt0"])
            TT("dve", gt[1][:], gt[1][:], bank(3), ALU.mult, ["gt1", "ps3"], ["gt1"])
            TT("dve", merged[:, db * 512:(db + 1) * 512], gt[0][:], gt[1][:], ALU.add, ["gt0", "gt1"], ["merged"])
        for k in range(4):
            vb = tmpf[k % 2]
            vk = "tmpf%d" % (k % 2)
            for half in range(2):
                pb = 4 + half
                for db in range(8):
                    MM(bank(pb), merged[:, db * 512 + k * 128:db * 512 + (k + 1) * 128], wo_v[:, db, half * 512:(half + 1) * 512], db == 0, db == 7,
                       ["merged", "wx"], ["ps%d" % pb])
            STT("dve", vb[:], xtok[:, k * 1024:(k + 1) * 1024], ALPHA, PS[2][:], ALU.mult, ALU.add, ["xtok", "ps4", "ps5"], [vk])
            c0 = SM_NM + (k % 2) * 4
            P.add("dve", lambda e, o=sm[:, c0:c0 + 1], ii=vb[:]: e.reduce_sum(o, ii, mybir.AxisListType.X), r=[vk], w=["ln_a%d" % (k % 2)])
            TS("dve", sm[:, c0:c0 + 1], sm[:, c0:c0 + 1], -1.0 / D, None, ALU.mult, None, ["ln_a%d" % (k % 2)], ["ln_b%d" % (k % 2)])
            sq = tmpf[2 + k % 2]
            MSET("dve", sm[:, c0 + 1:c0 + 2], 0.0, ["ln_c%d" % (k % 2)])
            ACT(sq[:], vb[:], AF.Square, [vk, "ln_b%d" % (k % 2), "ln_c%d" % (k % 2)], ["tmpf%d" % (2 + k % 2), "ln_c%d" % (k % 2)],
                bias=sm[:, c0:c0 + 1], accum=sm[:, c0 + 1:c0 + 2])
            TS("dve", sm[:, c0 + 2:c0 + 3], sm[:, c0 + 1:c0 + 2], 1.0 / D, EPS, ALU.mult, ALU.add, ["ln_c%d" % (k % 2)], ["ln_d%d" % (k % 2)])
            ACT(sm[:, c0 + 2:c0 + 3], sm[:, c0 + 2:c0 + 3], AF.Sqrt, ["ln_d%d" % (k % 2)], ["ln_d2%d" % (k % 2)])
            P.add("dve", lambda e, o=sm[:, c0 + 2:c0 + 3]: e.reciprocal(o, o), r=["ln_d2%d" % (k % 2)], w=["ln_e%d" % (k % 2)])
            TS("dve", vb[:], vb[:], sm[:, c0:c0 + 1], sm[:, c0 + 2:c0 + 3], ALU.add, ALU.mult, [vk, "ln_b%d" % (k % 2), "ln_e%d" % (k % 2)], [vk])
            TT("dve", vb[:], vb[:], lnp[:, 0:1024], ALU.mult, [vk, "lnp"], [vk])
            TT("pool", sq[:], vb[:], lnp[:, 1024:2048], ALU.add, [vk, "lnp"], ["tmpf%d" % (2 + k % 2)])
            DMA(y_out[s, k * 128:(k + 1) * 128, :], sq[:], ["tmpf%d" % (2 + k % 2)], ["yout%d_%d" % (s, k)], "yo%d" % (k % 2))
            pending.append("yout%d_%d" % (s, k))
    run_stage(P, st)
    es.close()
    return nc


def _prep_inputs(inputs):
    f = lambda a: np.ascontiguousarray(np.asarray(a, dtype=np.float32))
    x = f(inputs["x"])
    w_in = f(inputs["w_in"])[0]

    def wl(c0, c1):
        return np.ascontiguousarray(w_in[:, c0:c1].reshape(8, 128, c1 - c0).transpose(1, 0, 2))

    def wsq(w):
        return np.ascontiguousarray(f(w)[0].reshape(8, 128, 1024).transpose(1, 0, 2))

    common = {
        "w_z": wl(0, 1024), "w_x": wl(1024, 2560), "w_dt": wl(2560, 2576), "w_q": wl(2576, 3600),
        "w_k": wl(3600, 4624), "w_v": wl(4624, 5648), "w_gb": wl(5648, 6672), "w_gm": wl(6672, 8720),
        "w_a": wsq(inputs["w_a"]), "w_b": wsq(inputs["w_b"]), "w_o": wsq(inputs["w_o"]),
    }
    r = np.arange(128)
    ident = (r[:, None] == r[None, :]).astype(np.float32)
    LT = (r[:, None] <= r[None, :]).astype(np.float32)
    SU = (r[:, None] > r[None, :]).astype(np.float32)
    ones = np.ones((128, 128), np.float32)
    perm = np.zeros((128, 128), np.float32)
    for m in range(2):
        for d in range(8):
            perm[m * 64 + d + 8, m * 64 + d] = -1.0
            perm[m * 64 + d, m * 64 + d + 8] = 1.0
    bc = lambda v, n: np.broadcast_to(f(v).reshape(1, n), (128, n))
    conv_w = f(inputs["conv_w"])[0]
    cw = np.zeros((128, 48), np.float32)
    for blk in range(12):
        for w in range(4):
            cw[:, blk * 4 + w] = conv_w[w, blk * 128:(blk + 1) * 128]
    conv_b = f(inputs["conv_b"])[0]
    cb = conv_b.reshape(12, 128).T
    bg = f(inputs["b_gate"])[0].reshape(16, 128).T
    sln = f(inputs["subln_w"])[0].reshape(128, 1)
    nw = f(inputs["ssd_norm_w"])[0].reshape(8, 128).T
    lnp = np.concatenate([bc(inputs["ln_g"], 1024), bc(inputs["ln_b"], 1024)], axis=1)
    pos = np.arange(S, dtype=np.float32)
    inv_freq = (np.float32(500000.0) ** (-np.arange(0, 16, 2, dtype=np.float32) / np.float32(16))).astype(np.float32)
    ang = (pos[:, None] * inv_freq[None, :]).astype(np.float32)
    cos, sin = np.cos(ang).astype(np.float32), np.sin(ang).astype(np.float32)
    rope = np.zeros((128, 2, S), np.float32)
    rope[:, 0, :] = 1.0
    for m in range(2):
        for dd in range(16):
            rope[m * 64 + dd, 0, :] = cos[:, dd % 8]
            rope[m * 64 + dd, 1, :] = sin[:, dd % 8]
    maskA = np.zeros((8, 512), np.float32)
    for k in range(512):
        maskA[k // 64, k] = 1.0
    in_maps = []
    for c in range(8):
        b, j = c // 4, c % 4
        tiles = [j, 7 - j, 8 + j, 15 - j]
        xb = x[b]
        xT = np.ascontiguousarray(xb.T.reshape(8, 128, S).transpose(1, 0, 2))
        xo = np.zeros((128, 8, 4, 515), np.float32)
        xtok = np.zeros((4, 512, D), np.float32)
        ropeO = np.zeros((128, 2, 4, 512), np.float32)
        sel = np.zeros((128, 16), np.float32)
        maskB = np.zeros((8, 16, 512), np.float32)
        for s, t in enumerate(tiles):
            lo = t * 512
            if t > 0:
                xo[:, :, s, :] = xT[:, :, lo - 3:lo + 512]
            else:
                xo[:, :, s, 3:] = xT[:, :, 0:512]
            xtok[s] = xb[lo:lo + 512]
            ropeO[:, :, s, :] = rope[:, :, lo:lo + 512]
            sel[:, t] = 1.0
            L = KLEN[s]
            for pi in range(4):
                kt = L - 4 + pi
                if kt < t:
                    pass
                elif kt > t:
                    maskB[:, pi + 4 * s, :] = NEG
                else:
                    for rr in range(8):
                        q = np.arange(512)
                        maskB[rr, pi + 4 * s, :] = np.where(q // 64 >= rr, 0.0, NEG)
        cstv = np.concatenate([ident, LT, SU, ones, perm, bc(inputs["a_log"], 16), bc(inputs["dt_bias"], 16), bc(inputs["d_skip"], 16),
                               bc(inputs["lambda_q1"], 64), bc(inputs["lambda_k1"], 64), bc(inputs["lambda_q2"], 64), bc(inputs["lambda_k2"], 64),
                               cw, cb, bg, sln, sel, nw], axis=1).astype(np.float32)
        assert cstv.shape[1] == NCST
        m = dict(common)
        m.update({"xT_all": xT, "xT_own": xo, "x_own": xtok, "cst": np.ascontiguousarray(cstv),
                  "cbrow": conv_b.reshape(1, 1536).copy(), "lnp": np.ascontiguousarray(lnp),
                  "msk": np.ascontiguousarray(np.concatenate([maskA, maskB.reshape(8, 8192)], axis=1)),
                  "ropeA": rope, "ropeO": ropeO})
        in_maps.append(m)
    return in_maps


def kernel(**inputs):
    in_maps = _prep_inputs(inputs)
    nc = build_nc()
    res = run_bass_kernel_spmd(nc, in_maps, core_ids=list(range(8)))
    out = np.zeros((2, S, D), np.float32)
    for c in range(8):
        b, j = c // 4, c % 4
        tiles = [j, 7 - j, 8 + j, 15 - j]
        y = np.asarray(res.results[c]["y_out"], dtype=np.float32)
        for s, t in enumerate(tiles):
            out[b, t * 512:(t + 1) * 512] = y[s]
    return out
```

```python
import math
import sys
import os
from contextlib import ExitStack
import numpy as np
import concourse.bass as bass
import concourse.mybir as mybir
from concourse.bass_utils import run_bass_kernel_spmd

F32 = mybir.dt.float32
BF16 = mybir.dt.bfloat16
AF = mybir.ActivationFunctionType
ALU = mybir.AluOpType

D = 1024
S = 8192
NT = 16
KLEN = (4, 8, 12, 16)
EPS = 1e-5
ALPHA = 2.0 ** 0.25
LAMBDA_INIT = 0.2
NEG = -30000.0

C_ID, C_LT, C_SU, C_ONE, C_PERM = 0, 128, 256, 384, 512
C_ALOG, C_DTB, C_DSK = 640, 656, 672
C_LQ1, C_LK1, C_LQ2, C_LK2 = 688, 752, 816, 880
C_CW, C_CB, C_BG, C_SLN, C_SEL, C_NW = 944, 992, 1004, 1020, 1021, 1037
NCST = 1045


class _Op:
    __slots__ = ("eng", "fn", "deps", "dma", "idx", "need", "sem", "val", "src")


class Prog:
    def __init__(self, tag):
        self.tag = tag
        self.ops = []
        self.wr = {}
        self.rd = {}
        self.base = {}
        self.limit = None

    def add(self, eng, fn, r=(), w=(), dma=None):
        if self.limit is not None and len(self.ops) >= self.limit:
            return None
        o = _Op()
        o.eng, o.fn, o.dma, o.idx, o.need = eng, fn, dma, len(self.ops), False
        f = sys._getframe(1)
        o.src = (f.f_lineno, f.f_back.f_lineno if f.f_back else 0)
        deps = {}
        for k in r:
            for x in self.wr.get(k, ()):
                deps[x] = "raw"
        for k in w:
            if self.rd.get(k):
                self.base[k] = list(self.wr.get(k, ())) + list(self.rd[k])
                self.wr[k] = []
                self.rd[k] = []
            for x in self.base.get(k, ()):
                deps.setdefault(x, "war")
        for k in w:
            self.wr.setdefault(k, []).append(o.idx)
            self.rd.setdefault(k, [])
        for k in r:
            self.rd.setdefault(k, []).append(o.idx)
            self.wr.setdefault(k, [])
        deps.pop(o.idx, None)
        o.deps = deps
        self.ops.append(o)
        return o

    def finalize(self):
        cnt = {}
        for o in self.ops:
            if o.dma is not None:
                o.sem = self.tag + "d_" + o.dma
                cnt[o.sem] = cnt.get(o.sem, 0) + 16
                o.val = cnt[o.sem]
        for o in self.ops:
            best = {}
            for d, kind in o.deps.items():
                p = self.ops[d]
                if p.dma is None:
                    if p.eng == o.eng and o.eng == "pe" and o.dma is None:
                        continue
                    ch = "e_" + p.eng
                else:
                    ch = p.sem
                if ch not in best or best[ch] < d:
                    best[ch] = d
            o.deps = list(best.values())
            for d in o.deps:
                self.ops[d].need = True
        for o in self.ops:
            if o.dma is None and o.need:
                o.sem = self.tag + "e_" + o.eng
                cnt[o.sem] = cnt.get(o.sem, 0) + 1
                o.val = cnt[o.sem]
        return sorted(cnt.keys())

    def emit(self, eng, e, sems):
        waited = {}
        for o in self.ops:
            if o.eng != eng:
                continue
            for d in o.deps:
                p = self.ops[d]
                if waited.get(p.sem, 0) < p.val:
                    e.wait_ge(sems[p.sem], p.val)
                    waited[p.sem] = p.val
            ins = o.fn(e)
            if o.dma is not None:
                ins.then_inc(sems[o.sem], 16)
            elif o.need:
                ins.then_inc(sems[o.sem], 1)


def build_nc(stop=None, nta=NT, limit=None):
    nc = bass.Bass("TRN2", target_bir_lowering=False)
    dt_in = lambda n, shp, t=F32: nc.dram_tensor(n, shp, t, kind="ExternalInput").ap()
    xT_all = dt_in("xT_all", [128, 8, S])
    xT_own = dt_in("xT_own", [128, 8, 4, 515])
    x_own = dt_in("x_own", [4, 512, D])
    w_k = dt_in("w_k", [128, 8, 1024])
    w_v = dt_in("w_v", [128, 8, 1024])
    w_x = dt_in("w_x", [128, 8, 1536])
    w_dt = dt_in("w_dt", [128, 8, 16])
    w_q = dt_in("w_q", [128, 8, 1024])
    w_z = dt_in("w_z", [128, 8, 1024])
    w_gb = dt_in("w_gb", [128, 8, 1024])
    w_gm = dt_in("w_gm", [128, 8, 2048])
    w_a = dt_in("w_a", [128, 8, 1024])
    w_b = dt_in("w_b", [128, 8, 1024])
    w_o = dt_in("w_o", [128, 8, 1024])
    cst_d = dt_in("cst", [128, NCST])
    cbrow_d = dt_in("cbrow", [1, 1536])
    lnp_d = dt_in("lnp", [128, 2048])
    msk_d = dt_in("msk", [8, 512 + 16 * 512])
    ropeA = dt_in("ropeA", [128, 2, S])
    ropeO = dt_in("ropeO", [128, 2, 4, 512])
    y_out = nc.dram_tensor("y_out", [4, 512, D], F32, kind="ExternalOutput").ap()
    ksc = nc.dram_tensor("ksc", [8, 128, S], BF16).ap()
    vsc = nc.dram_tensor("vsc", [S, 1024], BF16).ap()

    hs_d = nc.dram_tensor("hs_d", [4, 128, 1024], F32).ap()
    ysd_d = nc.dram_tensor("ysd_d", [4, 128, 4096], BF16).ap()
    od_d = nc.dram_tensor("od_d", [4, 128, 4096], BF16).ap()

    es = ExitStack()
    st = ExitStack()
    P = Prog("s")
    pending = []
    sb = lambda n, shp, t=F32: es.enter_context(nc.sbuf_tensor(n, shp, t))
    sl = lambda n, shp, t=F32: st.enter_context(nc.sbuf_tensor(n, shp, t))
    cst = sb("cstt", [128, NCST])
    cbf = sb("cbf", [128, 640], BF16)
    diag = sb("diag", [128, 48 * 128], BF16)
    cbrow_b = sb("cbrowb", [1, 1536], BF16)
    onesrow_b = sb("onesrow", [1, 128], BF16)
    sm = sb("sm", [128, 64])
    PS = [es.enter_context(nc.psum_tensor("ps%d" % i, [128, 1024], F32)) for i in range(4)]

    def run_stage(P, st):
        if os.environ.get("KDBG"):
            print("STAGE", P.tag, "nops", len(P.ops), "last", P.ops[-1].eng, P.ops[-1].src)
        P.limit = None
        P.add("sp", lambda e: e.nop(), r=list(pending), w=["done"])
        del pending[:]
        names = P.finalize()
        sems = {n: st.enter_context(nc.semaphore(n)) for n in names}
        with nc.Block() as block:
            @block.sync
            def _(e):
                P.emit("sp", e, sems)

            @block.tensor
            def _(e):
                P.emit("pe", e, sems)

            @block.scalar
            def _(e):
                P.emit("act", e, sems)

            @block.vector
            def _(e):
                P.emit("dve", e, sems)

            @block.gpsimd
            def _(e):
                P.emit("pool", e, sems)
        st.close()
        nc.all_engine_barrier()

    WB = sl("WB", [128, 28800], BF16)
    wst = [sl("wst%d" % i, [128, 2048], F32) for i in range(2)]
    xst = [sl("xst0", [128, 4 * 515], F32)] * 2
    xb = [sl("xb%d" % i, [128, 8 * 515], BF16) for i in range(2)]
    uT = sl("uT", [128, 12 * 515], BF16)
    rt = [sl("rt0", [128, 1024])] * 2
    kc = [sl("kc%d" % i, [128, 512], BF16) for i in range(2)]
    tmpf = [sl("tmpf%d" % i, [128, 1536 if i == 3 else 512]) for i in range(4)]
    kTall = [sl("kTall0", [128, 8 * 512], BF16)] * 2
    vall = [sl("vall0", [128, 4 * 1024], BF16)] * 2
    xs_t = sl("xs_t", [128, 4 * 1024], BF16)
    B_t = sl("B_t", [128, 4 * 256], BF16)
    dtt = sl("dtt", [128, 64])
    adt = sl("adt", [128, 64])
    dk = sl("dk", [128, 64])
    wl = sl("wl", [128, 64])
    cdT = sl("cdT", [128, 16])
    xdtd = sl("xdtd", [128, 4 * 1024], BF16)
    hcur = sl("hcur", [128, 1024])
    hs = [sl("hs%d" % i, [128, 1024]) for i in range(4)]

    def bank(i):
        return PS[i // 2][:, (i % 2) * 512:(i % 2) * 512 + 512]

    cI = cst[:, C_ID:C_ID + 128]
    cLT = cst[:, C_LT:C_LT + 128]
    cSU = cst[:, C_SU:C_SU + 128]
    cONE = cst[:, C_ONE:C_ONE + 128]
    bI = cbf[:, 0:128]
    bONE = cbf[:, 384:512]
    bPERM = cbf[:, 512:640]
    SM_A, SM_NLAM, SM_T0, SM_T1, SM_SLN, SM_NM, SM_SS, SM_RS = 0, 16, 17, 18, 19, 20, 24, 28

    dma_rr = [0]

    def DMA(out, in_, r, w, key, eng="sp"):
        P.add(eng, lambda e, o=out, i=in_: e.dma_start(out=o, in_=i), r=r, w=w, dma=key)

    def MM(out, lhsT, rhs, start, stop, r, w):
        P.add("pe", lambda e, o=out, l=lhsT, rr=rhs, s=start, t=stop: e.matmul(o, lhsT=l, rhs=rr, start=s, stop=t),
              r=r, w=w)

    def ACT(out, in_, func, r, w, bias=None, scale=None, accum=None):
        def fn(e, o=out, i=in_, f=func, b=bias, sc=scale, a=accum):
            kw = {}
            if b is not None:
                kw["bias"] = b
            if sc is not None:
                kw["scale"] = sc
            if a is not None:
                kw["accum_out"] = a
            return e.activation(o, i, f, **kw)
        P.add("act", fn, r=r, w=w)

    def TT(eng, out, in0, in1, op, r, w):
        P.add(eng, lambda e, o=out, a=in0, b=in1, p=op: e.tensor_tensor(out=o, in0=a, in1=b, op=p), r=r, w=w)

    def TS(eng, out, in0, s1, s2, op0, op1, r, w):
        if op1 is None:
            P.add(eng, lambda e, o=out, a=in0, x=s1, p=op0: e.tensor_scalar(o, a, x, None, p), r=r, w=w)
        else:
            P.add(eng, lambda e, o=out, a=in0, x=s1, y=s2, p=op0, q=op1: e.tensor_scalar(o, a, x, y, p, q), r=r, w=w)

    def STT(eng, out, in0, sc, in1, op0, op1, r, w):
        P.add(eng, lambda e, o=out, a=in0, s=sc, b=in1, p=op0, q=op1: e.scalar_tensor_tensor(o, a, s, b, p, q), r=r, w=w)

    def CP(eng, out, in_, r, w):
        P.add(eng, lambda e, o=out, i=in_: e.tensor_copy(out=o, in_=i), r=r, w=w)

    def MSET(eng, ap, v, w):
        P.add(eng, lambda e, a=ap, x=v: e.memset(a, x), r=(), w=w)

    DMA(cst[:], cst_d, (), ["cst"], "c0")
    DMA(tmpf[3][0:1, 0:1536], cbrow_d, (), ["tmpf3"], "c1")
    CP("dve", cbf[:], cst[:, 0:640], ["cst"], ["cbf"])
    CP("dve", cbrow_b[:], tmpf[3][0:1, 0:1536], ["tmpf3"], ["cbrowb"])
    MSET("dve", onesrow_b[:], 1.0, ["onesrow"])
    MSET("dve", uT[:], 0.0, ["uT%d" % b for b in range(12)])
    MSET("dve", hcur[:], 0.0, ["hcur"])
    for i in range(4):
        MSET("dve", hs[i][:], 0.0, ["hs%d" % i])
    for blk in range(12):
        for w in range(4):
            TS("dve", diag[:, (blk * 4 + w) * 128:(blk * 4 + w + 1) * 128], cI,
               cst[:, C_CW + blk * 4 + w:C_CW + blk * 4 + w + 1], None, ALU.mult, None, ["cst"], ["diag"])
    ACT(sm[:, SM_A:SM_A + 16], cst[:, C_ALOG:C_ALOG + 16], AF.Exp, ["cst"], ["sm_a0"])
    TS("dve", sm[:, SM_A:SM_A + 16], sm[:, SM_A:SM_A + 16], -1.0, None, ALU.mult, None, ["sm_a0"], ["sm_a"])
    TT("dve", tmpf[0][:, 0:64], cst[:, C_LQ1:C_LQ1 + 64], cst[:, C_LK1:C_LK1 + 64], ALU.mult, ["cst"], ["lamt0"])
    TT("dve", tmpf[0][:, 64:128], cst[:, C_LQ2:C_LQ2 + 64], cst[:, C_LK2:C_LK2 + 64], ALU.mult, ["cst"], ["lamt1"])
    P.add("dve", lambda e: e.reduce_sum(sm[:, SM_T0:SM_T0 + 1], tmpf[0][:, 0:64], mybir.AxisListType.X), r=["lamt0"], w=["lam_s0"])
    P.add("dve", lambda e: e.reduce_sum(sm[:, SM_T1:SM_T1 + 1], tmpf[0][:, 64:128], mybir.AxisListType.X), r=["lamt1"], w=["lam_s1"])
    ACT(sm[:, SM_T0:SM_T0 + 2], sm[:, SM_T0:SM_T0 + 2], AF.Exp, ["lam_s0", "lam_s1"], ["lam_e"])
    TT("dve", sm[:, SM_NLAM:SM_NLAM + 1], sm[:, SM_T1:SM_T1 + 1], sm[:, SM_T0:SM_T0 + 1], ALU.subtract, ["lam_e"], ["nlam0"])
    TS("dve", sm[:, SM_NLAM:SM_NLAM + 1], sm[:, SM_NLAM:SM_NLAM + 1], -LAMBDA_INIT, None, ALU.add, None, ["nlam0"], ["nlam"])
    TS("dve", sm[:, SM_SLN:SM_SLN + 1], cst[:, C_SLN:C_SLN + 1], 1.0 - LAMBDA_INIT, None, ALU.mult, None, ["cst"], ["sln"])

    wstate = {"i": 0}

    def load_weights(dram, ncols, wb_off, key, scale_rows=None):
        c0 = 0
        while c0 < ncols:
            cw = min(256, ncols - c0)
            i = wstate["i"] % 2
            wstate["i"] += 1
            stv = wst[i][:, 0:8 * cw].rearrange("p (a c) -> p a c", a=8)
            DMA(stv, dram[:, :, c0:c0 + cw], (), ["wst%d" % i], "wst%d" % i)
            dst = WB[:, wb_off:wb_off + 8 * ncols].rearrange("p (a c) -> p a c", a=8)[:, :, c0:c0 + cw]
            if wstate["i"] % 2:
                CP("dve", dst, stv, ["wst%d" % i], [key])
            else:
                P.add("act", lambda e, o=dst, ii=stv: e.copy(o, ii), r=["wst%d" % i], w=[key])
            c0 += cw

    def wv(wb_off, ncols):
        return WB[:, wb_off:wb_off + 8 * ncols].rearrange("p (a c) -> p a c", a=8)

    def load_x(src, ncol, i, eng="pool"):
        xv = xb[i][:, 0:8 * ncol].rearrange("p (a c) -> p a c", a=8)
        for hf in range(2):
            stv = xst[0][:, 0:4 * ncol].rearrange("p (a c) -> p a c", a=4)
            DMA(stv, src[:, hf * 4:(hf + 1) * 4, :], (), ["xst0"], "xst0")
            if eng == "mix":
                if hf == 0:
                    CP("dve", xv[:, hf * 4:(hf + 1) * 4, :], stv, ["xst0"], ["xb%d" % i])
                else:
                    P.add("act", lambda e, o=xv[:, hf * 4:(hf + 1) * 4, :], ii=stv: e.copy(o, ii), r=["xst0"], w=["xb%d" % i])
            else:
                CP(eng, xv[:, hf * 4:(hf + 1) * 4, :], stv, ["xst0"], ["xb%d" % i])
        return xv

    ev = {"i": 0}

    def evac(out, in_, r, w):
        ev["i"] += 1
        if ev["i"] % 2:
            P.add("act", lambda e, o=out, i=in_: e.copy(o, i), r=r, w=w)
        else:
            CP("dve", out, in_, r, w)

    def softplus_dt(psap, k, rk):
        d = dtt[:, k * 16:(k + 1) * 16]
        TT("dve", d, psap, cst[:, C_DTB:C_DTB + 16], ALU.add, [rk, "cst"], ["dtt%d" % k])
        ACT(d, d, AF.Exp, ["dtt%d" % k], ["dtt%d" % k])
        ACT(d, d, AF.Ln, ["dtt%d" % k], ["dtt%d" % k], bias=1.0)
        TT("dve", adt[:, k * 16:(k + 1) * 16], d, sm[:, SM_A:SM_A + 16], ALU.mult, ["dtt%d" % k, "sm_a"], ["adt%d" % k])

    def conv_tok(k, blks, pb, xoff):
        for gi in range(0, len(blks), 4):
            grp = blks[gi:gi + 4]
            pk = "ps%d" % pb
            for j, blk in enumerate(grp):
                o = bank(pb)[:, j * 128:(j + 1) * 128]
                for w in range(4):
                    MM(o, uT[:, blk * 515 + k * 128 + w: blk * 515 + k * 128 + w + 128],
                       diag[:, (blk * 4 + w) * 128:(blk * 4 + w + 1) * 128], w == 0, False,
                       ["uT%d" % blk, "diag"], [pk])
                MM(o, onesrow_b[0:1, :], cbrow_b[0:1, blk * 128:(blk + 1) * 128], False, True,
                   ["onesrow", "cbrowb"], [pk])
            n = len(grp) * 128
            b0 = grp[0]
            if b0 < 8:
                ACT(xs_t[:, k * 1024 + b0 * 128:k * 1024 + b0 * 128 + n], bank(pb)[:, 0:n], AF.Silu, [pk], ["xs_t%d" % k])
            else:
                ACT(B_t[:, k * 256:(k + 1) * 256], bank(pb)[:, 0:n], AF.Silu, [pk], ["B_t%d" % k])
            pb = pb + 1 if pb % 2 == 0 else pb - 1
        return pb

    def proj_u(xv, blks, wx_off, col0, pbs, halo):
        wx = wv(wx_off, 1536)
        for bi, blk in enumerate(blks):
            pb = pbs[bi % len(pbs)]
            pk = "ps%d" % pb
            uk = "uT%d" % blk
            if halo == "carry":
                CP("dve", uT[:, blk * 515:blk * 515 + 3], uT[:, blk * 515 + 512:blk * 515 + 515], [uk], [uk])
            else:
                for dc in range(8):
                    MM(bank(pb)[:, 0:3], wx[:, dc, blk * 128:(blk + 1) * 128], xv[:, dc, 0:3], dc == 0, dc == 7,
                       ["wx", halo], [pk])
                evac(uT[:, blk * 515:blk * 515 + 3], bank(pb)[:, 0:3], [pk], [uk])
            for dc in range(8):
                MM(bank(pb), wx[:, dc, blk * 128:(blk + 1) * 128], xv[:, dc, col0:col0 + 512], dc == 0, dc == 7,
                   ["wx", halo if halo != "carry" else "xbcur"], [pk])
            evac(uT[:, blk * 515 + 3:blk * 515 + 515], bank(pb), [pk], [uk])

    OFF_K, OFF_V, OFF_X, OFF_DT = 0, 8192, 16384, 16384 + 12288
    load_weights(w_k, 1024, OFF_K, "wk")
    load_weights(w_v, 1024, OFF_V, "wvv")
    load_weights(w_x, 1536, OFF_X, "wx")
    load_weights(w_dt, 16, OFF_DT, "wdt")
    wk_v, wv_v, wdt_v = wv(OFF_K, 1024), wv(OFF_V, 1024), wv(OFF_DT, 16)

    def rope_block(h, pb, srcw, srck, xv, col0, rtab, dst, dstk, xk):
        pk = "ps%d" % pb
        for dc in range(8):
            MM(bank(pb), srcw[:, dc, h * 128:(h + 1) * 128], xv[:, dc, col0:col0 + 512], dc == 0, dc == 7, [srck, xk], [pk])
        kcb = kc[h % 2]
        kk = "kc%d" % (h % 2)
        P.add("act", lambda e, o=kcb[:], i=bank(pb): e.copy(o, i), r=[pk], w=[kk])
        pb2 = pb + 2
        pk2 = "ps%d" % pb2
        MM(bank(pb2), bPERM, kcb[:], True, True, ["cbf", kk], [pk2])
        t = tmpf[h % 2][:, 0:512]
        tk = "tmpf%d" % (h % 2)
        TT("dve", t, bank(pb2), rtab[:, 512:1024], ALU.mult, [pk2, "rt"], [tk])
        TT("dve", tmpf[2 + h % 2][:, 0:512], kcb[:], rtab[:, 0:512], ALU.mult, [kk, "rt"], ["tmpf%d" % (2 + h % 2)])
        TT("dve", dst, tmpf[2 + h % 2][:, 0:512], t, ALU.add, [tk, "tmpf%d" % (2 + h % 2)], [dstk])

    xv_next = load_x(xT_all[:, :, 0:512], 512, 0)
    for T in range(nta):
        i = T % 2
        xv = xv_next
        xk = "xb%d" % i
        if T + 1 < nta:
            xv_next = load_x(xT_all[:, :, (T + 1) * 512:(T + 2) * 512], 512, 1 - i)
        rtab = rt[i]
        DMA(rtab[:].rearrange("p (a c) -> p a c", a=2), ropeA[:, :, T * 512:(T + 1) * 512], (), ["rt0"], "rt0")
        def kproj(h):
            pk = h % 2
            for dc in range(8):
                MM(bank(pk), wk_v[:, dc, h * 128:(h + 1) * 128], xv[:, dc, :], dc == 0, dc == 7, ["wk", xk], ["ps%d" % pk])
        kproj(0)
        for h in range(8):
            pk = h % 2
            kcb = kc[h % 2]
            kk = "kc%d" % (h % 2)
            P.add("act", lambda e, o=kcb[:], ii=bank(pk): e.copy(o, ii), r=["ps%d" % pk], w=[kk])
            if h + 1 < 8:
                kproj(h + 1)
            MM(bank(2 + pk), bPERM, kcb[:], True, True, ["cbf", kk], ["ps%d" % (2 + pk)])
            t = tmpf[h % 2][:, 0:512]
            TT("dve", t, bank(2 + pk), rtab[:, 512:1024], ALU.mult, ["ps%d" % (2 + pk), "rt0"], ["tmpf%d" % (h % 2)])
            TT("dve", tmpf[2 + h % 2][:, 0:512], kcb[:], rtab[:, 0:512], ALU.mult, [kk, "rt0"], ["tmpf%d" % (2 + h % 2)])
            TT("dve", kTall[i][:, h * 512:(h + 1) * 512], tmpf[2 + h % 2][:, 0:512], t, ALU.add,
               ["tmpf%d" % (h % 2), "tmpf%d" % (2 + h % 2)], ["kTall0"])
        DMA(ksc[:, :, T * 512:(T + 1) * 512].rearrange("h r t -> r h t"),
            kTall[i][:].rearrange("p (h t) -> p h t", h=8), ["kTall0"], ["ksc%d" % T], "kst0", eng="pool")
        pending.append("ksc%d" % T)
        for k in range(4):
            for half in range(2):
                pb = 4 + (k * 2 + half) % 2
                for dc in range(8):
                    MM(bank(pb), xv[:, dc, k * 128:(k + 1) * 128], wv_v[:, dc, half * 512:(half + 1) * 512], dc == 0, dc == 7,
                       ["wvv", xk], ["ps%d" % pb])
                evac(vall[i][:, k * 1024 + half * 512:k * 1024 + half * 512 + 512], bank(pb), ["ps%d" % pb], ["vall0"])
        DMA(vsc[T * 512:(T + 1) * 512, :].rearrange("(k p) c -> p k c", p=128),
            vall[i][:].rearrange("p (k c) -> p k c", k=4), ["vall0"], ["vsc%d" % T], "vst0", eng="pool")
        pending.append("vsc%d" % T)
        wx = wv(OFF_X, 1536)
        for bi, blk in enumerate(range(10)):
            pb = 6 + bi % 2
            uk = "uT%d" % blk
            CP("dve", uT[:, blk * 515:blk * 515 + 3], uT[:, blk * 515 + 512:blk * 515 + 515], [uk], [uk])
            for dc in range(8):
                MM(bank(pb), wx[:, dc, blk * 128:(blk + 1) * 128], xv[:, dc, :], dc == 0, dc == 7, ["wx", xk], ["ps%d" % pb])
            evac(uT[:, blk * 515 + 3:blk * 515 + 515], bank(pb), ["ps%d" % pb], [uk])
        for k in range(4):
            o = bank(4)[:, k * 16:(k + 1) * 16]
            for dc in range(8):
                MM(o, xv[:, dc, k * 128:(k + 1) * 128], wdt_v[:, dc, :], dc == 0, dc == 7, ["wdt", xk], ["ps4"])
        for k in range(4):
            softplus_dt(bank(4)[:, k * 16:(k + 1) * 16], k, "ps4")
        pb = 0
        for k in range(4):
            pb = conv_tok(k, list(range(8)), pb, 0)
            pb = conv_tok(k, [8, 9], pb, 0)
        for k in range(4):
            o = bank(5)[:, k * 16:(k + 1) * 16]
            MM(o, cSU, adt[:, k * 16:(k + 1) * 16], True, k == 3, ["cst", "adt%d" % k], ["ps5"])
            for k2 in range(k + 1, 4):
                MM(o, cONE, adt[:, k2 * 16:(k2 + 1) * 16], False, k2 == 3, ["cst", "adt%d" % k2], ["ps5"])
        o = bank(5)[:, 64:80]
        for k in range(4):
            MM(o, cONE, adt[:, k * 16:(k + 1) * 16], k == 0, k == 3, ["cst", "adt%d" % k], ["ps5"])
        ACT(dk[:, 0:64], bank(5)[:, 0:64], AF.Exp, ["ps5"], ["dk"])
        ACT(cdT[:], bank(5)[:, 64:80], AF.Exp, ["ps5"], ["cdT"])
        TT("dve", wl[:], dtt[:], dk[:], ALU.mult, ["dk"] + ["dtt%d" % k for k in range(4)], ["wl"])
        for k in range(4):
            TT("dve", xdtd[:, k * 1024:(k + 1) * 1024].rearrange("p (h q) -> p h q", h=16),
               xs_t[:, k * 1024:(k + 1) * 1024].rearrange("p (h q) -> p h q", h=16),
               wl[:, k * 16:(k + 1) * 16].unsqueeze(2).to_broadcast([128, 16, 64]), ALU.mult,
               ["xs_t%d" % k, "wl"], ["xdtd%d" % k])
        for g in range(2):
            for k in range(4):
                MM(bank(6 + g), B_t[:, k * 256 + g * 128:k * 256 + (g + 1) * 128], xdtd[:, k * 1024 + g * 512:k * 1024 + (g + 1) * 512],
                   k == 0, k == 3, ["B_t%d" % k, "xdtd%d" % k], ["ps%d" % (6 + g)])
        STT("dve", hs[T // 4][:], hcur[:], cst[:, C_SEL + T:C_SEL + T + 1], hs[T // 4][:], ALU.mult, ALU.add,
            ["hcur", "cst", "hs%d" % (T // 4)], ["hs%d" % (T // 4)])
        TT("dve", hcur[:].rearrange("p (h q) -> p h q", h=16), hcur[:].rearrange("p (h q) -> p h q", h=16),
           cdT[:].unsqueeze(2).to_broadcast([128, 16, 64]), ALU.mult, ["hcur", "cdT"], ["hcur"])
        TT("dve", hcur[:], hcur[:], PS[3][:], ALU.add, ["hcur", "ps6", "ps7"], ["hcur"])

    for i in range(4):
        DMA(hs_d[i], hs[i][:], ["hs%d" % i], ["hsd%d" % i], "hst", eng="sp")
        pending.append("hsd%d" % i)
    run_stage(P, st)
    if stop == "A":
        es.close()
        return nc

    st = ExitStack()
    P = Prog("o")
    if stop == "O1":
        P.limit = limit
    WB = sl("WB1", [128, 28800], BF16)
    wst = [sl("wst%d_1" % i, [128, 2048], F32) for i in range(2)]
    xst = [sl("xst0_1", [128, 4 * 515], F32)] * 2
    xb = [sl("xb0_1", [128, 8 * 515], BF16)] * 2
    uT = sl("uT_1", [128, 12 * 515], BF16)
    tmpf = [sl("tmpf%d_1" % i, [128, 1024]) for i in range(3)]
    xs_t = sl("xs_t_1", [128, 4 * 1024], BF16)
    B_t = sl("B_t_1", [128, 4 * 256], BF16)
    zs = sl("zs", [128, 4 * 1024], BF16)
    BCT = sl("BCT", [128, 4 * 512], BF16)
    dtt = sl("dtt_1", [128, 64])
    adt = sl("adt_1", [128, 64])
    ex3 = sl("ex3", [128, 48])
    xdtd = sl("xdtd_1", [128, 1024], BF16)
    xdt = sl("xdt", [128, 1024], BF16)
    hcur = sl("hcur_1", [128, 1024])
    prevb = sl("prevb", [128, 1024], BF16)
    Rm = sl("Rm", [128, 2048])
    Lex = sl("Lex", [128, 2048], BF16)
    cbm = sl("cbm", [128, 256])
    MT = sl("MT", [128, 2048], BF16)
    ystage = sl("ystage", [128, 4096], BF16)

    OFF_Z = 0
    load_weights(w_z, 1024, OFF_Z, "wk")
    load_weights(w_x, 1536, OFF_X, "wx")
    load_weights(w_dt, 16, OFF_DT, "wdt")
    wdt_v = wv(OFF_DT, 16)
    wz_v = wv(OFF_Z, 1024)
    wx = wv(OFF_X, 1536)
    normw = cst[:, C_NW:C_NW + 8]
    for s in range(4):
        i = 0
        xv = load_x(xT_own[:, :, s, :], 515, i, eng="mix")
        xk = "xb%d" % i
        for bi, blk in enumerate(range(12)):
            pb = bi % 2
            uk = "uT%d" % blk
            for dc in range(8):
                MM(bank(pb)[:, 0:3], wx[:, dc, blk * 128:(blk + 1) * 128], xv[:, dc, 0:3], dc == 0, dc == 7, ["wx", xk], ["ps%d" % pb])
            evac(uT[:, blk * 515:blk * 515 + 3], bank(pb)[:, 0:3], ["ps%d" % pb], [uk])
            for dc in range(8):
                MM(bank(pb), wx[:, dc, blk * 128:(blk + 1) * 128], xv[:, dc, 3:515], dc == 0, dc == 7, ["wx", xk], ["ps%d" % pb])
            evac(uT[:, blk * 515 + 3:blk * 515 + 515], bank(pb), ["ps%d" % pb], [uk])
        for bi, blk in enumerate((8, 9, 10, 11)):
            pb = 2 + bi % 2
            for w in range(4):
                MM(bank(pb), diag[:, (blk * 4 + w) * 128:(blk * 4 + w + 1) * 128], uT[:, blk * 515 + w:blk * 515 + w + 512],
                   w == 0, w == 3, ["uT%d" % blk, "diag"], ["ps%d" % pb])
            ACT(BCT[:, bi * 512:(bi + 1) * 512], bank(pb), AF.Silu, ["ps%d" % pb, "cst"], ["BCT%d" % bi],
                bias=cst[:, C_CB + blk:C_CB + blk + 1])
        for k in range(4):
            o = bank(4)[:, k * 16:(k + 1) * 16]
            for dc in range(8):
                MM(o, xv[:, dc, 3 + k * 128:3 + (k + 1) * 128], wdt_v[:, dc, :], dc == 0, dc == 7, ["wdt", xk], ["ps4"])
        for k in range(4):
            softplus_dt(bank(4)[:, k * 16:(k + 1) * 16], k, "ps4")
        pb = 6
        for k in range(4):
            pb = conv_tok(k, list(range(8)), pb, 0)
            pb = conv_tok(k, [8, 9], pb, 0)
        for k in range(4):
            for half in range(2):
                pb = (k * 2 + half) % 2
                for dc in range(8):
                    MM(bank(pb), xv[:, dc, 3 + k * 128:3 + (k + 1) * 128], wz_v[:, dc, half * 512:(half + 1) * 512], dc == 0, dc == 7,
                       ["wk", xk], ["ps%d" % pb])
                ACT(zs[:, k * 1024 + half * 512:k * 1024 + half * 512 + 512], bank(pb), AF.Silu, ["ps%d" % pb], ["zs%d" % k])
        DMA(hcur[:], hs_d[s], (), ["hcur"], "hld")
        for k in range(4):
            CP("pool", prevb[:], hcur[:], ["hcur"], ["prevb"])
            a_k = adt[:, k * 16:(k + 1) * 16]
            ak = "adt%d" % k
            TT("dve", Rm[:].rearrange("p (h l) -> p h l", h=16), a_k.unsqueeze(2).to_broadcast([128, 16, 128]),
               cLT.unsqueeze(1).to_broadcast([128, 16, 128]), ALU.mult, [ak, "cst"], ["Rm"])
            for q in range(4):
                MM(bank(q), cSU, Rm[:, q * 512:(q + 1) * 512], True, True, ["cst", "Rm"], ["ps%d" % q])
            ACT(Lex[:, 0:1024], PS[0][:], AF.Exp, ["ps0", "ps1"], ["Lex0"])
            ACT(Lex[:, 1024:2048], PS[1][:], AF.Exp, ["ps2", "ps3"], ["Lex1"])
            for g in range(2):
                MM(bank(4)[:, g * 128:(g + 1) * 128], BCT[:, g * 512 + k * 128:g * 512 + (k + 1) * 128],
                   BCT[:, (2 + g) * 512 + k * 128:(2 + g) * 512 + (k + 1) * 128], True, True, ["BCT%d" % g, "BCT%d" % (2 + g)], ["ps4"])
            TT("dve", cbm[:].rearrange("p (g l) -> p g l", g=2), bank(4)[:, 0:256].rearrange("p (g l) -> p g l", g=2),
               cLT.unsqueeze(1).to_broadcast([128, 2, 128]), ALU.mult, ["ps4", "cst"], ["cbm"])
            TT("dve", MT[:].rearrange("p (g h l) -> p g h l", g=2, h=8), Lex[:].rearrange("p (g h l) -> p g h l", g=2, h=8),
               cbm[:].rearrange("p (g l) -> p g l", g=2).unsqueeze(2).to_broadcast([128, 2, 8, 128]), ALU.mult,
               ["Lex0", "Lex1", "cbm"], ["MT"])
            MM(bank(5)[:, 0:16], cLT, a_k, True, True, ["cst", ak], ["ps5"])
            MM(bank(5)[:, 16:32], cSU, a_k, True, True, ["cst", ak], ["ps5"])
            MM(bank(5)[:, 32:48], cONE, a_k, True, True, ["cst", ak], ["ps5"])
            ACT(ex3[:], bank(5)[:, 0:48], AF.Exp, ["ps5"], ["ex3"])
            xsk = xs_t[:, k * 1024:(k + 1) * 1024].rearrange("p (h q) -> p h q", h=16)
            TT("dve", xdt[:].rearrange("p (h q) -> p h q", h=16), xsk,
               dtt[:, k * 16:(k + 1) * 16].unsqueeze(2).to_broadcast([128, 16, 64]), ALU.mult, ["xs_t%d" % k, "dtt%d" % k], ["xdt"])
            TT("dve", xdtd[:, 0:1024].rearrange("p (h q) -> p h q", h=16), xdt[:].rearrange("p (h q) -> p h q", h=16),
               ex3[:, 16:32].unsqueeze(2).to_broadcast([128, 16, 64]), ALU.mult, ["xdt", "ex3"], ["xdtd0"])
            for h in range(16):
                MM(PS[0][:, h * 64:(h + 1) * 64], MT[:, h * 128:(h + 1) * 128], xdt[:, h * 64:(h + 1) * 64], True, True,
                   ["MT", "xdt"], ["ps%d" % (h // 8)])
            for g in range(2):
                MM(bank(2 + g), BCT[:, (2 + g) * 512 + k * 128:(2 + g) * 512 + (k + 1) * 128], prevb[:, g * 512:(g + 1) * 512], True, True,
                   ["BCT%d" % (2 + g), "prevb"], ["ps%d" % (2 + g)])
            t1, t2 = tmpf[0], tmpf[1]
            TT("dve", t1[:].rearrange("p (h q) -> p h q", h=16), PS[1][:].rearrange("p (h q) -> p h q", h=16),
               ex3[:, 0:16].unsqueeze(2).to_broadcast([128, 16, 64]), ALU.mult, ["ps2", "ps3", "ex3"], ["tmpf0"])
            TT("dve", t1[:], t1[:], PS[0][:], ALU.add, ["tmpf0", "ps0", "ps1"], ["tmpf0"])
            TT("pool", t2[:].rearrange("p (h q) -> p h q", h=16), xsk,
               cst[:, C_DSK:C_DSK + 16].unsqueeze(2).to_broadcast([128, 16, 64]), ALU.mult, ["xs_t%d" % k, "cst"], ["tmpf1"])
            TT("dve", t1[:], t1[:], t2[:], ALU.add, ["tmpf0", "tmpf1"], ["tmpf0"])
            TT("dve", t1[:], t1[:], zs[:, k * 1024:(k + 1) * 1024], ALU.mult, ["tmpf0", "zs%d" % k], ["tmpf0"])
            MSET("dve", sm[:, SM_SS:SM_SS + 2], 0.0, ["ssq0", "ssq1"])
            for g in range(2):
                ACT(t2[:, g * 512:(g + 1) * 512], t1[:, g * 512:(g + 1) * 512], AF.Square, ["tmpf0", "ssq%d" % g], ["tmpf1", "ssq%d" % g],
                    accum=sm[:, SM_SS + g:SM_SS + g + 1])
            TS("dve", sm[:, SM_RS:SM_RS + 2], sm[:, SM_SS:SM_SS + 2], 1.0 / 512.0, EPS, ALU.mult, ALU.add, ["ssq0", "ssq1"], ["rs0"])
            ACT(sm[:, SM_RS:SM_RS + 2], sm[:, SM_RS:SM_RS + 2], AF.Sqrt, ["rs0"], ["rs1"])
            P.add("dve", lambda e: e.reciprocal(sm[:, SM_RS:SM_RS + 2], sm[:, SM_RS:SM_RS + 2]), r=["rs1"], w=["rs"])
            t3 = tmpf[2]
            for g in range(2):
                TS("dve", t3[:, g * 512:(g + 1) * 512], t1[:, g * 512:(g + 1) * 512], sm[:, SM_RS + g:SM_RS + g + 1], None,
                   ALU.mult, None, ["tmpf0", "rs"], ["tmpf2"])
            for cc in range(8):
                P.add("pe", lambda e, o=PS[3][:, cc * 128:(cc + 1) * 128], ii=t3[:, cc * 128:(cc + 1) * 128]: e.transpose(o, ii, cI),
                      r=["tmpf2", "cst"], w=["ps%d" % (6 + cc // 4)])
            for cc in range(8):
                ACT(ystage[:, cc * 512 + k * 128:cc * 512 + (k + 1) * 128],
                    PS[3][:, cc * 128:(cc + 1) * 128], AF.Copy, ["ps%d" % (6 + cc // 4), "cst"], ["ystage"], scale=normw[:, cc:cc + 1])
            if k < 3:
                for g in range(2):
                    MM(bank(6 + g), B_t[:, k * 256 + g * 128:k * 256 + (g + 1) * 128], xdtd[:, g * 512:(g + 1) * 512], True, True,
                       ["B_t%d" % k, "xdtd0"], ["ps%d" % (6 + g)])
                TT("dve", hcur[:].rearrange("p (h q) -> p h q", h=16), hcur[:].rearrange("p (h q) -> p h q", h=16),
                   ex3[:, 32:48].unsqueeze(2).to_broadcast([128, 16, 64]), ALU.mult, ["hcur", "ex3"], ["hcur"])
                TT("dve", hcur[:], hcur[:], PS[3][:], ALU.add, ["hcur", "ps6", "ps7"], ["hcur"])
        DMA(ysd_d[s], ystage[:], ["ystage"], ["ysd%d" % s], "yst", eng="sp")
        pending.append("ysd%d" % s)

    run_stage(P, st)
    if stop == "O1":
        es.close()
        return nc

    st = ExitStack()
    P = Prog("a")
    if stop == "O2":
        P.limit = limit
    WB = sl("WB2", [128, 16384], BF16)
    wst = [sl("wst%d_2" % i, [128, 2048], F32) for i in range(2)]
    xst = [sl("xst0_2", [128, 4 * 515], F32)] * 2
    xb = [sl("xb0_2", [128, 8 * 515], BF16)] * 2
    rt = [sl("rt0_2", [128, 1024])] * 2
    kc = [sl("kc%d_2" % i, [128, 512], BF16) for i in range(2)]
    tmpf = [sl("tmpf%d_2" % i, [128, 2560 if i == 3 else 512]) for i in range(4)]
    kTall = [sl("qT_2", [128, 8 * 512], BF16), sl("gbT_2", [128, 8 * 512], BF16)]
    vall = [sl("vall%d_2" % i, [128, 4 * 1024], BF16) for i in range(2)]
    pT = [sl("pT%d" % i, [128, 1024], BF16) for i in range(8)]
    sAB = [sl("sAB%d" % i, [128, 1024], BF16) for i in range(3)]
    sC2 = [sl("sC2_%d" % i, [128, 1024], BF16) for i in range(2)]
    m01 = sl("m01", [128, 16 * 512], BF16)
    b01 = sl("b01", [8, 2048], BF16)
    osqb = sl("osqb", [128, 512], BF16)
    mskb = sl("mskb", [8, 2560], BF16)
    ostage = sl("ostage", [128, 4096], BF16)

    OFF_Q, OFF_GB = 0, 8192
    load_weights(w_q, 1024, OFF_Q, "wk")
    load_weights(w_gb, 1024, OFF_GB, "wvv")
    wq_v, wgb_v = wv(OFF_Q, 1024), wv(OFF_GB, 1024)
    mA = mskb[0:8, 0:512]
    ldc = [0]
    for s in range(4):
        i = 0
        xv = load_x(xT_own[:, :, s, :], 515, i, eng="mix")
        xk = "xb%d" % i
        rtab = rt[i]
        DMA(rtab[:].rearrange("p (a c) -> p a c", a=2), ropeO[:, :, s, :], (), ["rt0"], "rt0")
        DMA(tmpf[3][0:8, 0:512], msk_d[:, 0:512], (), ["tmpf3"], "mk0")
        DMA(tmpf[3][0:8, 512:2560], msk_d[:, 512 + s * 2048:512 + (s + 1) * 2048], (), ["tmpf3"], "mk1")
        CP("dve", mskb[:], tmpf[3][0:8, 0:2560], ["tmpf3"], ["mskb"])
        qT = kTall[0]
        gbT = kTall[1]
        for h in range(8):
            pk = h % 2
            for dc in range(8):
                MM(bank(pk), wq_v[:, dc, h * 128:(h + 1) * 128], xv[:, dc, 3:515], dc == 0, dc == 7, ["wk", xk], ["ps%d" % pk])
            kcb = kc[h % 2]
            kk = "kc%d" % (h % 2)
            P.add("act", lambda e, o=kcb[:], ii=bank(pk): e.copy(o, ii), r=["ps%d" % pk], w=[kk])
            MM(bank(2 + pk), bPERM, kcb[:], True, True, ["cbf", kk], ["ps%d" % (2 + pk)])
            t = tmpf[h % 2][:, 0:512]
            TT("dve", t, bank(2 + pk), rtab[:, 512:1024], ALU.mult, ["ps%d" % (2 + pk), "rt0"], ["tmpf%d" % (h % 2)])
            TT("dve", tmpf[2 + h % 2][:, 0:512], kcb[:], rtab[:, 0:512], ALU.mult, [kk, "rt0"], ["tmpf%d" % (2 + h % 2)])
            TT("dve", qT[:, h * 512:(h + 1) * 512], tmpf[2 + h % 2][:, 0:512], t, ALU.add,
               ["tmpf%d" % (h % 2), "tmpf%d" % (2 + h % 2)], ["kTall0"])
            pb = 4 + h % 2
            for dc in range(8):
                MM(bank(pb), wgb_v[:, dc, h * 128:(h + 1) * 128], xv[:, dc, 3:515], dc == 0, dc == 7, ["wvv", xk], ["ps%d" % pb])
            ACT(gbT[:, h * 512:(h + 1) * 512], bank(pb), AF.Silu, ["ps%d" % pb], ["kTall1"])
        TS("dve", b01[:], mskb[0:8, 512:2560], 0.0, None, ALU.is_equal, None, ["mskb"], ["b01"])
        for pair in range(4):
            for kb in range(4):
                idx = pair * 4 + kb
                pbm = idx % 4
                MM(bank(pbm), mA[:, kb * 128:(kb + 1) * 128], b01[0:8, pair * 512:(pair + 1) * 512], True, True, ["mskb", "b01"], ["ps%d" % pbm])
                evac(m01[:, idx * 512:(idx + 1) * 512], bank(pbm), ["ps%d" % pbm], ["m01"])
        L = KLEN[s]
        fin_q = []
        for h in range(8):
            units = []
            ng = L // 4
            jbase = ldc[0]
            ldc[0] += ng
            gbuf = {}
            for kg in range(ng):
                j = (jbase + kg) % 2
                kTg = vall[j][:, 0:2048]
                vg = vall[j][:, 2048:4096].rearrange("p (kb e) -> p kb e", kb=16)
                gbuf[kg] = (j, kTg, vg)
                for ktl in range(4):
                    for kb in range(4):
                        units.append((j, kTg, vg, ktl, kg * 4 + ktl, kb))

            def issue(kg):
                j, kTg, vg = gbuf[kg]
                DMA(kTg, ksc[h, :, kg * 2048:(kg + 1) * 2048], ["ksc%d" % t_ for t_ in range(kg * 4, kg * 4 + 4)], ["kTg%d" % j], "kld%d" % j)
                DMA(vg, vsc[kg * 2048:(kg + 1) * 2048, h * 128:(h + 1) * 128].rearrange("(kb p) e -> p kb e", p=128),
                    ["vsc%d" % t_ for t_ in range(kg * 4, kg * 4 + 4)], ["vg%d" % j], "vld%d" % j)

            issue(0)
            if ng > 1:
                issue(1)
            nu = len(units)

            def qk(u):
                j, kTg, vg, ktl, kt, kb = units[u]
                for m in range(2):
                    pb = (u % 2) * 2 + m
                    MM(bank(pb), kTg[m * 64:(m + 1) * 64, ktl * 512 + kb * 128:ktl * 512 + (kb + 1) * 128],
                       qT[m * 64:(m + 1) * 64, h * 512:(h + 1) * 512], True, True, ["kTg%d" % j, "kTall0"], ["ps%d" % pb])

            def ex(u):
                j, kTg, vg, ktl, kt, kb = units[u]
                pk_ = "pT%d" % (u % 8)
                ACT(pT[u % 8][:], PS[u % 2][:], AF.Exp, ["ps%d" % ((u % 2) * 2), "ps%d" % ((u % 2) * 2 + 1)], [pk_], scale=0.125)
                if kt >= L - 4:
                    idx = (kt - (L - 4)) * 4 + kb
                    pv3 = pT[u % 8][:].rearrange("p (m q) -> p m q", m=2)
                    TT("dve", pv3, pv3, m01[:, idx * 512:(idx + 1) * 512].unsqueeze(1).to_broadcast([128, 2, 512]), ALU.mult,
                       [pk_, "m01"], [pk_])

            def pv(u):
                j, kTg, vg, ktl, kt, kb = units[u]
                for m in range(2):
                    MM(bank(4 + m), vg[:, ktl * 4 + kb, :], pT[u % 8][:, m * 512:(m + 1) * 512], u == 0, u == nu - 1,
                       ["vg%d" % j, "pT%d" % (u % 8)], ["ps%d" % (4 + m)])

            def adds(g):
                u0 = 4 * g
                TT("dve", sAB[0][:], pT[u0 % 8][:], pT[(u0 + 1) % 8][:], ALU.add, ["pT%d" % (u0 % 8), "pT%d" % ((u0 + 1) % 8)], ["sAB0"])
                TT("dve", sAB[1][:], pT[(u0 + 2) % 8][:], pT[(u0 + 3) % 8][:], ALU.add, ["pT%d" % ((u0 + 2) % 8), "pT%d" % ((u0 + 3) % 8)], ["sAB1"])
                sc = sC2[g % 2]
                TT("dve", sc[:], sAB[0][:], sAB[1][:], ALU.add, ["sAB0", "sAB1"], ["sC%d" % (g % 2)])

            def sums(g):
                sc = sC2[g % 2]
                for m in range(2):
                    MM(bank(6 + m), bONE, sc[:, m * 512:(m + 1) * 512], g == 0, g == nu // 4 - 1, ["cbf", "sC%d" % (g % 2)], ["ps%d" % (6 + m)])

            qk(0)
            qk(1)
            for u in range(nu):
                if u % 16 == 1 and u // 16 >= 1 and u // 16 + 1 < ng:
                    issue(u // 16 + 1)
                ex(u)
                if u + 2 < nu:
                    qk(u + 2)
                if u >= 1:
                    pv(u - 1)
                if u % 4 == 3:
                    adds(u // 4)
                if u % 4 == 2 and u >= 6:
                    sums(u // 4 - 1)
                for _ in range(3):
                    if fin_q:
                        fin_q.pop(0)()
            pv(nu - 1)
            sums(nu // 4 - 1)
            ob0, ob1, r0 = tmpf[1][:, 0:512], tmpf[2][:, 0:512], tmpf[0][:, 0:512]
            fs0, fs1 = tmpf[3][:, 0:512], tmpf[3][:, 512:1024]
            while fin_q:
                fin_q.pop(0)()
            CP("dve", ob0, bank(4), ["ps4"], ["tmpf1"])
            P.add("act", lambda e: e.copy(ob1, bank(5)), r=["ps5"], w=["tmpf2"])
            CP("dve", fs0, bank(6), ["ps6"], ["tmpf3"])
            P.add("act", lambda e: e.copy(fs1, bank(7)), r=["ps7"], w=["tmpf3"])

            def mk_fin(h=h):
                q = []
                q.append(lambda: ACT(r0, fs0, AF.Ln, ["tmpf3"], ["tmpf0"]))
                q.append(lambda: ACT(r0, r0, AF.Exp, ["tmpf0"], ["tmpf0"], scale=-1.0))
                q.append(lambda: TT("dve", ob0, ob0, r0, ALU.mult, ["tmpf1", "tmpf0"], ["tmpf1"]))
                q.append(lambda: ACT(r0, fs1, AF.Ln, ["tmpf3"], ["tmpf0"]))
                q.append(lambda: ACT(r0, r0, AF.Exp, ["tmpf0"], ["tmpf0"], scale=-1.0))
                q.append(lambda: TT("dve", ob1, ob1, r0, ALU.mult, ["tmpf2", "tmpf0"], ["tmpf2"]))
                q.append(lambda: STT("dve", ob0, ob1, sm[:, SM_NLAM:SM_NLAM + 1], ob0, ALU.mult, ALU.add, ["tmpf1", "tmpf2", "nlam"], ["tmpf1"]))
                q.append(lambda: TT("dve", osqb[:], ob0, ob0, ALU.mult, ["tmpf1"], ["osqb"]))
                q.append(lambda: MM(bank(7), bONE, osqb[:], True, True, ["cbf", "osqb"], ["ps7"]))
                q.append(lambda: ACT(r0, bank(7), AF.Ln, ["ps7"], ["tmpf0"], scale=1.0 / 128.0, bias=EPS))
                q.append(lambda: ACT(r0, r0, AF.Exp, ["tmpf0"], ["tmpf0"], scale=-0.5))
                q.append(lambda: TT("dve", ob0, ob0, r0, ALU.mult, ["tmpf1", "tmpf0"], ["tmpf1"]))
                q.append(lambda: STT("dve", ostage[:, h * 512:(h + 1) * 512], ob0, sm[:, SM_SLN:SM_SLN + 1], gbT[:, h * 512:(h + 1) * 512],
                                     ALU.mult, ALU.mult, ["tmpf1", "sln", "kTall1"], ["ostage"]))
                return q
            fin_q.extend(mk_fin())
        while fin_q:
            fin_q.pop(0)()
        DMA(od_d[s], ostage[:], ["ostage"], ["od%d" % s], "ost", eng="sp")
        pending.append("od%d" % s)

    run_stage(P, st)
    if stop == "O2":
        es.close()
        return nc

    st = ExitStack()
    P = Prog("m")
    if stop == "O3":
        P.limit = limit
    WB = sl("WB3", [128, 28800], BF16)
    wst = [sl("wst%d_3" % i, [128, 2048], F32) for i in range(2)]
    xst = [sl("xst0_3", [128, 4 * 515], F32)] * 2
    xb = [sl("xb0_3", [128, 8 * 515], BF16)] * 2
    lnp = sl("lnpt", [128, 2048])
    yT = sl("yT", [128, 4096], BF16)
    oTs = sl("oTs", [128, 4096], BF16)
    merged = sl("merged", [128, 8 * 512], BF16)
    gt = [sl("gt%d" % i, [128, 512]) for i in range(2)]
    xtok = sl("xtok", [128, 4 * 1024])
    tmpf = [sl("tmpf%d_3" % i, [128, 1024]) for i in range(4)]
    DMA(lnp[:], lnp_d, (), ["lnp"], "c2")

    OFF_A, OFF_B, OFF_O, OFF_GM = 0, 8192, 16384, 24576
    load_weights(w_a, 1024, OFF_A, "wk")
    load_weights(w_b, 1024, OFF_B, "wvv")
    load_weights(w_o, 1024, OFF_O, "wx")
    wa_v, wb_v, wo_v = wv(OFF_A, 1024), wv(OFF_B, 1024), wv(OFF_O, 1024)
    gmi = [0]
    for s in range(4):
        i = 0
        xv = load_x(xT_own[:, :, s, :], 515, i, eng="mix")
        DMA(yT[:], ysd_d[s], (), ["yT"], "yld")
        DMA(oTs[:], od_d[s], (), ["oTs"], "old")
        xk = "xb%d" % i
        DMA(xtok[:].rearrange("p (k c) -> p k c", k=4), x_own[s].rearrange("(k p) c -> p k c", p=128), (), ["xtok"], "xtok")
        for db in range(8):
            j = gmi[0] % 2
            gmi[0] += 1
            goff = OFF_GM + j * 2048
            stv = wst[j][:, 0:2048].rearrange("p (a b c) -> p a b c", a=8, b=2)
            DMA(stv, w_gm[:, :, :].rearrange("p a (b c) -> p a b c", b=2)[:, :, :, db * 128:(db + 1) * 128], (), ["wst%d" % j], "wst%d" % j)
            CP("dve", WB[:, goff:goff + 2048].rearrange("p (a b c) -> p a b c", a=8, b=2), stv, ["wst%d" % j], ["wgm%d" % j])
            wg = WB[:, goff:goff + 2048].rearrange("p (a b c) -> p a b c", a=8, b=2)
            for br in range(2):
                for dc in range(8):
                    MM(bank(br), wg[:, dc, br, :], xv[:, dc, 3:515], dc == 0, dc == 7, ["wgm%d" % j, xk], ["ps%d" % br])
                ACT(gt[br][:], bank(br), AF.Sigmoid, ["ps%d" % br, "cst"], ["gt%d" % br], bias=cst[:, C_BG + br * 8 + db:C_BG + br * 8 + db + 1])
            for cc in range(8):
                MM(bank(2), wa_v[:, cc, db * 128:(db + 1) * 128], yT[:, cc * 512:(cc + 1) * 512], cc == 0, cc == 7,
                   ["wk", "yT"], ["ps2"])
            for cc in range(8):
                MM(bank(3), wb_v[:, cc, db * 128:(db + 1) * 128], oTs[:, cc * 512:(cc + 1) * 512], cc == 0, cc == 7,
                   ["wvv", "oTs"], ["ps3"])
            TT("dve", gt[0][:], gt[0][:], bank(2), ALU.mult, ["gt0", "ps2"], ["gt0"])
            TT("dve", gt[1][:], gt[1][:], bank(3), ALU.mult, ["gt1", "ps3"], ["gt1"])
            TT("dve", merged[:, db * 512:(db + 1) * 512], gt[0][:], gt[1][:], ALU.add, ["gt0", "gt1"], ["merged"])
        for k in range(4):
            vb = tmpf[k % 2]
            vk = "tmpf%d" % (k % 2)
            for half in range(2):
                pb = 4 + half
                for db in range(8):
                    MM(bank(pb), merged[:, db * 512 + k * 128:db * 512 + (k + 1) * 128], wo_v[:, db, half * 512:(half + 1) * 512], db == 0, db == 7,
                       ["merged", "wx"], ["ps%d" % pb])
            STT("dve", vb[:], xtok[:, k * 1024:(k + 1) * 1024], ALPHA, PS[2][:], ALU.mult, ALU.add, ["xtok", "ps4", "ps5"], [vk])
            c0 = SM_NM + (k % 2) * 4
            P.add("dve", lambda e, o=sm[:, c0:c0 + 1], ii=vb[:]: e.reduce_sum(o, ii, mybir.AxisListType.X), r=[vk], w=["ln_a%d" % (k % 2)])
            TS("dve", sm[:, c0:c0 + 1], sm[:, c0:c0 + 1], -1.0 / D, None, ALU.mult, None, ["ln_a%d" % (k % 2)], ["ln_b%d" % (k % 2)])
            sq = tmpf[2 + k % 2]
            MSET("dve", sm[:, c0 + 1:c0 + 2], 0.0, ["ln_c%d" % (k % 2)])
            ACT(sq[:], vb[:], AF.Square, [vk, "ln_b%d" % (k % 2), "ln_c%d" % (k % 2)], ["tmpf%d" % (2 + k % 2), "ln_c%d" % (k % 2)],
                bias=sm[:, c0:c0 + 1], accum=sm[:, c0 + 1:c0 + 2])
            TS("dve", sm[:, c0 + 2:c0 + 3], sm[:, c0 + 1:c0 + 2], 1.0 / D, EPS, ALU.mult, ALU.add, ["ln_c%d" % (k % 2)], ["ln_d%d" % (k % 2)])
            ACT(sm[:, c0 + 2:c0 + 3], sm[:, c0 + 2:c0 + 3], AF.Sqrt, ["ln_d%d" % (k % 2)], ["ln_d2%d" % (k % 2)])
            P.add("dve", lambda e, o=sm[:, c0 + 2:c0 + 3]: e.reciprocal(o, o), r=["ln_d2%d" % (k % 2)], w=["ln_e%d" % (k % 2)])
            TS("dve", vb[:], vb[:], sm[:, c0:c0 + 1], sm[:, c0 + 2:c0 + 3], ALU.add, ALU.mult, [vk, "ln_b%d" % (k % 2), "ln_e%d" % (k % 2)], [vk])
            TT("dve", vb[:], vb[:], lnp[:, 0:1024], ALU.mult, [vk, "lnp"], [vk])
            TT("pool", sq[:], vb[:], lnp[:, 1024:2048], ALU.add, [vk, "lnp"], ["tmpf%d" % (2 + k % 2)])
            DMA(y_out[s, k * 128:(k + 1) * 128, :], sq[:], ["tmpf%d" % (2 + k % 2)], ["yout%d_%d" % (s, k)], "yo%d" % (k % 2))
            pending.append("yout%d_%d" % (s, k))
    run_stage(P, st)
    es.close()
    return nc


def _prep_inputs(inputs):
    f = lambda a: np.ascontiguousarray(np.asarray(a, dtype=np.float32))
    x = f(inputs["x"])
    w_in = f(inputs["w_in"])[0]

    def wl(c0, c1):
        return np.ascontiguousarray(w_in[:, c0:c1].reshape(8, 128, c1 - c0).transpose(1, 0, 2))

    def wsq(w):
        return np.ascontiguousarray(f(w)[0].reshape(8, 128, 1024).transpose(1, 0, 2))

    common = {
        "w_z": wl(0, 1024), "w_x": wl(1024, 2560), "w_dt": wl(2560, 2576), "w_q": wl(2576, 3600),
        "w_k": wl(3600, 4624), "w_v": wl(4624, 5648), "w_gb": wl(5648, 6672), "w_gm": wl(6672, 8720),
        "w_a": wsq(inputs["w_a"]), "w_b": wsq(inputs["w_b"]), "w_o": wsq(inputs["w_o"]),
    }
    r = np.arange(128)
    ident = (r[:, None] == r[None, :]).astype(np.float32)
    LT = (r[:, None] <= r[None, :]).astype(np.float32)
    SU = (r[:, None] > r[None, :]).astype(np.float32)
    ones = np.ones((128, 128), np.float32)
    perm = np.zeros((128, 128), np.float32)
    for m in range(2):
        for d in range(8):
            perm[m * 64 + d + 8, m * 64 + d] = -1.0
            perm[m * 64 + d, m * 64 + d + 8] = 1.0
    bc = lambda v, n: np.broadcast_to(f(v).reshape(1, n), (128, n))
    conv_w = f(inputs["conv_w"])[0]
    cw = np.zeros((128, 48), np.float32)
    for blk in range(12):
        for w in range(4):
            cw[:, blk * 4 + w] = conv_w[w, blk * 128:(blk + 1) * 128]
    conv_b = f(inputs["conv_b"])[0]
    cb = conv_b.reshape(12, 128).T
    bg = f(inputs["b_gate"])[0].reshape(16, 128).T
    sln = f(inputs["subln_w"])[0].reshape(128, 1)
    nw = f(inputs["ssd_norm_w"])[0].reshape(8, 128).T
    lnp = np.concatenate([bc(inputs["ln_g"], 1024), bc(inputs["ln_b"], 1024)], axis=1)
    pos = np.arange(S, dtype=np.float32)
    inv_freq = (np.float32(500000.0) ** (-np.arange(0, 16, 2, dtype=np.float32) / np.float32(16))).astype(np.float32)
    ang = (pos[:, None] * inv_freq[None, :]).astype(np.float32)
    cos, sin = np.cos(ang).astype(np.float32), np.sin(ang).astype(np.float32)
    rope = np.zeros((128, 2, S), np.float32)
    rope[:, 0, :] = 1.0
    for m in range(2):
        for dd in range(16):
            rope[m * 64 + dd, 0, :] = cos[:, dd % 8]
            rope[m * 64 + dd, 1, :] = sin[:, dd % 8]
    maskA = np.zeros((8, 512), np.float32)
    for k in range(512):
        maskA[k // 64, k] = 1.0
    in_maps = []
    for c in range(8):
        b, j = c // 4, c % 4
        tiles = [j, 7 - j, 8 + j, 15 - j]
        xb = x[b]
        xT = np.ascontiguousarray(xb.T.reshape(8, 128, S).transpose(1, 0, 2))
        xo = np.zeros((128, 8, 4, 515), np.float32)
        xtok = np.zeros((4, 512, D), np.float32)
        ropeO = np.zeros((128, 2, 4, 512), np.float32)
        sel = np.zeros((128, 16), np.float32)
        maskB = np.zeros((8, 16, 512), np.float32)
        for s, t in enumerate(tiles):
            lo = t * 512
            if t > 0:
                xo[:, :, s, :] = xT[:, :, lo - 3:lo + 512]
            else:
                xo[:, :, s, 3:] = xT[:, :, 0:512]
            xtok[s] = xb[lo:lo + 512]
            ropeO[:, :, s, :] = rope[:, :, lo:lo + 512]
            sel[:, t] = 1.0
            L = KLEN[s]
            for pi in range(4):
                kt = L - 4 + pi
                if kt < t:
                    pass
                elif kt > t:
                    maskB[:, pi + 4 * s, :] = NEG
                else:
                    for rr in range(8):
                        q = np.arange(512)
                        maskB[rr, pi + 4 * s, :] = np.where(q // 64 >= rr, 0.0, NEG)
        cstv = np.concatenate([ident, LT, SU, ones, perm, bc(inputs["a_log"], 16), bc(inputs["dt_bias"], 16), bc(inputs["d_skip"], 16),
                               bc(inputs["lambda_q1"], 64), bc(inputs["lambda_k1"], 64), bc(inputs["lambda_q2"], 64), bc(inputs["lambda_k2"], 64),
                               cw, cb, bg, sln, sel, nw], axis=1).astype(np.float32)
        assert cstv.shape[1] == NCST
        m = dict(common)
        m.update({"xT_all": xT, "xT_own": xo, "x_own": xtok, "cst": np.ascontiguousarray(cstv),
                  "cbrow": conv_b.reshape(1, 1536).copy(), "lnp": np.ascontiguousarray(lnp),
                  "msk": np.ascontiguousarray(np.concatenate([maskA, maskB.reshape(8, 8192)], axis=1)),
                  "ropeA": rope, "ropeO": ropeO})
        in_maps.append(m)
    return in_maps


def kernel(**inputs):
    in_maps = _prep_inputs(inputs)
    nc = build_nc()
    res = run_bass_kernel_spmd(nc, in_maps, core_ids=list(range(8)))
    out = np.zeros((2, S, D), np.float32)
    for c in range(8):
        b, j = c // 4, c % 4
        tiles = [j, 7 - j, 8 + j, 15 - j]
        y = np.asarray(res.results[c]["y_out"], dtype=np.float32)
        for s, t in enumerate(tiles):
            out[b, t * 512:(t + 1) * 512] = y[s]
    return out
```

```python
import math
import sys
import os
from contextlib import ExitStack
import numpy as np
import concourse.bass as bass
import concourse.mybir as mybir
from concourse.bass_utils import run_bass_kernel_spmd

F32 = mybir.dt.float32
BF16 = mybir.dt.bfloat16
AF = mybir.ActivationFunctionType
ALU = mybir.AluOpType

D = 1024
S = 8192
NT = 16
KLEN = (4, 8, 12, 16)
EPS = 1e-5
ALPHA = 2.0 ** 0.25
LAMBDA_INIT = 0.2
NEG = -30000.0

C_ID, C_LT, C_SU, C_ONE, C_PERM = 0, 128, 256, 384, 512
C_ALOG, C_DTB, C_DSK = 640, 656, 672
C_LQ1, C_LK1, C_LQ2, C_LK2 = 688, 752, 816, 880
C_CW, C_CB, C_BG, C_SLN, C_SEL, C_NW = 944, 992, 1004, 1020, 1021, 1037
NCST = 1045


class _Op:
    __slots__ = ("eng", "fn", "deps", "dma", "idx", "need", "sem", "val", "src")


class Prog:
    def __init__(self, tag):
        self.tag = tag
        self.ops = []
        self.wr = {}
        self.rd = {}
        self.base = {}
        self.limit = None

    def add(self, eng, fn, r=(), w=(), dma=None):
        if self.limit is not None and len(self.ops) >= self.limit:
            return None
        o = _Op()
        o.eng, o.fn, o.dma, o.idx, o.need = eng, fn, dma, len(self.ops), False
        f = sys._getframe(1)
        o.src = (f.f_lineno, f.f_back.f_lineno if f.f_back else 0)
        deps = {}
        for k in r:
            for x in self.wr.get(k, ()):
                deps[x] = "raw"
        for k in w:
            if self.rd.get(k):
                self.base[k] = list(self.wr.get(k, ())) + list(self.rd[k])
                self.wr[k] = []
                self.rd[k] = []
            for x in self.base.get(k, ()):
                deps.setdefault(x, "war")
        for k in w:
            self.wr.setdefault(k, []).append(o.idx)
            self.rd.setdefault(k, [])
        for k in r:
            self.rd.setdefault(k, []).append(o.idx)
            self.wr.setdefault(k, [])
        deps.pop(o.idx, None)
        o.deps = deps
        self.ops.append(o)
        return o

    def finalize(self):
        cnt = {}
        for o in self.ops:
            if o.dma is not None:
                o.sem = self.tag + "d_" + o.dma
                cnt[o.sem] = cnt.get(o.sem, 0) + 16
                o.val = cnt[o.sem]
        for o in self.ops:
            best = {}
            for d, kind in o.deps.items():
                p = self.ops[d]
                if p.dma is None:
                    if p.eng == o.eng and o.eng == "pe" and o.dma is None:
                        continue
                    ch = "e_" + p.eng
                else:
                    ch = p.sem
                if ch not in best or best[ch] < d:
                    best[ch] = d
            o.deps = list(best.values())
            for d in o.deps:
                self.ops[d].need = True
        for o in self.ops:
            if o.dma is None and o.need:
                o.sem = self.tag + "e_" + o.eng
                cnt[o.sem] = cnt.get(o.sem, 0) + 1
                o.val = cnt[o.sem]
        return sorted(cnt.keys())

    def emit(self, eng, e, sems):
        waited = {}
        for o in self.ops:
            if o.eng != eng:
                continue
            for d in o.deps:
                p = self.ops[d]
                if waited.get(p.sem, 0) < p.val:
                    e.wait_ge(sems[p.sem], p.val)
                    waited[p.sem] = p.val
            ins = o.fn(e)
            if o.dma is not None:
                ins.then_inc(sems[o.sem], 16)
            elif o.need:
                ins.then_inc(sems[o.sem], 1)


def build_nc(stop=None, nta=NT, limit=None):
    nc = bass.Bass("TRN2", target_bir_lowering=False)
    dt_in = lambda n, shp, t=F32: nc.dram_tensor(n, shp, t, kind="ExternalInput").ap()
    xT_all = dt_in("xT_all", [128, 8, S])
    xT_own = dt_in("xT_own", [128, 8, 4, 515])
    x_own = dt_in("x_own", [4, 512, D])
    w_k = dt_in("w_k", [128, 8, 1024])
    w_v = dt_in("w_v", [128, 8, 1024])
    w_x = dt_in("w_x", [128, 8, 1536])
    w_dt = dt_in("w_dt", [128, 8, 16])
    w_q = dt_in("w_q", [128, 8, 1024])
    w_z = dt_in("w_z", [128, 8, 1024])
    w_gb = dt_in("w_gb", [128, 8, 1024])
    w_gm = dt_in("w_gm", [128, 8, 2048])
    w_a = dt_in("w_a", [128, 8, 1024])
    w_b = dt_in("w_b", [128, 8, 1024])
    w_o = dt_in("w_o", [128, 8, 1024])
    cst_d = dt_in("cst", [128, NCST])
    cbrow_d = dt_in("cbrow", [1, 1536])
    lnp_d = dt_in("lnp", [128, 2048])
    msk_d = dt_in("msk", [8, 512 + 16 * 512])
    ropeA = dt_in("ropeA", [128, 2, S])
    ropeO = dt_in("ropeO", [128, 2, 4, 512])
    y_out = nc.dram_tensor("y_out", [4, 512, D], F32, kind="ExternalOutput").ap()
    ksc = nc.dram_tensor("ksc", [8, 128, S], BF16).ap()
    vsc = nc.dram_tensor("vsc", [S, 1024], BF16).ap()

    hs_d = nc.dram_tensor("hs_d", [4, 128, 1024], F32).ap()
    ysd_d = nc.dram_tensor("ysd_d", [4, 128, 4096], BF16).ap()
    od_d = nc.dram_tensor("od_d", [4, 128, 4096], BF16).ap()

    es = ExitStack()
    st = ExitStack()
    P = Prog("s")
    pending = []
    sb = lambda n, shp, t=F32: es.enter_context(nc.sbuf_tensor(n, shp, t))
    sl = lambda n, shp, t=F32: st.enter_context(nc.sbuf_tensor(n, shp, t))
    cst = sb("cstt", [128, NCST])
    cbf = sb("cbf", [128, 640], BF16)
    diag = sb("diag", [128, 48 * 128], BF16)
    cbrow_b = sb("cbrowb", [1, 1536], BF16)
    onesrow_b = sb("onesrow", [1, 128], BF16)
    sm = sb("sm", [128, 64])
    PS = [es.enter_context(nc.psum_tensor("ps%d" % i, [128, 1024], F32)) for i in range(4)]

    def run_stage(P, st):
        if os.environ.get("KDBG"):
            print("STAGE", P.tag, "nops", len(P.ops), "last", P.ops[-1].eng, P.ops[-1].src)
        P.limit = None
        P.add("sp", lambda e: e.nop(), r=list(pending), w=["done"])
        del pending[:]
        names = P.finalize()
        sems = {n: st.enter_context(nc.semaphore(n)) for n in names}
        with nc.Block() as block:
            @block.sync
            def _(e):
                P.emit("sp", e, sems)

            @block.tensor
            def _(e):
                P.emit("pe", e, sems)

            @block.scalar
            def _(e):
                P.emit("act", e, sems)

            @block.vector
            def _(e):
                P.emit("dve", e, sems)

            @block.gpsimd
            def _(e):
                P.emit("pool", e, sems)
        st.close()
        nc.all_engine_barrier()

    WB = sl("WB", [128, 28800], BF16)
    wst = [sl("wst%d" % i, [128, 2048], F32) for i in range(2)]
    xst = [sl("xst0", [128, 4 * 515], F32)] * 2
    xb = [sl("xb%d" % i, [128, 8 * 515], BF16) for i in range(2)]
    uT = sl("uT", [128, 12 * 515], BF16)
    rt = [sl("rt0", [128, 1024])] * 2
    kc = [sl("kc%d" % i, [128, 512], BF16) for i in range(2)]
    tmpf = [sl("tmpf%d" % i, [128, 1536 if i == 3 else 512]) for i in range(4)]
    kTall = [sl("kTall0", [128, 8 * 512], BF16)] * 2
    vall = [sl("vall0", [128, 4 * 1024], BF16)] * 2
    xs_t = sl("xs_t", [128, 4 * 1024], BF16)
    B_t = sl("B_t", [128, 4 * 256], BF16)
    dtt = sl("dtt", [128, 64])
    adt = sl("adt", [128, 64])
    dk = sl("dk", [128, 64])
    wl = sl("wl", [128, 64])
    cdT = sl("cdT", [128, 16])
    xdtd = sl("xdtd", [128, 4 * 1024], BF16)
    hcur = sl("hcur", [128, 1024])
    hs = [sl("hs%d" % i, [128, 1024]) for i in range(4)]

    def bank(i):
        return PS[i // 2][:, (i % 2) * 512:(i % 2) * 512 + 512]

    cI = cst[:, C_ID:C_ID + 128]
    cLT = cst[:, C_LT:C_LT + 128]
    cSU = cst[:, C_SU:C_SU + 128]
    cONE = cst[:, C_ONE:C_ONE + 128]
    bI = cbf[:, 0:128]
    bONE = cbf[:, 384:512]
    bPERM = cbf[:, 512:640]
    SM_A, SM_NLAM, SM_T0, SM_T1, SM_SLN, SM_NM, SM_SS, SM_RS = 0, 16, 17, 18, 19, 20, 24, 28

    dma_rr = [0]

    def DMA(out, in_, r, w, key, eng="sp"):
        P.add(eng, lambda e, o=out, i=in_: e.dma_start(out=o, in_=i), r=r, w=w, dma=key)

    def MM(out, lhsT, rhs, start, stop, r, w):
        P.add("pe", lambda e, o=out, l=lhsT, rr=rhs, s=start, t=stop: e.matmul(o, lhsT=l, rhs=rr, start=s, stop=t),
              r=r, w=w)

    def ACT(out, in_, func, r, w, bias=None, scale=None, accum=None):
        def fn(e, o=out, i=in_, f=func, b=bias, sc=scale, a=accum):
            kw = {}
            if b is not None:
                kw["bias"] = b
            if sc is not None:
                kw["scale"] = sc
            if a is not None:
                kw["accum_out"] = a
            return e.activation(o, i, f, **kw)
        P.add("act", fn, r=r, w=w)

    def TT(eng, out, in0, in1, op, r, w):
        P.add(eng, lambda e, o=out, a=in0, b=in1, p=op: e.tensor_tensor(out=o, in0=a, in1=b, op=p), r=r, w=w)

    def TS(eng, out, in0, s1, s2, op0, op1, r, w):
        if op1 is None:
            P.add(eng, lambda e, o=out, a=in0, x=s1, p=op0: e.tensor_scalar(o, a, x, None, p), r=r, w=w)
        else:
            P.add(eng, lambda e, o=out, a=in0, x=s1, y=s2, p=op0, q=op1: e.tensor_scalar(o, a, x, y, p, q), r=r, w=w)

    def STT(eng, out, in0, sc, in1, op0, op1, r, w):
        P.add(eng, lambda e, o=out, a=in0, s=sc, b=in1, p=op0, q=op1: e.scalar_tensor_tensor(o, a, s, b, p, q), r=r, w=w)

    def CP(eng, out, in_, r, w):
        P.add(eng, lambda e, o=out, i=in_: e.tensor_copy(out=o, in_=i), r=r, w=w)

    def MSET(eng, ap, v, w):
        P.add(eng, lambda e, a=ap, x=v: e.memset(a, x), r=(), w=w)

    DMA(cst[:], cst_d, (), ["cst"], "c0")
    DMA(tmpf[3][0:1, 0:1536], cbrow_d, (), ["tmpf3"], "c1")
    CP("dve", cbf[:], cst[:, 0:640], ["cst"], ["cbf"])
    CP("dve", cbrow_b[:], tmpf[3][0:1, 0:1536], ["tmpf3"], ["cbrowb"])
    MSET("dve", onesrow_b[:], 1.0, ["onesrow"])
    MSET("dve", uT[:], 0.0, ["uT%d" % b for b in range(12)])
    MSET("dve", hcur[:], 0.0, ["hcur"])
    for i in range(4):
        MSET("dve", hs[i][:], 0.0, ["hs%d" % i])
    for blk in range(12):
        for w in range(4):
            TS("dve", diag[:, (blk * 4 + w) * 128:(blk * 4 + w + 1) * 128], cI,
               cst[:, C_CW + blk * 4 + w:C_CW + blk * 4 + w + 1], None, ALU.mult, None, ["cst"], ["diag"])
    ACT(sm[:, SM_A:SM_A + 16], cst[:, C_ALOG:C_ALOG + 16], AF.Exp, ["cst"], ["sm_a0"])
    TS("dve", sm[:, SM_A:SM_A + 16], sm[:, SM_A:SM_A + 16], -1.0, None, ALU.mult, None, ["sm_a0"], ["sm_a"])
    TT("dve", tmpf[0][:, 0:64], cst[:, C_LQ1:C_LQ1 + 64], cst[:, C_LK1:C_LK1 + 64], ALU.mult, ["cst"], ["lamt0"])
    TT("dve", tmpf[0][:, 64:128], cst[:, C_LQ2:C_LQ2 + 64], cst[:, C_LK2:C_LK2 + 64], ALU.mult, ["cst"], ["lamt1"])
    P.add("dve", lambda e: e.reduce_sum(sm[:, SM_T0:SM_T0 + 1], tmpf[0][:, 0:64], mybir.AxisListType.X), r=["lamt0"], w=["lam_s0"])
    P.add("dve", lambda e: e.reduce_sum(sm[:, SM_T1:SM_T1 + 1], tmpf[0][:, 64:128], mybir.AxisListType.X), r=["lamt1"], w=["lam_s1"])
    ACT(sm[:, SM_T0:SM_T0 + 2], sm[:, SM_T0:SM_T0 + 2], AF.Exp, ["lam_s0", "lam_s1"], ["lam_e"])
    TT("dve", sm[:, SM_NLAM:SM_NLAM + 1], sm[:, SM_T1:SM_T1 + 1], sm[:, SM_T0:SM_T0 + 1], ALU.subtract, ["lam_e"], ["nlam0"])
    TS("dve", sm[:, SM_NLAM:SM_NLAM + 1], sm[:, SM_NLAM:SM_NLAM + 1], -LAMBDA_INIT, None, ALU.add, None, ["nlam0"], ["nlam"])
    TS("dve", sm[:, SM_SLN:SM_SLN + 1], cst[:, C_SLN:C_SLN + 1], 1.0 - LAMBDA_INIT, None, ALU.mult, None, ["cst"], ["sln"])

    wstate = {"i": 0}

    def load_weights(dram, ncols, wb_off, key, scale_rows=None):
        c0 = 0
        while c0 < ncols:
            cw = min(256, ncols - c0)
            i = wstate["i"] % 2
            wstate["i"] += 1
            stv = wst[i][:, 0:8 * cw].rearrange("p (a c) -> p a c", a=8)
            DMA(stv, dram[:, :, c0:c0 + cw], (), ["wst%d" % i], "wst%d" % i)
            dst = WB[:, wb_off:wb_off + 8 * ncols].rearrange("p (a c) -> p a c", a=8)[:, :, c0:c0 + cw]
            if wstate["i"] % 2:
                CP("dve", dst, stv, ["wst%d" % i], [key])
            else:
                P.add("act", lambda e, o=dst, ii=stv: e.copy(o, ii), r=["wst%d" % i], w=[key])
            c0 += cw

    def wv(wb_off, ncols):
        return WB[:, wb_off:wb_off + 8 * ncols].rearrange("p (a c) -> p a c", a=8)

    def load_x(src, ncol, i, eng="pool"):
        xv = xb[i][:, 0:8 * ncol].rearrange("p (a c) -> p a c", a=8)
        for hf in range(2):
            stv = xst[0][:, 0:4 * ncol].rearrange("p (a c) -> p a c", a=4)
            DMA(stv, src[:, hf * 4:(hf + 1) * 4, :], (), ["xst0"], "xst0")
            if eng == "mix":
                if hf == 0:
                    CP("dve", xv[:, hf * 4:(hf + 1) * 4, :], stv, ["xst0"], ["xb%d" % i])
                else:
                    P.add("act", lambda e, o=xv[:, hf * 4:(hf + 1) * 4, :], ii=stv: e.copy(o, ii), r=["xst0"], w=["xb%d" % i])
            else:
                CP(eng, xv[:, hf * 4:(hf + 1) * 4, :], stv, ["xst0"], ["xb%d" % i])
        return xv

    ev = {"i": 0}

    def evac(out, in_, r, w):
        ev["i"] += 1
        if ev["i"] % 2:
            P.add("act", lambda e, o=out, i=in_: e.copy(o, i), r=r, w=w)
        else:
            CP("dve", out, in_, r, w)

    def softplus_dt(psap, k, rk):
        d = dtt[:, k * 16:(k + 1) * 16]
        TT("dve", d, psap, cst[:, C_DTB:C_DTB + 16], ALU.add, [rk, "cst"], ["dtt%d" % k])
        ACT(d, d, AF.Exp, ["dtt%d" % k], ["dtt%d" % k])
        ACT(d, d, AF.Ln, ["dtt%d" % k], ["dtt%d" % k], bias=1.0)
        TT("dve", adt[:, k * 16:(k + 1) * 16], d, sm[:, SM_A:SM_A + 16], ALU.mult, ["dtt%d" % k, "sm_a"], ["adt%d" % k])

    def conv_tok(k, blks, pb, xoff):
        for gi in range(0, len(blks), 4):
            grp = blks[gi:gi + 4]
            pk = "ps%d" % pb
            for j, blk in enumerate(grp):
                o = bank(pb)[:, j * 128:(j + 1) * 128]
                for w in range(4):
                    MM(o, uT[:, blk * 515 + k * 128 + w: blk * 515 + k * 128 + w + 128],
                       diag[:, (blk * 4 + w) * 128:(blk * 4 + w + 1) * 128], w == 0, False,
                       ["uT%d" % blk, "diag"], [pk])
                MM(o, onesrow_b[0:1, :], cbrow_b[0:1, blk * 128:(blk + 1) * 128], False, True,
                   ["onesrow", "cbrowb"], [pk])
            n = len(grp) * 128
            b0 = grp[0]
            if b0 < 8:
                ACT(xs_t[:, k * 1024 + b0 * 128:k * 1024 + b0 * 128 + n], bank(pb)[:, 0:n], AF.Silu, [pk], ["xs_t%d" % k])
            else:
                ACT(B_t[:, k * 256:(k + 1) * 256], bank(pb)[:, 0:n], AF.Silu, [pk], ["B_t%d" % k])
            pb = pb + 1 if pb % 2 == 0 else pb - 1
        return pb

    def proj_u(xv, blks, wx_off, col0, pbs, halo):
        wx = wv(wx_off, 1536)
        for bi, blk in enumerate(blks):
            pb = pbs[bi % len(pbs)]
            pk = "ps%d" % pb
            uk = "uT%d" % blk
            if halo == "carry":
                CP("dve", uT[:, blk * 515:blk * 515 + 3], uT[:, blk * 515 + 512:blk * 515 + 515], [uk], [uk])
            else:
                for dc in range(8):
                    MM(bank(pb)[:, 0:3], wx[:, dc, blk * 128:(blk + 1) * 128], xv[:, dc, 0:3], dc == 0, dc == 7,
                       ["wx", halo], [pk])
                evac(uT[:, blk * 515:blk * 515 + 3], bank(pb)[:, 0:3], [pk], [uk])
            for dc in range(8):
                MM(bank(pb), wx[:, dc, blk * 128:(blk + 1) * 128], xv[:, dc, col0:col0 + 512], dc == 0, dc == 7,
                   ["wx", halo if halo != "carry" else "xbcur"], [pk])
            evac(uT[:, blk * 515 + 3:blk * 515 + 515], bank(pb), [pk], [uk])

    OFF_K, OFF_V, OFF_X, OFF_DT = 0, 8192, 16384, 16384 + 12288
    load_weights(w_k, 1024, OFF_K, "wk")
    load_weights(w_v, 1024, OFF_V, "wvv")
    load_weights(w_x, 1536, OFF_X, "wx")
    load_weights(w_dt, 16, OFF_DT, "wdt")
    wk_v, wv_v, wdt_v = wv(OFF_K, 1024), wv(OFF_V, 1024), wv(OFF_DT, 16)

    def rope_block(h, pb, srcw, srck, xv, col0, rtab, dst, dstk, xk):
        pk = "ps%d" % pb
        for dc in range(8):
            MM(bank(pb), srcw[:, dc, h * 128:(h + 1) * 128], xv[:, dc, col0:col0 + 512], dc == 0, dc == 7, [srck, xk], [pk])
        kcb = kc[h % 2]
        kk = "kc%d" % (h % 2)
        P.add("act", lambda e, o=kcb[:], i=bank(pb): e.copy(o, i), r=[pk], w=[kk])
        pb2 = pb + 2
        pk2 = "ps%d" % pb2
        MM(bank(pb2), bPERM, kcb[:], True, True, ["cbf", kk], [pk2])
        t = tmpf[h % 2][:, 0:512]
        tk = "tmpf%d" % (h % 2)
        TT("dve", t, bank(pb2), rtab[:, 512:1024], ALU.mult, [pk2, "rt"], [tk])
        TT("dve", tmpf[2 + h % 2][:, 0:512], kcb[:], rtab[:, 0:512], ALU.mult, [kk, "rt"], ["tmpf%d" % (2 + h % 2)])
        TT("dve", dst, tmpf[2 + h % 2][:, 0:512], t, ALU.add, [tk, "tmpf%d" % (2 + h % 2)], [dstk])

    xv_next = load_x(xT_all[:, :, 0:512], 512, 0)
    for T in range(nta):
        i = T % 2
        xv = xv_next
        xk = "xb%d" % i
        if T + 1 < nta:
            xv_next = load_x(xT_all[:, :, (T + 1) * 512:(T + 2) * 512], 512, 1 - i)
        rtab = rt[i]
        DMA(rtab[:].rearrange("p (a c) -> p a c", a=2), ropeA[:, :, T * 512:(T + 1) * 512], (), ["rt0"], "rt0")
        def kproj(h):
            pk = h % 2
            for dc in range(8):
                MM(bank(pk), wk_v[:, dc, h * 128:(h + 1) * 128], xv[:, dc, :], dc == 0, dc == 7, ["wk", xk], ["ps%d" % pk])
        kproj(0)
        for h in range(8):
            pk = h % 2
            kcb = kc[h % 2]
            kk = "kc%d" % (h % 2)
            P.add("act", lambda e, o=kcb[:], ii=bank(pk): e.copy(o, ii), r=["ps%d" % pk], w=[kk])
            if h + 1 < 8:
                kproj(h + 1)
            MM(bank(2 + pk), bPERM, kcb[:], True, True, ["cbf", kk], ["ps%d" % (2 + pk)])
            t = tmpf[h % 2][:, 0:512]
            TT("dve", t, bank(2 + pk), rtab[:, 512:1024], ALU.mult, ["ps%d" % (2 + pk), "rt0"], ["tmpf%d" % (h % 2)])
            TT("dve", tmpf[2 + h % 2][:, 0:512], kcb[:], rtab[:, 0:512], ALU.mult, [kk, "rt0"], ["tmpf%d" % (2 + h % 2)])
            TT("dve", kTall[i][:, h * 512:(h + 1) * 512], tmpf[2 + h % 2][:, 0:512], t, ALU.add,
               ["tmpf%d" % (h % 2), "tmpf%d" % (2 + h % 2)], ["kTall0"])
        DMA(ksc[:, :, T * 512:(T + 1) * 512].rearrange("h r t -> r h t"),
            kTall[i][:].rearrange("p (h t) -> p h t", h=8), ["kTall0"], ["ksc%d" % T], "kst0", eng="pool")
        pending.append("ksc%d" % T)
        for k in range(4):
            for half in range(2):
                pb = 4 + (k * 2 + half) % 2
                for dc in range(8):
                    MM(bank(pb), xv[:, dc, k * 128:(k + 1) * 128], wv_v[:, dc, half * 512:(half + 1) * 512], dc == 0, dc == 7,
                       ["wvv", xk], ["ps%d" % pb])
                evac(vall[i][:, k * 1024 + half * 512:k * 1024 + half * 512 + 512], bank(pb), ["ps%d" % pb], ["vall0"])
        DMA(vsc[T * 512:(T + 1) * 512, :].rearrange("(k p) c -> p k c", p=128),
            vall[i][:].rearrange("p (k c) -> p k c", k=4), ["vall0"], ["vsc%d" % T], "vst0", eng="pool")
        pending.append("vsc%d" % T)
        wx = wv(OFF_X, 1536)
        for bi, blk in enumerate(range(10)):
            pb = 6 + bi % 2
            uk = "uT%d" % blk
            CP("dve", uT[:, blk * 515:blk * 515 + 3], uT[:, blk * 515 + 512:blk * 515 + 515], [uk], [uk])
            for dc in range(8):
                MM(bank(pb), wx[:, dc, blk * 128:(blk + 1) * 128], xv[:, dc, :], dc == 0, dc == 7, ["wx", xk], ["ps%d" % pb])
            evac(uT[:, blk * 515 + 3:blk * 515 + 515], bank(pb), ["ps%d" % pb], [uk])
        for k in range(4):
            o = bank(4)[:, k * 16:(k + 1) * 16]
            for dc in range(8):
                MM(o, xv[:, dc, k * 128:(k + 1) * 128], wdt_v[:, dc, :], dc == 0, dc == 7, ["wdt", xk], ["ps4"])
        for k in range(4):
            softplus_dt(bank(4)[:, k * 16:(k + 1) * 16], k, "ps4")
        pb = 0
        for k in range(4):
            pb = conv_tok(k, list(range(8)), pb, 0)
            pb = conv_tok(k, [8, 9], pb, 0)
        for k in range(4):
            o = bank(5)[:, k * 16:(k + 1) * 16]
            MM(o, cSU, adt[:, k * 16:(k + 1) * 16], True, k == 3, ["cst", "adt%d" % k], ["ps5"])
            for k2 in range(k + 1, 4):
                MM(o, cONE, adt[:, k2 * 16:(k2 + 1) * 16], False, k2 == 3, ["cst", "adt%d" % k2], ["ps5"])
        o = bank(5)[:, 64:80]
        for k in range(4):
            MM(o, cONE, adt[:, k * 16:(k + 1) * 16], k == 0, k == 3, ["cst", "adt%d" % k], ["ps5"])
        ACT(dk[:, 0:64], bank(5)[:, 0:64], AF.Exp, ["ps5"], ["dk"])
        ACT(cdT[:], bank(5)[:, 64:80], AF.Exp, ["ps5"], ["cdT"])
        TT("dve", wl[:], dtt[:], dk[:], ALU.mult, ["dk"] + ["dtt%d" % k for k in range(4)], ["wl"])
        for k in range(4):
            TT("dve", xdtd[:, k * 1024:(k + 1) * 1024].rearrange("p (h q) -> p h q", h=16),
               xs_t[:, k * 1024:(k + 1) * 1024].rearrange("p (h q) -> p h q", h=16),
               wl[:, k * 16:(k + 1) * 16].unsqueeze(2).to_broadcast([128, 16, 64]), ALU.mult,
               ["xs_t%d" % k, "wl"], ["xdtd%d" % k])
        for g in range(2):
            for k in range(4):
                MM(bank(6 + g), B_t[:, k * 256 + g * 128:k * 256 + (g + 1) * 128], xdtd[:, k * 1024 + g * 512:k * 1024 + (g + 1) * 512],
                   k == 0, k == 3, ["B_t%d" % k, "xdtd%d" % k], ["ps%d" % (6 + g)])
        STT("dve", hs[T // 4][:], hcur[:], cst[:, C_SEL + T:C_SEL + T + 1], hs[T // 4][:], ALU.mult, ALU.add,
            ["hcur", "cst", "hs%d" % (T // 4)], ["hs%d" % (T // 4)])
        TT("dve", hcur[:].rearrange("p (h q) -> p h q", h=16), hcur[:].rearrange("p (h q) -> p h q", h=16),
           cdT[:].unsqueeze(2).to_broadcast([128, 16, 64]), ALU.mult, ["hcur", "cdT"], ["hcur"])
        TT("dve", hcur[:], hcur[:], PS[3][:], ALU.add, ["hcur", "ps6", "ps7"], ["hcur"])

    for i in range(4):
        DMA(hs_d[i], hs[i][:], ["hs%d" % i], ["hsd%d" % i], "hst", eng="sp")
        pending.append("hsd%d" % i)
    run_stage(P, st)
    if stop == "A":
        es.close()
        return nc

    st = ExitStack()
    P = Prog("o")
    if stop == "O1":
        P.limit = limit
    WB = sl("WB1", [128, 28800], BF16)
    wst = [sl("wst%d_1" % i, [128, 2048], F32) for i in range(2)]
    xst = [sl("xst0_1", [128, 4 * 515], F32)] * 2
    xb = [sl("xb0_1", [128, 8 * 515], BF16)] * 2
    uT = sl("uT_1", [128, 12 * 515], BF16)
    tmpf = [sl("tmpf%d_1" % i, [128, 1024]) for i in range(3)]
    xs_t = sl("xs_t_1", [128, 4 * 1024], BF16)
    B_t = sl("B_t_1", [128, 4 * 256], BF16)
    zs = sl("zs", [128, 4 * 1024], BF16)
    BCT = sl("BCT", [128, 4 * 512], BF16)
    dtt = sl("dtt_1", [128, 64])
    adt = sl("adt_1", [128, 64])
    ex3 = sl("ex3", [128, 48])
    xdtd = sl("xdtd_1", [128, 1024], BF16)
    xdt = sl("xdt", [128, 1024], BF16)
    hcur = sl("hcur_1", [128, 1024])
    prevb = sl("prevb", [128, 1024], BF16)
    Rm = sl("Rm", [128, 2048])
    Lex = sl("Lex", [128, 2048], BF16)
    cbm = sl("cbm", [128, 256])
    MT = sl("MT", [128, 2048], BF16)
    ystage = sl("ystage", [128, 4096], BF16)

    OFF_Z = 0
    xv0 = load_x(xT_own[:, :, 0, :], 515, 0, eng="mix")
    load_weights(w_z, 1024, OFF_Z, "wk")
    load_weights(w_x, 1536, OFF_X, "wx")
    load_weights(w_dt, 16, OFF_DT, "wdt")
    wdt_v = wv(OFF_DT, 16)
    wz_v = wv(OFF_Z, 1024)
    wx = wv(OFF_X, 1536)
    normw = cst[:, C_NW:C_NW + 8]
    for s in range(4):
        i = 0
        xv = xv0 if s == 0 else load_x(xT_own[:, :, s, :], 515, i, eng="mix")
        xk = "xb%d" % i
        for bi, blk in enumerate(range(12)):
            pb = bi % 2
            uk = "uT%d" % blk
            for dc in range(8):
                MM(bank(pb)[:, 0:3], wx[:, dc, blk * 128:(blk + 1) * 128], xv[:, dc, 0:3], dc == 0, dc == 7, ["wx", xk], ["ps%d" % pb])
            evac(uT[:, blk * 515:blk * 515 + 3], bank(pb)[:, 0:3], ["ps%d" % pb], [uk])
            for dc in range(8):
                MM(bank(pb), wx[:, dc, blk * 128:(blk + 1) * 128], xv[:, dc, 3:515], dc == 0, dc == 7, ["wx", xk], ["ps%d" % pb])
            evac(uT[:, blk * 515 + 3:blk * 515 + 515], bank(pb), ["ps%d" % pb], [uk])
        for bi, blk in enumerate((8, 9, 10, 11)):
            pb = 2 + bi % 2
            for w in range(4):
                MM(bank(pb), diag[:, (blk * 4 + w) * 128:(blk * 4 + w + 1) * 128], uT[:, blk * 515 + w:blk * 515 + w + 512],
                   w == 0, w == 3, ["uT%d" % blk, "diag"], ["ps%d" % pb])
            ACT(BCT[:, bi * 512:(bi + 1) * 512], bank(pb), AF.Silu, ["ps%d" % pb, "cst"], ["BCT%d" % bi],
                bias=cst[:, C_CB + blk:C_CB + blk + 1])
        for k in range(4):
            o = bank(4)[:, k * 16:(k + 1) * 16]
            for dc in range(8):
                MM(o, xv[:, dc, 3 + k * 128:3 + (k + 1) * 128], wdt_v[:, dc, :], dc == 0, dc == 7, ["wdt", xk], ["ps4"])
        for k in range(4):
            softplus_dt(bank(4)[:, k * 16:(k + 1) * 16], k, "ps4")
        pb = 6
        for k in range(4):
            pb = conv_tok(k, list(range(8)), pb, 0)
            pb = conv_tok(k, [8, 9], pb, 0)
        for k in range(4):
            for half in range(2):
                pb = (k * 2 + half) % 2
                for dc in range(8):
                    MM(bank(pb), xv[:, dc, 3 + k * 128:3 + (k + 1) * 128], wz_v[:, dc, half * 512:(half + 1) * 512], dc == 0, dc == 7,
                       ["wk", xk], ["ps%d" % pb])
                ACT(zs[:, k * 1024 + half * 512:k * 1024 + half * 512 + 512], bank(pb), AF.Silu, ["ps%d" % pb], ["zs%d" % k])
        DMA(hcur[:], hs_d[s], (), ["hcur"], "hld")
        for k in range(4):
            CP("pool", prevb[:], hcur[:], ["hcur"], ["prevb"])
            a_k = adt[:, k * 16:(k + 1) * 16]
            ak = "adt%d" % k
            TT("dve", Rm[:].rearrange("p (h l) -> p h l", h=16), a_k.unsqueeze(2).to_broadcast([128, 16, 128]),
               cLT.unsqueeze(1).to_broadcast([128, 16, 128]), ALU.mult, [ak, "cst"], ["Rm"])
            for q in range(4):
                MM(bank(q), cSU, Rm[:, q * 512:(q + 1) * 512], True, True, ["cst", "Rm"], ["ps%d" % q])
            ACT(Lex[:, 0:1024], PS[0][:], AF.Exp, ["ps0", "ps1"], ["Lex0"])
            ACT(Lex[:, 1024:2048], PS[1][:], AF.Exp, ["ps2", "ps3"], ["Lex1"])
            for g in range(2):
                MM(bank(4)[:, g * 128:(g + 1) * 128], BCT[:, g * 512 + k * 128:g * 512 + (k + 1) * 128],
                   BCT[:, (2 + g) * 512 + k * 128:(2 + g) * 512 + (k + 1) * 128], True, True, ["BCT%d" % g, "BCT%d" % (2 + g)], ["ps4"])
            TT("dve", cbm[:].rearrange("p (g l) -> p g l", g=2), bank(4)[:, 0:256].rearrange("p (g l) -> p g l", g=2),
               cLT.unsqueeze(1).to_broadcast([128, 2, 128]), ALU.mult, ["ps4", "cst"], ["cbm"])
            TT("dve", MT[:].rearrange("p (g h l) -> p g h l", g=2, h=8), Lex[:].rearrange("p (g h l) -> p g h l", g=2, h=8),
               cbm[:].rearrange("p (g l) -> p g l", g=2).unsqueeze(2).to_broadcast([128, 2, 8, 128]), ALU.mult,
               ["Lex0", "Lex1", "cbm"], ["MT"])
            MM(bank(5)[:, 0:16], cLT, a_k, True, True, ["cst", ak], ["ps5"])
            MM(bank(5)[:, 16:32], cSU, a_k, True, True, ["cst", ak], ["ps5"])
            MM(bank(5)[:, 32:48], cONE, a_k, True, True, ["cst", ak], ["ps5"])
            ACT(ex3[:], bank(5)[:, 0:48], AF.Exp, ["ps5"], ["ex3"])
            xsk = xs_t[:, k * 1024:(k + 1) * 1024].rearrange("p (h q) -> p h q", h=16)
            TT("dve", xdt[:].rearrange("p (h q) -> p h q", h=16), xsk,
               dtt[:, k * 16:(k + 1) * 16].unsqueeze(2).to_broadcast([128, 16, 64]), ALU.mult, ["xs_t%d" % k, "dtt%d" % k], ["xdt"])
            TT("dve", xdtd[:, 0:1024].rearrange("p (h q) -> p h q", h=16), xdt[:].rearrange("p (h q) -> p h q", h=16),
               ex3[:, 16:32].unsqueeze(2).to_broadcast([128, 16, 64]), ALU.mult, ["xdt", "ex3"], ["xdtd0"])
            for h in range(16):
                MM(PS[0][:, h * 64:(h + 1) * 64], MT[:, h * 128:(h + 1) * 128], xdt[:, h * 64:(h + 1) * 64], True, True,
                   ["MT", "xdt"], ["ps%d" % (h // 8)])
            for g in range(2):
                MM(bank(2 + g), BCT[:, (2 + g) * 512 + k * 128:(2 + g) * 512 + (k + 1) * 128], prevb[:, g * 512:(g + 1) * 512], True, True,
                   ["BCT%d" % (2 + g), "prevb"], ["ps%d" % (2 + g)])
            t1, t2 = tmpf[0], tmpf[1]
            TT("dve", t1[:].rearrange("p (h q) -> p h q", h=16), PS[1][:].rearrange("p (h q) -> p h q", h=16),
               ex3[:, 0:16].unsqueeze(2).to_broadcast([128, 16, 64]), ALU.mult, ["ps2", "ps3", "ex3"], ["tmpf0"])
            TT("dve", t1[:], t1[:], PS[0][:], ALU.add, ["tmpf0", "ps0", "ps1"], ["tmpf0"])
            TT("pool", t2[:].rearrange("p (h q) -> p h q", h=16), xsk,
               cst[:, C_DSK:C_DSK + 16].unsqueeze(2).to_broadcast([128, 16, 64]), ALU.mult, ["xs_t%d" % k, "cst"], ["tmpf1"])
            TT("dve", t1[:], t1[:], t2[:], ALU.add, ["tmpf0", "tmpf1"], ["tmpf0"])
            TT("dve", t1[:], t1[:], zs[:, k * 1024:(k + 1) * 1024], ALU.mult, ["tmpf0", "zs%d" % k], ["tmpf0"])
            MSET("dve", sm[:, SM_SS:SM_SS + 2], 0.0, ["ssq0", "ssq1"])
            for g in range(2):
                ACT(t2[:, g * 512:(g + 1) * 512], t1[:, g * 512:(g + 1) * 512], AF.Square, ["tmpf0", "ssq%d" % g], ["tmpf1", "ssq%d" % g],
                    accum=sm[:, SM_SS + g:SM_SS + g + 1])
            TS("dve", sm[:, SM_RS:SM_RS + 2], sm[:, SM_SS:SM_SS + 2], 1.0 / 512.0, EPS, ALU.mult, ALU.add, ["ssq0", "ssq1"], ["rs0"])
            ACT(sm[:, SM_RS:SM_RS + 2], sm[:, SM_RS:SM_RS + 2], AF.Sqrt, ["rs0"], ["rs1"])
            P.add("dve", lambda e: e.reciprocal(sm[:, SM_RS:SM_RS + 2], sm[:, SM_RS:SM_RS + 2]), r=["rs1"], w=["rs"])
            t3 = tmpf[2]
            for g in range(2):
                TS("dve", t3[:, g * 512:(g + 1) * 512], t1[:, g * 512:(g + 1) * 512], sm[:, SM_RS + g:SM_RS + g + 1], None,
                   ALU.mult, None, ["tmpf0", "rs"], ["tmpf2"])
            for cc in range(8):
                P.add("pe", lambda e, o=PS[3][:, cc * 128:(cc + 1) * 128], ii=t3[:, cc * 128:(cc + 1) * 128]: e.transpose(o, ii, cI),
                      r=["tmpf2", "cst"], w=["ps%d" % (6 + cc // 4)])
            for cc in range(8):
                ACT(ystage[:, cc * 512 + k * 128:cc * 512 + (k + 1) * 128],
                    PS[3][:, cc * 128:(cc + 1) * 128], AF.Copy, ["ps%d" % (6 + cc // 4), "cst"], ["ystage"], scale=normw[:, cc:cc + 1])
            if k < 3:
                for g in range(2):
                    MM(bank(6 + g), B_t[:, k * 256 + g * 128:k * 256 + (g + 1) * 128], xdtd[:, g * 512:(g + 1) * 512], True, True,
                       ["B_t%d" % k, "xdtd0"], ["ps%d" % (6 + g)])
                TT("dve", hcur[:].rearrange("p (h q) -> p h q", h=16), hcur[:].rearrange("p (h q) -> p h q", h=16),
                   ex3[:, 32:48].unsqueeze(2).to_broadcast([128, 16, 64]), ALU.mult, ["hcur", "ex3"], ["hcur"])
                TT("dve", hcur[:], hcur[:], PS[3][:], ALU.add, ["hcur", "ps6", "ps7"], ["hcur"])
        DMA(ysd_d[s], ystage[:], ["ystage"], ["ysd%d" % s], "yst", eng="sp")
        pending.append("ysd%d" % s)

    run_stage(P, st)
    if stop == "O1":
        es.close()
        return nc

    st = ExitStack()
    P = Prog("a")
    if stop == "O2":
        P.limit = limit
    WB = sl("WB2", [128, 16384], BF16)
    wst = [sl("wst%d_2" % i, [128, 2048], F32) for i in range(2)]
    xst = [sl("xst0_2", [128, 4 * 515], F32)] * 2
    xb = [sl("xb0_2", [128, 8 * 515], BF16)] * 2
    rt = [sl("rt0_2", [128, 1024])] * 2
    kc = [sl("kc%d_2" % i, [128, 512], BF16) for i in range(2)]
    tmpf = [sl("tmpf%d_2" % i, [128, 2560 if i == 3 else 512]) for i in range(4)]
    kTall = [sl("qT_2", [128, 8 * 512], BF16), sl("gbT_2", [128, 8 * 512], BF16)]
    vall = [sl("vall%d_2" % i, [128, 4 * 1024], BF16) for i in range(2)]
    pT = [sl("pT%d" % i, [128, 1024], BF16) for i in range(8)]
    sAB = [sl("sAB%d" % i, [128, 1024], BF16) for i in range(3)]
    sC2 = [sl("sC2_%d" % i, [128, 1024], BF16) for i in range(2)]
    m01 = sl("m01", [128, 16 * 512], BF16)
    b01 = sl("b01", [8, 2048], BF16)
    osqb = sl("osqb", [128, 512], BF16)
    mskb = sl("mskb", [8, 2560], BF16)
    ostage = sl("ostage", [128, 4096], BF16)

    OFF_Q, OFF_GB = 0, 8192
    xv0 = load_x(xT_own[:, :, 0, :], 515, 0, eng="mix")
    load_weights(w_q, 1024, OFF_Q, "wk")
    load_weights(w_gb, 1024, OFF_GB, "wvv")
    wq_v, wgb_v = wv(OFF_Q, 1024), wv(OFF_GB, 1024)
    mA = mskb[0:8, 0:512]
    ldc = [0]
    for s in range(4):
        i = 0
        xv = xv0 if s == 0 else load_x(xT_own[:, :, s, :], 515, i, eng="mix")
        xk = "xb%d" % i
        rtab = rt[i]
        DMA(rtab[:].rearrange("p (a c) -> p a c", a=2), ropeO[:, :, s, :], (), ["rt0"], "rt0")
        DMA(tmpf[3][0:8, 0:512], msk_d[:, 0:512], (), ["tmpf3"], "mk0")
        DMA(tmpf[3][0:8, 512:2560], msk_d[:, 512 + s * 2048:512 + (s + 1) * 2048], (), ["tmpf3"], "mk1")
        CP("dve", mskb[:], tmpf[3][0:8, 0:2560], ["tmpf3"], ["mskb"])
        qT = kTall[0]
        gbT = kTall[1]
        for h in range(8):
            pk = h % 2
            for dc in range(8):
                MM(bank(pk), wq_v[:, dc, h * 128:(h + 1) * 128], xv[:, dc, 3:515], dc == 0, dc == 7, ["wk", xk], ["ps%d" % pk])
            kcb = kc[h % 2]
            kk = "kc%d" % (h % 2)
            P.add("act", lambda e, o=kcb[:], ii=bank(pk): e.copy(o, ii), r=["ps%d" % pk], w=[kk])
            MM(bank(2 + pk), bPERM, kcb[:], True, True, ["cbf", kk], ["ps%d" % (2 + pk)])
            t = tmpf[h % 2][:, 0:512]
            TT("dve", t, bank(2 + pk), rtab[:, 512:1024], ALU.mult, ["ps%d" % (2 + pk), "rt0"], ["tmpf%d" % (h % 2)])
            TT("dve", tmpf[2 + h % 2][:, 0:512], kcb[:], rtab[:, 0:512], ALU.mult, [kk, "rt0"], ["tmpf%d" % (2 + h % 2)])
            TT("dve", qT[:, h * 512:(h + 1) * 512], tmpf[2 + h % 2][:, 0:512], t, ALU.add,
               ["tmpf%d" % (h % 2), "tmpf%d" % (2 + h % 2)], ["kTall0"])
            pb = 4 + h % 2
            for dc in range(8):
                MM(bank(pb), wgb_v[:, dc, h * 128:(h + 1) * 128], xv[:, dc, 3:515], dc == 0, dc == 7, ["wvv", xk], ["ps%d" % pb])
            ACT(gbT[:, h * 512:(h + 1) * 512], bank(pb), AF.Silu, ["ps%d" % pb], ["kTall1"])
        TS("dve", b01[:], mskb[0:8, 512:2560], 0.0, None, ALU.is_equal, None, ["mskb"], ["b01"])
        for pair in range(4):
            for kb in range(4):
                idx = pair * 4 + kb
                pbm = idx % 4
                MM(bank(pbm), mA[:, kb * 128:(kb + 1) * 128], b01[0:8, pair * 512:(pair + 1) * 512], True, True, ["mskb", "b01"], ["ps%d" % pbm])
                evac(m01[:, idx * 512:(idx + 1) * 512], bank(pbm), ["ps%d" % pbm], ["m01"])
        L = KLEN[s]
        fin_q = []
        for h in range(8):
            units = []
            ng = L // 4
            jbase = ldc[0]
            ldc[0] += ng
            gbuf = {}
            for kg in range(ng):
                j = (jbase + kg) % 2
                kTg = vall[j][:, 0:2048]
                vg = vall[j][:, 2048:4096].rearrange("p (kb e) -> p kb e", kb=16)
                gbuf[kg] = (j, kTg, vg)
                for ktl in range(4):
                    for kb in range(4):
                        units.append((j, kTg, vg, ktl, kg * 4 + ktl, kb))

            def issue(kg):
                j, kTg, vg = gbuf[kg]
                DMA(kTg, ksc[h, :, kg * 2048:(kg + 1) * 2048], ["ksc%d" % t_ for t_ in range(kg * 4, kg * 4 + 4)], ["kTg%d" % j], "kld%d" % j)
                DMA(vg, vsc[kg * 2048:(kg + 1) * 2048, h * 128:(h + 1) * 128].rearrange("(kb p) e -> p kb e", p=128),
                    ["vsc%d" % t_ for t_ in range(kg * 4, kg * 4 + 4)], ["vg%d" % j], "vld%d" % j)

            issue(0)
            if ng > 1:
                issue(1)
            nu = len(units)

            def qk(u):
                j, kTg, vg, ktl, kt, kb = units[u]
                for m in range(2):
                    pb = (u % 2) * 2 + m
                    MM(bank(pb), kTg[m * 64:(m + 1) * 64, ktl * 512 + kb * 128:ktl * 512 + (kb + 1) * 128],
                       qT[m * 64:(m + 1) * 64, h * 512:(h + 1) * 512], True, True, ["kTg%d" % j, "kTall0"], ["ps%d" % pb])

            def ex(u):
                j, kTg, vg, ktl, kt, kb = units[u]
                pk_ = "pT%d" % (u % 8)
                ACT(pT[u % 8][:], PS[u % 2][:], AF.Exp, ["ps%d" % ((u % 2) * 2), "ps%d" % ((u % 2) * 2 + 1)], [pk_], scale=0.125)
                if kt >= L - 4:
                    idx = (kt - (L - 4)) * 4 + kb
                    pv3 = pT[u % 8][:].rearrange("p (m q) -> p m q", m=2)
                    TT("dve", pv3, pv3, m01[:, idx * 512:(idx + 1) * 512].unsqueeze(1).to_broadcast([128, 2, 512]), ALU.mult,
                       [pk_, "m01"], [pk_])

            def pv(u):
                j, kTg, vg, ktl, kt, kb = units[u]
                for m in range(2):
                    MM(bank(4 + m), vg[:, ktl * 4 + kb, :], pT[u % 8][:, m * 512:(m + 1) * 512], u == 0, u == nu - 1,
                       ["vg%d" % j, "pT%d" % (u % 8)], ["ps%d" % (4 + m)])

            def adds(g):
                u0 = 4 * g
                TT("dve", sAB[0][:], pT[u0 % 8][:], pT[(u0 + 1) % 8][:], ALU.add, ["pT%d" % (u0 % 8), "pT%d" % ((u0 + 1) % 8)], ["sAB0"])
                TT("dve", sAB[1][:], pT[(u0 + 2) % 8][:], pT[(u0 + 3) % 8][:], ALU.add, ["pT%d" % ((u0 + 2) % 8), "pT%d" % ((u0 + 3) % 8)], ["sAB1"])
                sc = sC2[g % 2]
                TT("dve", sc[:], sAB[0][:], sAB[1][:], ALU.add, ["sAB0", "sAB1"], ["sC%d" % (g % 2)])

            def sums(g):
                sc = sC2[g % 2]
                for m in range(2):
                    MM(bank(6 + m), bONE, sc[:, m * 512:(m + 1) * 512], g == 0, g == nu // 4 - 1, ["cbf", "sC%d" % (g % 2)], ["ps%d" % (6 + m)])

            qk(0)
            qk(1)
            for u in range(nu):
                if u % 16 == 1 and u // 16 >= 1 and u // 16 + 1 < ng:
                    issue(u // 16 + 1)
                ex(u)
                if u + 2 < nu:
                    qk(u + 2)
                if u >= 1:
                    pv(u - 1)
                if u % 4 == 3:
                    adds(u // 4)
                if u % 4 == 2 and u >= 6:
                    sums(u // 4 - 1)
                for _ in range(3):
                    if fin_q:
                        fin_q.pop(0)()
            pv(nu - 1)
            sums(nu // 4 - 1)
            ob0, ob1, r0 = tmpf[1][:, 0:512], tmpf[2][:, 0:512], tmpf[0][:, 0:512]
            fs0, fs1 = tmpf[3][:, 0:512], tmpf[3][:, 512:1024]
            while fin_q:
                fin_q.pop(0)()
            CP("dve", ob0, bank(4), ["ps4"], ["tmpf1"])
            P.add("act", lambda e: e.copy(ob1, bank(5)), r=["ps5"], w=["tmpf2"])
            CP("dve", fs0, bank(6), ["ps6"], ["tmpf3"])
            P.add("act", lambda e: e.copy(fs1, bank(7)), r=["ps7"], w=["tmpf3"])

            def mk_fin(h=h):
                q = []
                q.append(lambda: ACT(r0, fs0, AF.Ln, ["tmpf3"], ["tmpf0"]))
                q.append(lambda: ACT(r0, r0, AF.Exp, ["tmpf0"], ["tmpf0"], scale=-1.0))
                q.append(lambda: TT("dve", ob0, ob0, r0, ALU.mult, ["tmpf1", "tmpf0"], ["tmpf1"]))
                q.append(lambda: ACT(r0, fs1, AF.Ln, ["tmpf3"], ["tmpf0"]))
                q.append(lambda: ACT(r0, r0, AF.Exp, ["tmpf0"], ["tmpf0"], scale=-1.0))
                q.append(lambda: TT("dve", ob1, ob1, r0, ALU.mult, ["tmpf2", "tmpf0"], ["tmpf2"]))
                q.append(lambda: STT("dve", ob0, ob1, sm[:, SM_NLAM:SM_NLAM + 1], ob0, ALU.mult, ALU.add, ["tmpf1", "tmpf2", "nlam"], ["tmpf1"]))
                q.append(lambda: TT("dve", osqb[:], ob0, ob0, ALU.mult, ["tmpf1"], ["osqb"]))
                q.append(lambda: MM(bank(7), bONE, osqb[:], True, True, ["cbf", "osqb"], ["ps7"]))
                q.append(lambda: ACT(r0, bank(7), AF.Ln, ["ps7"], ["tmpf0"], scale=1.0 / 128.0, bias=EPS))
                q.append(lambda: ACT(r0, r0, AF.Exp, ["tmpf0"], ["tmpf0"], scale=-0.5))
                q.append(lambda: TT("dve", ob0, ob0, r0, ALU.mult, ["tmpf1", "tmpf0"], ["tmpf1"]))
                q.append(lambda: STT("dve", ostage[:, h * 512:(h + 1) * 512], ob0, sm[:, SM_SLN:SM_SLN + 1], gbT[:, h * 512:(h + 1) * 512],
                                     ALU.mult, ALU.mult, ["tmpf1", "sln", "kTall1"], ["ostage"]))
                return q
            fin_q.extend(mk_fin())
        while fin_q:
            fin_q.pop(0)()
        DMA(od_d[s], ostage[:], ["ostage"], ["od%d" % s], "ost", eng="sp")
        pending.append("od%d" % s)

    run_stage(P, st)
    if stop == "O2":
        es.close()
        return nc

    st = ExitStack()
    P = Prog("m")
    if stop == "O3":
        P.limit = limit
    WB = sl("WB3", [128, 28800], BF16)
    wst = [sl("wst%d_3" % i, [128, 2048], F32) for i in range(2)]
    xst = [sl("xst0_3", [128, 4 * 515], F32)] * 2
    xb = [sl("xb0_3", [128, 8 * 515], BF16)] * 2
    lnp = sl("lnpt", [128, 2048])
    yT = sl("yT", [128, 4096], BF16)
    oTs = sl("oTs", [128, 4096], BF16)
    merged = sl("merged", [128, 8 * 512], BF16)
    gt = [sl("gt%d" % i, [128, 512]) for i in range(2)]
    xtok = sl("xtok", [128, 4 * 1024])
    tmpf = [sl("tmpf%d_3" % i, [128, 1024]) for i in range(4)]
    DMA(lnp[:], lnp_d, (), ["lnp"], "c2")

    OFF_A, OFF_B, OFF_O, OFF_GM = 0, 8192, 16384, 24576
    xv0 = load_x(xT_own[:, :, 0, :], 515, 0, eng="mix")
    load_weights(w_a, 1024, OFF_A, "wk")
    load_weights(w_b, 1024, OFF_B, "wvv")
    load_weights(w_o, 1024, OFF_O, "wx")
    wa_v, wb_v, wo_v = wv(OFF_A, 1024), wv(OFF_B, 1024), wv(OFF_O, 1024)
    gmi = [0]
    for s in range(4):
        i = 0
        xv = xv0 if s == 0 else load_x(xT_own[:, :, s, :], 515, i, eng="mix")
        DMA(yT[:], ysd_d[s], (), ["yT"], "yld")
        DMA(oTs[:], od_d[s], (), ["oTs"], "old")
        xk = "xb%d" % i
        DMA(xtok[:].rearrange("p (k c) -> p k c", k=4), x_own[s].rearrange("(k p) c -> p k c", p=128), (), ["xtok"], "xtok")
        for db in range(8):
            j = gmi[0] % 2
            gmi[0] += 1
            goff = OFF_GM + j * 2048
            stv = wst[j][:, 0:2048].rearrange("p (a b c) -> p a b c", a=8, b=2)
            DMA(stv, w_gm[:, :, :].rearrange("p a (b c) -> p a b c", b=2)[:, :, :, db * 128:(db + 1) * 128], (), ["wst%d" % j], "wst%d" % j)
            CP("dve", WB[:, goff:goff + 2048].rearrange("p (a b c) -> p a b c", a=8, b=2), stv, ["wst%d" % j], ["wgm%d" % j])
            wg = WB[:, goff:goff + 2048].rearrange("p (a b c) -> p a b c", a=8, b=2)
            for br in range(2):
                for dc in range(8):
                    MM(bank(br), wg[:, dc, br, :], xv[:, dc, 3:515], dc == 0, dc == 7, ["wgm%d" % j, xk], ["ps%d" % br])
                ACT(gt[br][:], bank(br), AF.Sigmoid, ["ps%d" % br, "cst"], ["gt%d" % br], bias=cst[:, C_BG + br * 8 + db:C_BG + br * 8 + db + 1])
            for cc in range(8):
                MM(bank(2), wa_v[:, cc, db * 128:(db + 1) * 128], yT[:, cc * 512:(cc + 1) * 512], cc == 0, cc == 7,
                   ["wk", "yT"], ["ps2"])
            for cc in range(8):
                MM(bank(3), wb_v[:, cc, db * 128:(db + 1) * 128], oTs[:, cc * 512:(cc + 1) * 512], cc == 0, cc == 7,
                   ["wvv", "oTs"], ["ps3"])
            TT("dve", gt[0][:], gt[0][:], bank(2), ALU.mult, ["gt0", "ps2"], ["gt0"])
            TT("dve", gt[1][:], gt[1][:], bank(3), ALU.mult, ["gt1", "ps3"], ["gt1"])
            TT("dve", merged[:, db * 512:(db + 1) * 512], gt[0][:], gt[1][:], ALU.add, ["gt0", "gt1"], ["merged"])
        for k in range(4):
            vb = tmpf[k % 2]
            vk = "tmpf%d" % (k % 2)
            for half in range(2):
                pb = 4 + half
                for db in range(8):
                    MM(bank(pb), merged[:, db * 512 + k * 128:db * 512 + (k + 1) * 128], wo_v[:, db, half * 512:(half + 1) * 512], db == 0, db == 7,
                       ["merged", "wx"], ["ps%d" % pb])
            STT("dve", vb[:], xtok[:, k * 1024:(k + 1) * 1024], ALPHA, PS[2][:], ALU.mult, ALU.add, ["xtok", "ps4", "ps5"], [vk])
            c0 = SM_NM + (k % 2) * 4
            P.add("dve", lambda e, o=sm[:, c0:c0 + 1], ii=vb[:]: e.reduce_sum(o, ii, mybir.AxisListType.X), r=[vk], w=["ln_a%d" % (k % 2)])
            TS("dve", sm[:, c0:c0 + 1], sm[:, c0:c0 + 1], -1.0 / D, None, ALU.mult, None, ["ln_a%d" % (k % 2)], ["ln_b%d" % (k % 2)])
            sq = tmpf[2 + k % 2]
            MSET("dve", sm[:, c0 + 1:c0 + 2], 0.0, ["ln_c%d" % (k % 2)])
            ACT(sq[:], vb[:], AF.Square, [vk, "ln_b%d" % (k % 2), "ln_c%d" % (k % 2)], ["tmpf%d" % (2 + k % 2), "ln_c%d" % (k % 2)],
                bias=sm[:, c0:c0 + 1], accum=sm[:, c0 + 1:c0 + 2])
            TS("dve", sm[:, c0 + 2:c0 + 3], sm[:, c0 + 1:c0 + 2], 1.0 / D, EPS, ALU.mult, ALU.add, ["ln_c%d" % (k % 2)], ["ln_d%d" % (k % 2)])
            ACT(sm[:, c0 + 2:c0 + 3], sm[:, c0 + 2:c0 + 3], AF.Sqrt, ["ln_d%d" % (k % 2)], ["ln_d2%d" % (k % 2)])
            P.add("dve", lambda e, o=sm[:, c0 + 2:c0 + 3]: e.reciprocal(o, o), r=["ln_d2%d" % (k % 2)], w=["ln_e%d" % (k % 2)])
            TS("dve", vb[:], vb[:], sm[:, c0:c0 + 1], sm[:, c0 + 2:c0 + 3], ALU.add, ALU.mult, [vk, "ln_b%d" % (k % 2), "ln_e%d" % (k % 2)], [vk])
            TT("dve", vb[:], vb[:], lnp[:, 0:1024], ALU.mult, [vk, "lnp"], [vk])
            TT("pool", sq[:], vb[:], lnp[:, 1024:2048], ALU.add, [vk, "lnp"], ["tmpf%d" % (2 + k % 2)])
            DMA(y_out[s, k * 128:(k + 1) * 128, :], sq[:], ["tmpf%d" % (2 + k % 2)], ["yout%d_%d" % (s, k)], "yo%d" % (k % 2))
            pending.append("yout%d_%d" % (s, k))
    run_stage(P, st)
    es.close()
    return nc


def _prep_inputs(inputs):
    f = lambda a: np.ascontiguousarray(np.asarray(a, dtype=np.float32))
    x = f(inputs["x"])
    w_in = f(inputs["w_in"])[0]

    def wl(c0, c1):
        return np.ascontiguousarray(w_in[:, c0:c1].reshape(8, 128, c1 - c0).transpose(1, 0, 2))

    def wsq(w):
        return np.ascontiguousarray(f(w)[0].reshape(8, 128, 1024).transpose(1, 0, 2))

    common = {
        "w_z": wl(0, 1024), "w_x": wl(1024, 2560), "w_dt": wl(2560, 2576), "w_q": wl(2576, 3600),
        "w_k": wl(3600, 4624), "w_v": wl(4624, 5648), "w_gb": wl(5648, 6672), "w_gm": wl(6672, 8720),
        "w_a": wsq(inputs["w_a"]), "w_b": wsq(inputs["w_b"]), "w_o": wsq(inputs["w_o"]),
    }
    r = np.arange(128)
    ident = (r[:, None] == r[None, :]).astype(np.float32)
    LT = (r[:, None] <= r[None, :]).astype(np.float32)
    SU = (r[:, None] > r[None, :]).astype(np.float32)
    ones = np.ones((128, 128), np.float32)
    perm = np.zeros((128, 128), np.float32)
    for m in range(2):
        for d in range(8):
            perm[m * 64 + d + 8, m * 64 + d] = -1.0
            perm[m * 64 + d, m * 64 + d + 8] = 1.0
    bc = lambda v, n: np.broadcast_to(f(v).reshape(1, n), (128, n))
    conv_w = f(inputs["conv_w"])[0]
    cw = np.zeros((128, 48), np.float32)
    for blk in range(12):
        for w in range(4):
            cw[:, blk * 4 + w] = conv_w[w, blk * 128:(blk + 1) * 128]
    conv_b = f(inputs["conv_b"])[0]
    cb = conv_b.reshape(12, 128).T
    bg = f(inputs["b_gate"])[0].reshape(16, 128).T
    sln = f(inputs["subln_w"])[0].reshape(128, 1)
    nw = f(inputs["ssd_norm_w"])[0].reshape(8, 128).T
    lnp = np.concatenate([bc(inputs["ln_g"], 1024), bc(inputs["ln_b"], 1024)], axis=1)
    pos = np.arange(S, dtype=np.float32)
    inv_freq = (np.float32(500000.0) ** (-np.arange(0, 16, 2, dtype=np.float32) / np.float32(16))).astype(np.float32)
    ang = (pos[:, None] * inv_freq[None, :]).astype(np.float32)
    cos, sin = np.cos(ang).astype(np.float32), np.sin(ang).astype(np.float32)
    rope = np.zeros((128, 2, S), np.float32)
    rope[:, 0, :] = 1.0
    for m in range(2):
        for dd in range(16):
            rope[m * 64 + dd, 0, :] = cos[:, dd % 8]
            rope[m * 64 + dd, 1, :] = sin[:, dd % 8]
    maskA = np.zeros((8, 512), np.float32)
    for k in range(512):
        maskA[k // 64, k] = 1.0
    in_maps = []
    for c in range(8):
        b, j = c // 4, c % 4
        tiles = [j, 7 - j, 8 + j, 15 - j]
        xb = x[b]
        xT = np.ascontiguousarray(xb.T.reshape(8, 128, S).transpose(1, 0, 2))
        xo = np.zeros((128, 8, 4, 515), np.float32)
        xtok = np.zeros((4, 512, D), np.float32)
        ropeO = np.zeros((128, 2, 4, 512), np.float32)
        sel = np.zeros((128, 16), np.float32)
        maskB = np.zeros((8, 16, 512), np.float32)
        for s, t in enumerate(tiles):
            lo = t * 512
            if t > 0:
                xo[:, :, s, :] = xT[:, :, lo - 3:lo + 512]
            else:
                xo[:, :, s, 3:] = xT[:, :, 0:512]
            xtok[s] = xb[lo:lo + 512]
            ropeO[:, :, s, :] = rope[:, :, lo:lo + 512]
            sel[:, t] = 1.0
            L = KLEN[s]
            for pi in range(4):
                kt = L - 4 + pi
                if kt < t:
                    pass
                elif kt > t:
                    maskB[:, pi + 4 * s, :] = NEG
                else:
                    for rr in range(8):
                        q = np.arange(512)
                        maskB[rr, pi + 4 * s, :] = np.where(q // 64 >= rr, 0.0, NEG)
        cstv = np.concatenate([ident, LT, SU, ones, perm, bc(inputs["a_log"], 16), bc(inputs["dt_bias"], 16), bc(inputs["d_skip"], 16),
                               bc(inputs["lambda_q1"], 64), bc(inputs["lambda_k1"], 64), bc(inputs["lambda_q2"], 64), bc(inputs["lambda_k2"], 64),
                               cw, cb, bg, sln, sel, nw], axis=1).astype(np.float32)
        assert cstv.shape[1] == NCST
        m = dict(common)
        m.update({"xT_all": xT, "xT_own": xo, "x_own": xtok, "cst": np.ascontiguousarray(cstv),
                  "cbrow": conv_b.reshape(1, 1536).copy(), "lnp": np.ascontiguousarray(lnp),
                  "msk": np.ascontiguousarray(np.concatenate([maskA, maskB.reshape(8, 8192)], axis=1)),
                  "ropeA": rope, "ropeO": ropeO})
        in_maps.append(m)
    return in_maps


def kernel(**inputs):
    in_maps = _prep_inputs(inputs)
    nc = build_nc()
    res = run_bass_kernel_spmd(nc, in_maps, core_ids=list(range(8)))
    out = np.zeros((2, S, D), np.float32)
    for c in range(8):
        b, j = c // 4, c % 4
        tiles = [j, 7 - j, 8 + j, 15 - j]
        y = np.asarray(res.results[c]["y_out"], dtype=np.float32)
        for s, t in enumerate(tiles):
            out[b, t * 512:(t + 1) * 512] = y[s]
    return out
```

```python
import math
import sys
import os
from contextlib import ExitStack
import numpy as np
import concourse.bass as bass
import concourse.mybir as mybir
from concourse.bass_utils import run_bass_kernel_spmd

F32 = mybir.dt.float32
BF16 = mybir.dt.bfloat16
AF = mybir.ActivationFunctionType
ALU = mybir.AluOpType

D = 1024
S = 8192
NT = 16
KLEN = (4, 8, 12, 16)
EPS = 1e-5
ALPHA = 2.0 ** 0.25
LAMBDA_INIT = 0.2
NEG = -30000.0

C_ID, C_LT, C_SU, C_ONE, C_PERM = 0, 128, 256, 384, 512
C_ALOG, C_DTB, C_DSK = 640, 656, 672
C_LQ1, C_LK1, C_LQ2, C_LK2 = 688, 752, 816, 880
C_CW, C_CB, C_BG, C_SLN, C_SEL, C_NW = 944, 992, 1004, 1020, 1021, 1037
NCST = 1045


class _Op:
    __slots__ = ("eng", "fn", "deps", "dma", "idx", "need", "sem", "val", "src")


class Prog:
    def __init__(self, tag):
        self.tag = tag
        self.ops = []
        self.wr = {}
        self.rd = {}
        self.base = {}
        self.limit = None

    def add(self, eng, fn, r=(), w=(), dma=None):
        if self.limit is not None and len(self.ops) >= self.limit:
            return None
        o = _Op()
        o.eng, o.fn, o.dma, o.idx, o.need = eng, fn, dma, len(self.ops), False
        f = sys._getframe(1)
        o.src = (f.f_lineno, f.f_back.f_lineno if f.f_back else 0)
        deps = {}
        for k in r:
            for x in self.wr.get(k, ()):
                deps[x] = "raw"
        for k in w:
            if self.rd.get(k):
                self.base[k] = list(self.wr.get(k, ())) + list(self.rd[k])
                self.wr[k] = []
                self.rd[k] = []
            for x in self.base.get(k, ()):
                deps.setdefault(x, "war")
        for k in w:
            self.wr.setdefault(k, []).append(o.idx)
            self.rd.setdefault(k, [])
        for k in r:
            self.rd.setdefault(k, []).append(o.idx)
            self.wr.setdefault(k, [])
        deps.pop(o.idx, None)
        o.deps = deps
        self.ops.append(o)
        return o

    def finalize(self):
        cnt = {}
        for o in self.ops:
            if o.dma is not None:
                o.sem = self.tag + "d_" + o.dma
                cnt[o.sem] = cnt.get(o.sem, 0) + 16
                o.val = cnt[o.sem]
        for o in self.ops:
            best = {}
            for d, kind in o.deps.items():
                p = self.ops[d]
                if p.dma is None:
                    if p.eng == o.eng and o.eng == "pe" and o.dma is None:
                        continue
                    ch = "e_" + p.eng
                else:
                    ch = p.sem
                if ch not in best or best[ch] < d:
                    best[ch] = d
            o.deps = list(best.values())
            for d in o.deps:
                self.ops[d].need = True
        for o in self.ops:
            if o.dma is None and o.need:
                o.sem = self.tag + "e_" + o.eng
                cnt[o.sem] = cnt.get(o.sem, 0) + 1
                o.val = cnt[o.sem]
        return sorted(cnt.keys())

    def emit(self, eng, e, sems):
        waited = {}
        for o in self.ops:
            if o.eng != eng:
                continue
            for d in o.deps:
                p = self.ops[d]
                if waited.get(p.sem, 0) < p.val:
                    e.wait_ge(sems[p.sem], p.val)
                    waited[p.sem] = p.val
            ins = o.fn(e)
            if o.dma is not None:
                ins.then_inc(sems[o.sem], 16)
            elif o.need:
                ins.then_inc(sems[o.sem], 1)


def build_nc(stop=None, nta=NT, limit=None):
    nc = bass.Bass("TRN2", target_bir_lowering=False)
    dt_in = lambda n, shp, t=F32: nc.dram_tensor(n, shp, t, kind="ExternalInput").ap()
    xT_all = dt_in("xT_all", [128, 8, S])
    xT_own = dt_in("xT_own", [128, 8, 4, 515])
    x_own = dt_in("x_own", [4, 512, D])
    w_k = dt_in("w_k", [128, 8, 1024])
    w_v = dt_in("w_v", [128, 8, 1024])
    w_x = dt_in("w_x", [128, 8, 1536])
    w_dt = dt_in("w_dt", [128, 8, 16])
    w_q = dt_in("w_q", [128, 8, 1024])
    w_z = dt_in("w_z", [128, 8, 1024])
    w_gb = dt_in("w_gb", [128, 8, 1024])
    w_gm = dt_in("w_gm", [128, 8, 2048])
    w_a = dt_in("w_a", [128, 8, 1024])
    w_b = dt_in("w_b", [128, 8, 1024])
    w_o = dt_in("w_o", [128, 8, 1024])
    cst_d = dt_in("cst", [128, NCST])
    cbrow_d = dt_in("cbrow", [1, 1536])
    lnp_d = dt_in("lnp", [128, 2048])
    msk_d = dt_in("msk", [8, 512 + 16 * 512])
    ropeA = dt_in("ropeA", [128, 2, S])
    ropeO = dt_in("ropeO", [128, 2, 4, 512])
    y_out = nc.dram_tensor("y_out", [4, 512, D], F32, kind="ExternalOutput").ap()
    ksc = nc.dram_tensor("ksc", [8, 128, S], BF16).ap()
    vsc = nc.dram_tensor("vsc", [S, 1024], BF16).ap()

    hs_d = nc.dram_tensor("hs_d", [4, 128, 1024], F32).ap()
    ysd_d = nc.dram_tensor("ysd_d", [4, 128, 4096], BF16).ap()
    od_d = nc.dram_tensor("od_d", [4, 128, 4096], BF16).ap()

    es = ExitStack()
    st = ExitStack()
    P = Prog("s")
    pending = []
    sb = lambda n, shp, t=F32: es.enter_context(nc.sbuf_tensor(n, shp, t))
    sl = lambda n, shp, t=F32: st.enter_context(nc.sbuf_tensor(n, shp, t))
    cst = sb("cstt", [128, NCST])
    cbf = sb("cbf", [128, 640], BF16)
    diag = sb("diag", [128, 48 * 128], BF16)
    cbrow_b = sb("cbrowb", [1, 1536], BF16)
    onesrow_b = sb("onesrow", [1, 128], BF16)
    sm = sb("sm", [128, 64])
    PS = [es.enter_context(nc.psum_tensor("ps%d" % i, [128, 1024], F32)) for i in range(4)]

    def run_stage(P, st):
        if os.environ.get("KDBG"):
            print("STAGE", P.tag, "nops", len(P.ops), "last", P.ops[-1].eng, P.ops[-1].src)
        P.limit = None
        P.add("sp", lambda e: e.nop(), r=list(pending), w=["done"])
        del pending[:]
        names = P.finalize()
        sems = {n: st.enter_context(nc.semaphore(n)) for n in names}
        with nc.Block() as block:
            @block.sync
            def _(e):
                P.emit("sp", e, sems)

            @block.tensor
            def _(e):
                P.emit("pe", e, sems)

            @block.scalar
            def _(e):
                P.emit("act", e, sems)

            @block.vector
            def _(e):
                P.emit("dve", e, sems)

            @block.gpsimd
            def _(e):
                P.emit("pool", e, sems)
        st.close()
        nc.all_engine_barrier()

    WB = sl("WB", [128, 28800], BF16)
    wst = [sl("wst%d" % i, [128, 2048], F32) for i in range(2)]
    xst = [sl("xst0", [128, 4 * 515], F32)] * 2
    xb = [sl("xb%d" % i, [128, 8 * 515], BF16) for i in range(2)]
    uT = sl("uT", [128, 12 * 515], BF16)
    rt = [sl("rt0", [128, 1024])] * 2
    kc = [sl("kc%d" % i, [128, 512], BF16) for i in range(2)]
    tmpf = [sl("tmpf%d" % i, [128, 1536 if i == 3 else 512]) for i in range(4)]
    kTall = [sl("kTall0", [128, 8 * 512], BF16)] * 2
    vall = [sl("vall0", [128, 4 * 1024], BF16)] * 2
    xs_t = sl("xs_t", [128, 4 * 1024], BF16)
    B_t = sl("B_t", [128, 4 * 256], BF16)
    dtt = sl("dtt", [128, 64])
    adt = sl("adt", [128, 64])
    dk = sl("dk", [128, 64])
    wl = sl("wl", [128, 64])
    cdT = sl("cdT", [128, 16])
    xdtd = sl("xdtd", [128, 4 * 1024], BF16)
    hcur = sl("hcur", [128, 1024])
    hs = [sl("hs%d" % i, [128, 1024]) for i in range(4)]

    def bank(i):
        return PS[i // 2][:, (i % 2) * 512:(i % 2) * 512 + 512]

    cI = cst[:, C_ID:C_ID + 128]
    cLT = cst[:, C_LT:C_LT + 128]
    cSU = cst[:, C_SU:C_SU + 128]
    cONE = cst[:, C_ONE:C_ONE + 128]
    bI = cbf[:, 0:128]
    bONE = cbf[:, 384:512]
    bPERM = cbf[:, 512:640]
    SM_A, SM_NLAM, SM_T0, SM_T1, SM_SLN, SM_NM, SM_SS, SM_RS = 0, 16, 17, 18, 19, 20, 24, 28

    dma_rr = [0]

    def DMA(out, in_, r, w, key, eng="sp"):
        P.add(eng, lambda e, o=out, i=in_: e.dma_start(out=o, in_=i), r=r, w=w, dma=key)

    def MM(out, lhsT, rhs, start, stop, r, w):
        P.add("pe", lambda e, o=out, l=lhsT, rr=rhs, s=start, t=stop: e.matmul(o, lhsT=l, rhs=rr, start=s, stop=t),
              r=r, w=w)

    def ACT(out, in_, func, r, w, bias=None, scale=None, accum=None):
        def fn(e, o=out, i=in_, f=func, b=bias, sc=scale, a=accum):
            kw = {}
            if b is not None:
                kw["bias"] = b
            if sc is not None:
                kw["scale"] = sc
            if a is not None:
                kw["accum_out"] = a
            return e.activation(o, i, f, **kw)
        P.add("act", fn, r=r, w=w)

    def TT(eng, out, in0, in1, op, r, w):
        P.add(eng, lambda e, o=out, a=in0, b=in1, p=op: e.tensor_tensor(out=o, in0=a, in1=b, op=p), r=r, w=w)

    def TS(eng, out, in0, s1, s2, op0, op1, r, w):
        if op1 is None:
            P.add(eng, lambda e, o=out, a=in0, x=s1, p=op0: e.tensor_scalar(o, a, x, None, p), r=r, w=w)
        else:
            P.add(eng, lambda e, o=out, a=in0, x=s1, y=s2, p=op0, q=op1: e.tensor_scalar(o, a, x, y, p, q), r=r, w=w)

    def STT(eng, out, in0, sc, in1, op0, op1, r, w):
        P.add(eng, lambda e, o=out, a=in0, s=sc, b=in1, p=op0, q=op1: e.scalar_tensor_tensor(o, a, s, b, p, q), r=r, w=w)

    def CP(eng, out, in_, r, w):
        P.add(eng, lambda e, o=out, i=in_: e.tensor_copy(out=o, in_=i), r=r, w=w)

    def MSET(eng, ap, v, w):
        P.add(eng, lambda e, a=ap, x=v: e.memset(a, x), r=(), w=w)

    DMA(cst[:], cst_d, (), ["cst"], "c0")
    DMA(tmpf[3][0:1, 0:1536], cbrow_d, (), ["tmpf3"], "c1")
    CP("dve", cbf[:], cst[:, 0:640], ["cst"], ["cbf"])
    CP("dve", cbrow_b[:], tmpf[3][0:1, 0:1536], ["tmpf3"], ["cbrowb"])
    MSET("dve", onesrow_b[:], 1.0, ["onesrow"])
    MSET("dve", uT[:], 0.0, ["uT%d" % b for b in range(12)])
    MSET("dve", hcur[:], 0.0, ["hcur"])
    for i in range(4):
        MSET("dve", hs[i][:], 0.0, ["hs%d" % i])
    for blk in range(12):
        for w in range(4):
            TS("dve", diag[:, (blk * 4 + w) * 128:(blk * 4 + w + 1) * 128], cI,
               cst[:, C_CW + blk * 4 + w:C_CW + blk * 4 + w + 1], None, ALU.mult, None, ["cst"], ["diag"])
    ACT(sm[:, SM_A:SM_A + 16], cst[:, C_ALOG:C_ALOG + 16], AF.Exp, ["cst"], ["sm_a0"])
    TS("dve", sm[:, SM_A:SM_A + 16], sm[:, SM_A:SM_A + 16], -1.0, None, ALU.mult, None, ["sm_a0"], ["sm_a"])
    TT("dve", tmpf[0][:, 0:64], cst[:, C_LQ1:C_LQ1 + 64], cst[:, C_LK1:C_LK1 + 64], ALU.mult, ["cst"], ["lamt0"])
    TT("dve", tmpf[0][:, 64:128], cst[:, C_LQ2:C_LQ2 + 64], cst[:, C_LK2:C_LK2 + 64], ALU.mult, ["cst"], ["lamt1"])
    P.add("dve", lambda e: e.reduce_sum(sm[:, SM_T0:SM_T0 + 1], tmpf[0][:, 0:64], mybir.AxisListType.X), r=["lamt0"], w=["lam_s0"])
    P.add("dve", lambda e: e.reduce_sum(sm[:, SM_T1:SM_T1 + 1], tmpf[0][:, 64:128], mybir.AxisListType.X), r=["lamt1"], w=["lam_s1"])
    ACT(sm[:, SM_T0:SM_T0 + 2], sm[:, SM_T0:SM_T0 + 2], AF.Exp, ["lam_s0", "lam_s1"], ["lam_e"])
    TT("dve", sm[:, SM_NLAM:SM_NLAM + 1], sm[:, SM_T1:SM_T1 + 1], sm[:, SM_T0:SM_T0 + 1], ALU.subtract, ["lam_e"], ["nlam0"])
    TS("dve", sm[:, SM_NLAM:SM_NLAM + 1], sm[:, SM_NLAM:SM_NLAM + 1], -LAMBDA_INIT, None, ALU.add, None, ["nlam0"], ["nlam"])
    TS("dve", sm[:, SM_SLN:SM_SLN + 1], cst[:, C_SLN:C_SLN + 1], 1.0 - LAMBDA_INIT, None, ALU.mult, None, ["cst"], ["sln"])

    wstate = {"i": 0}

    def load_weights(dram, ncols, wb_off, key, scale_rows=None):
        c0 = 0
        while c0 < ncols:
            cw = min(256, ncols - c0)
            i = wstate["i"] % 2
            wstate["i"] += 1
            stv = wst[i][:, 0:8 * cw].rearrange("p (a c) -> p a c", a=8)
            DMA(stv, dram[:, :, c0:c0 + cw], (), ["wst%d" % i], "wst%d" % i)
            dst = WB[:, wb_off:wb_off + 8 * ncols].rearrange("p (a c) -> p a c", a=8)[:, :, c0:c0 + cw]
            if wstate["i"] % 2:
                CP("dve", dst, stv, ["wst%d" % i], [key])
            else:
                P.add("act", lambda e, o=dst, ii=stv: e.copy(o, ii), r=["wst%d" % i], w=[key])
            c0 += cw

    def wv(wb_off, ncols):
        return WB[:, wb_off:wb_off + 8 * ncols].rearrange("p (a c) -> p a c", a=8)

    def load_x(src, ncol, i, eng="pool"):
        xv = xb[i][:, 0:8 * ncol].rearrange("p (a c) -> p a c", a=8)
        for hf in range(2):
            stv = xst[0][:, 0:4 * ncol].rearrange("p (a c) -> p a c", a=4)
            DMA(stv, src[:, hf * 4:(hf + 1) * 4, :], (), ["xst0"], "xst0")
            if eng == "mix":
                if hf == 0:
                    CP("dve", xv[:, hf * 4:(hf + 1) * 4, :], stv, ["xst0"], ["xb%d" % i])
                else:
                    P.add("act", lambda e, o=xv[:, hf * 4:(hf + 1) * 4, :], ii=stv: e.copy(o, ii), r=["xst0"], w=["xb%d" % i])
            else:
                CP(eng, xv[:, hf * 4:(hf + 1) * 4, :], stv, ["xst0"], ["xb%d" % i])
        return xv

    ev = {"i": 0}

    def evac(out, in_, r, w):
        ev["i"] += 1
        if ev["i"] % 2:
            P.add("act", lambda e, o=out, i=in_: e.copy(o, i), r=r, w=w)
        else:
            CP("dve", out, in_, r, w)

    def softplus_dt(psap, k, rk):
        d = dtt[:, k * 16:(k + 1) * 16]
        TT("dve", d, psap, cst[:, C_DTB:C_DTB + 16], ALU.add, [rk, "cst"], ["dtt%d" % k])
        ACT(d, d, AF.Exp, ["dtt%d" % k], ["dtt%d" % k])
        ACT(d, d, AF.Ln, ["dtt%d" % k], ["dtt%d" % k], bias=1.0)
        TT("dve", adt[:, k * 16:(k + 1) * 16], d, sm[:, SM_A:SM_A + 16], ALU.mult, ["dtt%d" % k, "sm_a"], ["adt%d" % k])

    def conv_tok(k, blks, pb, xoff):
        for gi in range(0, len(blks), 4):
            grp = blks[gi:gi + 4]
            pk = "ps%d" % pb
            for j, blk in enumerate(grp):
                o = bank(pb)[:, j * 128:(j + 1) * 128]
                for w in range(4):
                    MM(o, uT[:, blk * 515 + k * 128 + w: blk * 515 + k * 128 + w + 128],
                       diag[:, (blk * 4 + w) * 128:(blk * 4 + w + 1) * 128], w == 0, False,
                       ["uT%d" % blk, "diag"], [pk])
                MM(o, onesrow_b[0:1, :], cbrow_b[0:1, blk * 128:(blk + 1) * 128], False, True,
                   ["onesrow", "cbrowb"], [pk])
            n = len(grp) * 128
            b0 = grp[0]
            if b0 < 8:
                ACT(xs_t[:, k * 1024 + b0 * 128:k * 1024 + b0 * 128 + n], bank(pb)[:, 0:n], AF.Silu, [pk], ["xs_t%d" % k])
            else:
                ACT(B_t[:, k * 256:(k + 1) * 256], bank(pb)[:, 0:n], AF.Silu, [pk], ["B_t%d" % k])
            pb = pb + 1 if pb % 2 == 0 else pb - 1
        return pb

    def proj_u(xv, blks, wx_off, col0, pbs, halo):
        wx = wv(wx_off, 1536)
        for bi, blk in enumerate(blks):
            pb = pbs[bi % len(pbs)]
            pk = "ps%d" % pb
            uk = "uT%d" % blk
            if halo == "carry":
                CP("dve", uT[:, blk * 515:blk * 515 + 3], uT[:, blk * 515 + 512:blk * 515 + 515], [uk], [uk])
            else:
                for dc in range(8):
                    MM(bank(pb)[:, 0:3], wx[:, dc, blk * 128:(blk + 1) * 128], xv[:, dc, 0:3], dc == 0, dc == 7,
                       ["wx", halo], [pk])
                evac(uT[:, blk * 515:blk * 515 + 3], bank(pb)[:, 0:3], [pk], [uk])
            for dc in range(8):
                MM(bank(pb), wx[:, dc, blk * 128:(blk + 1) * 128], xv[:, dc, col0:col0 + 512], dc == 0, dc == 7,
                   ["wx", halo if halo != "carry" else "xbcur"], [pk])
            evac(uT[:, blk * 515 + 3:blk * 515 + 515], bank(pb), [pk], [uk])

    OFF_K, OFF_V, OFF_X, OFF_DT = 0, 8192, 16384, 16384 + 12288
    load_weights(w_k, 1024, OFF_K, "wk")
    load_weights(w_v, 1024, OFF_V, "wvv")
    load_weights(w_x, 1536, OFF_X, "wx")
    load_weights(w_dt, 16, OFF_DT, "wdt")
    wk_v, wv_v, wdt_v = wv(OFF_K, 1024), wv(OFF_V, 1024), wv(OFF_DT, 16)

    def rope_block(h, pb, srcw, srck, xv, col0, rtab, dst, dstk, xk):
        pk = "ps%d" % pb
        for dc in range(8):
            MM(bank(pb), srcw[:, dc, h * 128:(h + 1) * 128], xv[:, dc, col0:col0 + 512], dc == 0, dc == 7, [srck, xk], [pk])
        kcb = kc[h % 2]
        kk = "kc%d" % (h % 2)
        P.add("act", lambda e, o=kcb[:], i=bank(pb): e.copy(o, i), r=[pk], w=[kk])
        pb2 = pb + 2
        pk2 = "ps%d" % pb2
        MM(bank(pb2), bPERM, kcb[:], True, True, ["cbf", kk], [pk2])
        t = tmpf[h % 2][:, 0:512]
        tk = "tmpf%d" % (h % 2)
        TT("dve", t, bank(pb2), rtab[:, 512:1024], ALU.mult, [pk2, "rt"], [tk])
        TT("dve", tmpf[2 + h % 2][:, 0:512], kcb[:], rtab[:, 0:512], ALU.mult, [kk, "rt"], ["tmpf%d" % (2 + h % 2)])
        TT("dve", dst, tmpf[2 + h % 2][:, 0:512], t, ALU.add, [tk, "tmpf%d" % (2 + h % 2)], [dstk])

    xv_next = load_x(xT_all[:, :, 0:512], 512, 0)
    for T in range(nta):
        i = T % 2
        xv = xv_next
        xk = "xb%d" % i
        if T + 1 < nta:
            xv_next = load_x(xT_all[:, :, (T + 1) * 512:(T + 2) * 512], 512, 1 - i)
        rtab = rt[i]
        DMA(rtab[:].rearrange("p (a c) -> p a c", a=2), ropeA[:, :, T * 512:(T + 1) * 512], (), ["rt0"], "rt0")
        def kproj(h):
            pk = h % 2
            for dc in range(8):
                MM(bank(pk), wk_v[:, dc, h * 128:(h + 1) * 128], xv[:, dc, :], dc == 0, dc == 7, ["wk", xk], ["ps%d" % pk])
        kproj(0)
        for h in range(8):
            pk = h % 2
            kcb = kc[h % 2]
            kk = "kc%d" % (h % 2)
            P.add("act", lambda e, o=kcb[:], ii=bank(pk): e.copy(o, ii), r=["ps%d" % pk], w=[kk])
            if h + 1 < 8:
                kproj(h + 1)
            MM(bank(2 + pk), bPERM, kcb[:], True, True, ["cbf", kk], ["ps%d" % (2 + pk)])
            t = tmpf[h % 2][:, 0:512]
            TT("dve", t, bank(2 + pk), rtab[:, 512:1024], ALU.mult, ["ps%d" % (2 + pk), "rt0"], ["tmpf%d" % (h % 2)])
            TT("dve", tmpf[2 + h % 2][:, 0:512], kcb[:], rtab[:, 0:512], ALU.mult, [kk, "rt0"], ["tmpf%d" % (2 + h % 2)])
            TT("dve", kTall[i][:, h * 512:(h + 1) * 512], tmpf[2 + h % 2][:, 0:512], t, ALU.add,
               ["tmpf%d" % (h % 2), "tmpf%d" % (2 + h % 2)], ["kTall0"])
        DMA(ksc[:, :, T * 512:(T + 1) * 512].rearrange("h r t -> r h t"),
            kTall[i][:].rearrange("p (h t) -> p h t", h=8), ["kTall0"], ["ksc%d" % T], "kst0", eng="pool")
        pending.append("ksc%d" % T)
        for k in range(4):
            for half in range(2):
                pb = 4 + (k * 2 + half) % 2
                for dc in range(8):
                    MM(bank(pb), xv[:, dc, k * 128:(k + 1) * 128], wv_v[:, dc, half * 512:(half + 1) * 512], dc == 0, dc == 7,
                       ["wvv", xk], ["ps%d" % pb])
                evac(vall[i][:, k * 1024 + half * 512:k * 1024 + half * 512 + 512], bank(pb), ["ps%d" % pb], ["vall0"])
        DMA(vsc[T * 512:(T + 1) * 512, :].rearrange("(k p) c -> p k c", p=128),
            vall[i][:].rearrange("p (k c) -> p k c", k=4), ["vall0"], ["vsc%d" % T], "vst0", eng="pool")
        pending.append("vsc%d" % T)
        wx = wv(OFF_X, 1536)
        for bi, blk in enumerate(range(10)):
            pb = 6 + bi % 2
            uk = "uT%d" % blk
            CP("dve", uT[:, blk * 515:blk * 515 + 3], uT[:, blk * 515 + 512:blk * 515 + 515], [uk], [uk])
            for dc in range(8):
                MM(bank(pb), wx[:, dc, blk * 128:(blk + 1) * 128], xv[:, dc, :], dc == 0, dc == 7, ["wx", xk], ["ps%d" % pb])
            evac(uT[:, blk * 515 + 3:blk * 515 + 515], bank(pb), ["ps%d" % pb], [uk])
        for k in range(4):
            o = bank(4)[:, k * 16:(k + 1) * 16]
            for dc in range(8):
                MM(o, xv[:, dc, k * 128:(k + 1) * 128], wdt_v[:, dc, :], dc == 0, dc == 7, ["wdt", xk], ["ps4"])
        for k in range(4):
            softplus_dt(bank(4)[:, k * 16:(k + 1) * 16], k, "ps4")
        pb = 0
        for k in range(4):
            pb = conv_tok(k, list(range(8)), pb, 0)
            pb = conv_tok(k, [8, 9], pb, 0)
        for k in range(4):
            o = bank(5)[:, k * 16:(k + 1) * 16]
            MM(o, cSU, adt[:, k * 16:(k + 1) * 16], True, k == 3, ["cst", "adt%d" % k], ["ps5"])
            for k2 in range(k + 1, 4):
                MM(o, cONE, adt[:, k2 * 16:(k2 + 1) * 16], False, k2 == 3, ["cst", "adt%d" % k2], ["ps5"])
        o = bank(5)[:, 64:80]
        for k in range(4):
            MM(o, cONE, adt[:, k * 16:(k + 1) * 16], k == 0, k == 3, ["cst", "adt%d" % k], ["ps5"])
        ACT(dk[:, 0:64], bank(5)[:, 0:64], AF.Exp, ["ps5"], ["dk"])
        ACT(cdT[:], bank(5)[:, 64:80], AF.Exp, ["ps5"], ["cdT"])
        TT("dve", wl[:], dtt[:], dk[:], ALU.mult, ["dk"] + ["dtt%d" % k for k in range(4)], ["wl"])
        for k in range(4):
            TT("dve", xdtd[:, k * 1024:(k + 1) * 1024].rearrange("p (h q) -> p h q", h=16),
               xs_t[:, k * 1024:(k + 1) * 1024].rearrange("p (h q) -> p h q", h=16),
               wl[:, k * 16:(k + 1) * 16].unsqueeze(2).to_broadcast([128, 16, 64]), ALU.mult,
               ["xs_t%d" % k, "wl"], ["xdtd%d" % k])
        for g in range(2):
            for k in range(4):
                MM(bank(6 + g), B_t[:, k * 256 + g * 128:k * 256 + (g + 1) * 128], xdtd[:, k * 1024 + g * 512:k * 1024 + (g + 1) * 512],
                   k == 0, k == 3, ["B_t%d" % k, "xdtd%d" % k], ["ps%d" % (6 + g)])
        STT("dve", hs[T // 4][:], hcur[:], cst[:, C_SEL + T:C_SEL + T + 1], hs[T // 4][:], ALU.mult, ALU.add,
            ["hcur", "cst", "hs%d" % (T // 4)], ["hs%d" % (T // 4)])
        TT("dve", hcur[:].rearrange("p (h q) -> p h q", h=16), hcur[:].rearrange("p (h q) -> p h q", h=16),
           cdT[:].unsqueeze(2).to_broadcast([128, 16, 64]), ALU.mult, ["hcur", "cdT"], ["hcur"])
        TT("dve", hcur[:], hcur[:], PS[3][:], ALU.add, ["hcur", "ps6", "ps7"], ["hcur"])

    for i in range(4):
        DMA(hs_d[i], hs[i][:], ["hs%d" % i], ["hsd%d" % i], "hst", eng="sp")
        pending.append("hsd%d" % i)
    run_stage(P, st)
    if stop == "A":
        es.close()
        return nc

    st = ExitStack()
    P = Prog("o")
    if stop == "O1":
        P.limit = limit
    WB = sl("WB1", [128, 28800], BF16)
    wst = [sl("wst%d_1" % i, [128, 2048], F32) for i in range(2)]
    xst = [sl("xst0_1", [128, 4 * 515], F32)] * 2
    xb = [sl("xb0_1", [128, 8 * 515], BF16)] * 2
    uT = sl("uT_1", [128, 12 * 515], BF16)
    tmpf = [sl("tmpf%d_1" % i, [128, 1024]) for i in range(3)]
    xs_t = sl("xs_t_1", [128, 4 * 1024], BF16)
    B_t = sl("B_t_1", [128, 4 * 256], BF16)
    zs = sl("zs", [128, 4 * 1024], BF16)
    BCT = sl("BCT", [128, 4 * 512], BF16)
    dtt = sl("dtt_1", [128, 64])
    adt = sl("adt_1", [128, 64])
    ex3 = sl("ex3", [128, 48])
    xdtd = sl("xdtd_1", [128, 1024], BF16)
    xdt = sl("xdt", [128, 1024], BF16)
    hcur = sl("hcur_1", [128, 1024])
    prevb = sl("prevb", [128, 1024], BF16)
    Rm = sl("Rm", [128, 2048])
    Lex = sl("Lex", [128, 2048], BF16)
    cbm = sl("cbm", [128, 256])
    MT = sl("MT", [128, 2048], BF16)
    ystage = sl("ystage", [128, 4096], BF16)

    OFF_Z = 0
    load_weights(w_z, 1024, OFF_Z, "wk")
    load_weights(w_x, 1536, OFF_X, "wx")
    load_weights(w_dt, 16, OFF_DT, "wdt")
    wdt_v = wv(OFF_DT, 16)
    wz_v = wv(OFF_Z, 1024)
    wx = wv(OFF_X, 1536)
    normw = cst[:, C_NW:C_NW + 8]
    for s in range(4):
        i = 0
        xv = load_x(xT_own[:, :, s, :], 515, i, eng="mix")
        xk = "xb%d" % i
        for bi, blk in enumerate(range(12)):
            pb = bi % 2
            uk = "uT%d" % blk
            for dc in range(8):
                MM(bank(pb)[:, 0:3], wx[:, dc, blk * 128:(blk + 1) * 128], xv[:, dc, 0:3], dc == 0, dc == 7, ["wx", xk], ["ps%d" % pb])
            evac(uT[:, blk * 515:blk * 515 + 3], bank(pb)[:, 0:3], ["ps%d" % pb], [uk])
            for dc in range(8):
                MM(bank(pb), wx[:, dc, blk * 128:(blk + 1) * 128], xv[:, dc, 3:515], dc == 0, dc == 7, ["wx", xk], ["ps%d" % pb])
            evac(uT[:, blk * 515 + 3:blk * 515 + 515], bank(pb), ["ps%d" % pb], [uk])
        for bi, blk in enumerate((8, 9, 10, 11)):
            pb = 2 + bi % 2
            for w in range(4):
                MM(bank(pb), diag[:, (blk * 4 + w) * 128:(blk * 4 + w + 1) * 128], uT[:, blk * 515 + w:blk * 515 + w + 512],
                   w == 0, w == 3, ["uT%d" % blk, "diag"], ["ps%d" % pb])
            ACT(BCT[:, bi * 512:(bi + 1) * 512], bank(pb), AF.Silu, ["ps%d" % pb, "cst"], ["BCT%d" % bi],
                bias=cst[:, C_CB + blk:C_CB + blk + 1])
        for k in range(4):
            o = bank(4)[:, k * 16:(k + 1) * 16]
            for dc in range(8):
                MM(o, xv[:, dc, 3 + k * 128:3 + (k + 1) * 128], wdt_v[:, dc, :], dc == 0, dc == 7, ["wdt", xk], ["ps4"])
        for k in range(4):
            softplus_dt(bank(4)[:, k * 16:(k + 1) * 16], k, "ps4")
        pb = 6
        for k in range(4):
            pb = conv_tok(k, list(range(8)), pb, 0)
            pb = conv_tok(k, [8, 9], pb, 0)
        for k in range(4):
            for half in range(2):
                pb = (k * 2 + half) % 2
                for dc in range(8):
                    MM(bank(pb), xv[:, dc, 3 + k * 128:3 + (k + 1) * 128], wz_v[:, dc, half * 512:(half + 1) * 512], dc == 0, dc == 7,
                       ["wk", xk], ["ps%d" % pb])
                ACT(zs[:, k * 1024 + half * 512:k * 1024 + half * 512 + 512], bank(pb), AF.Silu, ["ps%d" % pb], ["zs%d" % k])
        DMA(hcur[:], hs_d[s], (), ["hcur"], "hld")
        for k in range(4):
            CP("pool", prevb[:], hcur[:], ["hcur"], ["prevb"])
            a_k = adt[:, k * 16:(k + 1) * 16]
            ak = "adt%d" % k
            TT("dve", Rm[:].rearrange("p (h l) -> p h l", h=16), a_k.unsqueeze(2).to_broadcast([128, 16, 128]),
               cLT.unsqueeze(1).to_broadcast([128, 16, 128]), ALU.mult, [ak, "cst"], ["Rm"])
            for q in range(4):
                MM(bank(q), cSU, Rm[:, q * 512:(q + 1) * 512], True, True, ["cst", "Rm"], ["ps%d" % q])
            ACT(Lex[:, 0:1024], PS[0][:], AF.Exp, ["ps0", "ps1"], ["Lex0"])
            ACT(Lex[:, 1024:2048], PS[1][:], AF.Exp, ["ps2", "ps3"], ["Lex1"])
            for g in range(2):
                MM(bank(4)[:, g * 128:(g + 1) * 128], BCT[:, g * 512 + k * 128:g * 512 + (k + 1) * 128],
                   BCT[:, (2 + g) * 512 + k * 128:(2 + g) * 512 + (k + 1) * 128], True, True, ["BCT%d" % g, "BCT%d" % (2 + g)], ["ps4"])
            TT("dve", cbm[:].rearrange("p (g l) -> p g l", g=2), bank(4)[:, 0:256].rearrange("p (g l) -> p g l", g=2),
               cLT.unsqueeze(1).to_broadcast([128, 2, 128]), ALU.mult, ["ps4", "cst"], ["cbm"])
            TT("dve", MT[:].rearrange("p (g h l) -> p g h l", g=2, h=8), Lex[:].rearrange("p (g h l) -> p g h l", g=2, h=8),
               cbm[:].rearrange("p (g l) -> p g l", g=2).unsqueeze(2).to_broadcast([128, 2, 8, 128]), ALU.mult,
               ["Lex0", "Lex1", "cbm"], ["MT"])
            MM(bank(5)[:, 0:16], cLT, a_k, True, True, ["cst", ak], ["ps5"])
            MM(bank(5)[:, 16:32], cSU, a_k, True, True, ["cst", ak], ["ps5"])
            MM(bank(5)[:, 32:48], cONE, a_k, True, True, ["cst", ak], ["ps5"])
            ACT(ex3[:], bank(5)[:, 0:48], AF.Exp, ["ps5"], ["ex3"])
            xsk = xs_t[:, k * 1024:(k + 1) * 1024].rearrange("p (h q) -> p h q", h=16)
            TT("dve", xdt[:].rearrange("p (h q) -> p h q", h=16), xsk,
               dtt[:, k * 16:(k + 1) * 16].unsqueeze(2).to_broadcast([128, 16, 64]), ALU.mult, ["xs_t%d" % k, "dtt%d" % k], ["xdt"])
            TT("dve", xdtd[:, 0:1024].rearrange("p (h q) -> p h q", h=16), xdt[:].rearrange("p (h q) -> p h q", h=16),
               ex3[:, 16:32].unsqueeze(2).to_broadcast([128, 16, 64]), ALU.mult, ["xdt", "ex3"], ["xdtd0"])
            for h in range(16):
                MM(PS[0][:, h * 64:(h + 1) * 64], MT[:, h * 128:(h + 1) * 128], xdt[:, h * 64:(h + 1) * 64], True, True,
                   ["MT", "xdt"], ["ps%d" % (h // 8)])
            for g in range(2):
                MM(bank(2 + g), BCT[:, (2 + g) * 512 + k * 128:(2 + g) * 512 + (k + 1) * 128], prevb[:, g * 512:(g + 1) * 512], True, True,
                   ["BCT%d" % (2 + g), "prevb"], ["ps%d" % (2 + g)])
            t1, t2 = tmpf[0], tmpf[1]
            TT("dve", t1[:].rearrange("p (h q) -> p h q", h=16), PS[1][:].rearrange("p (h q) -> p h q", h=16),
               ex3[:, 0:16].unsqueeze(2).to_broadcast([128, 16, 64]), ALU.mult, ["ps2", "ps3", "ex3"], ["tmpf0"])
            TT("dve", t1[:], t1[:], PS[0][:], ALU.add, ["tmpf0", "ps0", "ps1"], ["tmpf0"])
            TT("pool", t2[:].rearrange("p (h q) -> p h q", h=16), xsk,
               cst[:, C_DSK:C_DSK + 16].unsqueeze(2).to_broadcast([128, 16, 64]), ALU.mult, ["xs_t%d" % k, "cst"], ["tmpf1"])
            TT("dve", t1[:], t1[:], t2[:], ALU.add, ["tmpf0", "tmpf1"], ["tmpf0"])
            TT("dve", t1[:], t1[:], zs[:, k * 1024:(k + 1) * 1024], ALU.mult, ["tmpf0", "zs%d" % k], ["tmpf0"])
            MSET("dve", sm[:, SM_SS:SM_SS + 2], 0.0, ["ssq0", "ssq1"])
            for g in range(2):
                ACT(t2[:, g * 512:(g + 1) * 512], t1[:, g * 512:(g + 1) * 512], AF.Square, ["tmpf0", "ssq%d" % g], ["tmpf1", "ssq%d" % g],
                    accum=sm[:, SM_SS + g:SM_SS + g + 1])
            TS("dve", sm[:, SM_RS:SM_RS + 2], sm[:, SM_SS:SM_SS + 2], 1.0 / 512.0, EPS, ALU.mult, ALU.add, ["ssq0", "ssq1"], ["rs0"])
            ACT(sm[:, SM_RS:SM_RS + 2], sm[:, SM_RS:SM_RS + 2], AF.Sqrt, ["rs0"], ["rs1"])
            P.add("dve", lambda e: e.reciprocal(sm[:, SM_RS:SM_RS + 2], sm[:, SM_RS:SM_RS + 2]), r=["rs1"], w=["rs"])
            t3 = tmpf[2]
            for g in range(2):
                TS("dve", t3[:, g * 512:(g + 1) * 512], t1[:, g * 512:(g + 1) * 512], sm[:, SM_RS + g:SM_RS + g + 1], None,
                   ALU.mult, None, ["tmpf0", "rs"], ["tmpf2"])
            for cc in range(8):
                P.add("pe", lambda e, o=PS[3][:, cc * 128:(cc + 1) * 128], ii=t3[:, cc * 128:(cc + 1) * 128]: e.transpose(o, ii, cI),
                      r=["tmpf2", "cst"], w=["ps%d" % (6 + cc // 4)])
            for cc in range(8):
                ACT(ystage[:, cc * 512 + k * 128:cc * 512 + (k + 1) * 128],
                    PS[3][:, cc * 128:(cc + 1) * 128], AF.Copy, ["ps%d" % (6 + cc // 4), "cst"], ["ystage"], scale=normw[:, cc:cc + 1])
            if k < 3:
                for g in range(2):
                    MM(bank(6 + g), B_t[:, k * 256 + g * 128:k * 256 + (g + 1) * 128], xdtd[:, g * 512:(g + 1) * 512], True, True,
                       ["B_t%d" % k, "xdtd0"], ["ps%d" % (6 + g)])
                TT("dve", hcur[:].rearrange("p (h q) -> p h q", h=16), hcur[:].rearrange("p (h q) -> p h q", h=16),
                   ex3[:, 32:48].unsqueeze(2).to_broadcast([128, 16, 64]), ALU.mult, ["hcur", "ex3"], ["hcur"])
                TT("dve", hcur[:], hcur[:], PS[3][:], ALU.add, ["hcur", "ps6", "ps7"], ["hcur"])
        DMA(ysd_d[s], ystage[:], ["ystage"], ["ysd%d" % s], "yst", eng="sp")
        pending.append("ysd%d" % s)

    run_stage(P, st)
    if stop == "O1":
        es.close()
        return nc

    st = ExitStack()
    P = Prog("a")
    if stop == "O2":
        P.limit = limit
    WB = sl("WB2", [128, 16384], BF16)
    wst = [sl("wst%d_2" % i, [128, 2048], F32) for i in range(2)]
    xst = [sl("xst0_2", [128, 4 * 515], F32)] * 2
    xb = [sl("xb0_2", [128, 8 * 515], BF16)] * 2
    rt = [sl("rt0_2", [128, 1024])] * 2
    kc = [sl("kc%d_2" % i, [128, 512], BF16) for i in range(2)]
    tmpf = [sl("tmpf%d_2" % i, [128, 2560 if i == 3 else 512]) for i in range(4)]
    kTall = [sl("qT_2", [128, 8 * 512], BF16), sl("gbT_2", [128, 8 * 512], BF16)]
    vall = [sl("vall%d_2" % i, [128, 4 * 1024], BF16) for i in range(2)]
    pT = [sl("pT%d" % i, [128, 1024], BF16) for i in range(8)]
    sAB = [sl("sAB%d" % i, [128, 1024], BF16) for i in range(3)]
    sC2 = [sl("sC2_%d" % i, [128, 1024], BF16) for i in range(2)]
    m01 = sl("m01", [128, 16 * 512], BF16)
    b01 = sl("b01", [8, 2048], BF16)
    osqb = sl("osqb", [128, 512], BF16)
    mskb = sl("mskb", [8, 2560], BF16)
    ostage = sl("ostage", [128, 4096], BF16)

    OFF_Q, OFF_GB = 0, 8192
    load_weights(w_q, 1024, OFF_Q, "wk")
    load_weights(w_gb, 1024, OFF_GB, "wvv")
    wq_v, wgb_v = wv(OFF_Q, 1024), wv(OFF_GB, 1024)
    mA = mskb[0:8, 0:512]
    ldc = [0]
    for s in range(4):
        i = 0
        xv = load_x(xT_own[:, :, s, :], 515, i, eng="mix")
        xk = "xb%d" % i
        rtab = rt[i]
        DMA(rtab[:].rearrange("p (a c) -> p a c", a=2), ropeO[:, :, s, :], (), ["rt0"], "rt0")
        DMA(tmpf[3][0:8, 0:512], msk_d[:, 0:512], (), ["tmpf3"], "mk0")
        DMA(tmpf[3][0:8, 512:2560], msk_d[:, 512 + s * 2048:512 + (s + 1) * 2048], (), ["tmpf3"], "mk1")
        CP("dve", mskb[:], tmpf[3][0:8, 0:2560], ["tmpf3"], ["mskb"])
        qT = kTall[0]
        gbT = kTall[1]
        for h in range(8):
            pk = h % 2
            for dc in range(8):
                MM(bank(pk), wq_v[:, dc, h * 128:(h + 1) * 128], xv[:, dc, 3:515], dc == 0, dc == 7, ["wk", xk], ["ps%d" % pk])
            kcb = kc[h % 2]
            kk = "kc%d" % (h % 2)
            P.add("act", lambda e, o=kcb[:], ii=bank(pk): e.copy(o, ii), r=["ps%d" % pk], w=[kk])
            MM(bank(2 + pk), bPERM, kcb[:], True, True, ["cbf", kk], ["ps%d" % (2 + pk)])
            t = tmpf[h % 2][:, 0:512]
            TT("dve", t, bank(2 + pk), rtab[:, 512:1024], ALU.mult, ["ps%d" % (2 + pk), "rt0"], ["tmpf%d" % (h % 2)])
            TT("dve", tmpf[2 + h % 2][:, 0:512], kcb[:], rtab[:, 0:512], ALU.mult, [kk, "rt0"], ["tmpf%d" % (2 + h % 2)])
            TT("dve", qT[:, h * 512:(h + 1) * 512], tmpf[2 + h % 2][:, 0:512], t, ALU.add,
               ["tmpf%d" % (h % 2), "tmpf%d" % (2 + h % 2)], ["kTall0"])
            pb = 4 + h % 2
            for dc in range(8):
                MM(bank(pb), wgb_v[:, dc, h * 128:(h + 1) * 128], xv[:, dc, 3:515], dc == 0, dc == 7, ["wvv", xk], ["ps%d" % pb])
            ACT(gbT[:, h * 512:(h + 1) * 512], bank(pb), AF.Silu, ["ps%d" % pb], ["kTall1"])
        TS("dve", b01[:], mskb[0:8, 512:2560], 0.0, None, ALU.is_equal, None, ["mskb"], ["b01"])
        for pair in range(4):
            for kb in range(4):
                idx = pair * 4 + kb
                pbm = idx % 4
                MM(bank(pbm), mA[:, kb * 128:(kb + 1) * 128], b01[0:8, pair * 512:(pair + 1) * 512], True, True, ["mskb", "b01"], ["ps%d" % pbm])
                evac(m01[:, idx * 512:(idx + 1) * 512], bank(pbm), ["ps%d" % pbm], ["m01"])
        L = KLEN[s]
        fin_q = []
        for h in range(8):
            units = []
            ng = L // 4
            jbase = ldc[0]
            ldc[0] += ng
            gbuf = {}
            for kg in range(ng):
                j = (jbase + kg) % 2
                kTg = vall[j][:, 0:2048]
                vg = vall[j][:, 2048:4096].rearrange("p (kb e) -> p kb e", kb=16)
                gbuf[kg] = (j, kTg, vg)
                for ktl in range(4):
                    for kb in range(4):
                        units.append((j, kTg, vg, ktl, kg * 4 + ktl, kb))

            def issue(kg):
                j, kTg, vg = gbuf[kg]
                DMA(kTg, ksc[h, :, kg * 2048:(kg + 1) * 2048], ["ksc%d" % t_ for t_ in range(kg * 4, kg * 4 + 4)], ["kTg%d" % j], "kld%d" % j)
                DMA(vg, vsc[kg * 2048:(kg + 1) * 2048, h * 128:(h + 1) * 128].rearrange("(kb p) e -> p kb e", p=128),
                    ["vsc%d" % t_ for t_ in range(kg * 4, kg * 4 + 4)], ["vg%d" % j], "vld%d" % j)

            issue(0)
            if ng > 1:
                issue(1)
            nu = len(units)

            def qk(u):
                j, kTg, vg, ktl, kt, kb = units[u]
                for m in range(2):
                    pb = (u % 2) * 2 + m
                    MM(bank(pb), kTg[m * 64:(m + 1) * 64, ktl * 512 + kb * 128:ktl * 512 + (kb + 1) * 128],
                       qT[m * 64:(m + 1) * 64, h * 512:(h + 1) * 512], True, True, ["kTg%d" % j, "kTall0"], ["ps%d" % pb])

            def ex(u):
                j, kTg, vg, ktl, kt, kb = units[u]
                pk_ = "pT%d" % (u % 8)
                ACT(pT[u % 8][:], PS[u % 2][:], AF.Exp, ["ps%d" % ((u % 2) * 2), "ps%d" % ((u % 2) * 2 + 1)], [pk_], scale=0.125)
                if kt >= L - 4:
                    idx = (kt - (L - 4)) * 4 + kb
                    pv3 = pT[u % 8][:].rearrange("p (m q) -> p m q", m=2)
                    TT("dve", pv3, pv3, m01[:, idx * 512:(idx + 1) * 512].unsqueeze(1).to_broadcast([128, 2, 512]), ALU.mult,
                       [pk_, "m01"], [pk_])

            def pv(u):
                j, kTg, vg, ktl, kt, kb = units[u]
                for m in range(2):
                    MM(bank(4 + m), vg[:, ktl * 4 + kb, :], pT[u % 8][:, m * 512:(m + 1) * 512], u == 0, u == nu - 1,
                       ["vg%d" % j, "pT%d" % (u % 8)], ["ps%d" % (4 + m)])

            def adds(g):
                u0 = 4 * g
                TT("dve", sAB[0][:], pT[u0 % 8][:], pT[(u0 + 1) % 8][:], ALU.add, ["pT%d" % (u0 % 8), "pT%d" % ((u0 + 1) % 8)], ["sAB0"])
                TT("dve", sAB[1][:], pT[(u0 + 2) % 8][:], pT[(u0 + 3) % 8][:], ALU.add, ["pT%d" % ((u0 + 2) % 8), "pT%d" % ((u0 + 3) % 8)], ["sAB1"])
                sc = sC2[g % 2]
                TT("dve", sc[:], sAB[0][:], sAB[1][:], ALU.add, ["sAB0", "sAB1"], ["sC%d" % (g % 2)])

            def sums(g):
                sc = sC2[g % 2]
                for m in range(2):
                    MM(bank(6 + m), bONE, sc[:, m * 512:(m + 1) * 512], g == 0, g == nu // 4 - 1, ["cbf", "sC%d" % (g % 2)], ["ps%d" % (6 + m)])

            qk(0)
            qk(1)
            for u in range(nu):
                if u % 16 == 1 and u // 16 >= 1 and u // 16 + 1 < ng:
                    issue(u // 16 + 1)
                ex(u)
                if u + 2 < nu:
                    qk(u + 2)
                if u >= 1:
                    pv(u - 1)
                if u % 4 == 3:
                    adds(u // 4)
                if u % 4 == 2 and u >= 6:
                    sums(u // 4 - 1)
                for _ in range(3):
                    if fin_q:
                        fin_q.pop(0)()
            pv(nu - 1)
            sums(nu // 4 - 1)
            ob0, ob1, r0 = tmpf[1][:, 0:512], tmpf[2][:, 0:512], tmpf[0][:, 0:512]
            fs0, fs1 = tmpf[3][:, 0:512], tmpf[3][:, 512:1024]
            while fin_q:
                fin_q.pop(0)()
            CP("dve", ob0, bank(4), ["ps4"], ["tmpf1"])
            P.add("act", lambda e: e.copy(ob1, bank(5)), r=["ps5"], w=["tmpf2"])
            CP("dve", fs0, bank(6), ["ps6"], ["tmpf3"])
            P.add("act", lambda e: e.copy(fs1, bank(7)), r=["ps7"], w=["tmpf3"])

            def mk_fin(h=h):
                q = []
                q.append(lambda: ACT(r0, fs0, AF.Ln, ["tmpf3"], ["tmpf0"]))
                q.append(lambda: ACT(r0, r0, AF.Exp, ["tmpf0"], ["tmpf0"], scale=-1.0))
                q.append(lambda: TT("dve", ob0, ob0, r0, ALU.mult, ["tmpf1", "tmpf0"], ["tmpf1"]))
                q.append(lambda: ACT(r0, fs1, AF.Ln, ["tmpf3"], ["tmpf0"]))
                q.append(lambda: ACT(r0, r0, AF.Exp, ["tmpf0"], ["tmpf0"], scale=-1.0))
                q.append(lambda: TT("dve", ob1, ob1, r0, ALU.mult, ["tmpf2", "tmpf0"], ["tmpf2"]))
                q.append(lambda: STT("dve", ob0, ob1, sm[:, SM_NLAM:SM_NLAM + 1], ob0, ALU.mult, ALU.add, ["tmpf1", "tmpf2", "nlam"], ["tmpf1"]))
                q.append(lambda: TT("dve", osqb[:], ob0, ob0, ALU.mult, ["tmpf1"], ["osqb"]))
                q.append(lambda: MM(bank(7), bONE, osqb[:], True, True, ["cbf", "osqb"], ["ps7"]))
                q.append(lambda: ACT(r0, bank(7), AF.Ln, ["ps7"], ["tmpf0"], scale=1.0 / 128.0, bias=EPS))
                q.append(lambda: ACT(r0, r0, AF.Exp, ["tmpf0"], ["tmpf0"], scale=-0.5))
                q.append(lambda: TT("dve", ob0, ob0, r0, ALU.mult, ["tmpf1", "tmpf0"], ["tmpf1"]))
                q.append(lambda: STT("dve", ostage[:, h * 512:(h + 1) * 512], ob0, sm[:, SM_SLN:SM_SLN + 1], gbT[:, h * 512:(h + 1) * 512],
                                     ALU.mult, ALU.mult, ["tmpf1", "sln", "kTall1"], ["ostage"]))
                return q
            fin_q.extend(mk_fin())
        while fin_q:
            fin_q.pop(0)()
        DMA(od_d[s], ostage[:], ["ostage"], ["od%d" % s], "ost", eng="sp")
        pending.append("od%d" % s)

    run_stage(P, st)
    if stop == "O2":
        es.close()
        return nc

    st = ExitStack()
    P = Prog("m")
    if stop == "O3":
        P.limit = limit
    WB = sl("WB3", [128, 28800], BF16)
    wst = [sl("wst%d_3" % i, [128, 2048], F32) for i in range(2)]
    xst = [sl("xst0_3", [128, 4 * 515], F32)] * 2
    xb = [sl("xb0_3", [128, 8 * 515], BF16)] * 2
    lnp = sl("lnpt", [128, 2048])
    yT = sl("yT", [128, 4096], BF16)
    oTs = sl("oTs", [128, 4096], BF16)
    merged = sl("merged", [128, 8 * 512], BF16)
    gt = [sl("gt%d" % i, [128, 512]) for i in range(4)]
    xtok = sl("xtok", [128, 4 * 1024])
    tmpf = [sl("tmpf%d_3" % i, [128, 1024]) for i in range(4)]
    DMA(lnp[:], lnp_d, (), ["lnp"], "c2")

    OFF_A, OFF_B, OFF_O, OFF_GM = 0, 8192, 16384, 24576
    load_weights(w_a, 1024, OFF_A, "wk")
    load_weights(w_b, 1024, OFF_B, "wvv")
    load_weights(w_o, 1024, OFF_O, "wx")
    wa_v, wb_v, wo_v = wv(OFF_A, 1024), wv(OFF_B, 1024), wv(OFF_O, 1024)
    gmi = [0]
    for s in range(4):
        i = 0
        xv = load_x(xT_own[:, :, s, :], 515, i, eng="mix")
        DMA(yT[:], ysd_d[s], (), ["yT"], "yld")
        DMA(oTs[:], od_d[s], (), ["oTs"], "old")
        xk = "xb%d" % i
        DMA(xtok[:].rearrange("p (k c) -> p k c", k=4), x_own[s].rearrange("(k p) c -> p k c", p=128), (), ["xtok"], "xtok")
        for db in range(8):
            j = gmi[0] % 2
            gmi[0] += 1
            goff = OFF_GM + j * 2048
            stv = wst[j][:, 0:2048].rearrange("p (a b c) -> p a b c", a=8, b=2)
            DMA(stv, w_gm[:, :, :].rearrange("p a (b c) -> p a b c", b=2)[:, :, :, db * 128:(db + 1) * 128], (), ["wst%d" % j], "wst%d" % j)
            CP("dve", WB[:, goff:goff + 2048].rearrange("p (a b c) -> p a b c", a=8, b=2), stv, ["wst%d" % j], ["wgm%d" % j])
            wg = WB[:, goff:goff + 2048].rearrange("p (a b c) -> p a b c", a=8, b=2)
            par = db % 2
            pg, pa = 4 * par, 2 + 4 * par
            g0, g1 = gt[2 * par], gt[2 * par + 1]
            gk0, gk1 = "gt%d" % (2 * par), "gt%d" % (2 * par + 1)
            for br in range(2):
                for dc in range(8):
                    MM(bank(pg + br), wg[:, dc, br, :], xv[:, dc, 3:515], dc == 0, dc == 7, ["wgm%d" % j, xk], ["ps%d" % (pg + br)])
                ACT(gt[2 * par + br][:], bank(pg + br), AF.Sigmoid, ["ps%d" % (pg + br), "cst"], ["gt%d" % (2 * par + br)],
                    bias=cst[:, C_BG + br * 8 + db:C_BG + br * 8 + db + 1])
            for cc in range(8):
                MM(bank(pa), wa_v[:, cc, db * 128:(db + 1) * 128], yT[:, cc * 512:(cc + 1) * 512], cc == 0, cc == 7,
                   ["wk", "yT"], ["ps%d" % pa])
            for cc in range(8):
                MM(bank(pa + 1), wb_v[:, cc, db * 128:(db + 1) * 128], oTs[:, cc * 512:(cc + 1) * 512], cc == 0, cc == 7,
                   ["wvv", "oTs"], ["ps%d" % (pa + 1)])
            TT("dve", g0[:], g0[:], bank(pa), ALU.mult, [gk0, "ps%d" % pa], [gk0])
            TT("dve", g1[:], g1[:], bank(pa + 1), ALU.mult, [gk1, "ps%d" % (pa + 1)], [gk1])
            TT("dve", merged[:, db * 512:(db + 1) * 512], g0[:], g1[:], ALU.add, [gk0, gk1], ["merged"])
        for k in range(4):
            vb = tmpf[k % 2]
            vk = "tmpf%d" % (k % 2)
            for half in range(2):
                pb = 4 + half
                for db in range(8):
                    MM(bank(pb), merged[:, db * 512 + k * 128:db * 512 + (k + 1) * 128], wo_v[:, db, half * 512:(half + 1) * 512], db == 0, db == 7,
                       ["merged", "wx"], ["ps%d" % pb])
            STT("dve", vb[:], xtok[:, k * 1024:(k + 1) * 1024], ALPHA, PS[2][:], ALU.mult, ALU.add, ["xtok", "ps4", "ps5"], [vk])
            c0 = SM_NM + (k % 2) * 4
            P.add("dve", lambda e, o=sm[:, c0:c0 + 1], ii=vb[:]: e.reduce_sum(o, ii, mybir.AxisListType.X), r=[vk], w=["ln_a%d" % (k % 2)])
            TS("dve", sm[:, c0:c0 + 1], sm[:, c0:c0 + 1], -1.0 / D, None, ALU.mult, None, ["ln_a%d" % (k % 2)], ["ln_b%d" % (k % 2)])
            sq = tmpf[2 + k % 2]
            MSET("dve", sm[:, c0 + 1:c0 + 2], 0.0, ["ln_c%d" % (k % 2)])
            ACT(sq[:], vb[:], AF.Square, [vk, "ln_b%d" % (k % 2), "ln_c%d" % (k % 2)], ["tmpf%d" % (2 + k % 2), "ln_c%d" % (k % 2)],
                bias=sm[:, c0:c0 + 1], accum=sm[:, c0 + 1:c0 + 2])
            TS("dve", sm[:, c0 + 2:c0 + 3], sm[:, c0 + 1:c0 + 2], 1.0 / D, EPS, ALU.mult, ALU.add, ["ln_c%d" % (k % 2)], ["ln_d%d" % (k % 2)])
            ACT(sm[:, c0 + 2:c0 + 3], sm[:, c0 + 2:c0 + 3], AF.Sqrt, ["ln_d%d" % (k % 2)], ["ln_d2%d" % (k % 2)])
            P.add("dve", lambda e, o=sm[:, c0 + 2:c0 + 3]: e.reciprocal(o, o), r=["ln_d2%d" % (k % 2)], w=["ln_e%d" % (k % 2)])
            TS("dve", vb[:], vb[:], sm[:, c0:c0 + 1], sm[:, c0 + 2:c0 + 3], ALU.add, ALU.mult, [vk, "ln_b%d" % (k % 2), "ln_e%d" % (k % 2)], [vk])
            TT("dve", vb[:], vb[:], lnp[:, 0:1024], ALU.mult, [vk, "lnp"], [vk])
            TT("pool", sq[:], vb[:], lnp[:, 1024:2048], ALU.add, [vk, "lnp"], ["tmpf%d" % (2 + k % 2)])
            DMA(y_out[s, k * 128:(k + 1) * 128, :], sq[:], ["tmpf%d" % (2 + k % 2)], ["yout%d_%d" % (s, k)], "yo%d" % (k % 2))
            pending.append("yout%d_%d" % (s, k))
    run_stage(P, st)
    es.close()
    return nc


def _prep_inputs(inputs):
    f = lambda a: np.ascontiguousarray(np.asarray(a, dtype=np.float32))
    x = f(inputs["x"])
    w_in = f(inputs["w_in"])[0]

    def wl(c0, c1):
        return np.ascontiguousarray(w_in[:, c0:c1].reshape(8, 128, c1 - c0).transpose(1, 0, 2))

    def wsq(w):
        return np.ascontiguousarray(f(w)[0].reshape(8, 128, 1024).transpose(1, 0, 2))

    common = {
        "w_z": wl(0, 1024), "w_x": wl(1024, 2560), "w_dt": wl(2560, 2576), "w_q": wl(2576, 3600),
        "w_k": wl(3600, 4624), "w_v": wl(4624, 5648), "w_gb": wl(5648, 6672), "w_gm": wl(6672, 8720),
        "w_a": wsq(inputs["w_a"]), "w_b": wsq(inputs["w_b"]), "w_o": wsq(inputs["w_o"]),
    }
    r = np.arange(128)
    ident = (r[:, None] == r[None, :]).astype(np.float32)
    LT = (r[:, None] <= r[None, :]).astype(np.float32)
    SU = (r[:, None] > r[None, :]).astype(np.float32)
    ones = np.ones((128, 128), np.float32)
    perm = np.zeros((128, 128), np.float32)
    for m in range(2):
        for d in range(8):
            perm[m * 64 + d + 8, m * 64 + d] = -1.0
            perm[m * 64 + d, m * 64 + d + 8] = 1.0
    bc = lambda v, n: np.broadcast_to(f(v).reshape(1, n), (128, n))
    conv_w = f(inputs["conv_w"])[0]
    cw = np.zeros((128, 48), np.float32)
    for blk in range(12):
        for w in range(4):
            cw[:, blk * 4 + w] = conv_w[w, blk * 128:(blk + 1) * 128]
    conv_b = f(inputs["conv_b"])[0]
    cb = conv_b.reshape(12, 128).T
    bg = f(inputs["b_gate"])[0].reshape(16, 128).T
    sln = f(inputs["subln_w"])[0].reshape(128, 1)
    nw = f(inputs["ssd_norm_w"])[0].reshape(8, 128).T
    lnp = np.concatenate([bc(inputs["ln_g"], 1024), bc(inputs["ln_b"], 1024)], axis=1)
    pos = np.arange(S, dtype=np.float32)
    inv_freq = (np.float32(500000.0) ** (-np.arange(0, 16, 2, dtype=np.float32) / np.float32(16))).astype(np.float32)
    ang = (pos[:, None] * inv_freq[None, :]).astype(np.float32)
    cos, sin = np.cos(ang).astype(np.float32), np.sin(ang).astype(np.float32)
    rope = np.zeros((128, 2, S), np.float32)
    rope[:, 0, :] = 1.0
    for m in range(2):
        for dd in range(16):
            rope[m * 64 + dd, 0, :] = cos[:, dd % 8]
            rope[m * 64 + dd, 1, :] = sin[:, dd % 8]
    maskA = np.zeros((8, 512), np.float32)
    for k in range(512):
        maskA[k // 64, k] = 1.0
    in_maps = []
    for c in range(8):
        b, j = c // 4, c % 4
        tiles = [j, 7 - j, 8 + j, 15 - j]
        xb = x[b]
        xT = np.ascontiguousarray(xb.T.reshape(8, 128, S).transpose(1, 0, 2))
        xo = np.zeros((128, 8, 4, 515), np.float32)
        xtok = np.zeros((4, 512, D), np.float32)
        ropeO = np.zeros((128, 2, 4, 512), np.float32)
        sel = np.zeros((128, 16), np.float32)
        maskB = np.zeros((8, 16, 512), np.float32)
        for s, t in enumerate(tiles):
            lo = t * 512
            if t > 0:
                xo[:, :, s, :] = xT[:, :, lo - 3:lo + 512]
            else:
                xo[:, :, s, 3:] = xT[:, :, 0:512]
            xtok[s] = xb[lo:lo + 512]
            ropeO[:, :, s, :] = rope[:, :, lo:lo + 512]
            sel[:, t] = 1.0
            L = KLEN[s]
            for pi in range(4):
                kt = L - 4 + pi
                if kt < t:
                    pass
                elif kt > t:
                    maskB[:, pi + 4 * s, :] = NEG
                else:
                    for rr in range(8):
                        q = np.arange(512)
                        maskB[rr, pi + 4 * s, :] = np.where(q // 64 >= rr, 0.0, NEG)
        cstv = np.concatenate([ident, LT, SU, ones, perm, bc(inputs["a_log"], 16), bc(inputs["dt_bias"], 16), bc(inputs["d_skip"], 16),
                               bc(inputs["lambda_q1"], 64), bc(inputs["lambda_k1"], 64), bc(inputs["lambda_q2"], 64), bc(inputs["lambda_k2"], 64),
                               cw, cb, bg, sln, sel, nw], axis=1).astype(np.float32)
        assert cstv.shape[1] == NCST
        m = dict(common)
        m.update({"xT_all": xT, "xT_own": xo, "x_own": xtok, "cst": np.ascontiguousarray(cstv),
                  "cbrow": conv_b.reshape(1, 1536).copy(), "lnp": np.ascontiguousarray(lnp),
                  "msk": np.ascontiguousarray(np.concatenate([maskA, maskB.reshape(8, 8192)], axis=1)),
                  "ropeA": rope, "ropeO": ropeO})
        in_maps.append(m)
    return in_maps


def kernel(**inputs):
    in_maps = _prep_inputs(inputs)
    nc = build_nc()
    res = run_bass_kernel_spmd(nc, in_maps, core_ids=list(range(8)))
    out = np.zeros((2, S, D), np.float32)
    for c in range(8):
        b, j = c // 4, c % 4
        tiles = [j, 7 - j, 8 + j, 15 - j]
        y = np.asarray(res.results[c]["y_out"], dtype=np.float32)
        for s, t in enumerate(tiles):
            out[b, t * 512:(t + 1) * 512] = y[s]
    return out
```

```python
import math
import sys
import os
from contextlib import ExitStack
import numpy as np
import concourse.bass as bass
import concourse.mybir as mybir
from concourse.bass_utils import run_bass_kernel_spmd

F32 = mybir.dt.float32
BF16 = mybir.dt.bfloat16
AF = mybir.ActivationFunctionType
ALU = mybir.AluOpType

D = 1024
S = 8192
NT = 16
KLEN = (4, 8, 12, 16)
EPS = 1e-5
ALPHA = 2.0 ** 0.25
LAMBDA_INIT = 0.2
NEG = -30000.0

C_ID, C_LT, C_SU, C_ONE, C_PERM = 0, 128, 256, 384, 512
C_ALOG, C_DTB, C_DSK = 640, 656, 672
C_LQ1, C_LK1, C_LQ2, C_LK2 = 688, 752, 816, 880
C_CW, C_CB, C_BG, C_SLN, C_SEL, C_NW = 944, 992, 1004, 1020, 1021, 1037
NCST = 1045


class _Op:
    __slots__ = ("eng", "fn", "deps", "dma", "idx", "need", "sem", "val", "src")


class Prog:
    def __init__(self, tag):
        self.tag = tag
        self.ops = []
        self.wr = {}
        self.rd = {}
        self.base = {}
        self.limit = None

    def add(self, eng, fn, r=(), w=(), dma=None):
        if self.limit is not None and len(self.ops) >= self.limit:
            return None
        o = _Op()
        o.eng, o.fn, o.dma, o.idx, o.need = eng, fn, dma, len(self.ops), False
        f = sys._getframe(1)
        o.src = (f.f_lineno, f.f_back.f_lineno if f.f_back else 0)
        deps = {}
        for k in r:
            for x in self.wr.get(k, ()):
                deps[x] = "raw"
        for k in w:
            if self.rd.get(k):
                self.base[k] = list(self.wr.get(k, ())) + list(self.rd[k])
                self.wr[k] = []
                self.rd[k] = []
            for x in self.base.get(k, ()):
                deps.setdefault(x, "war")
        for k in w:
            self.wr.setdefault(k, []).append(o.idx)
            self.rd.setdefault(k, [])
        for k in r:
            self.rd.setdefault(k, []).append(o.idx)
            self.wr.setdefault(k, [])
        deps.pop(o.idx, None)
        o.deps = deps
        self.ops.append(o)
        return o

    def finalize(self):
        cnt = {}
        for o in self.ops:
            if o.dma is not None:
                o.sem = self.tag + "d_" + o.dma
                cnt[o.sem] = cnt.get(o.sem, 0) + 16
                o.val = cnt[o.sem]
        for o in self.ops:
            best = {}
            for d, kind in o.deps.items():
                p = self.ops[d]
                if p.dma is None:
                    if p.eng == o.eng and o.eng == "pe" and o.dma is None:
                        continue
                    ch = "e_" + p.eng
                else:
                    ch = p.sem
                if ch not in best or best[ch] < d:
                    best[ch] = d
            o.deps = list(best.values())
            for d in o.deps:
                self.ops[d].need = True
        for o in self.ops:
            if o.dma is None and o.need:
                o.sem = self.tag + "e_" + o.eng
                cnt[o.sem] = cnt.get(o.sem, 0) + 1
                o.val = cnt[o.sem]
        return sorted(cnt.keys())

    def emit(self, eng, e, sems):
        waited = {}
        for o in self.ops:
            if o.eng != eng:
                continue
            for d in o.deps:
                p = self.ops[d]
                if waited.get(p.sem, 0) < p.val:
                    e.wait_ge(sems[p.sem], p.val)
                    waited[p.sem] = p.val
            ins = o.fn(e)
            if o.dma is not None:
                ins.then_inc(sems[o.sem], 16)
            elif o.need:
                ins.then_inc(sems[o.sem], 1)


def build_nc(stop=None, nta=NT, limit=None):
    nc = bass.Bass("TRN2", target_bir_lowering=False)
    dt_in = lambda n, shp, t=F32: nc.dram_tensor(n, shp, t, kind="ExternalInput").ap()
    xT_all = dt_in("xT_all", [128, 8, S])
    xT_own = dt_in("xT_own", [128, 8, 4, 515])
    x_own = dt_in("x_own", [4, 512, D])
    w_k = dt_in("w_k", [128, 8, 1024])
    w_v = dt_in("w_v", [128, 8, 1024])
    w_x = dt_in("w_x", [128, 8, 1536])
    w_dt = dt_in("w_dt", [128, 8, 16])
    w_q = dt_in("w_q", [128, 8, 1024])
    w_z = dt_in("w_z", [128, 8, 1024])
    w_gb = dt_in("w_gb", [128, 8, 1024])
    w_gm = dt_in("w_gm", [128, 8, 2048])
    w_a = dt_in("w_a", [128, 8, 1024])
    w_b = dt_in("w_b", [128, 8, 1024])
    w_o = dt_in("w_o", [128, 8, 1024])
    cst_d = dt_in("cst", [128, NCST])
    cbrow_d = dt_in("cbrow", [1, 1536])
    lnp_d = dt_in("lnp", [128, 2048])
    msk_d = dt_in("msk", [8, 512 + 16 * 512])
    ropeA = dt_in("ropeA", [128, 2, S])
    ropeO = dt_in("ropeO", [128, 2, 4, 512])
    y_out = nc.dram_tensor("y_out", [4, 512, D], F32, kind="ExternalOutput").ap()
    ksc = nc.dram_tensor("ksc", [8, 128, S], BF16).ap()
    vsc = nc.dram_tensor("vsc", [S, 1024], BF16).ap()

    hs_d = nc.dram_tensor("hs_d", [4, 128, 1024], F32).ap()
    ysd_d = nc.dram_tensor("ysd_d", [4, 128, 4096], BF16).ap()
    od_d = nc.dram_tensor("od_d", [4, 128, 4096], BF16).ap()

    es = ExitStack()
    st = ExitStack()
    P = Prog("s")
    pending = []
    sb = lambda n, shp, t=F32: es.enter_context(nc.sbuf_tensor(n, shp, t))
    sl = lambda n, shp, t=F32: st.enter_context(nc.sbuf_tensor(n, shp, t))
    cst = sb("cstt", [128, NCST])
    cbf = sb("cbf", [128, 640], BF16)
    diag = sb("diag", [128, 48 * 128], BF16)
    cbrow_b = sb("cbrowb", [1, 1536], BF16)
    onesrow_b = sb("onesrow", [1, 128], BF16)
    sm = sb("sm", [128, 64])
    PS = [es.enter_context(nc.psum_tensor("ps%d" % i, [128, 1024], F32)) for i in range(4)]

    def run_stage(P, st):
        if os.environ.get("KDBG"):
            print("STAGE", P.tag, "nops", len(P.ops), "last", P.ops[-1].eng, P.ops[-1].src)
        P.limit = None
        P.add("sp", lambda e: e.nop(), r=list(pending), w=["done"])
        del pending[:]
        names = P.finalize()
        sems = {n: st.enter_context(nc.semaphore(n)) for n in names}
        with nc.Block() as block:
            @block.sync
            def _(e):
                P.emit("sp", e, sems)

            @block.tensor
            def _(e):
                P.emit("pe", e, sems)

            @block.scalar
            def _(e):
                P.emit("act", e, sems)

            @block.vector
            def _(e):
                P.emit("dve", e, sems)

            @block.gpsimd
            def _(e):
                P.emit("pool", e, sems)
        st.close()
        nc.all_engine_barrier()

    WB = sl("WB", [128, 28800], BF16)
    wst = [sl("wst%d" % i, [128, 2048], F32) for i in range(2)]
    xst = [sl("xst0", [128, 4 * 515], F32)] * 2
    xb = [sl("xb%d" % i, [128, 8 * 515], BF16) for i in range(2)]
    uT = sl("uT", [128, 12 * 515], BF16)
    rt = [sl("rt0", [128, 1024])] * 2
    kc = [sl("kc%d" % i, [128, 512], BF16) for i in range(2)]
    tmpf = [sl("tmpf%d" % i, [128, 1536 if i == 3 else 512]) for i in range(4)]
    kTall = [sl("kTall0", [128, 8 * 512], BF16)] * 2
    vall = [sl("vall0", [128, 4 * 1024], BF16)] * 2
    xs_t = sl("xs_t", [128, 4 * 1024], BF16)
    B_t = sl("B_t", [128, 4 * 256], BF16)
    dtt = sl("dtt", [128, 64])
    adt = sl("adt", [128, 64])
    dk = sl("dk", [128, 64])
    wl = sl("wl", [128, 64])
    cdT = sl("cdT", [128, 16])
    xdtd = sl("xdtd", [128, 4 * 1024], BF16)
    hcur = sl("hcur", [128, 1024])
    hs = [sl("hs%d" % i, [128, 1024]) for i in range(4)]

    def bank(i):
        return PS[i // 2][:, (i % 2) * 512:(i % 2) * 512 + 512]

    cI = cst[:, C_ID:C_ID + 128]
    cLT = cst[:, C_LT:C_LT + 128]
    cSU = cst[:, C_SU:C_SU + 128]
    cONE = cst[:, C_ONE:C_ONE + 128]
    bI = cbf[:, 0:128]
    bONE = cbf[:, 384:512]
    bPERM = cbf[:, 512:640]
    SM_A, SM_NLAM, SM_T0, SM_T1, SM_SLN, SM_NM, SM_SS, SM_RS = 0, 16, 17, 18, 19, 20, 24, 28

    dma_rr = [0]

    def DMA(out, in_, r, w, key, eng="sp"):
        P.add(eng, lambda e, o=out, i=in_: e.dma_start(out=o, in_=i), r=r, w=w, dma=key)

    def MM(out, lhsT, rhs, start, stop, r, w):
        P.add("pe", lambda e, o=out, l=lhsT, rr=rhs, s=start, t=stop: e.matmul(o, lhsT=l, rhs=rr, start=s, stop=t),
              r=r, w=w)

    def ACT(out, in_, func, r, w, bias=None, scale=None, accum=None):
        def fn(e, o=out, i=in_, f=func, b=bias, sc=scale, a=accum):
            kw = {}
            if b is not None:
                kw["bias"] = b
            if sc is not None:
                kw["scale"] = sc
            if a is not None:
                kw["accum_out"] = a
            return e.activation(o, i, f, **kw)
        P.add("act", fn, r=r, w=w)

    def TT(eng, out, in0, in1, op, r, w):
        P.add(eng, lambda e, o=out, a=in0, b=in1, p=op: e.tensor_tensor(out=o, in0=a, in1=b, op=p), r=r, w=w)

    def TS(eng, out, in0, s1, s2, op0, op1, r, w):
        if op1 is None:
            P.add(eng, lambda e, o=out, a=in0, x=s1, p=op0: e.tensor_scalar(o, a, x, None, p), r=r, w=w)
        else:
            P.add(eng, lambda e, o=out, a=in0, x=s1, y=s2, p=op0, q=op1: e.tensor_scalar(o, a, x, y, p, q), r=r, w=w)

    def STT(eng, out, in0, sc, in1, op0, op1, r, w):
        P.add(eng, lambda e, o=out, a=in0, s=sc, b=in1, p=op0, q=op1: e.scalar_tensor_tensor(o, a, s, b, p, q), r=r, w=w)

    def CP(eng, out, in_, r, w):
        P.add(eng, lambda e, o=out, i=in_: e.tensor_copy(out=o, in_=i), r=r, w=w)

    def MSET(eng, ap, v, w):
        P.add(eng, lambda e, a=ap, x=v: e.memset(a, x), r=(), w=w)

    DMA(cst[:], cst_d, (), ["cst"], "c0")
    DMA(tmpf[3][0:1, 0:1536], cbrow_d, (), ["tmpf3"], "c1")
    CP("dve", cbf[:], cst[:, 0:640], ["cst"], ["cbf"])
    CP("dve", cbrow_b[:], tmpf[3][0:1, 0:1536], ["tmpf3"], ["cbrowb"])
    MSET("dve", onesrow_b[:], 1.0, ["onesrow"])
    MSET("dve", uT[:], 0.0, ["uT%d" % b for b in range(12)])
    MSET("dve", hcur[:], 0.0, ["hcur"])
    for i in range(4):
        MSET("dve", hs[i][:], 0.0, ["hs%d" % i])
    for blk in range(12):
        for w in range(4):
            TS("dve", diag[:, (blk * 4 + w) * 128:(blk * 4 + w + 1) * 128], cI,
               cst[:, C_CW + blk * 4 + w:C_CW + blk * 4 + w + 1], None, ALU.mult, None, ["cst"], ["diag"])
    ACT(sm[:, SM_A:SM_A + 16], cst[:, C_ALOG:C_ALOG + 16], AF.Exp, ["cst"], ["sm_a0"])
    TS("dve", sm[:, SM_A:SM_A + 16], sm[:, SM_A:SM_A + 16], -1.0, None, ALU.mult, None, ["sm_a0"], ["sm_a"])
    TT("dve", tmpf[0][:, 0:64], cst[:, C_LQ1:C_LQ1 + 64], cst[:, C_LK1:C_LK1 + 64], ALU.mult, ["cst"], ["lamt0"])
    TT("dve", tmpf[0][:, 64:128], cst[:, C_LQ2:C_LQ2 + 64], cst[:, C_LK2:C_LK2 + 64], ALU.mult, ["cst"], ["lamt1"])
    P.add("dve", lambda e: e.reduce_sum(sm[:, SM_T0:SM_T0 + 1], tmpf[0][:, 0:64], mybir.AxisListType.X), r=["lamt0"], w=["lam_s0"])
    P.add("dve", lambda e: e.reduce_sum(sm[:, SM_T1:SM_T1 + 1], tmpf[0][:, 64:128], mybir.AxisListType.X), r=["lamt1"], w=["lam_s1"])
    ACT(sm[:, SM_T0:SM_T0 + 2], sm[:, SM_T0:SM_T0 + 2], AF.Exp, ["lam_s0", "lam_s1"], ["lam_e"])
    TT("dve", sm[:, SM_NLAM:SM_NLAM + 1], sm[:, SM_T1:SM_T1 + 1], sm[:, SM_T0:SM_T0 + 1], ALU.subtract, ["lam_e"], ["nlam0"])
    TS("dve", sm[:, SM_NLAM:SM_NLAM + 1], sm[:, SM_NLAM:SM_NLAM + 1], -LAMBDA_INIT, None, ALU.add, None, ["nlam0"], ["nlam"])
    TS("dve", sm[:, SM_SLN:SM_SLN + 1], cst[:, C_SLN:C_SLN + 1], 1.0 - LAMBDA_INIT, None, ALU.mult, None, ["cst"], ["sln"])

    wstate = {"i": 0}

    def load_weights(dram, ncols, wb_off, key, scale_rows=None):
        c0 = 0
        while c0 < ncols:
            cw = min(256, ncols - c0)
            i = wstate["i"] % 2
            wstate["i"] += 1
            stv = wst[i][:, 0:8 * cw].rearrange("p (a c) -> p a c", a=8)
            DMA(stv, dram[:, :, c0:c0 + cw], (), ["wst%d" % i], "wst%d" % i)
            dst = WB[:, wb_off:wb_off + 8 * ncols].rearrange("p (a c) -> p a c", a=8)[:, :, c0:c0 + cw]
            if wstate["i"] % 2:
                CP("dve", dst, stv, ["wst%d" % i], [key])
            else:
                P.add("act", lambda e, o=dst, ii=stv: e.copy(o, ii), r=["wst%d" % i], w=[key])
            c0 += cw

    def wv(wb_off, ncols):
        return WB[:, wb_off:wb_off + 8 * ncols].rearrange("p (a c) -> p a c", a=8)

    def load_x(src, ncol, i, eng="pool"):
        xv = xb[i][:, 0:8 * ncol].rearrange("p (a c) -> p a c", a=8)
        for hf in range(2):
            stv = xst[0][:, 0:4 * ncol].rearrange("p (a c) -> p a c", a=4)
            DMA(stv, src[:, hf * 4:(hf + 1) * 4, :], (), ["xst0"], "xst0")
            if eng == "mix":
                if hf == 0:
                    CP("dve", xv[:, hf * 4:(hf + 1) * 4, :], stv, ["xst0"], ["xb%d" % i])
                else:
                    P.add("act", lambda e, o=xv[:, hf * 4:(hf + 1) * 4, :], ii=stv: e.copy(o, ii), r=["xst0"], w=["xb%d" % i])
            else:
                CP(eng, xv[:, hf * 4:(hf + 1) * 4, :], stv, ["xst0"], ["xb%d" % i])
        return xv

    ev = {"i": 0}

    def evac(out, in_, r, w):
        ev["i"] += 1
        if ev["i"] % 2:
            P.add("act", lambda e, o=out, i=in_: e.copy(o, i), r=r, w=w)
        else:
            CP("dve", out, in_, r, w)

    def softplus_dt(psap, k, rk):
        d = dtt[:, k * 16:(k + 1) * 16]
        TT("dve", d, psap, cst[:, C_DTB:C_DTB + 16], ALU.add, [rk, "cst"], ["dtt%d" % k])
        ACT(d, d, AF.Exp, ["dtt%d" % k], ["dtt%d" % k])
        ACT(d, d, AF.Ln, ["dtt%d" % k], ["dtt%d" % k], bias=1.0)
        TT("dve", adt[:, k * 16:(k + 1) * 16], d, sm[:, SM_A:SM_A + 16], ALU.mult, ["dtt%d" % k, "sm_a"], ["adt%d" % k])

    def conv_tok(k, blks, pb, xoff):
        for gi in range(0, len(blks), 4):
            grp = blks[gi:gi + 4]
            pk = "ps%d" % pb
            for j, blk in enumerate(grp):
                o = bank(pb)[:, j * 128:(j + 1) * 128]
                for w in range(4):
                    MM(o, uT[:, blk * 515 + k * 128 + w: blk * 515 + k * 128 + w + 128],
                       diag[:, (blk * 4 + w) * 128:(blk * 4 + w + 1) * 128], w == 0, False,
                       ["uT%d" % blk, "diag"], [pk])
                MM(o, onesrow_b[0:1, :], cbrow_b[0:1, blk * 128:(blk + 1) * 128], False, True,
                   ["onesrow", "cbrowb"], [pk])
            n = len(grp) * 128
            b0 = grp[0]
            if b0 < 8:
                ACT(xs_t[:, k * 1024 + b0 * 128:k * 1024 + b0 * 128 + n], bank(pb)[:, 0:n], AF.Silu, [pk], ["xs_t%d" % k])
            else:
                ACT(B_t[:, k * 256:(k + 1) * 256], bank(pb)[:, 0:n], AF.Silu, [pk], ["B_t%d" % k])
            pb = pb + 1 if pb % 2 == 0 else pb - 1
        return pb

    def proj_u(xv, blks, wx_off, col0, pbs, halo):
        wx = wv(wx_off, 1536)
        for bi, blk in enumerate(blks):
            pb = pbs[bi % len(pbs)]
            pk = "ps%d" % pb
            uk = "uT%d" % blk
            if halo == "carry":
                CP("dve", uT[:, blk * 515:blk * 515 + 3], uT[:, blk * 515 + 512:blk * 515 + 515], [uk], [uk])
            else:
                for dc in range(8):
                    MM(bank(pb)[:, 0:3], wx[:, dc, blk * 128:(blk + 1) * 128], xv[:, dc, 0:3], dc == 0, dc == 7,
                       ["wx", halo], [pk])
                evac(uT[:, blk * 515:blk * 515 + 3], bank(pb)[:, 0:3], [pk], [uk])
            for dc in range(8):
                MM(bank(pb), wx[:, dc, blk * 128:(blk + 1) * 128], xv[:, dc, col0:col0 + 512], dc == 0, dc == 7,
                   ["wx", halo if halo != "carry" else "xbcur"], [pk])
            evac(uT[:, blk * 515 + 3:blk * 515 + 515], bank(pb), [pk], [uk])

    OFF_K, OFF_V, OFF_X, OFF_DT = 0, 8192, 16384, 16384 + 12288
    load_weights(w_k, 1024, OFF_K, "wk")
    load_weights(w_v, 1024, OFF_V, "wvv")
    load_weights(w_x, 1536, OFF_X, "wx")
    load_weights(w_dt, 16, OFF_DT, "wdt")
    wk_v, wv_v, wdt_v = wv(OFF_K, 1024), wv(OFF_V, 1024), wv(OFF_DT, 16)

    def rope_block(h, pb, srcw, srck, xv, col0, rtab, dst, dstk, xk):
        pk = "ps%d" % pb
        for dc in range(8):
            MM(bank(pb), srcw[:, dc, h * 128:(h + 1) * 128], xv[:, dc, col0:col0 + 512], dc == 0, dc == 7, [srck, xk], [pk])
        kcb = kc[h % 2]
        kk = "kc%d" % (h % 2)
        P.add("act", lambda e, o=kcb[:], i=bank(pb): e.copy(o, i), r=[pk], w=[kk])
        pb2 = pb + 2
        pk2 = "ps%d" % pb2
        MM(bank(pb2), bPERM, kcb[:], True, True, ["cbf", kk], [pk2])
        t = tmpf[h % 2][:, 0:512]
        tk = "tmpf%d" % (h % 2)
        TT("dve", t, bank(pb2), rtab[:, 512:1024], ALU.mult, [pk2, "rt"], [tk])
        TT("dve", tmpf[2 + h % 2][:, 0:512], kcb[:], rtab[:, 0:512], ALU.mult, [kk, "rt"], ["tmpf%d" % (2 + h % 2)])
        TT("dve", dst, tmpf[2 + h % 2][:, 0:512], t, ALU.add, [tk, "tmpf%d" % (2 + h % 2)], [dstk])

    xv_next = load_x(xT_all[:, :, 0:512], 512, 0)
    for T in range(nta):
        i = T % 2
        xv = xv_next
        xk = "xb%d" % i
        if T + 1 < nta:
            xv_next = load_x(xT_all[:, :, (T + 1) * 512:(T + 2) * 512], 512, 1 - i)
        rtab = rt[i]
        DMA(rtab[:].rearrange("p (a c) -> p a c", a=2), ropeA[:, :, T * 512:(T + 1) * 512], (), ["rt0"], "rt0")
        def kproj(h):
            pk = h % 2
            for dc in range(8):
                MM(bank(pk), wk_v[:, dc, h * 128:(h + 1) * 128], xv[:, dc, :], dc == 0, dc == 7, ["wk", xk], ["ps%d" % pk])
        kproj(0)
        for h in range(8):
            pk = h % 2
            kcb = kc[h % 2]
            kk = "kc%d" % (h % 2)
            P.add("act", lambda e, o=kcb[:], ii=bank(pk): e.copy(o, ii), r=["ps%d" % pk], w=[kk])
            if h + 1 < 8:
                kproj(h + 1)
            MM(bank(2 + pk), bPERM, kcb[:], True, True, ["cbf", kk], ["ps%d" % (2 + pk)])
            t = tmpf[h % 2][:, 0:512]
            TT("dve", t, bank(2 + pk), rtab[:, 512:1024], ALU.mult, ["ps%d" % (2 + pk), "rt0"], ["tmpf%d" % (h % 2)])
            TT("dve", tmpf[2 + h % 2][:, 0:512], kcb[:], rtab[:, 0:512], ALU.mult, [kk, "rt0"], ["tmpf%d" % (2 + h % 2)])
            TT("dve", kTall[i][:, h * 512:(h + 1) * 512], tmpf[2 + h % 2][:, 0:512], t, ALU.add,
               ["tmpf%d" % (h % 2), "tmpf%d" % (2 + h % 2)], ["kTall0"])
        DMA(ksc[:, :, T * 512:(T + 1) * 512].rearrange("h r t -> r h t"),
            kTall[i][:].rearrange("p (h t) -> p h t", h=8), ["kTall0"], ["ksc%d" % T], "kst0", eng="pool")
        pending.append("ksc%d" % T)
        for k in range(4):
            for half in range(2):
                pb = 4 + (k * 2 + half) % 2
                for dc in range(8):
                    MM(bank(pb), xv[:, dc, k * 128:(k + 1) * 128], wv_v[:, dc, half * 512:(half + 1) * 512], dc == 0, dc == 7,
                       ["wvv", xk], ["ps%d" % pb])
                evac(vall[i][:, k * 1024 + half * 512:k * 1024 + half * 512 + 512], bank(pb), ["ps%d" % pb], ["vall0"])
        DMA(vsc[T * 512:(T + 1) * 512, :].rearrange("(k p) c -> p k c", p=128),
            vall[i][:].rearrange("p (k c) -> p k c", k=4), ["vall0"], ["vsc%d" % T], "vst0", eng="pool")
        pending.append("vsc%d" % T)
        wx = wv(OFF_X, 1536)
        for bi, blk in enumerate(range(10)):
            pb = 6 + bi % 2
            uk = "uT%d" % blk
            CP("dve", uT[:, blk * 515:blk * 515 + 3], uT[:, blk * 515 + 512:blk * 515 + 515], [uk], [uk])
            for dc in range(8):
                MM(bank(pb), wx[:, dc, blk * 128:(blk + 1) * 128], xv[:, dc, :], dc == 0, dc == 7, ["wx", xk], ["ps%d" % pb])
            evac(uT[:, blk * 515 + 3:blk * 515 + 515], bank(pb), ["ps%d" % pb], [uk])
        for k in range(4):
            o = bank(4)[:, k * 16:(k + 1) * 16]
            for dc in range(8):
                MM(o, xv[:, dc, k * 128:(k + 1) * 128], wdt_v[:, dc, :], dc == 0, dc == 7, ["wdt", xk], ["ps4"])
        for k in range(4):
            softplus_dt(bank(4)[:, k * 16:(k + 1) * 16], k, "ps4")
        pb = 0
        for k in range(4):
            pb = conv_tok(k, list(range(8)), pb, 0)
            pb = conv_tok(k, [8, 9], pb, 0)
        for k in range(4):
            o = bank(5)[:, k * 16:(k + 1) * 16]
            MM(o, cSU, adt[:, k * 16:(k + 1) * 16], True, k == 3, ["cst", "adt%d" % k], ["ps5"])
            for k2 in range(k + 1, 4):
                MM(o, cONE, adt[:, k2 * 16:(k2 + 1) * 16], False, k2 == 3, ["cst", "adt%d" % k2], ["ps5"])
        o = bank(5)[:, 64:80]
        for k in range(4):
            MM(o, cONE, adt[:, k * 16:(k + 1) * 16], k == 0, k == 3, ["cst", "adt%d" % k], ["ps5"])
        ACT(dk[:, 0:64], bank(5)[:, 0:64], AF.Exp, ["ps5"], ["dk"])
        ACT(cdT[:], bank(5)[:, 64:80], AF.Exp, ["ps5"], ["cdT"])
        TT("dve", wl[:], dtt[:], dk[:], ALU.mult, ["dk"] + ["dtt%d" % k for k in range(4)], ["wl"])
        for k in range(4):
            TT("dve", xdtd[:, k * 1024:(k + 1) * 1024].rearrange("p (h q) -> p h q", h=16),
               xs_t[:, k * 1024:(k + 1) * 1024].rearrange("p (h q) -> p h q", h=16),
               wl[:, k * 16:(k + 1) * 16].unsqueeze(2).to_broadcast([128, 16, 64]), ALU.mult,
               ["xs_t%d" % k, "wl"], ["xdtd%d" % k])
        for g in range(2):
            for k in range(4):
                MM(bank(6 + g), B_t[:, k * 256 + g * 128:k * 256 + (g + 1) * 128], xdtd[:, k * 1024 + g * 512:k * 1024 + (g + 1) * 512],
                   k == 0, k == 3, ["B_t%d" % k, "xdtd%d" % k], ["ps%d" % (6 + g)])
        STT("dve", hs[T // 4][:], hcur[:], cst[:, C_SEL + T:C_SEL + T + 1], hs[T // 4][:], ALU.mult, ALU.add,
            ["hcur", "cst", "hs%d" % (T // 4)], ["hs%d" % (T // 4)])
        TT("dve", hcur[:].rearrange("p (h q) -> p h q", h=16), hcur[:].rearrange("p (h q) -> p h q", h=16),
           cdT[:].unsqueeze(2).to_broadcast([128, 16, 64]), ALU.mult, ["hcur", "cdT"], ["hcur"])
        TT("dve", hcur[:], hcur[:], PS[3][:], ALU.add, ["hcur", "ps6", "ps7"], ["hcur"])

    for i in range(4):
        DMA(hs_d[i], hs[i][:], ["hs%d" % i], ["hsd%d" % i], "hst", eng="sp")
        pending.append("hsd%d" % i)
    run_stage(P, st)
    if stop == "A":
        es.close()
        return nc

    st = ExitStack()
    P = Prog("o")
    if stop == "O1":
        P.limit = limit
    WB = sl("WB1", [128, 28800], BF16)
    wst = [sl("wst%d_1" % i, [128, 2048], F32) for i in range(2)]
    xst = [sl("xst0_1", [128, 4 * 515], F32)] * 2
    xb = [sl("xb0_1", [128, 8 * 515], BF16)] * 2
    uT = sl("uT_1", [128, 12 * 515], BF16)
    tmpf = [sl("tmpf%d_1" % i, [128, 1024]) for i in range(3)]
    xs_t = sl("xs_t_1", [128, 4 * 1024], BF16)
    B_t = sl("B_t_1", [128, 4 * 256], BF16)
    zs = sl("zs", [128, 4 * 1024], BF16)
    BCT = sl("BCT", [128, 4 * 512], BF16)
    dtt = sl("dtt_1", [128, 64])
    adt = sl("adt_1", [128, 64])
    ex3 = sl("ex3", [128, 48])
    xdtd = sl("xdtd_1", [128, 1024], BF16)
    xdt = sl("xdt", [128, 1024], BF16)
    hcur = sl("hcur_1", [128, 1024])
    prevb = sl("prevb", [128, 1024], BF16)
    Rm = sl("Rm", [128, 2048])
    Lex = sl("Lex", [128, 2048], BF16)
    cbm = sl("cbm", [128, 256])
    MT = sl("MT", [128, 2048], BF16)
    ystage = sl("ystage", [128, 4096], BF16)

    OFF_Z = 0
    load_weights(w_z, 1024, OFF_Z, "wk")
    load_weights(w_x, 1536, OFF_X, "wx")
    load_weights(w_dt, 16, OFF_DT, "wdt")
    wdt_v = wv(OFF_DT, 16)
    wz_v = wv(OFF_Z, 1024)
    wx = wv(OFF_X, 1536)
    normw = cst[:, C_NW:C_NW + 8]
    for s in range(4):
        i = 0
        xv = load_x(xT_own[:, :, s, :], 515, i, eng="mix")
        xk = "xb%d" % i
        for bi, blk in enumerate(range(12)):
            pb = bi % 2
            uk = "uT%d" % blk
            for dc in range(8):
                MM(bank(pb)[:, 0:3], wx[:, dc, blk * 128:(blk + 1) * 128], xv[:, dc, 0:3], dc == 0, dc == 7, ["wx", xk], ["ps%d" % pb])
            evac(uT[:, blk * 515:blk * 515 + 3], bank(pb)[:, 0:3], ["ps%d" % pb], [uk])
            for dc in range(8):
                MM(bank(pb), wx[:, dc, blk * 128:(blk + 1) * 128], xv[:, dc, 3:515], dc == 0, dc == 7, ["wx", xk], ["ps%d" % pb])
            evac(uT[:, blk * 515 + 3:blk * 515 + 515], bank(pb), ["ps%d" % pb], [uk])
        for bi, blk in enumerate((8, 9, 10, 11)):
            pb = 2 + bi % 2
            for w in range(4):
                MM(bank(pb), diag[:, (blk * 4 + w) * 128:(blk * 4 + w + 1) * 128], uT[:, blk * 515 + w:blk * 515 + w + 512],
                   w == 0, w == 3, ["uT%d" % blk, "diag"], ["ps%d" % pb])
            ACT(BCT[:, bi * 512:(bi + 1) * 512], bank(pb), AF.Silu, ["ps%d" % pb, "cst"], ["BCT%d" % bi],
                bias=cst[:, C_CB + blk:C_CB + blk + 1])
        for k in range(4):
            o = bank(4)[:, k * 16:(k + 1) * 16]
            for dc in range(8):
                MM(o, xv[:, dc, 3 + k * 128:3 + (k + 1) * 128], wdt_v[:, dc, :], dc == 0, dc == 7, ["wdt", xk], ["ps4"])
        for k in range(4):
            softplus_dt(bank(4)[:, k * 16:(k + 1) * 16], k, "ps4")
        pb = 6
        for k in range(4):
            pb = conv_tok(k, list(range(8)), pb, 0)
            pb = conv_tok(k, [8, 9], pb, 0)
        for k in range(4):
            for half in range(2):
                pb = (k * 2 + half) % 2
                for dc in range(8):
                    MM(bank(pb), xv[:, dc, 3 + k * 128:3 + (k + 1) * 128], wz_v[:, dc, half * 512:(half + 1) * 512], dc == 0, dc == 7,
                       ["wk", xk], ["ps%d" % pb])
                ACT(zs[:, k * 1024 + half * 512:k * 1024 + half * 512 + 512], bank(pb), AF.Silu, ["ps%d" % pb], ["zs%d" % k])
        DMA(hcur[:], hs_d[s], (), ["hcur"], "hld")
        for k in range(4):
            CP("pool", prevb[:], hcur[:], ["hcur"], ["prevb"])
            a_k = adt[:, k * 16:(k + 1) * 16]
            ak = "adt%d" % k
            TT("dve", Rm[:].rearrange("p (h l) -> p h l", h=16), a_k.unsqueeze(2).to_broadcast([128, 16, 128]),
               cLT.unsqueeze(1).to_broadcast([128, 16, 128]), ALU.mult, [ak, "cst"], ["Rm"])
            for q in range(4):
                MM(bank(q), cSU, Rm[:, q * 512:(q + 1) * 512], True, True, ["cst", "Rm"], ["ps%d" % q])
            ACT(Lex[:, 0:1024], PS[0][:], AF.Exp, ["ps0", "ps1"], ["Lex0"])
            ACT(Lex[:, 1024:2048], PS[1][:], AF.Exp, ["ps2", "ps3"], ["Lex1"])
            for g in range(2):
                MM(bank(4)[:, g * 128:(g + 1) * 128], BCT[:, g * 512 + k * 128:g * 512 + (k + 1) * 128],
                   BCT[:, (2 + g) * 512 + k * 128:(2 + g) * 512 + (k + 1) * 128], True, True, ["BCT%d" % g, "BCT%d" % (2 + g)], ["ps4"])
            TT("dve", cbm[:].rearrange("p (g l) -> p g l", g=2), bank(4)[:, 0:256].rearrange("p (g l) -> p g l", g=2),
               cLT.unsqueeze(1).to_broadcast([128, 2, 128]), ALU.mult, ["ps4", "cst"], ["cbm"])
            TT("dve", MT[:].rearrange("p (g h l) -> p g h l", g=2, h=8), Lex[:].rearrange("p (g h l) -> p g h l", g=2, h=8),
               cbm[:].rearrange("p (g l) -> p g l", g=2).unsqueeze(2).to_broadcast([128, 2, 8, 128]), ALU.mult,
               ["Lex0", "Lex1", "cbm"], ["MT"])
            MM(bank(5)[:, 0:16], cLT, a_k, True, True, ["cst", ak], ["ps5"])
            MM(bank(5)[:, 16:32], cSU, a_k, True, True, ["cst", ak], ["ps5"])
            MM(bank(5)[:, 32:48], cONE, a_k, True, True, ["cst", ak], ["ps5"])
            ACT(ex3[:], bank(5)[:, 0:48], AF.Exp, ["ps5"], ["ex3"])
            xsk = xs_t[:, k * 1024:(k + 1) * 1024].rearrange("p (h q) -> p h q", h=16)
            TT("dve", xdt[:].rearrange("p (h q) -> p h q", h=16), xsk,
               dtt[:, k * 16:(k + 1) * 16].unsqueeze(2).to_broadcast([128, 16, 64]), ALU.mult, ["xs_t%d" % k, "dtt%d" % k], ["xdt"])
            TT("dve", xdtd[:, 0:1024].rearrange("p (h q) -> p h q", h=16), xdt[:].rearrange("p (h q) -> p h q", h=16),
               ex3[:, 16:32].unsqueeze(2).to_broadcast([128, 16, 64]), ALU.mult, ["xdt", "ex3"], ["xdtd0"])
            for h in range(16):
                MM(PS[0][:, h * 64:(h + 1) * 64], MT[:, h * 128:(h + 1) * 128], xdt[:, h * 64:(h + 1) * 64], True, True,
                   ["MT", "xdt"], ["ps%d" % (h // 8)])
            for g in range(2):
                MM(bank(2 + g), BCT[:, (2 + g) * 512 + k * 128:(2 + g) * 512 + (k + 1) * 128], prevb[:, g * 512:(g + 1) * 512], True, True,
                   ["BCT%d" % (2 + g), "prevb"], ["ps%d" % (2 + g)])
            t1, t2 = tmpf[0], tmpf[1]
            TT("dve", t1[:].rearrange("p (h q) -> p h q", h=16), PS[1][:].rearrange("p (h q) -> p h q", h=16),
               ex3[:, 0:16].unsqueeze(2).to_broadcast([128, 16, 64]), ALU.mult, ["ps2", "ps3", "ex3"], ["tmpf0"])
            TT("dve", t1[:], t1[:], PS[0][:], ALU.add, ["tmpf0", "ps0", "ps1"], ["tmpf0"])
            TT("pool", t2[:].rearrange("p (h q) -> p h q", h=16), xsk,
               cst[:, C_DSK:C_DSK + 16].unsqueeze(2).to_broadcast([128, 16, 64]), ALU.mult, ["xs_t%d" % k, "cst"], ["tmpf1"])
            TT("dve", t1[:], t1[:], t2[:], ALU.add, ["tmpf0", "tmpf1"], ["tmpf0"])
            TT("dve", t1[:], t1[:], zs[:, k * 1024:(k + 1) * 1024], ALU.mult, ["tmpf0", "zs%d" % k], ["tmpf0"])
            MSET("dve", sm[:, SM_SS:SM_SS + 2], 0.0, ["ssq0", "ssq1"])
            for g in range(2):
                ACT(t2[:, g * 512:(g + 1) * 512], t1[:, g * 512:(g + 1) * 512], AF.Square, ["tmpf0", "ssq%d" % g], ["tmpf1", "ssq%d" % g],
                    accum=sm[:, SM_SS + g:SM_SS + g + 1])
            TS("dve", sm[:, SM_RS:SM_RS + 2], sm[:, SM_SS:SM_SS + 2], 1.0 / 512.0, EPS, ALU.mult, ALU.add, ["ssq0", "ssq1"], ["rs0"])
            ACT(sm[:, SM_RS:SM_RS + 2], sm[:, SM_RS:SM_RS + 2], AF.Sqrt, ["rs0"], ["rs1"])
            P.add("dve", lambda e: e.reciprocal(sm[:, SM_RS:SM_RS + 2], sm[:, SM_RS:SM_RS + 2]), r=["rs1"], w=["rs"])
            t3 = tmpf[2]
            for g in range(2):
                TS("dve", t3[:, g * 512:(g + 1) * 512], t1[:, g * 512:(g + 1) * 512], sm[:, SM_RS + g:SM_RS + g + 1], None,
                   ALU.mult, None, ["tmpf0", "rs"], ["tmpf2"])
            for cc in range(8):
                P.add("pe", lambda e, o=PS[3][:, cc * 128:(cc + 1) * 128], ii=t3[:, cc * 128:(cc + 1) * 128]: e.transpose(o, ii, cI),
                      r=["tmpf2", "cst"], w=["ps%d" % (6 + cc // 4)])
            for cc in range(8):
                ACT(ystage[:, cc * 512 + k * 128:cc * 512 + (k + 1) * 128],
                    PS[3][:, cc * 128:(cc + 1) * 128], AF.Copy, ["ps%d" % (6 + cc // 4), "cst"], ["ystage"], scale=normw[:, cc:cc + 1])
            if k < 3:
                for g in range(2):
                    MM(bank(6 + g), B_t[:, k * 256 + g * 128:k * 256 + (g + 1) * 128], xdtd[:, g * 512:(g + 1) * 512], True, True,
                       ["B_t%d" % k, "xdtd0"], ["ps%d" % (6 + g)])
                TT("dve", hcur[:].rearrange("p (h q) -> p h q", h=16), hcur[:].rearrange("p (h q) -> p h q", h=16),
                   ex3[:, 32:48].unsqueeze(2).to_broadcast([128, 16, 64]), ALU.mult, ["hcur", "ex3"], ["hcur"])
                TT("dve", hcur[:], hcur[:], PS[3][:], ALU.add, ["hcur", "ps6", "ps7"], ["hcur"])
        DMA(ysd_d[s], ystage[:], ["ystage"], ["ysd%d" % s], "yst", eng="sp")
        pending.append("ysd%d" % s)

    run_stage(P, st)
    if stop == "O1":
        es.close()
        return nc

    st = ExitStack()
    P = Prog("a")
    if stop == "O2":
        P.limit = limit
    WB = sl("WB2", [128, 16384], BF16)
    wst = [sl("wst%d_2" % i, [128, 2048], F32) for i in range(2)]
    xst = [sl("xst0_2", [128, 4 * 515], F32)] * 2
    xb = [sl("xb0_2", [128, 8 * 515], BF16)] * 2
    rt = [sl("rt0_2", [128, 1024])] * 2
    kc = [sl("kc%d_2" % i, [128, 512], BF16) for i in range(2)]
    tmpf = [sl("tmpf%d_2" % i, [128, 2560 if i == 3 else 512]) for i in range(4)]
    kTall = [sl("qT_2", [128, 8 * 512], BF16), sl("gbT_2", [128, 8 * 512], BF16)]
    vall = [sl("vall%d_2" % i, [128, 4 * 1024], BF16) for i in range(2)]
    pT = [sl("pT%d" % i, [128, 1024], BF16) for i in range(8)]
    sAB = [sl("sAB%d" % i, [128, 1024], BF16) for i in range(3)]
    sC2 = [sl("sC2_%d" % i, [128, 1024], BF16) for i in range(2)]
    m01 = sl("m01", [128, 16 * 512], BF16)
    b01 = sl("b01", [8, 2048], BF16)
    osqb = sl("osqb", [128, 512], BF16)
    mskb = sl("mskb", [8, 2560], BF16)
    ostage = sl("ostage", [128, 4096], BF16)

    OFF_Q, OFF_GB = 0, 8192
    load_weights(w_q, 1024, OFF_Q, "wk")
    load_weights(w_gb, 1024, OFF_GB, "wvv")
    wq_v, wgb_v = wv(OFF_Q, 1024), wv(OFF_GB, 1024)
    mA = mskb[0:8, 0:512]
    ldc = [0]
    for s in range(4):
        i = 0
        xv = load_x(xT_own[:, :, s, :], 515, i, eng="mix")
        xk = "xb%d" % i
        rtab = rt[i]
        DMA(rtab[:].rearrange("p (a c) -> p a c", a=2), ropeO[:, :, s, :], (), ["rt0"], "rt0")
        DMA(tmpf[3][0:8, 0:512], msk_d[:, 0:512], (), ["tmpf3"], "mk0")
        DMA(tmpf[3][0:8, 512:2560], msk_d[:, 512 + s * 2048:512 + (s + 1) * 2048], (), ["tmpf3"], "mk1")
        CP("dve", mskb[:], tmpf[3][0:8, 0:2560], ["tmpf3"], ["mskb"])
        qT = kTall[0]
        gbT = kTall[1]
        def qproj(h):
            pk = h % 2
            for dc in range(8):
                MM(bank(pk), wq_v[:, dc, h * 128:(h + 1) * 128], xv[:, dc, 3:515], dc == 0, dc == 7, ["wk", xk], ["ps%d" % pk])

        def gproj(h):
            pb = 4 + h % 2
            for dc in range(8):
                MM(bank(pb), wgb_v[:, dc, h * 128:(h + 1) * 128], xv[:, dc, 3:515], dc == 0, dc == 7, ["wvv", xk], ["ps%d" % pb])
            ACT(gbT[:, h * 512:(h + 1) * 512], bank(pb), AF.Silu, ["ps%d" % pb], ["kTall1"])

        qproj(0)
        for h in range(8):
            pk = h % 2
            kcb = kc[h % 2]
            kk = "kc%d" % (h % 2)
            P.add("act", lambda e, o=kcb[:], ii=bank(pk): e.copy(o, ii), r=["ps%d" % pk], w=[kk])
            gproj(h)
            if h + 1 < 8:
                qproj(h + 1)
            MM(bank(2 + pk), bPERM, kcb[:], True, True, ["cbf", kk], ["ps%d" % (2 + pk)])
            t = tmpf[h % 2][:, 0:512]
            TT("dve", t, bank(2 + pk), rtab[:, 512:1024], ALU.mult, ["ps%d" % (2 + pk), "rt0"], ["tmpf%d" % (h % 2)])
            TT("dve", tmpf[2 + h % 2][:, 0:512], kcb[:], rtab[:, 0:512], ALU.mult, [kk, "rt0"], ["tmpf%d" % (2 + h % 2)])
            TT("dve", qT[:, h * 512:(h + 1) * 512], tmpf[2 + h % 2][:, 0:512], t, ALU.add,
               ["tmpf%d" % (h % 2), "tmpf%d" % (2 + h % 2)], ["kTall0"])
        TS("dve", b01[:], mskb[0:8, 512:2560], 0.0, None, ALU.is_equal, None, ["mskb"], ["b01"])
        for pair in range(4):
            for kb in range(4):
                idx = pair * 4 + kb
                pbm = idx % 4
                MM(bank(pbm), mA[:, kb * 128:(kb + 1) * 128], b01[0:8, pair * 512:(pair + 1) * 512], True, True, ["mskb", "b01"], ["ps%d" % pbm])
                evac(m01[:, idx * 512:(idx + 1) * 512], bank(pbm), ["ps%d" % pbm], ["m01"])
        L = KLEN[s]
        fin_q = []
        for h in range(8):
            units = []
            ng = L // 4
            jbase = ldc[0]
            ldc[0] += ng
            gbuf = {}
            for kg in range(ng):
                j = (jbase + kg) % 2
                kTg = vall[j][:, 0:2048]
                vg = vall[j][:, 2048:4096].rearrange("p (kb e) -> p kb e", kb=16)
                gbuf[kg] = (j, kTg, vg)
                for ktl in range(4):
                    for kb in range(4):
                        units.append((j, kTg, vg, ktl, kg * 4 + ktl, kb))

            def issue(kg):
                j, kTg, vg = gbuf[kg]
                DMA(kTg, ksc[h, :, kg * 2048:(kg + 1) * 2048], ["ksc%d" % t_ for t_ in range(kg * 4, kg * 4 + 4)], ["kTg%d" % j], "kld%d" % j)
                DMA(vg, vsc[kg * 2048:(kg + 1) * 2048, h * 128:(h + 1) * 128].rearrange("(kb p) e -> p kb e", p=128),
                    ["vsc%d" % t_ for t_ in range(kg * 4, kg * 4 + 4)], ["vg%d" % j], "vld%d" % j)

            issue(0)
            if ng > 1:
                issue(1)
            nu = len(units)

            def qk(u):
                j, kTg, vg, ktl, kt, kb = units[u]
                for m in range(2):
                    pb = (u % 2) * 2 + m
                    MM(bank(pb), kTg[m * 64:(m + 1) * 64, ktl * 512 + kb * 128:ktl * 512 + (kb + 1) * 128],
                       qT[m * 64:(m + 1) * 64, h * 512:(h + 1) * 512], True, True, ["kTg%d" % j, "kTall0"], ["ps%d" % pb])

            def ex(u):
                j, kTg, vg, ktl, kt, kb = units[u]
                pk_ = "pT%d" % (u % 8)
                ACT(pT[u % 8][:], PS[u % 2][:], AF.Exp, ["ps%d" % ((u % 2) * 2), "ps%d" % ((u % 2) * 2 + 1)], [pk_], scale=0.125)
                if kt >= L - 4:
                    idx = (kt - (L - 4)) * 4 + kb
                    pv3 = pT[u % 8][:].rearrange("p (m q) -> p m q", m=2)
                    TT("dve", pv3, pv3, m01[:, idx * 512:(idx + 1) * 512].unsqueeze(1).to_broadcast([128, 2, 512]), ALU.mult,
                       [pk_, "m01"], [pk_])

            def pv(u):
                j, kTg, vg, ktl, kt, kb = units[u]
                for m in range(2):
                    MM(bank(4 + m), vg[:, ktl * 4 + kb, :], pT[u % 8][:, m * 512:(m + 1) * 512], u == 0, u == nu - 1,
                       ["vg%d" % j, "pT%d" % (u % 8)], ["ps%d" % (4 + m)])

            def adds(g):
                u0 = 4 * g
                TT("dve", sAB[0][:], pT[u0 % 8][:], pT[(u0 + 1) % 8][:], ALU.add, ["pT%d" % (u0 % 8), "pT%d" % ((u0 + 1) % 8)], ["sAB0"])
                TT("dve", sAB[1][:], pT[(u0 + 2) % 8][:], pT[(u0 + 3) % 8][:], ALU.add, ["pT%d" % ((u0 + 2) % 8), "pT%d" % ((u0 + 3) % 8)], ["sAB1"])
                sc = sC2[g % 2]
                TT("dve", sc[:], sAB[0][:], sAB[1][:], ALU.add, ["sAB0", "sAB1"], ["sC%d" % (g % 2)])

            def sums(g):
                sc = sC2[g % 2]
                for m in range(2):
                    MM(bank(6 + m), bONE, sc[:, m * 512:(m + 1) * 512], g == 0, g == nu // 4 - 1, ["cbf", "sC%d" % (g % 2)], ["ps%d" % (6 + m)])

            qk(0)
            qk(1)
            for u in range(nu):
                if u % 16 == 1 and u // 16 >= 1 and u // 16 + 1 < ng:
                    issue(u // 16 + 1)
                ex(u)
                if u + 2 < nu:
                    qk(u + 2)
                if u >= 1:
                    pv(u - 1)
                if u % 4 == 3:
                    adds(u // 4)
                if u % 4 == 2 and u >= 6:
                    sums(u // 4 - 1)
                for _ in range(3):
                    if fin_q:
                        fin_q.pop(0)()
            pv(nu - 1)
            sums(nu // 4 - 1)
            ob0, ob1, r0 = tmpf[1][:, 0:512], tmpf[2][:, 0:512], tmpf[0][:, 0:512]
            fs0, fs1 = tmpf[3][:, 0:512], tmpf[3][:, 512:1024]
            while fin_q:
                fin_q.pop(0)()
            CP("dve", ob0, bank(4), ["ps4"], ["tmpf1"])
            P.add("act", lambda e: e.copy(ob1, bank(5)), r=["ps5"], w=["tmpf2"])
            CP("dve", fs0, bank(6), ["ps6"], ["tmpf3"])
            P.add("act", lambda e: e.copy(fs1, bank(7)), r=["ps7"], w=["tmpf3"])

            def mk_fin(h=h):
                q = []
                q.append(lambda: ACT(r0, fs0, AF.Ln, ["tmpf3"], ["tmpf0"]))
                q.append(lambda: ACT(r0, r0, AF.Exp, ["tmpf0"], ["tmpf0"], scale=-1.0))
                q.append(lambda: TT("dve", ob0, ob0, r0, ALU.mult, ["tmpf1", "tmpf0"], ["tmpf1"]))
                q.append(lambda: ACT(r0, fs1, AF.Ln, ["tmpf3"], ["tmpf0"]))
                q.append(lambda: ACT(r0, r0, AF.Exp, ["tmpf0"], ["tmpf0"], scale=-1.0))
                q.append(lambda: TT("dve", ob1, ob1, r0, ALU.mult, ["tmpf2", "tmpf0"], ["tmpf2"]))
                q.append(lambda: STT("dve", ob0, ob1, sm[:, SM_NLAM:SM_NLAM + 1], ob0, ALU.mult, ALU.add, ["tmpf1", "tmpf2", "nlam"], ["tmpf1"]))
                q.append(lambda: TT("dve", osqb[:], ob0, ob0, ALU.mult, ["tmpf1"], ["osqb"]))
                q.append(lambda: MM(bank(7), bONE, osqb[:], True, True, ["cbf", "osqb"], ["ps7"]))
                q.append(lambda: ACT(r0, bank(7), AF.Ln, ["ps7"], ["tmpf0"], scale=1.0 / 128.0, bias=EPS))
                q.append(lambda: ACT(r0, r0, AF.Exp, ["tmpf0"], ["tmpf0"], scale=-0.5))
                q.append(lambda: TT("dve", ob0, ob0, r0, ALU.mult, ["tmpf1", "tmpf0"], ["tmpf1"]))
                q.append(lambda: STT("dve", ostage[:, h * 512:(h + 1) * 512], ob0, sm[:, SM_SLN:SM_SLN + 1], gbT[:, h * 512:(h + 1) * 512],
                                     ALU.mult, ALU.mult, ["tmpf1", "sln", "kTall1"], ["ostage"]))
                return q
            fin_q.extend(mk_fin())
        while fin_q:
            fin_q.pop(0)()
        DMA(od_d[s], ostage[:], ["ostage"], ["od%d" % s], "ost", eng="sp")
        pending.append("od%d" % s)

    run_stage(P, st)
    if stop == "O2":
        es.close()
        return nc

    st = ExitStack()
    P = Prog("m")
    if stop == "O3":
        P.limit = limit
    WB = sl("WB3", [128, 28800], BF16)
    wst = [sl("wst%d_3" % i, [128, 2048], F32) for i in range(2)]
    xst = [sl("xst0_3", [128, 4 * 515], F32)] * 2
    xb = [sl("xb0_3", [128, 8 * 515], BF16)] * 2
    lnp = sl("lnpt", [128, 2048])
    yT = sl("yT", [128, 4096], BF16)
    oTs = sl("oTs", [128, 4096], BF16)
    merged = sl("merged", [128, 8 * 512], BF16)
    gt = [sl("gt%d" % i, [128, 512]) for i in range(4)]
    xtok = sl("xtok", [128, 4 * 1024])
    tmpf = [sl("tmpf%d_3" % i, [128, 1024]) for i in range(4)]
    DMA(lnp[:], lnp_d, (), ["lnp"], "c2")

    OFF_A, OFF_B, OFF_O, OFF_GM = 0, 8192, 16384, 24576
    load_weights(w_a, 1024, OFF_A, "wk")
    load_weights(w_b, 1024, OFF_B, "wvv")
    load_weights(w_o, 1024, OFF_O, "wx")
    wa_v, wb_v, wo_v = wv(OFF_A, 1024), wv(OFF_B, 1024), wv(OFF_O, 1024)
    gmi = [0]
    for s in range(4):
        i = 0
        xv = load_x(xT_own[:, :, s, :], 515, i, eng="mix")
        DMA(yT[:], ysd_d[s], (), ["yT"], "yld")
        DMA(oTs[:], od_d[s], (), ["oTs"], "old")
        xk = "xb%d" % i
        DMA(xtok[:].rearrange("p (k c) -> p k c", k=4), x_own[s].rearrange("(k p) c -> p k c", p=128), (), ["xtok"], "xtok")
        for db in range(8):
            j = gmi[0] % 2
            gmi[0] += 1
            goff = OFF_GM + j * 2048
            stv = wst[j][:, 0:2048].rearrange("p (a b c) -> p a b c", a=8, b=2)
            DMA(stv, w_gm[:, :, :].rearrange("p a (b c) -> p a b c", b=2)[:, :, :, db * 128:(db + 1) * 128], (), ["wst%d" % j], "wst%d" % j)
            CP("dve", WB[:, goff:goff + 2048].rearrange("p (a b c) -> p a b c", a=8, b=2), stv, ["wst%d" % j], ["wgm%d" % j])
            wg = WB[:, goff:goff + 2048].rearrange("p (a b c) -> p a b c", a=8, b=2)
            par = db % 2
            pg, pa = 4 * par, 2 + 4 * par
            g0, g1 = gt[2 * par], gt[2 * par + 1]
            gk0, gk1 = "gt%d" % (2 * par), "gt%d" % (2 * par + 1)
            for br in range(2):
                for dc in range(8):
                    MM(bank(pg + br), wg[:, dc, br, :], xv[:, dc, 3:515], dc == 0, dc == 7, ["wgm%d" % j, xk], ["ps%d" % (pg + br)])
                ACT(gt[2 * par + br][:], bank(pg + br), AF.Sigmoid, ["ps%d" % (pg + br), "cst"], ["gt%d" % (2 * par + br)],
                    bias=cst[:, C_BG + br * 8 + db:C_BG + br * 8 + db + 1])
            for cc in range(8):
                MM(bank(pa), wa_v[:, cc, db * 128:(db + 1) * 128], yT[:, cc * 512:(cc + 1) * 512], cc == 0, cc == 7,
                   ["wk", "yT"], ["ps%d" % pa])
            for cc in range(8):
                MM(bank(pa + 1), wb_v[:, cc, db * 128:(db + 1) * 128], oTs[:, cc * 512:(cc + 1) * 512], cc == 0, cc == 7,
                   ["wvv", "oTs"], ["ps%d" % (pa + 1)])
            TT("dve", g0[:], g0[:], bank(pa), ALU.mult, [gk0, "ps%d" % pa], [gk0])
            TT("dve", g1[:], g1[:], bank(pa + 1), ALU.mult, [gk1, "ps%d" % (pa + 1)], [gk1])
            TT("dve", merged[:, db * 512:(db + 1) * 512], g0[:], g1[:], ALU.add, [gk0, gk1], ["merged"])
        for k in range(4):
            vb = tmpf[k % 2]
            vk = "tmpf%d" % (k % 2)
            for half in range(2):
                pb = 4 + half
                for db in range(8):
                    MM(bank(pb), merged[:, db * 512 + k * 128:db * 512 + (k + 1) * 128], wo_v[:, db, half * 512:(half + 1) * 512], db == 0, db == 7,
                       ["merged", "wx"], ["ps%d" % pb])
            STT("dve", vb[:], xtok[:, k * 1024:(k + 1) * 1024], ALPHA, PS[2][:], ALU.mult, ALU.add, ["xtok", "ps4", "ps5"], [vk])
            c0 = SM_NM + (k % 2) * 4
            P.add("dve", lambda e, o=sm[:, c0:c0 + 1], ii=vb[:]: e.reduce_sum(o, ii, mybir.AxisListType.X), r=[vk], w=["ln_a%d" % (k % 2)])
            TS("dve", sm[:, c0:c0 + 1], sm[:, c0:c0 + 1], -1.0 / D, None, ALU.mult, None, ["ln_a%d" % (k % 2)], ["ln_b%d" % (k % 2)])
            sq = tmpf[2 + k % 2]
            MSET("dve", sm[:, c0 + 1:c0 + 2], 0.0, ["ln_c%d" % (k % 2)])
            ACT(sq[:], vb[:], AF.Square, [vk, "ln_b%d" % (k % 2), "ln_c%d" % (k % 2)], ["tmpf%d" % (2 + k % 2), "ln_c%d" % (k % 2)],
                bias=sm[:, c0:c0 + 1], accum=sm[:, c0 + 1:c0 + 2])
            TS("dve", sm[:, c0 + 2:c0 + 3], sm[:, c0 + 1:c0 + 2], 1.0 / D, EPS, ALU.mult, ALU.add, ["ln_c%d" % (k % 2)], ["ln_d%d" % (k % 2)])
            ACT(sm[:, c0 + 2:c0 + 3], sm[:, c0 + 2:c0 + 3], AF.Sqrt, ["ln_d%d" % (k % 2)], ["ln_d2%d" % (k % 2)])
            P.add("dve", lambda e, o=sm[:, c0 + 2:c0 + 3]: e.reciprocal(o, o), r=["ln_d2%d" % (k % 2)], w=["ln_e%d" % (k % 2)])
            TS("dve", vb[:], vb[:], sm[:, c0:c0 + 1], sm[:, c0 + 2:c0 + 3], ALU.add, ALU.mult, [vk, "ln_b%d" % (k % 2), "ln_e%d" % (k % 2)], [vk])
            TT("dve", vb[:], vb[:], lnp[:, 0:1024], ALU.mult, [vk, "lnp"], [vk])
            TT("pool", sq[:], vb[:], lnp[:, 1024:2048], ALU.add, [vk, "lnp"], ["tmpf%d" % (2 + k % 2)])
            DMA(y_out[s, k * 128:(k + 1) * 128, :], sq[:], ["tmpf%d" % (2 + k % 2)], ["yout%d_%d" % (s, k)], "yo%d" % (k % 2))
            pending.append("yout%d_%d" % (s, k))
    run_stage(P, st)
    es.close()
    return nc


def _prep_inputs(inputs):
    f = lambda a: np.ascontiguousarray(np.asarray(a, dtype=np.float32))
    x = f(inputs["x"])
    w_in = f(inputs["w_in"])[0]

    def wl(c0, c1):
        return np.ascontiguousarray(w_in[:, c0:c1].reshape(8, 128, c1 - c0).transpose(1, 0, 2))

    def wsq(w):
        return np.ascontiguousarray(f(w)[0].reshape(8, 128, 1024).transpose(1, 0, 2))

    common = {
        "w_z": wl(0, 1024), "w_x": wl(1024, 2560), "w_dt": wl(2560, 2576), "w_q": wl(2576, 3600),
        "w_k": wl(3600, 4624), "w_v": wl(4624, 5648), "w_gb": wl(5648, 6672), "w_gm": wl(6672, 8720),
        "w_a": wsq(inputs["w_a"]), "w_b": wsq(inputs["w_b"]), "w_o": wsq(inputs["w_o"]),
    }
    r = np.arange(128)
    ident = (r[:, None] == r[None, :]).astype(np.float32)
    LT = (r[:, None] <= r[None, :]).astype(np.float32)
    SU = (r[:, None] > r[None, :]).astype(np.float32)
    ones = np.ones((128, 128), np.float32)
    perm = np.zeros((128, 128), np.float32)
    for m in range(2):
        for d in range(8):
            perm[m * 64 + d + 8, m * 64 + d] = -1.0
            perm[m * 64 + d, m * 64 + d + 8] = 1.0
    bc = lambda v, n: np.broadcast_to(f(v).reshape(1, n), (128, n))
    conv_w = f(inputs["conv_w"])[0]
    cw = np.zeros((128, 48), np.float32)
    for blk in range(12):
        for w in range(4):
            cw[:, blk * 4 + w] = conv_w[w, blk * 128:(blk + 1) * 128]
    conv_b = f(inputs["conv_b"])[0]
    cb = conv_b.reshape(12, 128).T
    bg = f(inputs["b_gate"])[0].reshape(16, 128).T
    sln = f(inputs["subln_w"])[0].reshape(128, 1)
    nw = f(inputs["ssd_norm_w"])[0].reshape(8, 128).T
    lnp = np.concatenate([bc(inputs["ln_g"], 1024), bc(inputs["ln_b"], 1024)], axis=1)
    pos = np.arange(S, dtype=np.float32)
    inv_freq = (np.float32(500000.0) ** (-np.arange(0, 16, 2, dtype=np.float32) / np.float32(16))).astype(np.float32)
    ang = (pos[:, None] * inv_freq[None, :]).astype(np.float32)
    cos, sin = np.cos(ang).astype(np.float32), np.sin(ang).astype(np.float32)
    rope = np.zeros((128, 2, S), np.float32)
    rope[:, 0, :] = 1.0
    for m in range(2):
        for dd in range(16):
            rope[m * 64 + dd, 0, :] = cos[:, dd % 8]
            rope[m * 64 + dd, 1, :] = sin[:, dd % 8]
    maskA = np.zeros((8, 512), np.float32)
    for k in range(512):
        maskA[k // 64, k] = 1.0
    in_maps = []
    for c in range(8):
        b, j = c // 4, c % 4
        tiles = [j, 7 - j, 8 + j, 15 - j]
        xb = x[b]
        xT = np.ascontiguousarray(xb.T.reshape(8, 128, S).transpose(1, 0, 2))
        xo = np.zeros((128, 8, 4, 515), np.float32)
        xtok = np.zeros((4, 512, D), np.float32)
        ropeO = np.zeros((128, 2, 4, 512), np.float32)
        sel = np.zeros((128, 16), np.float32)
        maskB = np.zeros((8, 16, 512), np.float32)
        for s, t in enumerate(tiles):
            lo = t * 512
            if t > 0:
                xo[:, :, s, :] = xT[:, :, lo - 3:lo + 512]
            else:
                xo[:, :, s, 3:] = xT[:, :, 0:512]
            xtok[s] = xb[lo:lo + 512]
            ropeO[:, :, s, :] = rope[:, :, lo:lo + 512]
            sel[:, t] = 1.0
            L = KLEN[s]
            for pi in range(4):
                kt = L - 4 + pi
                if kt < t:
                    pass
                elif kt > t:
                    maskB[:, pi + 4 * s, :] = NEG
                else:
                    for rr in range(8):
                        q = np.arange(512)
                        maskB[rr, pi + 4 * s, :] = np.where(q // 64 >= rr, 0.0, NEG)
        cstv = np.concatenate([ident, LT, SU, ones, perm, bc(inputs["a_log"], 16), bc(inputs["dt_bias"], 16), bc(inputs["d_skip"], 16),
                               bc(inputs["lambda_q1"], 64), bc(inputs["lambda_k1"], 64), bc(inputs["lambda_q2"], 64), bc(inputs["lambda_k2"], 64),
                               cw, cb, bg, sln, sel, nw], axis=1).astype(np.float32)
        assert cstv.shape[1] == NCST
        m = dict(common)
        m.update({"xT_all": xT, "xT_own": xo, "x_own": xtok, "cst": np.ascontiguousarray(cstv),
                  "cbrow": conv_b.reshape(1, 1536).copy(), "lnp": np.ascontiguousarray(lnp),
                  "msk": np.ascontiguousarray(np.concatenate([maskA, maskB.reshape(8, 8192)], axis=1)),
                  "ropeA": rope, "ropeO": ropeO})
        in_maps.append(m)
    return in_maps


def kernel(**inputs):
    in_maps = _prep_inputs(inputs)
    nc = build_nc()
    res = run_bass_kernel_spmd(nc, in_maps, core_ids=list(range(8)))
    out = np.zeros((2, S, D), np.float32)
    for c in range(8):
        b, j = c // 4, c % 4
        tiles = [j, 7 - j, 8 + j, 15 - j]
        y = np.asarray(res.results[c]["y_out"], dtype=np.float32)
        for s, t in enumerate(tiles):
            out[b, t * 512:(t + 1) * 512] = y[s]
    return out
```
